# Optimizing a Trainium2 kernel written in Bass

```python
import math
import jax
import jax.numpy as jnp
from jax import lax
import numpy as np

D_MODEL = 1024
BATCH = 1
SEQ = 16384
DEPTH = 2
DEC_BATCH = 8
DEC_SEQ = 16
PAST_LEN = 1024

CHUNK = 64
Q_BLOCK = 128
HEAD_DIM = 64
ROT_DIM = HEAD_DIM // 4
ROPE_THETA = 500000.0
A_HEADS = D_MODEL // (4 * HEAD_DIM)
B_HEADS = D_MODEL // (2 * HEAD_DIM)
C_HEADS = D_MODEL // HEAD_DIM
A_W = 2 * A_HEADS * HEAD_DIM
B_W = B_HEADS * HEAD_DIM
C_PAST_CHUNKS = 8
C_BAND = (C_PAST_CHUNKS + 1) * CHUNK
REL_CLIP = 128
MEM_LEN = 256
MEM_HEADS = 4
MEM_HEAD_DIM = D_MODEL // MEM_HEADS
D_FF = 2816
N_EVEN = (DEPTH + 1) // 2
N_ODD = DEPTH // 2
RMS_EPS = 1e-6
NEG = -1e30

kernel_name = 'hybrid_streaming_encoder_step'


def rms_norm(x, g):
    xf = x.astype(jnp.float32)
    y = xf * lax.rsqrt(jnp.mean(xf * xf, axis=-1, keepdims=True) + RMS_EPS)
    return (y * g.astype(jnp.float32)).astype(x.dtype)


def partial_rope(x, pos):
    half = ROT_DIM // 2
    inv_freq = ROPE_THETA ** (-2.0 * jnp.arange(half, dtype=jnp.float32) / ROT_DIM)
    ang = pos.astype(jnp.float32)[:, None] * inv_freq[None, :]
    cos = jnp.cos(ang)[None, :, None, :]
    sin = jnp.sin(ang)[None, :, None, :]
    xr = x[..., :ROT_DIM].astype(jnp.float32)
    x1, x2 = xr[..., :half], xr[..., half:]
    rot = jnp.concatenate([x1 * cos - x2 * sin, x2 * cos + x1 * sin], axis=-1).astype(x.dtype)
    return jnp.concatenate([rot, x[..., ROT_DIM:]], axis=-1)


def chunk_visible(qpos, kpos):
    return (kpos[None, :] // CHUNK) <= (qpos[:, None] // CHUNK)


def sweep_blocks(fn, q, qpos):
    B, T = q.shape[0], q.shape[1]
    nb = T // Q_BLOCK
    qb = jnp.moveaxis(q.reshape((B, nb, Q_BLOCK) + q.shape[2:]), 1, 0)
    pb = qpos.reshape(nb, Q_BLOCK)
    out = jnp.moveaxis(lax.map(lambda a: fn(a[0], a[1]), (qb, pb)), 0, 1)
    return out.reshape((B, T) + out.shape[3:])


def diff_attention(q, k, v, lam, qpos, kpos):
    B, Tq = q.shape[0], q.shape[1]
    s = jnp.einsum('bqhd,bkhd->bhqk', q, k).astype(jnp.float32) * (HEAD_DIM ** -0.5)
    s = jnp.where(chunk_visible(qpos, kpos), s, NEG)
    p = jax.nn.softmax(s, axis=-1).reshape(B, A_HEADS, 2, Tq, -1)
    a = p[:, :, 0] - lam * p[:, :, 1]
    return jnp.einsum('bhqk,bkhe->bqhe', a.astype(v.dtype), v)


def stick_breaking(q, k, v, qpos, kpos):
    z = jnp.einsum('bqhd,bkhd->bhqk', q, k).astype(jnp.float32) * (HEAD_DIM ** -0.5)
    causal = kpos[None, :] < qpos[:, None]
    log_1m = jnp.where(causal, jax.nn.log_sigmoid(-z), 0.0)
    after = lax.cumsum(log_1m, axis=3, reverse=True) - log_1m
    w = jnp.where(causal, jnp.exp(jax.nn.log_sigmoid(z) + after), 0.0)
    return jnp.einsum('bhqk,bkhd->bqhd', w.astype(v.dtype), v)


def band_attention(q, k, v, bias_table, qpos, kpos):
    s = jnp.einsum('bqhd,bkhd->bhqk', q, k).astype(jnp.float32) * (HEAD_DIM ** -0.5)
    rel = jnp.clip(qpos[:, None] - kpos[None, :], -REL_CLIP, REL_CLIP) + REL_CLIP
    s = s + bias_table.astype(jnp.float32)[:, rel][None]
    qc = qpos[:, None] // CHUNK
    kc = kpos[None, :] // CHUNK
    vis = (kpos[None, :] >= 0) & (kc <= qc) & (qc - kc <= C_PAST_CHUNKS)
    p = jax.nn.softmax(jnp.where(vis, s, NEG), axis=-1)
    return jnp.einsum('bhqk,bkhd->bqhd', p.astype(v.dtype), v)


def mixer_ab(h, pos, past, w_in, w_out, gq, gk, lq1, lk1, lq2, lk2, subln_g, lam_init):
    B, T, _ = h.shape
    proj = h @ w_in
    aq, ak, av, bq, bk, bv = jnp.split(proj, 6, axis=-1)
    aq = partial_rope(rms_norm(aq.reshape(B, T, 2 * A_HEADS, HEAD_DIM), gq), pos)
    ak = partial_rope(rms_norm(ak.reshape(B, T, 2 * A_HEADS, HEAD_DIM), gk), pos)
    av = av.reshape(B, T, A_HEADS, 2 * HEAD_DIM)
    bq = bq.reshape(B, T, B_HEADS, HEAD_DIM)
    bk = bk.reshape(B, T, B_HEADS, HEAD_DIM)
    bv = bv.reshape(B, T, B_HEADS, HEAD_DIM)
    lam = (jnp.exp(jnp.sum(lq1.astype(jnp.float32) * lk1.astype(jnp.float32)))
           - jnp.exp(jnp.sum(lq2.astype(jnp.float32) * lk2.astype(jnp.float32))) + lam_init)
    if past is None:
        kpos = pos
        a_out = sweep_blocks(lambda qb, pb: diff_attention(qb, ak, av, lam, pb, kpos), aq, pos)
        b_out = sweep_blocks(lambda qb, pb: stick_breaking(qb, bk, bv, pb, kpos), bq, pos)
    else:
        pka, pva, pkb, pvb = past
        kpos = jnp.concatenate([jnp.arange(pka.shape[1]), pos])
        a_out = diff_attention(aq, jnp.concatenate([pka, ak], 1), jnp.concatenate([pva, av], 1), lam, pos, kpos)
        b_out = stick_breaking(bq, jnp.concatenate([pkb, bk], 1), jnp.concatenate([pvb, bv], 1), pos, kpos)
    a_out = rms_norm(a_out, subln_g) * (1.0 - lam_init)
    merged = jnp.concatenate([a_out.reshape(B, T, A_W), b_out.reshape(B, T, B_W)], axis=-1)
    return merged @ w_out, (ak, av, bk, bv)


def mixer_c(h, pos, past, w_in, w_out, gq, gk, bias_table):
    B, T, _ = h.shape
    q, k, v = jnp.split(h @ w_in, 3, axis=-1)
    q = rms_norm(q.reshape(B, T, C_HEADS, HEAD_DIM), gq)
    k = rms_norm(k.reshape(B, T, C_HEADS, HEAD_DIM), gk)
    v = v.reshape(B, T, C_HEADS, HEAD_DIM)
    if past is None:
        nc = T // CHUNK
        pad = C_PAST_CHUNKS * CHUNK
        kpad = jnp.pad(k, ((0, 0), (pad, 0), (0, 0), (0, 0)))
        vpad = jnp.pad(v, ((0, 0), (pad, 0), (0, 0), (0, 0)))
        qch = jnp.moveaxis(q.reshape(B, nc, CHUNK, C_HEADS, HEAD_DIM), 1, 0)

        def one_chunk(args):
            qc, c = args
            start = c * CHUNK
            kb = lax.dynamic_slice_in_dim(kpad, start, C_BAND, axis=1)
            vb = lax.dynamic_slice_in_dim(vpad, start, C_BAND, axis=1)
            qpos = start + jnp.arange(CHUNK)
            kpos = start - pad + jnp.arange(C_BAND)
            return band_attention(qc, kb, vb, bias_table, qpos, kpos)

        o = jnp.moveaxis(lax.map(one_chunk, (qch, jnp.arange(nc))), 0, 1).reshape(B, T, C_HEADS, HEAD_DIM)
        keep = min(pad, T)
        new_k, new_v = k[:, T - keep:], v[:, T - keep:]
    else:
        pk, pv = past
        pc = pk.shape[1]
        kall = jnp.concatenate([pk, k], 1)
        vall = jnp.concatenate([pv, v], 1)
        kpos = jnp.concatenate([pos[0] - pc + jnp.arange(pc), pos])
        o = band_attention(q, kall, vall, bias_table, pos, kpos)
        new_k, new_v = kall[:, -pc:], vall[:, -pc:]
    return o.reshape(B, T, D_MODEL) @ w_out, (new_k, new_v)


def memory_kv(mem, g_m, wk, wv, gk):
    B, M, _ = mem.shape
    m = rms_norm(mem, g_m)
    k = rms_norm((m @ wk).reshape(B, M, MEM_HEADS, MEM_HEAD_DIM), gk)
    v = (m @ wv).reshape(B, M, MEM_HEADS, MEM_HEAD_DIM)
    return k, v


def memory_attention(h, mk, mv, wq, wo, gq):
    B, T, _ = h.shape
    q = rms_norm((h @ wq).reshape(B, T, MEM_HEADS, MEM_HEAD_DIM), gq)
    s = jnp.einsum('bqhd,bkhd->bhqk', q, mk).astype(jnp.float32) * (MEM_HEAD_DIM ** -0.5)
    p = jax.nn.softmax(s, axis=-1)
    o = jnp.einsum('bhqk,bkhd->bqhd', p.astype(mv.dtype), mv).reshape(B, T, D_MODEL)
    return o @ wo


def swiglu(h, wg, wu, wd):
    return (jax.nn.silu(h @ wg) * (h @ wu)) @ wd


def lambda_init(layer):
    return 0.8 - 0.6 * math.exp(-0.3 * layer)


def setup_inputs(seed: int = 0) -> dict:
    key = jax.random.key(seed)
    ks = iter(jax.random.split(key, 64))

    def nrm(shape, scale=1.0):
        return scale * jax.random.normal(next(ks), shape, jnp.float32)

    def gain(shape):
        return 1.0 + 0.01 * jax.random.normal(next(ks), shape, jnp.float32)

    L, NE, NO, D, F = DEPTH, N_EVEN, N_ODD, D_MODEL, D_FF
    c_cache = min(C_PAST_CHUNKS * CHUNK, PAST_LEN)
    sd, sf = D ** -0.5, F ** -0.5
    return {
        'x_prompt': nrm((BATCH, SEQ, D)),
        'x_sample': nrm((DEC_BATCH, DEC_SEQ, D)),
        'cache_a_k': nrm((NE, DEC_BATCH, PAST_LEN, 2 * A_HEADS, HEAD_DIM)),
        'cache_a_v': nrm((NE, DEC_BATCH, PAST_LEN, A_HEADS, 2 * HEAD_DIM)),
        'cache_b_k': nrm((NE, DEC_BATCH, PAST_LEN, B_HEADS, HEAD_DIM)),
        'cache_b_v': nrm((NE, DEC_BATCH, PAST_LEN, B_HEADS, HEAD_DIM)),
        'cache_c_k': nrm((NO, DEC_BATCH, c_cache, C_HEADS, HEAD_DIM)),
        'cache_c_v': nrm((NO, DEC_BATCH, c_cache, C_HEADS, HEAD_DIM)),
        'cache_mem_k': nrm((L, DEC_BATCH, MEM_LEN, MEM_HEADS, MEM_HEAD_DIM)),
        'cache_mem_v': nrm((L, DEC_BATCH, MEM_LEN, MEM_HEADS, MEM_HEAD_DIM)),
        'mem_prompt': nrm((BATCH, MEM_LEN, D)),
        'ffn1_g': gain((L, D)),
        'ffn1_wg': nrm((L, D, F), sd),
        'ffn1_wu': nrm((L, D, F), sd),
        'ffn1_wd': nrm((L, F, D), sf),
        'ffn2_g': gain((L, D)),
        'ffn2_wg': nrm((L, D, F), sd),
        'ffn2_wu': nrm((L, D, F), sd),
        'ffn2_wd': nrm((L, F, D), sf),
        'mix_g': gain((L, D)),
        'ab_w_in': nrm((NE, D, 3 * A_W + 3 * B_W), sd),
        'ab_w_out': nrm((NE, A_W + B_W, D), (A_W + B_W) ** -0.5),
        'a_gq': gain((NE, HEAD_DIM)),
        'a_gk': gain((NE, HEAD_DIM)),
        'a_lq1': nrm((NE, HEAD_DIM), 0.1),
        'a_lk1': nrm((NE, HEAD_DIM), 0.1),
        'a_lq2': nrm((NE, HEAD_DIM), 0.1),
        'a_lk2': nrm((NE, HEAD_DIM), 0.1),
        'a_subln_g': gain((NE, 2 * HEAD_DIM)),
        'c_w_in': nrm((NO, D, 3 * D), sd),
        'c_w_out': nrm((NO, D, D), sd),
        'c_gq': gain((NO, HEAD_DIM)),
        'c_gk': gain((NO, HEAD_DIM)),
        'c_bias': nrm((NO, C_HEADS, 2 * REL_CLIP + 1), 0.2),
        'mem_g_x': gain((L, D)),
        'mem_g_m': gain((L, D)),
        'mem_wq': nrm((L, D, D), sd),
        'mem_wk': nrm((L, D, D), sd),
        'mem_wv': nrm((L, D, D), sd),
        'mem_wo': nrm((L, D, D), sd),
        'mem_gq': gain((L, MEM_HEAD_DIM)),
        'mem_gk': gain((L, MEM_HEAD_DIM)),
    }


def reference(x_prompt, x_sample, cache_a_k, cache_a_v, cache_b_k, cache_b_v, cache_c_k, cache_c_v,
              cache_mem_k, cache_mem_v, mem_prompt,
              ffn1_g, ffn1_wg, ffn1_wu, ffn1_wd, ffn2_g, ffn2_wg, ffn2_wu, ffn2_wd, mix_g,
              ab_w_in, ab_w_out, a_gq, a_gk, a_lq1, a_lk1, a_lq2, a_lk2, a_subln_g,
              c_w_in, c_w_out, c_gq, c_gk, c_bias,
              mem_g_x, mem_g_m, mem_wq, mem_wk, mem_wv, mem_wo, mem_gq, mem_gk):

    def trunk(x, pos, mem_k, mem_v, past_ab, past_c):
        new_ab, new_c = [], []
        for l in range(DEPTH):
            x = x + 0.5 * swiglu(rms_norm(x, ffn1_g[l]), ffn1_wg[l], ffn1_wu[l], ffn1_wd[l])
            h = rms_norm(x, mix_g[l])
            i = l // 2
            if l % 2 == 0:
                past = None if past_ab is None else (past_ab[0][i], past_ab[1][i], past_ab[2][i], past_ab[3][i])
                out, rows = mixer_ab(h, pos, past, ab_w_in[i], ab_w_out[i], a_gq[i], a_gk[i],
                                     a_lq1[i], a_lk1[i], a_lq2[i], a_lk2[i], a_subln_g[i], lambda_init(l))
                new_ab.append(rows)
            else:
                past = None if past_c is None else (past_c[0][i], past_c[1][i])
                out, rows = mixer_c(h, pos, past, c_w_in[i], c_w_out[i], c_gq[i], c_gk[i], c_bias[i])
                new_c.append(rows)
            x = x + out
            x = x + memory_attention(rms_norm(x, mem_g_x[l]), mem_k[l], mem_v[l], mem_wq[l], mem_wo[l], mem_gq[l])
            x = x + 0.5 * swiglu(rms_norm(x, ffn2_g[l]), ffn2_wg[l], ffn2_wu[l], ffn2_wd[l])
        return x, new_ab, new_c

    pos_p = jnp.arange(x_prompt.shape[1])
    mem_p = [memory_kv(mem_prompt, mem_g_m[l], mem_wk[l], mem_wv[l], mem_gk[l]) for l in range(DEPTH)]
    mem_k_list = [m[0] for m in mem_p]
    mem_v_list = [m[1] for m in mem_p]
    y_prompt, ab_p, c_p = trunk(x_prompt, pos_p, mem_k_list, mem_v_list, None, None)

    pos_s = PAST_LEN + jnp.arange(x_sample.shape[1])
    y_sample, ab_s, c_s = trunk(x_sample, pos_s, cache_mem_k, cache_mem_v,
                                (cache_a_k, cache_a_v, cache_b_k, cache_b_v), (cache_c_k, cache_c_v))

    a_k_p = jnp.stack([r[0] for r in ab_p])
    a_v_p = jnp.stack([r[1] for r in ab_p])
    b_k_p = jnp.stack([r[2] for r in ab_p])
    b_v_p = jnp.stack([r[3] for r in ab_p])
    c_k_p = jnp.stack([r[0] for r in c_p])
    c_v_p = jnp.stack([r[1] for r in c_p])
    mem_k_p = jnp.stack(mem_k_list)
    mem_v_p = jnp.stack(mem_v_list)
    a_k_s = jnp.stack([r[0] for r in ab_s])
    a_v_s = jnp.stack([r[1] for r in ab_s])
    b_k_s = jnp.stack([r[2] for r in ab_s])
    b_v_s = jnp.stack([r[3] for r in ab_s])
    c_k_s = jnp.stack([r[0] for r in c_s])
    c_v_s = jnp.stack([r[1] for r in c_s])
    return (y_prompt, y_sample, a_k_p, a_v_p, b_k_p, b_v_p, c_k_p, c_v_p, mem_k_p, mem_v_p,
            a_k_s, a_v_s, b_k_s, b_v_s, c_k_s, c_v_s)
```

```python
import os
import math
import numpy as np
import concourse.bass as bass
import concourse.mybir as mybir
from concourse.bass_utils import run_bass_kernel_spmd

F32, BF16 = mybir.dt.float32, mybir.dt.bfloat16
ALU = mybir.AluOpType
AF = mybir.ActivationFunctionType
AX = mybir.AxisListType

D = 1024
FF = 2816
NT = 128
NQ = 20
NOWN = 16
SMP = 128
EPS = 1e-6
NDMA = 24
ROT = 30000
STAGE = int(os.environ.get("KSTAGE", "9"))


class Buf:
    __slots__ = ("w", "r")

    def __init__(self):
        self.w = None
        self.r = {}


class Sched:
    def __init__(self, nc):
        self.nc = nc
        self.E = {"pe": nc.tensor, "act": nc.scalar, "dve": nc.vector, "pool": nc.gpsimd, "sp": nc.sync}
        self.sem, self.cnt, self.nsem = {}, {}, 0
        self.seen = {e: {} for e in self.E}
        self.allsems = []
        for e in self.E:
            self._rot(e)
        self.dsem = [nc.alloc_semaphore(f"dq{i}") for i in range(NDMA)]
        self.dcnt = [0] * NDMA
        self.dnext = 0
        self.nops = 0

    def _rot(self, e):
        self.sem[e] = self.nc.alloc_semaphore(f"s{e}{self.nsem}")
        self.nsem += 1
        self.cnt[e] = 0
        self.allsems.append([self.sem[e], 0])

    def _wait(self, e, tok):
        sem, val = tok[1], tok[2]
        if self.seen[e].get(sem.num, 0) >= val:
            return
        self.E[e].wait_ge(sem, val)
        self.seen[e][sem.num] = val

    def op(self, e, fn, reads=(), writes=(), dma=False):
        for b in reads:
            if b.w is not None:
                t = b.w
                if not (t[0] == e and e == "pe" and not t[3]):
                    self._wait(e, t)
        for b in writes:
            for t in ([b.w] if b.w is not None else []) + list(b.r.values()):
                if t[0] == e and not t[3]:
                    continue
                self._wait(e, t)
        if dma:
            i = self.dnext
            self.dnext = (i + 1) % NDMA
            if self.dcnt[i] > 0:
                self._wait(e, (e, self.dsem[i], 16 * self.dcnt[i], True))
            ins = fn(self.E[e])
            self.dcnt[i] += 1
            ins.then_inc(self.dsem[i], 16)
            tok = (e, self.dsem[i], 16 * self.dcnt[i], True)
        else:
            if self.cnt[e] >= ROT:
                self._rot(e)
            ins = fn(self.E[e])
            self.cnt[e] += 1
            ins.then_inc(self.sem[e], 1)
            tok = (e, self.sem[e], self.cnt[e], False)
            for s in self.allsems:
                if s[0] is self.sem[e]:
                    s[1] = self.cnt[e]
        for b in reads:
            b.r[tok[1].num] = tok
        for b in writes:
            b.w = tok
            b.r = {}
        self.nops += 1
        return tok

    def barrier(self):
        for e in self.E:
            for s, v in self.allsems:
                if v > 0:
                    self._wait(e, (None, s, v, False))
            for i in range(NDMA):
                if self.dcnt[i] > 0:
                    self._wait(e, (None, self.dsem[i], 16 * self.dcnt[i], True))


def build():
    nc = bass.Bass("TRN2", target_bir_lowering=False)
    S = Sched(nc)

    def din(name, shape):
        return nc.dram_tensor(name, list(shape), F32, kind="ExternalInput").ap()

    def dout(name, shape):
        return nc.dram_tensor(name, list(shape), F32, kind="ExternalOutput").ap()

    def sb(name, shape, dt=F32):
        return nc.alloc_sbuf_tensor("sb_" + name, list(shape), dt).ap()

    xp = din("xp", [NT, 128, D])
    xs = din("xs", [16, D])
    posd = din("pos", [128, NT + 1])
    validd = din("valid", [128, NT + 1])
    constd = din("consts", [128, 776])
    ca_k = din("ca_k", [1024, 512]); ca_v = din("ca_v", [1024, 512])
    cb_k = din("cb_k", [1024, 512]); cb_v = din("cb_v", [1024, 512])
    cc_k = din("cc_k", [512, 1024]); cc_v = din("cc_v", [512, 1024])
    cm_k = din("cm_k", [2, 256, 1024]); cm_v = din("cm_v", [2, 256, 1024])
    memp = din("memp", [256, 1024])
    W = {}
    for nm, shp in [("ffn1_g", [2, D]), ("ffn1_wg", [2, D, FF]), ("ffn1_wu", [2, D, FF]), ("ffn1_wd", [2, FF, D]),
                    ("ffn2_g", [2, D]), ("ffn2_wg", [2, D, FF]), ("ffn2_wu", [2, D, FF]), ("ffn2_wd", [2, FF, D]),
                    ("mix_g", [2, D]), ("ab_w_in", [1, D, 3072]), ("ab_w_out", [1, D, D]),
                    ("a_gq", [1, 64]), ("a_gk", [1, 64]), ("a_lq1", [1, 64]), ("a_lk1", [1, 64]),
                    ("a_lq2", [1, 64]), ("a_lk2", [1, 64]), ("a_subln_g", [1, 128]),
                    ("c_w_in", [1, D, 3072]), ("c_w_out", [1, D, D]), ("c_gq", [1, 64]), ("c_gk", [1, 64]),
                    ("c_bias", [1, 16, 257]), ("mem_g_x", [2, D]), ("mem_g_m", [2, D]),
                    ("mem_wq", [2, D, D]), ("mem_wk", [2, D, D]), ("mem_wv", [2, D, D]), ("mem_wo", [2, D, D]),
                    ("mem_gq", [2, 256]), ("mem_gk", [2, 256])]:
        W[nm] = din(nm, shp)

    o_y = dout("o_y", [NOWN, 128, D]); o_ys = dout("o_ys", [16, D])
    o_ak = dout("o_ak", [NOWN, 128, 512]); o_av = dout("o_av", [NOWN, 128, 512])
    o_bk = dout("o_bk", [NOWN, 128, 512]); o_bv = dout("o_bv", [NOWN, 128, 512])
    o_ck = dout("o_ck", [4, 128, D]); o_cv = dout("o_cv", [4, 128, D])
    o_mk = dout("o_mk", [2, 256, D]); o_mv = dout("o_mv", [2, 256, D])
    o_aks = dout("o_aks", [16, 512]); o_avs = dout("o_avs", [16, 512])
    o_bks = dout("o_bks", [16, 512]); o_bvs = dout("o_bvs", [16, 512])
    o_cks = dout("o_cks", [512, D]); o_cvs = dout("o_cvs", [512, D])

    def dscr(name, shape):
        return nc.dram_tensor(name, list(shape), BF16).ap()
    kta_s = dscr("kta_s", [NT + 9, 128, 512]); ktb_s = dscr("ktb_s", [NT + 9, 128, 512])
    va_s = dscr("va_s", [NT + 9, 128, 520]); vb_s = dscr("vb_s", [NT + 9, 128, 512])
    qta_s = dscr("qta_s", [NQ + 1, 128, 512]); qtb_s = dscr("qtb_s", [NQ + 1, 128, 512])
    scrB = {"kta": [Buf() for _ in range(NT + 9)], "ktb": [Buf() for _ in range(NT + 9)],
            "va": [Buf() for _ in range(NT + 9)], "vb": [Buf() for _ in range(NT + 9)],
            "qta": [Buf() for _ in range(NQ + 1)], "qtb": [Buf() for _ in range(NQ + 1)]}
    outB = Buf()
    inB = Buf()

    NX = NQ + 1
    X = sb("X", [128, NX, D])
    Xb = [Buf() for _ in range(NX)]
    HT = sb("HT", [128, 8, NQ * 128 + 16], BF16)
    HTb = [Buf() for _ in range(NX)]
    cst = sb("cst", [128, 776]); cstB = Buf()
    ident = sb("ident", [128, 128], BF16); negU = sb("negU", [128, 128], BF16)
    maskA = sb("maskA", [128, 128], BF16); maskB = sb("maskB", [128, 128], BF16)
    ones = sb("ones", [128, 1], BF16)
    cB = Buf()
    pos = sb("pos", [128, NT + 1]); valid = sb("valid", [128, NT + 1]); validb = sb("validb", [128, NT + 1], BF16)
    gT = sb("gT", [128, 80]); gTB = Buf()
    g64 = {n: sb("g_" + n, [128, 64]) for n in ("a_gq", "a_gk", "c_gq", "c_gk")}
    gB = Buf()

    S.op("sp", lambda e: e.dma_start(out=cst, in_=constd), [], [cstB], dma=True)
    S.op("sp", lambda e: e.dma_start(out=pos, in_=posd), [], [cB], dma=True)
    S.op("sp", lambda e: e.dma_start(out=valid, in_=validd), [], [cB], dma=True)
    S.op("dve", lambda e: e.tensor_copy(ident, cst[:, 0:128]), [cstB], [cB])
    S.op("dve", lambda e: e.tensor_copy(negU, cst[:, 128:256]), [cstB], [cB])
    S.op("dve", lambda e: e.tensor_copy(maskA, cst[:, 256:384]), [cstB], [cB])
    S.op("dve", lambda e: e.tensor_copy(maskB, cst[:, 384:512]), [cstB], [cB])
    S.op("dve", lambda e: e.memset(ones, 1.0), [], [cB])
    S.op("dve", lambda e: e.tensor_copy(validb, valid), [cB], [cB])
    for n in g64:
        S.op("sp", lambda e, n=n: e.dma_start(out=g64[n], in_=W[n][0:1, :].partition_broadcast(128)), [], [gB], dma=True)
    for n in ("a_gq", "c_gq"):
        S.op("dve", lambda e, n=n: e.tensor_scalar(g64[n], g64[n], 0.125, None, ALU.mult), [gB], [gB])

    Q = [nc.alloc_psum_tensor(f"q{i}", [128, 1024], F32).ap() for i in range(4)]
    PS = [Q[i // 2][:, (i % 2) * 512:(i % 2 + 1) * 512] for i in range(6)]
    PSB = [Buf() for _ in range(6)]
    PT = [Q[3][:, h * 512:(h + 1) * 512].bitcast(BF16) for h in range(2)]
    PTB = [Buf() for _ in range(2)]
    P1 = Q[3][:, 0:512]
    P1B = PTB[0]
    rr = {"ps": 0, "pt": 0, "ptn": 2}

    def getps():
        i = rr["ps"]; rr["ps"] = (i + 1) % 6
        return PS[i], PSB[i]

    def getpt():
        if rr["ptn"] == 1:
            return PT[1], PTB[1]
        i = rr["pt"]; rr["pt"] = (i + 1) % 2
        return PT[i], PTB[i]

    GID = {"ffn1_g": 0, "ffn2_g": 2, "mix_g": 4, "mem_g_x": 6, "mem_g_m": 8}
    graw = sb("graw", [80, 128]); grawB = Buf()
    for n, gi in GID.items():
        S.op("sp", lambda e, n=n, gi=gi: e.dma_start(out=graw[gi * 8:(gi + 2) * 8, :],
                                                     in_=W[n].rearrange("l (k p) -> (l k) p", p=128)), [], [grawB], dma=True)
    identf = cst[:, 0:128]
    pg, pgB = getps()
    S.op("pe", lambda e: e.transpose(pg[:, 0:80], graw[0:80, :], identf[0:80, 0:80]), [grawB, cstB], [pgB])
    S.op("dve", lambda e: e.tensor_copy(gT, pg[:, 0:80]), [pgB], [gTB])

    ARENA = 15360
    arena = sb("arena", [128, ARENA])

    def av(off, n, dt=F32):
        v = arena[:, off:off + n]
        return v if dt == F32 else v.bitcast(dt)
    Wb = [av(3072 * i, 3072, BF16).rearrange("p (a f) -> p a f", a=3) for i in range(2)]
    WbB = [[Buf() for _ in range(3)] for _ in range(2)]
    stg = [av(6144 + 2048 * i, 2048) for i in range(3)]
    stgB = [Buf() for _ in range(3)]
    hid = [av(12288 + 512 * i, 512, BF16).rearrange("p (j t) -> p j t", j=2) for i in range(2)]
    hidB = [Buf() for _ in range(2)]
    sgt = [av(13312 + 512 * i, 512) for i in range(2)]
    sgB = [Buf() for _ in range(2)]
    xn = [av(14336 + 512 * i, 512, BF16) for i in range(2)]
    xnB = [Buf() for _ in range(2)]
    junk = sb("junk", [128, D], BF16); junkB = Buf()
    st = [sb(f"st{i}", [128, 8]) for i in range(4)]
    stB = [Buf() for _ in range(4)]
    cosT = sb("cosT", [128, NX, 8]); sinT = sb("sinT", [128, NX, 8]); csB = Buf()
    wk = [sb(f"wk{i}", [128, 256]) for i in range(4)]
    wkB = [Buf() for _ in range(4)]
    wkb = [sb(f"wkb{i}", [128, 264], BF16) for i in range(2)]
    wkbB = [Buf() for _ in range(2)]
    ktile = [sb(f"ktile{i}", [128, 2, 128], BF16) for i in range(2)]
    ktB = [Buf() for _ in range(2)]
    rp = [sb(f"rp{i}", [128, 4, 8]) for i in range(4)]
    rpB = [Buf() for _ in range(4)]
    cnt = {"stg": 0, "xn": 0, "st": 0, "wk": 0, "wkb": 0, "kt": 0, "hid": 0, "sg": 0}

    def nxt(k, n):
        i = cnt[k]; cnt[k] = (i + 1) % n
        return i

    rs = sb("rs", [128, NX]); rsB = Buf()

    def rstd_tiles(tiles):
        S.op("dve", lambda e: e.memset(rs, 0.0), [], [rsB])
        for (i, rows) in tiles:
            S.op("act", lambda e, i=i, rows=rows: e.activation(junk[:rows, :], X[:rows, i, :], AF.Square, accum_out=rs[:rows, i:i + 1]),
                 [Xb[i]], [junkB, rsB])
        S.op("dve", lambda e: e.tensor_scalar(rs, rs, 1.0 / D, EPS, ALU.mult, ALU.add), [rsB], [rsB])
        S.op("act", lambda e: e.activation(rs, rs, AF.Ln), [rsB], [rsB])
        S.op("act", lambda e: e.activation(rs, rs, AF.Exp, scale=-0.5), [rsB], [rsB])

    def norm_to_HT(i, rows):
        r, rB = rs[:, i:i + 1], rsB
        j = nxt("xn", 2)
        S.op("pool", lambda e: e.tensor_scalar(xn[j][:rows, :], X[:rows, i, :], r[:rows, 0:1], None, ALU.mult),
             [Xb[i], rB], [xnB[j]])
        pt, ptB = getpt()
        for k in range(8):
            S.op("pe", lambda e, k=k: e.transpose(pt[:, k * 128:k * 128 + rows], xn[j][:rows, k * 128:(k + 1) * 128],
                                                  ident[:rows, :rows]), [xnB[j], cB], [ptB])
        src = pt.rearrange("p (k t) -> p k t", k=8)[:, :, 0:rows]
        dst = HT[:, :, i * 128:i * 128 + rows]
        if i % 2 == 0:
            S.op("act", lambda e: e.copy(dst, src), [ptB], [HTb[i]])
        else:
            S.op("dve", lambda e: e.tensor_copy(dst, src), [ptB], [HTb[i]])

    def load_w(dst, dstB, src_ap, gcol, kparts, width):
        s = nxt("stg", lim["stg"])
        S.op("sp", lambda e: e.dma_start(out=stg[s][:, 0:kparts * width].rearrange("p (k f) -> p k f", k=kparts),
                                         in_=src_ap.rearrange("(k p) f -> p k f", p=128)), [], [stgB[s]], dma=True)
        d3 = dst.rearrange("p (k f) -> p k f", k=kparts)
        s3 = stg[s][:, 0:kparts * width].rearrange("p (k f) -> p k f", k=kparts)
        if gcol is None:
            S.op("pool", lambda e: e.tensor_copy(d3, s3), [stgB[s]], [dstB])
        else:
            S.op("pool", lambda e: e.tensor_tensor(d3, s3, gT[:, gcol:gcol + kparts].unsqueeze(2).to_broadcast([128, kparts, width]),
                                                   ALU.mult), [stgB[s], gTB], [dstB])

    def ffn(l, which, tiles):
        pre = f"ffn{which}_"
        gcol = (GID[pre + "g"] + l) * 8
        rstd_tiles(tiles)
        for (i, rows) in tiles:
            norm_to_HT(i, rows)
        blocks = [tiles[b:b + 4] for b in range(0, len(tiles), 4)]
        for fg in range(FF // 256):
            ws = fg % 2
            load_w(Wb[ws][:, 0, :], WbB[ws][0], W[pre + "wg"][l][:, fg * 256:(fg + 1) * 256], gcol, 8, 256)
            load_w(Wb[ws][:, 1, :], WbB[ws][1], W[pre + "wu"][l][:, fg * 256:(fg + 1) * 256], gcol, 8, 256)
            load_w(Wb[ws][:, 2, :], WbB[ws][2], W[pre + "wd"][l][fg * 256:(fg + 1) * 256, :], None, 2, 1024)
            wg3 = Wb[ws][:, 0, :].rearrange("p (k f) -> p k f", k=8)
            wu3 = Wb[ws][:, 1, :].rearrange("p (k f) -> p k f", k=8)
            wd3 = Wb[ws][:, 2, :].rearrange("p (k f) -> p k f", k=2)
            for blk in blocks:
                c0 = blk[0][0] * 128
                ncol = (blk[-1][0] - blk[0][0]) * 128 + blk[-1][1]
                hB = [HTb[i] for i, _ in blk]
                hi = nxt("hid", 2)
                for j in range(2):
                    pgt, pgtB = getps()
                    put, putB = getps()
                    for k in range(8):
                        S.op("pe", lambda e, k=k, j=j: e.matmul(pgt[:, 0:ncol], wg3[:, k, j * 128:(j + 1) * 128], HT[:, k, c0:c0 + ncol],
                                                                start=(k == 0), stop=(k == 7)), hB + [WbB[ws][0]], [pgtB])
                    for k in range(8):
                        S.op("pe", lambda e, k=k, j=j: e.matmul(put[:, 0:ncol], wu3[:, k, j * 128:(j + 1) * 128], HT[:, k, c0:c0 + ncol],
                                                                start=(k == 0), stop=(k == 7)), hB + [WbB[ws][1]], [putB])
                    si = nxt("sg", 2)
                    S.op("act", lambda e: e.activation(sgt[si][:, 0:ncol], pgt[:, 0:ncol], AF.Silu), [pgtB], [sgB[si]])
                    S.op("dve", lambda e, j=j: e.tensor_tensor(hid[hi][:, j, 0:ncol], sgt[si][:, 0:ncol], put[:, 0:ncol], ALU.mult),
                         [sgB[si], putB], [hidB[hi]])
                for (i, rows) in blk:
                    o = i * 128 - c0
                    for half in range(2):
                        pd, pdB = getps()
                        for j in range(2):
                            S.op("pe", lambda e, j=j: e.matmul(pd[:rows, :], hid[hi][:, j, o:o + rows], wd3[:, j, half * 512:(half + 1) * 512],
                                                               start=(j == 0), stop=(j == 1)), [hidB[hi], WbB[ws][2]], [pdB])
                        S.op("dve", lambda e: e.scalar_tensor_tensor(X[:rows, i, half * 512:(half + 1) * 512], pd[:rows, :], 0.5,
                                                                     X[:rows, i, half * 512:(half + 1) * 512], ALU.mult, ALU.add),
                             [pdB, Xb[i]], [Xb[i]])

    rt = sb("rt", [128, NX, 8]); rtf = sb("rtf", [128, NX, 8]); rti = sb("rti", [128, NX, 8], mybir.dt.int32); rtB = Buf()

    def rope_tables(i0, n, u0):
        sl = slice(i0, i0 + n)
        pb = pos[:, u0:u0 + n].unsqueeze(2).to_broadcast([128, n, 8])
        fb = cst[:, 512:520].unsqueeze(1).to_broadcast([128, n, 8])
        for tab, shift in ((sinT, 0.0), (cosT, 0.25)):
            S.op("dve", lambda e: e.tensor_tensor(rt[:, sl, :], pb, fb, ALU.mult), [cB, cstB], [rtB])
            if shift:
                S.op("dve", lambda e, shift=shift: e.tensor_scalar(rt[:, sl, :], rt[:, sl, :], shift, None, ALU.add), [rtB], [rtB])
            S.op("dve", lambda e: e.tensor_copy(rti[:, sl, :], rt[:, sl, :]), [rtB], [rtB])
            S.op("dve", lambda e: e.tensor_copy(rtf[:, sl, :], rti[:, sl, :]), [rtB], [rtB])
            S.op("dve", lambda e: e.tensor_tensor(rt[:, sl, :], rt[:, sl, :], rtf[:, sl, :], ALU.subtract), [rtB], [rtB])
            S.op("dve", lambda e: e.tensor_scalar(rtf[:, sl, :], rt[:, sl, :], 0.5, None, ALU.is_gt), [rtB], [rtB])
            S.op("dve", lambda e: e.tensor_tensor(rt[:, sl, :], rt[:, sl, :], rtf[:, sl, :], ALU.subtract), [rtB], [rtB])
            S.op("dve", lambda e: e.tensor_scalar(rtf[:, sl, :], rt[:, sl, :], -0.5, None, ALU.is_lt), [rtB], [rtB])
            S.op("dve", lambda e: e.tensor_tensor(rt[:, sl, :], rt[:, sl, :], rtf[:, sl, :], ALU.add), [rtB], [rtB])
            S.op("act", lambda e, tab=tab: e.activation(tab[:, sl, :], rt[:, sl, :], AF.Sin, scale=2 * math.pi), [rtB], [csB, rtB])

    def to_T_and_store(src_bf, srcB, rows, dst_ap, dstB):
        pt, ptB = getpt()
        for k in range(2):
            S.op("pe", lambda e, k=k: e.transpose(pt[:, k * 128:k * 128 + rows], src_bf[:rows, k * 128:(k + 1) * 128], ident[:rows, :rows]),
                 [srcB, cB], [ptB])
        t = nxt("kt", 2)
        S.op("act", lambda e: e.copy(ktile[t][:, :, 0:rows], pt[:, 0:256].rearrange("p (k t) -> p k t", k=2)[:, :, 0:rows]), [ptB], [ktB[t]])
        S.op("pool", lambda e: e.dma_start(out=dst_ap, in_=ktile[t][:, :, 0:rows]), [ktB[t]], [dstB], dma=True)

    def proj_ab(tiles, us):
        gcol = (GID["mix_g"] + 0) * 8
        for pc in range(int(os.environ.get("KPC", "12"))):
            ws = pc % 2
            kind, hp = ("aq", "ak", "av", "bq", "bk", "bv")[pc // 2], pc % 2
            load_w(Wb[ws][:, 0, :], WbB[ws][0], W["ab_w_in"][0][:, pc * 256:(pc + 1) * 256], gcol, 8, 256)
            w3 = Wb[ws][:, 0, :].rearrange("p (k f) -> p k f", k=8)
            for (i, rows), u in zip(tiles, us):
                own = (u < NOWN) or (u == SMP)
                isq = kind in ("aq", "bq")
                if isq and not (u < NQ or u == SMP):
                    continue
                qi = NQ if u == SMP else u
                pp, ppB = getps()
                for k in range(8):
                    S.op("pe", lambda e, k=k: e.matmul(pp[:rows, 0:256], HT[:, k, i * 128:i * 128 + rows], w3[:, k, :],
                                                       start=(k == 0), stop=(k == 7)), [HTb[i], WbB[ws][0]], [ppB])
                src = pp[:rows, 0:256]
                csl = slice(hp * 256, (hp + 1) * 256)
                if kind in ("aq", "ak"):
                    g = g64["a_gq" if kind == "aq" else "a_gk"]
                    a, b = nxt("wk", 4), nxt("wk", 4)
                    s_ = nxt("st", 4)
                    S.op("act", lambda e: e.copy(wk[b][:rows, :], src), [ppB], [wkB[b]])
                    S.op("dve", lambda e: e.tensor_tensor(wk[a][:rows, :], wk[b][:rows, :], wk[b][:rows, :], ALU.mult), [wkB[b]], [wkB[a]])
                    S.op("dve", lambda e: e.tensor_reduce(st[s_][:rows, 0:4], wk[a][:rows, :].rearrange("p (h d) -> p h d", h=4), AX.X, ALU.add),
                         [wkB[a]], [stB[s_]])
                    S.op("dve", lambda e: e.tensor_scalar(st[s_][:rows, 0:4], st[s_][:rows, 0:4], 1.0 / 64, EPS, ALU.mult, ALU.add), [stB[s_]], [stB[s_]])
                    S.op("act", lambda e: e.activation(st[s_][:rows, 0:4], st[s_][:rows, 0:4], AF.Ln), [stB[s_]], [stB[s_]])
                    S.op("act", lambda e: e.activation(st[s_][:rows, 0:4], st[s_][:rows, 0:4], AF.Exp, scale=-0.5), [stB[s_]], [stB[s_]])
                    w3a = wk[a][:rows, :].rearrange("p (h d) -> p h d", h=4)
                    w3b = wk[b][:rows, :].rearrange("p (h d) -> p h d", h=4)
                    S.op("dve", lambda e: e.tensor_tensor(w3a, src.rearrange("p (h d) -> p h d", h=4),
                                                          st[s_][:rows, 0:4].unsqueeze(2).to_broadcast([rows, 4, 64]), ALU.mult), [ppB, stB[s_]], [wkB[a]])
                    S.op("dve", lambda e: e.tensor_tensor(w3b, w3a, g[:rows, :].unsqueeze(1).to_broadcast([rows, 4, 64]), ALU.mult), [wkB[a], gB], [wkB[b]])
                    cs = cosT[:rows, i, :].unsqueeze(1).to_broadcast([rows, 4, 8])
                    sn = sinT[:rows, i, :].unsqueeze(1).to_broadcast([rows, 4, 8])
                    x1, x2 = w3b[:, :, 0:8], w3b[:, :, 8:16]
                    r = [rp[q][:rows] for q in range(4)]
                    S.op("dve", lambda e: e.tensor_tensor(r[0], x1, cs, ALU.mult), [wkB[b], csB], [rpB[0]])
                    S.op("dve", lambda e: e.tensor_tensor(r[1], x2, sn, ALU.mult), [wkB[b], csB], [rpB[1]])
                    S.op("dve", lambda e: e.tensor_tensor(r[2], x2, cs, ALU.mult), [wkB[b], csB], [rpB[2]])
                    S.op("dve", lambda e: e.tensor_tensor(r[3], x1, sn, ALU.mult), [wkB[b], csB], [rpB[3]])
                    S.op("dve", lambda e: e.tensor_tensor(x1, r[0], r[1], ALU.subtract), [rpB[0], rpB[1]], [wkB[b]])
                    S.op("dve", lambda e: e.tensor_tensor(x2, r[2], r[3], ALU.add), [rpB[2], rpB[3]], [wkB[b]])
                    fin, finB = wk[b], wkB[b]
                    if kind == "ak" and own:
                        dst = o_aks[:, csl] if u == SMP else o_ak[u][:, csl]
                        S.op("pool", lambda e, dst=dst: e.dma_start(out=dst[:rows], in_=fin[:rows, :]), [finB], [], dma=True)
                    c = nxt("wkb", 2)
                    S.op("pool", lambda e: e.tensor_copy(wkb[c][:rows, 0:256], fin[:rows, :]), [finB], [wkbB[c]])
                    if kind == "ak":
                        to_T_and_store(wkb[c], wkbB[c], rows, kta_s[u].rearrange("p (k t) -> p k t", k=4)[:, hp * 2:hp * 2 + 2, 0:rows], scrB["kta"][u])
                    else:
                        to_T_and_store(wkb[c], wkbB[c], rows, qta_s[qi].rearrange("p (k t) -> p k t", k=4)[:, hp * 2:hp * 2 + 2, 0:rows], scrB["qta"][qi])
                elif kind in ("bq", "bk"):
                    c = nxt("wkb", 2)
                    if kind == "bk":
                        a = nxt("wk", 4)
                        S.op("act", lambda e: e.copy(wk[a][:rows, :], src), [ppB], [wkB[a]])
                        if own:
                            dst = o_bks[:, csl] if u == SMP else o_bk[u][:, csl]
                            S.op("pool", lambda e, dst=dst: e.dma_start(out=dst[:rows], in_=wk[a][:rows, :]), [wkB[a]], [], dma=True)
                        S.op("dve", lambda e: e.tensor_copy(wkb[c][:rows, 0:256], wk[a][:rows, :]), [wkB[a]], [wkbB[c]])
                        to_T_and_store(wkb[c], wkbB[c], rows, ktb_s[u].rearrange("p (k t) -> p k t", k=4)[:, hp * 2:hp * 2 + 2, 0:rows], scrB["ktb"][u])
                    else:
                        S.op("dve", lambda e: e.tensor_scalar(wkb[c][:rows, 0:256], src, 0.125, None, ALU.mult), [ppB], [wkbB[c]])
                        to_T_and_store(wkb[c], wkbB[c], rows, qtb_s[qi].rearrange("p (k t) -> p k t", k=4)[:, hp * 2:hp * 2 + 2, 0:rows], scrB["qtb"][qi])
                else:
                    a = nxt("wk", 4)
                    S.op("act", lambda e: e.copy(wk[a][:rows, :], src), [ppB], [wkB[a]])
                    if own:
                        od = {"av": (o_avs, o_av), "bv": (o_bvs, o_bv)}[kind]
                        dst = od[0][:, csl] if u == SMP else od[1][u][:, csl]
                        S.op("pool", lambda e, dst=dst: e.dma_start(out=dst[:rows], in_=wk[a][:rows, :]), [wkB[a]], [], dma=True)
                    c = nxt("wkb", 2)
                    if kind == "av":
                        v3 = wkb[c][:rows, 0:260].rearrange("p (h d) -> p h d", h=2)
                        S.op("dve", lambda e: e.tensor_scalar(v3[:, :, 0:128], wk[a][:rows, :].rearrange("p (h d) -> p h d", h=2), valid[:rows, u:u + 1], None, ALU.mult),
                             [wkB[a], cB], [wkbB[c]])
                        S.op("dve", lambda e: e.tensor_copy(v3[:, :, 128:130], valid[:rows, u:u + 1].unsqueeze(1).to_broadcast([rows, 2, 2])), [cB], [wkbB[c]])
                        S.op("pool", lambda e: e.dma_start(out=va_s[u][:rows, hp * 260:(hp + 1) * 260], in_=wkb[c][:rows, 0:260]), [wkbB[c]], [scrB["va"][u]], dma=True)
                    else:
                        S.op("dve", lambda e: e.tensor_scalar(wkb[c][:rows, 0:256], wk[a][:rows, :], valid[:rows, u:u + 1], None, ALU.mult), [wkB[a], cB], [wkbB[c]])
                        S.op("pool", lambda e: e.dma_start(out=vb_s[u][:rows, csl], in_=wkb[c][:rows, 0:256]), [wkbB[c]], [scrB["vb"][u]], dma=True)

    def front0(us, with_sample):
        tiles = [(i, 128) for i in range(len(us))]
        uu = list(us)
        for i, u in enumerate(us):
            S.op("sp", lambda e, i=i, u=u: e.dma_start(out=X[:, i, :], in_=xp[u]), [], [Xb[i]], dma=True)
        if with_sample:
            i = len(us)
            S.op("sp", lambda e: e.dma_start(out=X[0:16, i, :], in_=xs), [], [Xb[i]], dma=True)
            tiles.append((i, 16)); uu.append(SMP)
        DBG = int(os.environ.get("KDBG", "9"))
        if DBG >= 1:
            rope_tables(0, len(us), us[0])
            if with_sample:
                rope_tables(len(us), 1, SMP)
        if DBG >= 3:
            ffn(0, 1, tiles)
        if DBG >= 2:
            rstd_tiles(tiles)
            for (i, rows) in tiles:
                norm_to_HT(i, rows)
        if DBG >= 4:
            proj_ab(tiles, uu)
        return tiles, uu

    SQ = NQ
    CIDX = NT + 1
    small = {n: sb("sm_" + n, shp) for n, shp in (("den", [128, 4, 8]), ("carry", [128, 4, 8]), ("ecarry", [128, 4, 8]),
                                                   ("rden", [128, 8]), ("s4", [128, 4]), ("lam", [128, 4]))}
    smB = {n: Buf() for n in small}
    lt = sb("lt", [128, 4, 64]); ltB = Buf()
    gsub = sb("gsub", [128, 128]); gsubB = Buf()

    def cache_prep():
        for t in range(8):
            rsl = slice(t * 128, (t + 1) * 128)
            for (src, dst, key) in ((ca_k, kta_s, "kta"), (cb_k, ktb_s, "ktb")):
                s_ = nxt("stg", 3)
                S.op("sp", lambda e, src=src: e.dma_start(out=stg[s_][:, 0:512], in_=src[rsl, :]), [], [stgB[s_]], dma=True)
                j = nxt("xn", 2)
                S.op("pool", lambda e: e.tensor_copy(xn[j][:, 0:512], stg[s_][:, 0:512]), [stgB[s_]], [xnB[j]])
                pt, ptB = getpt()
                for k in range(4):
                    S.op("pe", lambda e, k=k: e.transpose(pt[:, k * 128:(k + 1) * 128], xn[j][:, k * 128:(k + 1) * 128], ident), [xnB[j], cB], [ptB])
                S.op("act", lambda e: e.copy(xn[j][:, 512:1024], pt[:, 0:512]), [ptB], [xnB[j]])
                S.op("pool", lambda e, dst=dst: e.dma_start(out=dst[CIDX + t], in_=xn[j][:, 512:1024]), [xnB[j]], [scrB[key][CIDX + t]], dma=True)
            s_ = nxt("stg", 3)
            S.op("sp", lambda e: e.dma_start(out=stg[s_][:, 0:512], in_=ca_v[rsl, :]), [], [stgB[s_]], dma=True)
            j = nxt("xn", 2)
            v4 = xn[j][:, 0:520].rearrange("p (h d) -> p h d", h=4)
            S.op("pool", lambda e: e.tensor_copy(v4[:, :, 0:128], stg[s_][:, 0:512].rearrange("p (h d) -> p h d", h=4)), [stgB[s_]], [xnB[j]])
            S.op("pool", lambda e: e.memset(v4[:, :, 128:130], 1.0), [], [xnB[j]])
            S.op("pool", lambda e: e.dma_start(out=va_s[CIDX + t], in_=xn[j][:, 0:520]), [xnB[j]], [scrB["va"][CIDX + t]], dma=True)
            s_ = nxt("stg", 3)
            S.op("sp", lambda e: e.dma_start(out=stg[s_][:, 0:512], in_=cb_v[rsl, :]), [], [stgB[s_]], dma=True)
            j = nxt("xn", 2)
            S.op("pool", lambda e: e.tensor_copy(xn[j][:, 0:512], stg[s_][:, 0:512]), [stgB[s_]], [xnB[j]])
            S.op("pool", lambda e: e.dma_start(out=vb_s[CIDX + t], in_=xn[j][:, 0:512]), [xnB[j]], [scrB["vb"][CIDX + t]], dma=True)

    def lambda_prep():
        for q, n in enumerate(("a_lq1", "a_lk1", "a_lq2", "a_lk2")):
            S.op("sp", lambda e, q=q, n=n: e.dma_start(out=lt[:, q, :], in_=W[n][0:1, :].partition_broadcast(128)), [], [ltB], dma=True)
        S.op("sp", lambda e: e.dma_start(out=gsub, in_=W["a_subln_g"][0:1, :].partition_broadcast(128)), [], [gsubB], dma=True)
        S.op("dve", lambda e: e.tensor_scalar(gsub, gsub, 0.8, None, ALU.mult), [gsubB], [gsubB])
        lam = small["lam"]
        S.op("dve", lambda e: e.tensor_tensor(lt[:, 0, :], lt[:, 0, :], lt[:, 1, :], ALU.mult), [ltB], [ltB])
        S.op("dve", lambda e: e.tensor_tensor(lt[:, 2, :], lt[:, 2, :], lt[:, 3, :], ALU.mult), [ltB], [ltB])
        S.op("dve", lambda e: e.tensor_reduce(lam[:, 0:1], lt[:, 0, :], AX.X, ALU.add), [ltB], [smB["lam"]])
        S.op("dve", lambda e: e.tensor_reduce(lam[:, 1:2], lt[:, 2, :], AX.X, ALU.add), [ltB], [smB["lam"]])
        S.op("act", lambda e: e.activation(lam[:, 0:2], lam[:, 0:2], AF.Exp), [smB["lam"]], [smB["lam"]])
        S.op("dve", lambda e: e.tensor_tensor(lam[:, 2:3], lam[:, 1:2], lam[:, 0:1], ALU.subtract), [smB["lam"]], [smB["lam"]])
        S.op("dve", lambda e: e.tensor_scalar(lam[:, 2:3], lam[:, 2:3], -0.2, None, ALU.add), [smB["lam"]], [smB["lam"]])

    def attn_l0():
        accA = [av(1024 * j, 1024) for j in range(4)]; accAB = [Buf() for _ in range(4)]
        accB = [av(4096 + 512 * j, 512) for j in range(4)]; accBB = [Buf() for _ in range(4)]
        QTe = av(6144, 1024, BF16).rearrange("p (k t) -> p k t", k=4); QTB = Buf()
        QTo = av(13856, 1024, BF16).rearrange("p (k t) -> p k t", k=4)
        QTeo = (QTe, QTo)
        S.op("pool", lambda e: e.memset(QTe[64:128, :, :], 0.0), [], [QTB])
        S.op("pool", lambda e: e.memset(QTo[0:64, :, :], 0.0), [], [QTB])
        kvb = [av(7168 + 528 * i, 528, BF16) for i in range(2)]; kvB = [Buf() for _ in range(2)]
        Eb = [av(8224 + 512 * i, 512, BF16) for i in range(2)]; EbB = [[Buf(), Buf()] for _ in range(2)]
        ef = [av(9248 + 512 * i, 512) for i in range(2)]; efB = [Buf() for _ in range(2)]
        pvs = [av(10272 + 256 * i, 256) for i in range(2)]; pvsB = [Buf() for _ in range(2)]
        tmpB = [av(10784 + 256 * i, 256) for i in range(2)]; tmpBB = [Buf() for _ in range(2)]
        ofin = av(11296, 1024); ofinB = Buf()
        mg = av(12320, 512, BF16); mgB = Buf()
        sqf = av(12832, 512); sqfB = Buf()
        aof = av(13344, 512); aofB = Buf()
        QhB = [[Buf(), Buf()] for _ in range(3)]
        den, carry, ecarry = small["den"], small["carry"], small["ecarry"]
        zc = {"z": 0, "e": 0}

        def pairA(j, nq, nk, KT, V, kB, diag):
            if int(os.environ.get("KPA", "9")) < 1:
                return
            zi = zc["z"]; zc["z"] = 1 - zi
            z = Q[zi]
            for m in range(8):
                r0 = (m % 2) * 64
                S.op("pe", lambda e, m=m, r0=r0: e.matmul(z[:nk, m * nq:(m + 1) * nq], KT[:, m // 2, 0:nk],
                                                         QTeo[m % 2][:, m // 2, j * 128:j * 128 + nq], start=True, stop=True),
                     [kB, QTB], [QhB[zi][0], QhB[zi][1]])
            ei = zc["e"]; zc["e"] = 1 - ei
            E = Eb[ei]
            KPA = int(os.environ.get("KPA", "9"))
            if KPA < 2:
                return
            if 8 * nq > 512:
                for hh in range(2):
                    S.op("act", lambda e, hh=hh: e.activation(E[:nk, hh * 512:(hh + 1) * 512], z[:nk, hh * 512:(hh + 1) * 512], AF.Exp), [QhB[zi][hh]], [EbB[ei][hh]])
            else:
                S.op("act", lambda e: e.activation(E[:nk, 0:8 * nq], z[:nk, 0:8 * nq], AF.Exp), [QhB[zi][0], QhB[zi][1]], [EbB[ei][0], EbB[ei][1]])
            if KPA < 3:
                return
            if diag:
                e3 = E[:nk, 0:8 * nq].rearrange("p (m q) -> p m q", m=8)
                S.op("pool", lambda e: e.tensor_tensor(e3, e3, maskA[:nk, :nq].unsqueeze(1).to_broadcast([nk, 8, nq]), ALU.mult),
                     [EbB[ei][0], EbB[ei][1], cB], [EbB[ei][0], EbB[ei][1]])
            pv = Q[2]
            for m in range(8):
                S.op("pe", lambda e, m=m: e.matmul(pv[:nq, m * 128:(m + 1) * 128], E[:nk, m * nq:(m + 1) * nq], V[:nk, m // 2, 0:128], start=True, stop=True),
                     [EbB[ei][0], EbB[ei][1], kB], [QhB[2][0], QhB[2][1]])
            if KPA < 4:
                return
            for m in range(8):
                S.op("pe", lambda e, m=m: e.matmul(P1[:nq, m:m + 1], E[:nk, m * nq:(m + 1) * nq], V[:nk, m // 2, 128:129], start=True, stop=True),
                     [EbB[ei][0], EbB[ei][1], kB], [P1B])
            if KPA < 5:
                return
            S.op("dve", lambda e: e.tensor_tensor(accA[j][:nq, :], accA[j][:nq, :], pv[:nq, :], ALU.add), [accAB[j], QhB[2][0], QhB[2][1]], [accAB[j]])
            S.op("dve", lambda e: e.tensor_tensor(den[:nq, j, :], den[:nq, j, :], P1[:nq, 0:8], ALU.add), [smB["den"], P1B], [smB["den"]])

        def pairB(j, nq, nk, KT, V, kB, diag):
            for hh in range(2):
                zh = Q[0][:, hh * 512:hh * 512 + 4 * nq]
                ar = Q[1][:, hh * 512:hh * 512 + 4 * nq]
                pvB = Q[2][:, hh * 512:hh * 512 + 256]
                spb = Eb[0][:, hh * 512:hh * 512 + 4 * nq]
                wb = Eb[1][:, hh * 512:hh * 512 + 4 * nq]

                def qk(out, h4, st_, sp_):
                    h = 4 * hh + h4
                    r0 = (h % 2) * 64
                    return lambda e: e.matmul(out[:nk, h4 * nq:(h4 + 1) * nq], KT[:, h // 2, 0:nk],
                                              QTeo[h % 2][:, h // 2, j * 128:j * 128 + nq], start=st_, stop=sp_)
                for h4 in range(4):
                    S.op("pe", qk(zh, h4, True, True), [kB, QTB], [QhB[0][hh]])
                S.op("act", lambda e: e.activation(ef[hh][:nk, 0:4 * nq], zh[:nk, :], AF.Exp), [QhB[0][hh]], [efB[hh]])
                S.op("act", lambda e: e.activation(spb[:nk, :], ef[hh][:nk, 0:4 * nq], AF.Ln, bias=1.0), [efB[hh]], [EbB[0][hh]])
                if diag:
                    s3 = spb[:nk, :].rearrange("p (m q) -> p m q", m=4)
                    S.op("pool", lambda e: e.tensor_tensor(s3, s3, maskB[:nk, :nq].unsqueeze(1).to_broadcast([nk, 4, nq]), ALU.mult),
                         [EbB[0][hh], cB], [EbB[0][hh]])
                for h4 in range(4):
                    S.op("pe", qk(ar, h4, True, False), [kB, QTB], [QhB[1][hh]])
                    S.op("pe", lambda e, h4=h4: e.matmul(ar[:nk, h4 * nq:(h4 + 1) * nq], negU[:nk, :nk], spb[:nk, h4 * nq:(h4 + 1) * nq], start=False, stop=True),
                         [EbB[0][hh], cB], [QhB[1][hh]])
                for h4 in range(4):
                    c_ = 4 * hh + h4
                    S.op("pe", lambda e, h4=h4, c_=c_: e.matmul(P1[:nq, c_:c_ + 1], spb[:nk, h4 * nq:(h4 + 1) * nq], ones[:nk, 0:1], start=True, stop=True),
                         [EbB[0][hh], cB], [P1B])
                S.op("act", lambda e: e.activation(wb[:nk, :], ar[:nk, :], AF.Exp), [QhB[1][hh]], [EbB[1][hh]])
                if diag:
                    w3 = wb[:nk, :].rearrange("p (m q) -> p m q", m=4)
                    S.op("pool", lambda e: e.tensor_tensor(w3, w3, maskB[:nk, :nq].unsqueeze(1).to_broadcast([nk, 4, nq]), ALU.mult),
                         [EbB[1][hh], cB], [EbB[1][hh]])
                for h4 in range(4):
                    S.op("pe", lambda e, h4=h4: e.matmul(pvB[:nq, h4 * 64:(h4 + 1) * 64], wb[:nk, h4 * nq:(h4 + 1) * nq], V[:nk, 4 * hh + h4, :], start=True, stop=True),
                         [EbB[1][hh], kB], [QhB[2][hh]])
                S.op("act", lambda e: e.copy(pvs[hh][:nq, :], pvB[:nq, :]), [QhB[2][hh]], [pvsB[hh]])
                hs = slice(4 * hh, 4 * hh + 4)
                S.op("dve", lambda e: e.tensor_tensor(tmpB[hh][:nq, :].rearrange("p (h d) -> p h d", h=4), pvs[hh][:nq, :].rearrange("p (h d) -> p h d", h=4),
                                                      ecarry[:nq, j, hs].unsqueeze(2).to_broadcast([nq, 4, 64]), ALU.mult), [pvsB[hh], smB["ecarry"]], [tmpBB[hh]])
                S.op("dve", lambda e: e.tensor_tensor(accB[j][:nq, hh * 256:(hh + 1) * 256], accB[j][:nq, hh * 256:(hh + 1) * 256], tmpB[hh][:nq, :], ALU.add),
                     [tmpBB[hh], accBB[j]], [accBB[j]])
                S.op("dve", lambda e: e.tensor_tensor(carry[:nq, j, hs], carry[:nq, j, hs], P1[:nq, hs], ALU.subtract), [smB["carry"], P1B], [smB["carry"]])
                S.op("act", lambda e: e.activation(ecarry[:nq, j, hs], carry[:nq, j, hs], AF.Exp), [smB["carry"]], [smB["ecarry"]])

        def finalize(j, nq, i):
            rden = small["rden"]
            S.op("dve", lambda e: e.tensor_scalar(rden[:nq, :], den[:nq, j, :], 1e-30, None, ALU.add), [smB["den"]], [smB["rden"]])
            S.op("dve", lambda e: e.reciprocal(rden[:nq, :], rden[:nq, :]), [smB["rden"]], [smB["rden"]])
            S.op("dve", lambda e: e.tensor_tensor(ofin[:nq, :].rearrange("p (m d) -> p m d", m=8), accA[j][:nq, :].rearrange("p (m d) -> p m d", m=8),
                                                  rden[:nq, :].unsqueeze(2).to_broadcast([nq, 8, 128]), ALU.mult), [accAB[j], smB["rden"]], [ofinB])
            o4 = ofin[:nq, :].rearrange("p (h t d) -> p h t d", h=4, t=2)
            ao3 = aof[:nq, :].rearrange("p (h d) -> p h d", h=4)
            S.op("dve", lambda e: e.scalar_tensor_tensor(ao3, o4[:, :, 1, :], small["lam"][:nq, 2:3], o4[:, :, 0, :], ALU.mult, ALU.add),
                 [ofinB, smB["lam"]], [aofB])
            s4 = small["s4"]
            S.op("dve", lambda e: e.tensor_tensor(sqf[:nq, :], aof[:nq, :], aof[:nq, :], ALU.mult), [aofB], [sqfB])
            S.op("dve", lambda e: e.tensor_reduce(s4[:nq, :], sqf[:nq, :].rearrange("p (h d) -> p h d", h=4), AX.X, ALU.add), [sqfB], [smB["s4"]])
            S.op("dve", lambda e: e.tensor_scalar(s4[:nq, :], s4[:nq, :], 1.0 / 128, EPS, ALU.mult, ALU.add), [smB["s4"]], [smB["s4"]])
            S.op("act", lambda e: e.activation(s4[:nq, :], s4[:nq, :], AF.Ln), [smB["s4"]], [smB["s4"]])
            S.op("act", lambda e: e.activation(s4[:nq, :], s4[:nq, :], AF.Exp, scale=-0.5), [smB["s4"]], [smB["s4"]])
            S.op("dve", lambda e: e.tensor_tensor(ao3, ao3, s4[:nq, :].unsqueeze(2).to_broadcast([nq, 4, 128]), ALU.mult), [aofB, smB["s4"]], [aofB])
            S.op("dve", lambda e: e.tensor_tensor(mg[:nq, 0:512].rearrange("p (h d) -> p h d", h=4), ao3, gsub[:nq, :].unsqueeze(1).to_broadcast([nq, 4, 128]), ALU.mult),
                 [aofB, gsubB], [mgB])
            S.op("pool", lambda e: e.tensor_copy(mg[:nq, 512:1024], accB[j][:nq, :]), [accBB[j]], [mgB])
            pt, ptB = getpt()
            for k in range(8):
                S.op("pe", lambda e, k=k: e.transpose(pt[:, k * 128:k * 128 + nq], mg[:nq, k * 128:(k + 1) * 128], ident[:nq, :nq]), [mgB, cB], [ptB])
            S.op("act", lambda e: e.copy(HT[:, :, i * 128:i * 128 + nq], pt.rearrange("p (k t) -> p k t", k=8)[:, :, 0:nq]), [ptB], [HTb[i]])

        groups = [list(range(g, g + 4)) for g in range(0, NQ, 4)] + [[SMP]]
        ngrp = int(os.environ.get("KNGRP", "99"))
        groups = groups[:ngrp] + ([groups[-1]] if ngrp < len(groups) else [])
        for grp in groups:
            sample = grp[0] == SMP
            nq = 16 if sample else 128
            keys = ([SMP] + [CIDX + t for t in range(7, -1, -1)]) if sample else list(range(grp[0], int(os.environ.get("KNKEY", str(NT)))))
            for pas in os.environ.get("KPASS", "AB"):
                for j, s in enumerate(grp):
                    qi = SQ if sample else s
                    src = (qta_s if pas == "A" else qtb_s)[qi].rearrange("p (k t) -> p k t", k=4)[:, :, 0:nq]
                    S.op("sp", lambda e, j=j, src=src: e.dma_start(out=QTe[0:64, :, j * 128:j * 128 + nq], in_=src[0:64]), [scrB["qta" if pas == "A" else "qtb"][qi]], [QTB], dma=True)
                    S.op("sp", lambda e, j=j, src=src: e.dma_start(out=QTo[64:128, :, j * 128:j * 128 + nq], in_=src[64:128]), [scrB["qta" if pas == "A" else "qtb"][qi]], [QTB], dma=True)
                    if pas == "A":
                        S.op("dve", lambda e, j=j: e.memset(accA[j], 0.0), [], [accAB[j]])
                    else:
                        S.op("dve", lambda e, j=j: e.memset(accB[j], 0.0), [], [accBB[j]])
                if pas == "A":
                    S.op("dve", lambda e: e.memset(den, 0.0), [], [smB["den"]])
                else:
                    S.op("dve", lambda e: e.memset(carry, 0.0), [], [smB["carry"]])
                    S.op("dve", lambda e: e.memset(ecarry, 1.0), [], [smB["ecarry"]])
                for idx, uk in enumerate(keys):
                    nk = 16 if uk == SMP else 128
                    b = idx % 2
                    KT = kvb[b][:, 0:512].rearrange("p (k t) -> p k t", k=4)
                    ksrc = (kta_s if pas == "A" else ktb_s)[uk].rearrange("p (k t) -> p k t", k=4)[:, :, 0:nk]
                    kkey, vkey = ("kta", "va") if pas == "A" else ("ktb", "vb")
                    S.op("sp", lambda e, KT=KT, ksrc=ksrc, nk=nk: e.dma_start(out=KT[:, :, 0:nk], in_=ksrc), [scrB[kkey][uk]], [kvB[b]], dma=True)
                    if pas == "A":
                        V = kvb[b][:, 512:1032].rearrange("p (h d) -> p h d", h=4)
                        S.op("sp", lambda e, b=b, uk=uk, nk=nk: e.dma_start(out=kvb[b][:nk, 512:1032], in_=va_s[uk][:nk, :]), [scrB[vkey][uk]], [kvB[b]], dma=True)
                    else:
                        V = kvb[b][:, 512:1024].rearrange("p (h d) -> p h d", h=8)
                        S.op("sp", lambda e, b=b, uk=uk, nk=nk: e.dma_start(out=kvb[b][:nk, 512:1024], in_=vb_s[uk][:nk, :]), [scrB[vkey][uk]], [kvB[b]], dma=True)
                    for j, s in enumerate(grp):
                        if not sample and uk < s:
                            continue
                        diag = (uk == SMP) if sample else (uk == s)
                        (pairA if pas == "A" else pairB)(j, nq, nk, KT, V, kvB[b], diag)
            if not os.environ.get("KNOFIN"):
                for j, s in enumerate(grp):
                    finalize(j, nq, NQ if sample else s)

    def linear_residual(wsrc, gcol, tiles, scale):
        for pc in range(4):
            ws = pc % 2
            load_w(Wb[ws][:, 0, :], WbB[ws][0], wsrc[:, pc * 256:(pc + 1) * 256], gcol, 8, 256)
            w3 = Wb[ws][:, 0, :].rearrange("p (k f) -> p k f", k=8)
            for (i, rows) in tiles:
                pp, ppB = getps()
                for k in range(8):
                    S.op("pe", lambda e, k=k: e.matmul(pp[:rows, 0:256], HT[:, k, i * 128:i * 128 + rows], w3[:, k, :], start=(k == 0), stop=(k == 7)),
                         [HTb[i], WbB[ws][0]], [ppB])
                S.op("dve", lambda e: e.scalar_tensor_tensor(X[:rows, i, pc * 256:(pc + 1) * 256], pp[:rows, 0:256], float(scale),
                                                             X[:rows, i, pc * 256:(pc + 1) * 256], ALU.mult, ALU.add), [ppB, Xb[i]], [Xb[i]])

    lim = {"stg": 3}
    gm = {"gq": lt.rearrange("p a b -> p (a b)"), "gk": sb("gm_gk", [128, 256])}; gmB = Buf()
    MKT = av(4096, 1024, BF16).rearrange("p (k t) -> p k t", k=8); MV = av(5120, 1024, BF16).rearrange("p (t f) -> p t f", t=2)
    MKTs = av(12288, 1024, BF16).rearrange("p (k t) -> p k t", k=8); MVs = av(13312, 1024, BF16).rearrange("p (t f) -> p t f", t=2)
    memB = {"k": Buf(), "v": Buf(), "ks": Buf(), "vs": Buf()}
    mw = [av(1024 + 1024 * i, 1024, BF16) for i in range(2)]; mwB = [Buf(), Buf()]
    MT = av(0, 1024, BF16).rearrange("p (k t) -> p k t", k=8); MTB = Buf()

    def head_norm(src_ps, srcB, rows, n, gtile, gtB, out_bf=None, out_bfB=None):
        a, b = nxt("wk", 4), nxt("wk", 4)
        s_ = nxt("st", 4)
        S.op("act", lambda e: e.copy(wk[a][:rows, :n], src_ps), [srcB], [wkB[a]])
        S.op("dve", lambda e: e.tensor_tensor(wk[b][:rows, :n], wk[a][:rows, :n], wk[a][:rows, :n], ALU.mult), [wkB[a]], [wkB[b]])
        S.op("dve", lambda e: e.tensor_reduce(st[s_][:rows, 0:1], wk[b][:rows, :n], AX.X, ALU.add), [wkB[b]], [stB[s_]])
        S.op("dve", lambda e: e.tensor_scalar(st[s_][:rows, 0:1], st[s_][:rows, 0:1], 1.0 / n, EPS, ALU.mult, ALU.add), [stB[s_]], [stB[s_]])
        S.op("act", lambda e: e.activation(st[s_][:rows, 0:1], st[s_][:rows, 0:1], AF.Ln), [stB[s_]], [stB[s_]])
        S.op("act", lambda e: e.activation(st[s_][:rows, 0:1], st[s_][:rows, 0:1], AF.Exp, scale=-0.5), [stB[s_]], [stB[s_]])
        S.op("dve", lambda e: e.tensor_scalar(wk[a][:rows, :n], wk[a][:rows, :n], st[s_][:rows, 0:1], None, ALU.mult), [wkB[a], stB[s_]], [wkB[a]])
        S.op("dve", lambda e: e.tensor_tensor(wk[b][:rows, :n], wk[a][:rows, :n], gtile[:rows, :n], ALU.mult), [wkB[a], gtB], [wkB[b]])
        return wk[b], wkB[b]

    def mem_prep(l):
        S.op("sp", lambda e: e.dma_start(out=gm["gq"], in_=W["mem_gq"][l:l + 1, :].partition_broadcast(128)), [], [gmB], dma=True)
        S.op("sp", lambda e: e.dma_start(out=gm["gk"], in_=W["mem_gk"][l:l + 1, :].partition_broadcast(128)), [], [gmB], dma=True)
        S.op("dve", lambda e: e.tensor_scalar(gm["gq"], gm["gq"], 1.0 / 16, None, ALU.mult), [gmB], [gmB])
        for t in range(2):
            s_ = nxt("stg", 3)
            S.op("sp", lambda e, t=t: e.dma_start(out=stg[s_][:, 0:1024], in_=memp[t * 128:(t + 1) * 128, :]), [], [stgB[s_]], dma=True)
            k_ = nxt("st", 4)
            S.op("dve", lambda e: e.memset(st[k_][:, 0:1], 0.0), [], [stB[k_]])
            S.op("act", lambda e: e.activation(junk, stg[s_][:, 0:1024], AF.Square, accum_out=st[k_][:, 0:1]), [stgB[s_]], [junkB, stB[k_]])
            S.op("dve", lambda e: e.tensor_scalar(st[k_][:, 0:1], st[k_][:, 0:1], 1.0 / D, EPS, ALU.mult, ALU.add), [stB[k_]], [stB[k_]])
            S.op("act", lambda e: e.activation(st[k_][:, 0:1], st[k_][:, 0:1], AF.Ln), [stB[k_]], [stB[k_]])
            S.op("act", lambda e: e.activation(st[k_][:, 0:1], st[k_][:, 0:1], AF.Exp, scale=-0.5), [stB[k_]], [stB[k_]])
            j = nxt("xn", 2)
            S.op("pool", lambda e: e.tensor_scalar(xn[j], stg[s_][:, 0:1024], st[k_][:, 0:1], None, ALU.mult), [stgB[s_], stB[k_]], [xnB[j]])
            pt, ptB = getpt()
            for k in range(8):
                S.op("pe", lambda e, k=k: e.transpose(pt[:, k * 128:(k + 1) * 128], xn[j][:, k * 128:(k + 1) * 128], ident), [xnB[j], cB], [ptB])
            S.op("act", lambda e, t=t: e.copy(MT[:, :, t * 128:(t + 1) * 128], pt.rearrange("p (k t) -> p k t", k=8)), [ptB], [MTB])
        gcol = (GID["mem_g_m"] + l) * 8
        for which in ("k", "v"):
            for h in range(4):
                ws = h % 2
                load_w(mw[ws], mwB[ws], W["mem_w" + which][l][:, h * 256:(h + 1) * 256], gcol, 8, 256)
                w3 = mw[ws].rearrange("p (k f) -> p k f", k=8)
                for t in range(2):
                    pp, ppB = getps()
                    for k in range(8):
                        S.op("pe", lambda e, k=k, t=t: e.matmul(pp[:, 0:256], MT[:, k, t * 128:(t + 1) * 128], w3[:, k, :], start=(k == 0), stop=(k == 7)),
                             [MTB, mwB[ws]], [ppB])
                    if which == "k":
                        fin, finB = head_norm(pp[:, 0:256], ppB, 128, 256, gm["gk"], gmB)
                        S.op("pool", lambda e, t=t, h=h: e.dma_start(out=o_mk[l][t * 128:(t + 1) * 128, h * 256:(h + 1) * 256], in_=fin[:, 0:256]), [finB], [], dma=True)
                        c = nxt("wkb", 2)
                        S.op("pool", lambda e: e.tensor_copy(wkb[c][:, 0:256], fin[:, 0:256]), [finB], [wkbB[c]])
                        pt, ptB = getpt()
                        for k in range(2):
                            S.op("pe", lambda e, k=k: e.transpose(pt[:, k * 128:(k + 1) * 128], wkb[c][:, k * 128:(k + 1) * 128], ident), [wkbB[c], cB], [ptB])
                        S.op("act", lambda e, t=t, h=h: e.copy(MKT[:, 2 * h:2 * h + 2, t * 128:(t + 1) * 128], pt[:, 0:256].rearrange("p (k t) -> p k t", k=2)), [ptB], [memB["k"]])
                    else:
                        a = nxt("wk", 4)
                        S.op("act", lambda e: e.copy(wk[a][:, :], pp[:, 0:256]), [ppB], [wkB[a]])
                        S.op("pool", lambda e, t=t, h=h: e.dma_start(out=o_mv[l][t * 128:(t + 1) * 128, h * 256:(h + 1) * 256], in_=wk[a][:, :]), [wkB[a]], [], dma=True)
                        S.op("pool", lambda e, t=t, h=h: e.tensor_copy(MV[:, t, h * 256:(h + 1) * 256], wk[a][:, :]), [wkB[a]], [memB["v"]])
        for t in range(2):
            s_ = nxt("stg", 3)
            S.op("sp", lambda e, t=t: e.dma_start(out=stg[s_][:, 0:1024], in_=cm_k[l][t * 128:(t + 1) * 128, :]), [], [stgB[s_]], dma=True)
            j = nxt("xn", 2)
            S.op("pool", lambda e: e.tensor_copy(xn[j], stg[s_][:, 0:1024]), [stgB[s_]], [xnB[j]])
            pt, ptB = getpt()
            for k in range(8):
                S.op("pe", lambda e, k=k: e.transpose(pt[:, k * 128:(k + 1) * 128], xn[j][:, k * 128:(k + 1) * 128], ident), [xnB[j], cB], [ptB])
            S.op("act", lambda e, t=t: e.copy(MKTs[:, :, t * 128:(t + 1) * 128], pt.rearrange("p (k t) -> p k t", k=8)), [ptB], [memB["ks"]])
            s_ = nxt("stg", 3)
            S.op("sp", lambda e, t=t: e.dma_start(out=stg[s_][:, 0:1024], in_=cm_v[l][t * 128:(t + 1) * 128, :]), [], [stgB[s_]], dma=True)
            S.op("pool", lambda e, t=t: e.tensor_copy(MVs[:, t, :], stg[s_][:, 0:1024]), [stgB[s_]], [memB["vs"]])

    def mem_attn(l, tiles):
        QM = av(10240, 512, BF16).rearrange("p (k t) -> p k t", k=8); QMB = Buf()
        Em = av(10752, 512, BF16); EmB = [Buf(), Buf()]
        og = av(11264, 1024); ogB = Buf()
        ogb = junk
        wq = [av(h * 1024, 1024, BF16) for h in range(4)]; wqB = [Buf() for _ in range(4)]
        rstd_tiles(tiles)
        for (i, rows) in tiles:
            norm_to_HT(i, rows)
        lim["stg"] = 2
        gcol = (GID["mem_g_x"] + l) * 8
        for h in range(4):
            load_w(wq[h], wqB[h], W["mem_wq"][l][:, h * 256:(h + 1) * 256], gcol, 8, 256)
        rr["ptn"] = 1
        zc = 0
        for (i, rows) in tiles:
            smp = (i == NQ)
            KTm, Vm, kB, vB = (MKTs, MVs, memB["ks"], memB["vs"]) if smp else (MKT, MV, memB["k"], memB["v"])
            for h in range(4):
                w3 = wq[h].rearrange("p (k f) -> p k f", k=8)
                pp, ppB = getps()
                for k in range(8):
                    S.op("pe", lambda e, k=k: e.matmul(pp[:rows, 0:256], HT[:, k, i * 128:i * 128 + rows], w3[:, k, :], start=(k == 0), stop=(k == 7)),
                         [HTb[i], wqB[h]], [ppB])
                fin, finB = head_norm(pp[:rows, 0:256], ppB, rows, 256, gm["gq"], gmB)
                c = nxt("wkb", 2)
                S.op("pool", lambda e: e.tensor_copy(wkb[c][:rows, 0:256], fin[:rows, 0:256]), [finB], [wkbB[c]])
                pt, ptB = getpt()
                for k in range(2):
                    S.op("pe", lambda e, k=k: e.transpose(pt[:, k * 128:k * 128 + rows], wkb[c][:rows, k * 128:(k + 1) * 128], ident[:rows, :rows]), [wkbB[c], cB], [ptB])
                S.op("act", lambda e, h=h: e.copy(QM[:, 2 * h:2 * h + 2, 0:rows], pt[:, 0:256].rearrange("p (k t) -> p k t", k=2)[:, :, 0:rows]), [ptB], [QMB])
            zi = zc; zc = 1 - zc
            z = Q[zi]; zB_ = [PSB[2 * zi], PSB[2 * zi + 1]]
            for h in range(4):
                for kt in range(2):
                    c0 = (h * 2 + kt) * rows
                    for cc in range(2):
                        S.op("pe", lambda e, h=h, kt=kt, cc=cc, c0=c0: e.matmul(z[:, c0:c0 + rows], KTm[:, 2 * h + cc, kt * 128:(kt + 1) * 128], QM[:, 2 * h + cc, 0:rows],
                                                                            start=(cc == 0), stop=(cc == 1)), [kB, QMB], zB_)
            if 8 * rows > 512:
                for hh in range(2):
                    S.op("act", lambda e, hh=hh: e.activation(Em[:, hh * 512:(hh + 1) * 512], z[:, hh * 512:(hh + 1) * 512], AF.Exp), [zB_[hh]], [EmB[hh]])
            else:
                S.op("act", lambda e: e.activation(Em[:, 0:8 * rows], z[:, 0:8 * rows], AF.Exp), zB_, EmB)
            pv = Q[2]; pvB_ = [PSB[4], PSB[5]]
            for h in range(4):
                for kt in range(2):
                    c0 = (h * 2 + kt) * rows
                    S.op("pe", lambda e, h=h, kt=kt, c0=c0: e.matmul(pv[:rows, h * 256:(h + 1) * 256], Em[:, c0:c0 + rows], Vm[:, kt, h * 256:(h + 1) * 256],
                                                                     start=(kt == 0), stop=(kt == 1)), EmB + [vB], [pvB_[h // 2]])
            for h in range(4):
                for kt in range(2):
                    c0 = (h * 2 + kt) * rows
                    S.op("pe", lambda e, h=h, kt=kt, c0=c0: e.matmul(P1[:rows, h:h + 1], Em[:, c0:c0 + rows], ones[:, 0:1], start=(kt == 0), stop=(kt == 1)), EmB + [cB], [P1B])
            for hh in range(2):
                S.op("act", lambda e, hh=hh: e.copy(og[:rows, hh * 512:(hh + 1) * 512], pv[:rows, hh * 512:(hh + 1) * 512]), [pvB_[hh]], [ogB])
            rden = small["rden"]
            S.op("dve", lambda e: e.reciprocal(rden[:rows, 0:4], P1[:rows, 0:4]), [P1B], [smB["rden"]])
            S.op("dve", lambda e: e.tensor_tensor(ogb[:rows, :].rearrange("p (h d) -> p h d", h=4), og[:rows, :].rearrange("p (h d) -> p h d", h=4),
                                                  rden[:rows, 0:4].unsqueeze(2).to_broadcast([rows, 4, 256]), ALU.mult), [ogB, smB["rden"]], [junkB])
            pt, ptB = getpt()
            for k in range(8):
                S.op("pe", lambda e, k=k: e.transpose(pt[:, k * 128:k * 128 + rows], ogb[:rows, k * 128:(k + 1) * 128], ident[:rows, :rows]), [junkB, cB], [ptB])
            S.op("act", lambda e: e.copy(HT[:, :, i * 128:i * 128 + rows], pt.rearrange("p (k t) -> p k t", k=8)[:, :, 0:rows]), [ptB], [HTb[i]])
        rr["ptn"] = 2
        S.barrier()
        linear_residual(W["mem_wo"][l], None, tiles, 1.0)
        lim["stg"] = 3

    NCS = NQ + 1 + 4
    kc_s = dscr("kc_s", [NCS, 128, 1024]); vc_s = dscr("vc_s", [NCS, 128, 1024]); qc_s = dscr("qc_s", [NOWN + 1, 128, 1024])
    kcB = [Buf() for _ in range(NCS)]; vcB = [Buf() for _ in range(NCS)]; qcB = [Buf() for _ in range(NOWN + 1)]
    tabx_t = nc.dram_tensor("tabx", [16, 513], F32)
    tabx = tabx_t.ap(); tabxB = Buf()
    negb = sb("negb", [128, 16]); negbB = Buf()
    mask4b = sb("mask4b", [128, 128], BF16)
    S.op("dve", lambda e: e.tensor_copy(mask4b, cst[:, 648:776]), [cstB], [cB])

    def norm4(src, ppB, rows, g):
        a, b = nxt("wk", 4), nxt("wk", 4)
        s_ = nxt("st", 4)
        S.op("act", lambda e: e.copy(wk[b][:rows, :], src), [ppB], [wkB[b]])
        S.op("dve", lambda e: e.tensor_tensor(wk[a][:rows, :], wk[b][:rows, :], wk[b][:rows, :], ALU.mult), [wkB[b]], [wkB[a]])
        S.op("dve", lambda e: e.tensor_reduce(st[s_][:rows, 0:4], wk[a][:rows, :].rearrange("p (h d) -> p h d", h=4), AX.X, ALU.add), [wkB[a]], [stB[s_]])
        S.op("dve", lambda e: e.tensor_scalar(st[s_][:rows, 0:4], st[s_][:rows, 0:4], 1.0 / 64, EPS, ALU.mult, ALU.add), [stB[s_]], [stB[s_]])
        S.op("act", lambda e: e.activation(st[s_][:rows, 0:4], st[s_][:rows, 0:4], AF.Ln), [stB[s_]], [stB[s_]])
        S.op("act", lambda e: e.activation(st[s_][:rows, 0:4], st[s_][:rows, 0:4], AF.Exp, scale=-0.5), [stB[s_]], [stB[s_]])
        w3a = wk[a][:rows, :].rearrange("p (h d) -> p h d", h=4)
        w3b = wk[b][:rows, :].rearrange("p (h d) -> p h d", h=4)
        S.op("dve", lambda e: e.tensor_tensor(w3a, w3b, st[s_][:rows, 0:4].unsqueeze(2).to_broadcast([rows, 4, 64]), ALU.mult), [wkB[b], stB[s_]], [wkB[a]])
        S.op("dve", lambda e: e.tensor_tensor(w3b, w3a, g[:rows, :].unsqueeze(1).to_broadcast([rows, 4, 64]), ALU.mult), [wkB[a], gB], [wkB[b]])
        return wk[b], wkB[b]

    def proj_c(tiles, us):
        gcol = (GID["mix_g"] + 1) * 8
        for pc in range(12):
            ws = pc % 2
            kind, hp = ("cq", "ck", "cv")[pc // 4], pc % 4
            load_w(Wb[ws][:, 0, :], WbB[ws][0], W["c_w_in"][0][:, pc * 256:(pc + 1) * 256], gcol, 8, 256)
            w3 = Wb[ws][:, 0, :].rearrange("p (k f) -> p k f", k=8)
            csl = slice(hp * 256, (hp + 1) * 256)
            for (i, rows), u in zip(tiles, us):
                smp = (u == SMP)
                if kind == "cq" and not (u < NOWN or smp):
                    continue
                ui = NQ if smp else u
                qi = NOWN if smp else u
                pp, ppB = getps()
                for k in range(8):
                    S.op("pe", lambda e, k=k: e.matmul(pp[:rows, 0:256], HT[:, k, i * 128:i * 128 + rows], w3[:, k, :], start=(k == 0), stop=(k == 7)),
                         [HTb[i], WbB[ws][0]], [ppB])
                src = pp[:rows, 0:256]
                if kind in ("cq", "ck"):
                    fin, finB = norm4(src, ppB, rows, g64["c_gq" if kind == "cq" else "c_gk"])
                    if kind == "ck" and (u < 4 or smp):
                        dst = o_cks[496:512, csl] if smp else o_ck[u][:, csl]
                        S.op("pool", lambda e, dst=dst: e.dma_start(out=dst[:rows], in_=fin[:rows, :]), [finB], [], dma=True)
                    c = nxt("wkb", 2)
                    S.op("pool", lambda e: e.tensor_copy(wkb[c][:rows, 0:256], fin[:rows, :]), [finB], [wkbB[c]])
                    if kind == "ck":
                        to_T_and_store(wkb[c], wkbB[c], rows, kc_s[ui].rearrange("p (k t) -> p k t", k=8)[:, hp * 2:hp * 2 + 2, 0:rows], kcB[ui])
                    else:
                        to_T_and_store(wkb[c], wkbB[c], rows, qc_s[qi].rearrange("p (k t) -> p k t", k=8)[:, hp * 2:hp * 2 + 2, 0:rows], qcB[qi])
                else:
                    a = nxt("wk", 4)
                    S.op("act", lambda e: e.copy(wk[a][:rows, :], src), [ppB], [wkB[a]])
                    if u < 4 or smp:
                        dst = o_cvs[496:512, csl] if smp else o_cv[u][:, csl]
                        S.op("pool", lambda e, dst=dst: e.dma_start(out=dst[:rows], in_=wk[a][:rows, :]), [wkB[a]], [], dma=True)
                    c = nxt("wkb", 2)
                    S.op("dve", lambda e: e.tensor_scalar(wkb[c][:rows, 0:256], wk[a][:rows, :], valid[:rows, u:u + 1], None, ALU.mult), [wkB[a], cB], [wkbB[c]])
                    S.op("pool", lambda e: e.dma_start(out=vc_s[ui][:rows, csl], in_=wkb[c][:rows, 0:256]), [wkbB[c]], [vcB[ui]], dma=True)

    def cache_prep_c():
        S.op("sp", lambda e: e.dma_start(out=o_cks[0:496, :], in_=cc_k[16:512, :]), [], [], dma=True)
        S.op("sp", lambda e: e.dma_start(out=o_cvs[0:496, :], in_=cc_v[16:512, :]), [], [], dma=True)
        for t in range(4):
            ci = NQ + 1 + t
            s_ = nxt("stg", 3)
            S.op("sp", lambda e, t=t: e.dma_start(out=stg[s_][:, 0:1024], in_=cc_k[t * 128:(t + 1) * 128, :]), [], [stgB[s_]], dma=True)
            S.op("pool", lambda e: e.tensor_copy(xn[0], stg[s_][:, 0:1024]), [stgB[s_]], [xnB[0]])
            pt, ptB = getpt()
            for k in range(8):
                S.op("pe", lambda e, k=k: e.transpose(pt[:, k * 128:(k + 1) * 128], xn[0][:, k * 128:(k + 1) * 128], ident), [xnB[0], cB], [ptB])
            S.op("act", lambda e: e.copy(xn[1], pt), [ptB], [xnB[1]])
            S.op("pool", lambda e, ci=ci: e.dma_start(out=kc_s[ci], in_=xn[1]), [xnB[1]], [kcB[ci]], dma=True)
            s_ = nxt("stg", 3)
            S.op("sp", lambda e, t=t: e.dma_start(out=stg[s_][:, 0:1024], in_=cc_v[t * 128:(t + 1) * 128, :]), [], [stgB[s_]], dma=True)
            S.op("pool", lambda e: e.tensor_copy(xn[0], stg[s_][:, 0:1024]), [stgB[s_]], [xnB[0]])
            S.op("pool", lambda e, ci=ci: e.dma_start(out=vc_s[ci], in_=xn[0]), [xnB[0]], [vcB[ci]], dma=True)

    def band_attn(tiles):
        EB = [av(1024 * d, 1024, BF16).rearrange("p (h q) -> p h q", h=16) for d in range(2)]; EBB = Buf()
        QTe = av(2048, 256, BF16).rearrange("p (k t) -> p k t", k=4); QTo = av(2304, 256, BF16).rearrange("p (k t) -> p k t", k=4); QTB = Buf()
        QTeo = (QTe, QTo)
        kvc = [av(2560 + 512 * i, 512, BF16) for i in range(5)]; kvcB = [Buf() for _ in range(5)]
        Ec = [av(5120 + 512 * i, 512, BF16) for i in range(5)]; EcB = [[Buf(), Buf()] for _ in range(5)]
        ogc = av(7680, 512); ogcB = Buf()
        mgc = av(8192, 512, BF16); mgcB = Buf()
        S.op("sp", lambda e: e.dma_start(out=tabx[:, 0:257], in_=W["c_bias"][0]), [], [tabxB], dma=True)
        S.op("sp", lambda e: e.dma_start(out=stg[2][0:16, 256:257], in_=W["c_bias"][0][:, 256:257], allow_slow_non_contiguous=True), [], [stgB[2]], dma=True)
        S.op("dve", lambda e: e.tensor_copy(stg[2][0:16, 0:256], stg[2][0:16, 256:257].to_broadcast([16, 256])), [stgB[2]], [stgB[2]])
        S.op("sp", lambda e: e.dma_start(out=tabx[:, 257:513], in_=stg[2][0:16, 0:256]), [stgB[2]], [tabxB], dma=True)
        for h in range(16):
            S.op("sp", lambda e, h=h: e.dma_start(out=negb[:, h:h + 1], in_=tabx[h:h + 1, 300:301].partition_broadcast(128)), [tabxB], [negbB], dma=True)
        S.op("dve", lambda e: e.tensor_scalar(negb, negb, -1.0, None, ALU.mult), [negbB], [negbB])
        for h in range(16):
            s_ = nxt("stg", 2)
            for d in range(2):
                S.op("sp", lambda e, h=h, d=d: e.dma_start(out=stg[s_][:, d * 128:(d + 1) * 128], in_=bass.AP(tabx_t, h * 513 + 1 + 128 * d, [[1, 128], [1, 128]])),
                     [tabxB], [stgB[s_]], dma=True)
            pp, ppB = getps()
            S.op("pe", lambda e: e.matmul(pp[:, 0:256], cst[:, 520:648], stg[s_][:, 0:256], start=True, stop=True), [stgB[s_], cstB], [ppB])
            for d in range(2):
                S.op("act", lambda e, h=h, d=d: e.activation(EB[d][:, h, :], pp[:, d * 128:(d + 1) * 128], AF.Exp, bias=negb[:, h:h + 1]), [ppB, negbB], [EBB])
        S.op("pool", lambda e: e.tensor_tensor(EB[0], EB[0], maskA.unsqueeze(1).to_broadcast([128, 16, 128]), ALU.mult), [EBB, cB], [EBB])
        S.barrier()
        S.op("pool", lambda e: e.memset(QTe[64:128, :, :], 0.0), [], [QTB])
        S.op("pool", lambda e: e.memset(QTo[0:64, :, :], 0.0), [], [QTB])
        zc = {"z": 0, "b": 0}
        for (i, nq) in tiles:
            smp = (i == NQ)
            qi = NOWN if smp else i
            if smp:
                keyspec = [(NQ, 0)] + [(NQ + 1 + t, 1 if t == 3 else 2) for t in (3, 2, 1, 0)]
            else:
                keyspec = [(i + d, typ) for d, typ in zip(range(5), (0, 1, 2, 2, 4))]
            for half in range(2):
                src = qc_s[qi].rearrange("p (k t) -> p k t", k=8)[:, 4 * half:4 * half + 4, 0:nq]
                S.op("sp", lambda e, src=src: e.dma_start(out=QTe[0:64, :, 0:nq], in_=src[0:64]), [qcB[qi]], [QTB], dma=True)
                S.op("sp", lambda e, src=src: e.dma_start(out=QTo[64:128, :, 0:nq], in_=src[64:128]), [qcB[qi]], [QTB], dma=True)
                nkeys = len(keyspec)
                for d, (ui, typ) in enumerate(keyspec):
                    nk = 16 if (smp and d == 0) else 128
                    b = d
                    Kh = kvc[b][:, 0:512].rearrange("p (k t) -> p k t", k=4)
                    Vh = kvc[b][:, 512:1024]
                    ks = kc_s[ui].rearrange("p (k t) -> p k t", k=8)[:, 4 * half:4 * half + 4, 0:nk]
                    S.op("sp", lambda e, Kh=Kh, ks=ks, nk=nk: e.dma_start(out=Kh[:, :, 0:nk], in_=ks), [kcB[ui]], [kvcB[b]], dma=True)
                    S.op("sp", lambda e, Vh=Vh, ui=ui, nk=nk: e.dma_start(out=Vh[:nk, :], in_=vc_s[ui][:nk, half * 512:(half + 1) * 512]), [vcB[ui]], [kvcB[b]], dma=True)
                    zi = zc["z"]; zc["z"] = 1 - zi
                    z = Q[zi]; zB_ = [PSB[2 * zi], PSB[2 * zi + 1]]
                    for m in range(8):
                        S.op("pe", lambda e, m=m: e.matmul(z[:nk, m * nq:(m + 1) * nq], Kh[:, m // 2, 0:nk], QTeo[m % 2][:, m // 2, 0:nq], start=True, stop=True),
                             [kvcB[b], QTB], zB_)
                    E = Ec[d]
                    if 8 * nq > 512:
                        for hh in range(2):
                            S.op("act", lambda e, hh=hh, E=E: e.activation(E[:nk, hh * 512:(hh + 1) * 512], z[:nk, hh * 512:(hh + 1) * 512], AF.Exp), [zB_[hh]], [EcB[d][hh]])
                    else:
                        S.op("act", lambda e, E=E: e.activation(E[:nk, 0:8 * nq], z[:nk, 0:8 * nq], AF.Exp), zB_, EcB[d])
                    e3 = E[:nk, 0:8 * nq].rearrange("p (m q) -> p m q", m=8)
                    if typ in (0, 1):
                        S.op("pool", lambda e, typ=typ, e3=e3, nk=nk: e.tensor_tensor(e3, e3, EB[typ][:nk, 8 * half:8 * half + 8, 0:nq], ALU.mult), EcB[d] + [EBB], EcB[d])
                    elif typ == 4:
                        S.op("pool", lambda e, e3=e3, nk=nk: e.tensor_tensor(e3, e3, mask4b[:nk, :nq].unsqueeze(1).to_broadcast([nk, 8, nq]), ALU.mult), EcB[d] + [cB], EcB[d])
                for m in range(8):
                    for d, (ui, typ) in enumerate(keyspec):
                        nk = 16 if (smp and d == 0) else 128
                        S.op("pe", lambda e, m=m, d=d, nk=nk: e.matmul(Q[2][:nq, m * 64:(m + 1) * 64], Ec[d][:nk, m * nq:(m + 1) * nq], kvc[d][:nk, 512 + m * 64:512 + (m + 1) * 64],
                                                                       start=(d == 0), stop=(d == nkeys - 1)), EcB[d] + [kvcB[d]], [PSB[4]])
                for m in range(8):
                    for d, (ui, typ) in enumerate(keyspec):
                        nk = 16 if (smp and d == 0) else 128
                        if smp:
                            vcol = validb[:nk, SMP:SMP + 1] if d == 0 else ones[:nk, 0:1]
                        else:
                            vcol = validb[:nk, ui:ui + 1]
                        S.op("pe", lambda e, m=m, d=d, nk=nk, vcol=vcol: e.matmul(P1[:nq, m:m + 1], Ec[d][:nk, m * nq:(m + 1) * nq], vcol, start=(d == 0), stop=(d == nkeys - 1)),
                             EcB[d] + [cB], [P1B])
                S.op("act", lambda e: e.copy(ogc[:nq, :], Q[2][:nq, 0:512]), [PSB[4]], [ogcB])
                rden = small["rden"]
                S.op("dve", lambda e: e.tensor_scalar(rden[:nq, :], P1[:nq, 0:8], 1e-30, None, ALU.add), [P1B], [smB["rden"]])
                S.op("dve", lambda e: e.reciprocal(rden[:nq, :], rden[:nq, :]), [smB["rden"]], [smB["rden"]])
                S.op("dve", lambda e: e.tensor_tensor(mgc[:nq, half * 512:(half + 1) * 512].rearrange("p (h d) -> p h d", h=8), ogc[:nq, :].rearrange("p (h d) -> p h d", h=8),
                                                      rden[:nq, :].unsqueeze(2).to_broadcast([nq, 8, 64]), ALU.mult), [ogcB, smB["rden"]], [mgcB])
            pt, ptB = getpt()
            for k in range(8):
                S.op("pe", lambda e, k=k: e.transpose(pt[:, k * 128:k * 128 + nq], mgc[:nq, k * 128:(k + 1) * 128], ident[:nq, :nq]), [mgcB, cB], [ptB])
            S.op("act", lambda e: e.copy(HT[:, :, i * 128:i * 128 + nq], pt.rearrange("p (k t) -> p k t", k=8)[:, :, 0:nq]), [ptB], [HTb[i]])

    nsb = int(os.environ.get("KNSB", "99"))
    older = [list(range(a, min(a + 16, NT))) for a in range(NQ, NT, 16)]
    if STAGE >= 2:
        cache_prep()
        lambda_prep()
    for sbk in older[:nsb]:
        front0(sbk, False)
    tiles, uu = front0(list(range(NQ)), True)

    if STAGE >= 2:
        S.barrier()
        rr["ptn"] = 1
        if not os.environ.get("KNOATT"):
            attn_l0()
        S.barrier()
        rr["ptn"] = 2
        linear_residual(W["ab_w_out"][0], None, tiles, 1.0)
    if STAGE >= 3:
        S.barrier()
        mem_prep(0)
        S.barrier()
        mem_attn(0, tiles)
    if STAGE >= 4:
        S.barrier()
        ffn(0, 2, tiles)

    tiles17 = [(i, 128) for i in range(NOWN)] + [(NQ, 16)]
    if STAGE >= 5:
        S.barrier()
        ffn(1, 1, tiles)
    if STAGE >= 6:
        rstd_tiles(tiles)
        for (i, rows) in tiles:
            norm_to_HT(i, rows)
        cache_prep_c()
        proj_c(tiles, uu)
        S.barrier()
        rr["ptn"] = 1
        band_attn(tiles17)
        rr["ptn"] = 2
        S.barrier()
        linear_residual(W["c_w_out"][0], None, tiles17, 1.0)
    if STAGE >= 7:
        S.barrier()
        mem_prep(1)
        S.barrier()
        mem_attn(1, tiles17)
    if STAGE >= 8:
        S.barrier()
        ffn(1, 2, tiles17)
    if True:
        for i in range(NOWN):
            S.op("pool", lambda e, i=i: e.dma_start(out=o_y[i], in_=X[:, i, :]), [Xb[i]], [], dma=True)
        S.op("pool", lambda e: e.dma_start(out=o_ys, in_=X[0:16, NQ, :]), [Xb[NQ]], [], dma=True)

    S.barrier()
    print("ops", S.nops, "sems", S.nsem, "sbuf_left", nc.sbuf_bytes_remaining)
    return nc


_NC = None


def _consts():
    c = np.zeros((128, 776), np.float32)
    c[:, 520:648] = np.eye(128)[::-1]
    c[:, 648:776] = ((np.arange(128)[None, :] // 64) <= (np.arange(128)[:, None] // 64)).astype(np.float32)
    c[:, 512:520] = ((500000.0 ** (-2.0 * np.arange(8) / 16.0)).astype(np.float32).astype(np.float64) / (2 * np.pi))[None]
    c[:, 0:128] = np.eye(128)
    j = np.arange(128)[:, None]; s = np.arange(128)[None, :]
    c[:, 128:256] = -(j >= s).astype(np.float32)
    c[:, 256:384] = ((j // 64) <= (s // 64)).astype(np.float32)
    c[:, 384:512] = (j < s).astype(np.float32)
    return c


def kernel(**inp):
    global _NC
    if _NC is None:
        _NC = build()
    nc = _NC
    f = lambda a: np.ascontiguousarray(np.asarray(a, dtype=np.float32))
    xprompt = f(inp["x_prompt"])[0].reshape(128, 128, D)
    in_maps = []
    wnames = ["ffn1_g", "ffn1_wg", "ffn1_wu", "ffn1_wd", "ffn2_g", "ffn2_wg", "ffn2_wu", "ffn2_wd", "mix_g", "ab_w_in", "ab_w_out",
              "a_gq", "a_gk", "a_lq1", "a_lk1", "a_lq2", "a_lk2", "a_subln_g", "c_w_in", "c_w_out", "c_gq", "c_gk", "c_bias",
              "mem_g_x", "mem_g_m", "mem_wq", "mem_wk", "mem_wv", "mem_wo", "mem_gq", "mem_gk"]
    wts = {n: f(inp[n]) for n in wnames}
    cst = _consts()
    for c in range(8):
        xpc = np.zeros((NT, 128, D), np.float32)
        pos = np.zeros((128, NT + 1), np.float32)
        val = np.zeros((128, NT + 1), np.float32)
        for u in range(NT):
            g = 16 * c + 15 - u
            if g >= 0:
                xpc[u] = xprompt[g]
                pos[:, u] = g * 128 + np.arange(128)
                val[:, u] = 1.0
        pos[:16, NT] = 1024 + np.arange(16)
        val[:16, NT] = 1.0
        m = {"xp": xpc, "xs": f(inp["x_sample"])[c], "pos": pos, "valid": val, "consts": cst,
             "ca_k": f(inp["cache_a_k"])[0, c].reshape(1024, 512), "ca_v": f(inp["cache_a_v"])[0, c].reshape(1024, 512),
             "cb_k": f(inp["cache_b_k"])[0, c].reshape(1024, 512), "cb_v": f(inp["cache_b_v"])[0, c].reshape(1024, 512),
             "cc_k": f(inp["cache_c_k"])[0, c].reshape(512, 1024), "cc_v": f(inp["cache_c_v"])[0, c].reshape(512, 1024),
             "cm_k": f(inp["cache_mem_k"])[:, c].reshape(2, 256, 1024), "cm_v": f(inp["cache_mem_v"])[:, c].reshape(2, 256, 1024),
             "memp": f(inp["mem_prompt"])[0]}
        m.update(wts)
        in_maps.append(m)
    res = run_bass_kernel_spmd(nc, in_maps, core_ids=list(range(8))).results

    def gat(name, width):
        out = np.zeros((128, 128, width), np.float32)
        for c in range(8):
            for u in range(NOWN):
                out[16 * c + 15 - u] = res[c][name][u]
        return out.reshape(16384, width)
    y_prompt = gat("o_y", D)[None]
    y_sample = np.stack([res[c]["o_ys"] for c in range(8)])
    a_k_p = gat("o_ak", 512).reshape(1, 1, 16384, 8, 64)
    a_v_p = gat("o_av", 512).reshape(1, 1, 16384, 4, 128)
    b_k_p = gat("o_bk", 512).reshape(1, 1, 16384, 8, 64)
    b_v_p = gat("o_bv", 512).reshape(1, 1, 16384, 8, 64)
    c_k_p = np.concatenate([res[7]["o_ck"][3 - t] for t in range(4)], 0).reshape(1, 1, 512, 16, 64)
    c_v_p = np.concatenate([res[7]["o_cv"][3 - t] for t in range(4)], 0).reshape(1, 1, 512, 16, 64)
    mem_k_p = res[0]["o_mk"].reshape(2, 1, 256, 4, 256)
    mem_v_p = res[0]["o_mv"].reshape(2, 1, 256, 4, 256)
    st = lambda n, shp: np.stack([res[c][n] for c in range(8)]).reshape(shp)
    a_k_s = st("o_aks", (1, 8, 16, 8, 64)); a_v_s = st("o_avs", (1, 8, 16, 4, 128))
    b_k_s = st("o_bks", (1, 8, 16, 8, 64)); b_v_s = st("o_bvs", (1, 8, 16, 8, 64))
    c_k_s = st("o_cks", (1, 8, 512, 16, 64)); c_v_s = st("o_cvs", (1, 8, 512, 16, 64))
    return (y_prompt, y_sample, a_k_p, a_v_p, b_k_p, b_v_p, c_k_p, c_v_p, mem_k_p, mem_v_p,
            a_k_s, a_v_s, b_k_s, b_v_s, c_k_s, c_v_s)
```

```python
import os
import math
import numpy as np
import concourse.bass as bass
import concourse.mybir as mybir
from concourse.bass_utils import run_bass_kernel_spmd

F32, BF16 = mybir.dt.float32, mybir.dt.bfloat16
ALU = mybir.AluOpType
AF = mybir.ActivationFunctionType
AX = mybir.AxisListType

D = 1024
FF = 2816
NT = 128
NQ = 20
NOWN = 16
SMP = 128
EPS = 1e-6
NDMA = 24
ROT = 30000
STAGE = int(os.environ.get("KSTAGE", "9"))


class Buf:
    __slots__ = ("w", "r")

    def __init__(self):
        self.w = None
        self.r = {}


class Sched:
    def __init__(self, nc):
        self.nc = nc
        self.E = {"pe": nc.tensor, "act": nc.scalar, "dve": nc.vector, "pool": nc.gpsimd, "sp": nc.sync}
        self.sem, self.cnt, self.nsem = {}, {}, 0
        self.seen = {e: {} for e in self.E}
        self.allsems = []
        for e in self.E:
            self._rot(e)
        self.dsem = [nc.alloc_semaphore(f"dq{i}") for i in range(NDMA)]
        self.dcnt = [0] * NDMA
        self.dnext = 0
        self.nops = 0

    def _rot(self, e):
        self.sem[e] = self.nc.alloc_semaphore(f"s{e}{self.nsem}")
        self.nsem += 1
        self.cnt[e] = 0
        self.allsems.append([self.sem[e], 0])

    def _wait(self, e, tok):
        sem, val = tok[1], tok[2]
        if self.seen[e].get(sem.num, 0) >= val:
            return
        self.E[e].wait_ge(sem, val)
        self.seen[e][sem.num] = val

    def op(self, e, fn, reads=(), writes=(), dma=False):
        for b in reads:
            if b.w is not None:
                t = b.w
                if not (t[0] == e and e == "pe" and not t[3]):
                    self._wait(e, t)
        for b in writes:
            for t in ([b.w] if b.w is not None else []) + list(b.r.values()):
                if t[0] == e and not t[3]:
                    continue
                self._wait(e, t)
        if dma:
            i = self.dnext
            self.dnext = (i + 1) % NDMA
            if self.dcnt[i] > 0:
                self._wait(e, (e, self.dsem[i], 16 * self.dcnt[i], True))
            ins = fn(self.E[e])
            self.dcnt[i] += 1
            ins.then_inc(self.dsem[i], 16)
            tok = (e, self.dsem[i], 16 * self.dcnt[i], True)
        else:
            if self.cnt[e] >= ROT:
                self._rot(e)
            ins = fn(self.E[e])
            self.cnt[e] += 1
            ins.then_inc(self.sem[e], 1)
            tok = (e, self.sem[e], self.cnt[e], False)
            for s in self.allsems:
                if s[0] is self.sem[e]:
                    s[1] = self.cnt[e]
        for b in reads:
            b.r[tok[1].num] = tok
        for b in writes:
            b.w = tok
            b.r = {}
        self.nops += 1
        return tok

    def barrier(self):
        for e in self.E:
            for s, v in self.allsems:
                if v > 0:
                    self._wait(e, (None, s, v, False))
            for i in range(NDMA):
                if self.dcnt[i] > 0:
                    self._wait(e, (None, self.dsem[i], 16 * self.dcnt[i], True))


def build():
    nc = bass.Bass("TRN2", target_bir_lowering=False)
    S = Sched(nc)

    def din(name, shape):
        return nc.dram_tensor(name, list(shape), F32, kind="ExternalInput").ap()

    def dout(name, shape):
        return nc.dram_tensor(name, list(shape), F32, kind="ExternalOutput").ap()

    def sb(name, shape, dt=F32):
        return nc.alloc_sbuf_tensor("sb_" + name, list(shape), dt).ap()

    xp = din("xp", [NT, 128, D])
    xs = din("xs", [16, D])
    posd = din("pos", [128, NT + 1])
    validd = din("valid", [128, NT + 1])
    constd = din("consts", [128, 776])
    ca_k = din("ca_k", [1024, 512]); ca_v = din("ca_v", [1024, 512])
    cb_k = din("cb_k", [1024, 512]); cb_v = din("cb_v", [1024, 512])
    cc_k = din("cc_k", [512, 1024]); cc_v = din("cc_v", [512, 1024])
    cm_k = din("cm_k", [2, 256, 1024]); cm_v = din("cm_v", [2, 256, 1024])
    memp = din("memp", [256, 1024])
    W = {}
    for nm, shp in [("ffn1_g", [2, D]), ("ffn1_wg", [2, D, FF]), ("ffn1_wu", [2, D, FF]), ("ffn1_wd", [2, FF, D]),
                    ("ffn2_g", [2, D]), ("ffn2_wg", [2, D, FF]), ("ffn2_wu", [2, D, FF]), ("ffn2_wd", [2, FF, D]),
                    ("mix_g", [2, D]), ("ab_w_in", [1, D, 3072]), ("ab_w_out", [1, D, D]),
                    ("a_gq", [1, 64]), ("a_gk", [1, 64]), ("a_lq1", [1, 64]), ("a_lk1", [1, 64]),
                    ("a_lq2", [1, 64]), ("a_lk2", [1, 64]), ("a_subln_g", [1, 128]),
                    ("c_w_in", [1, D, 3072]), ("c_w_out", [1, D, D]), ("c_gq", [1, 64]), ("c_gk", [1, 64]),
                    ("c_bias", [1, 16, 257]), ("mem_g_x", [2, D]), ("mem_g_m", [2, D]),
                    ("mem_wq", [2, D, D]), ("mem_wk", [2, D, D]), ("mem_wv", [2, D, D]), ("mem_wo", [2, D, D]),
                    ("mem_gq", [2, 256]), ("mem_gk", [2, 256])]:
        W[nm] = din(nm, shp)

    o_y = dout("o_y", [NOWN, 128, D]); o_ys = dout("o_ys", [16, D])
    o_ak = dout("o_ak", [NOWN, 128, 512]); o_av = dout("o_av", [NOWN, 128, 512])
    o_bk = dout("o_bk", [NOWN, 128, 512]); o_bv = dout("o_bv", [NOWN, 128, 512])
    o_ck = dout("o_ck", [4, 128, D]); o_cv = dout("o_cv", [4, 128, D])
    o_mk = dout("o_mk", [2, 256, D]); o_mv = dout("o_mv", [2, 256, D])
    o_aks = dout("o_aks", [16, 512]); o_avs = dout("o_avs", [16, 512])
    o_bks = dout("o_bks", [16, 512]); o_bvs = dout("o_bvs", [16, 512])
    o_cks = dout("o_cks", [512, D]); o_cvs = dout("o_cvs", [512, D])

    def dscr(name, shape):
        return nc.dram_tensor(name, list(shape), BF16).ap()
    kta_s = dscr("kta_s", [NT + 9, 128, 512]); ktb_s = dscr("ktb_s", [NT + 9, 128, 512])
    va_s = dscr("va_s", [NT + 9, 128, 520]); vb_s = dscr("vb_s", [NT + 9, 128, 512])
    qta_s = dscr("qta_s", [NQ + 1, 128, 512]); qtb_s = dscr("qtb_s", [NQ + 1, 128, 512])
    scrB = {"kta": [Buf() for _ in range(NT + 9)], "ktb": [Buf() for _ in range(NT + 9)],
            "va": [Buf() for _ in range(NT + 9)], "vb": [Buf() for _ in range(NT + 9)],
            "qta": [Buf() for _ in range(NQ + 1)], "qtb": [Buf() for _ in range(NQ + 1)]}
    outB = Buf()
    inB = Buf()

    NX = NQ + 1
    X = sb("X", [128, NX, D])
    Xb = [Buf() for _ in range(NX)]
    HT = sb("HT", [128, 8, NQ * 128 + 16], BF16)
    HTb = [Buf() for _ in range(NX)]
    cst = sb("cst", [128, 776]); cstB = Buf()
    ident = sb("ident", [128, 128], BF16); negU = sb("negU", [128, 128], BF16)
    maskA = sb("maskA", [128, 128], BF16); maskB = sb("maskB", [128, 128], BF16)
    ones = sb("ones", [128, 1], BF16)
    cB = Buf()
    pos = sb("pos", [128, NT + 1]); valid = sb("valid", [128, NT + 1]); validb = sb("validb", [128, NT + 1], BF16)
    gT = sb("gT", [128, 80]); gTB = Buf()
    g64 = {n: sb("g_" + n, [128, 64]) for n in ("a_gq", "a_gk", "c_gq", "c_gk")}
    gB = Buf()

    S.op("sp", lambda e: e.dma_start(out=cst, in_=constd), [], [cstB], dma=True)
    S.op("sp", lambda e: e.dma_start(out=pos, in_=posd), [], [cB], dma=True)
    S.op("sp", lambda e: e.dma_start(out=valid, in_=validd), [], [cB], dma=True)
    S.op("dve", lambda e: e.tensor_copy(ident, cst[:, 0:128]), [cstB], [cB])
    S.op("dve", lambda e: e.tensor_copy(negU, cst[:, 128:256]), [cstB], [cB])
    S.op("dve", lambda e: e.tensor_copy(maskA, cst[:, 256:384]), [cstB], [cB])
    S.op("dve", lambda e: e.tensor_copy(maskB, cst[:, 384:512]), [cstB], [cB])
    S.op("dve", lambda e: e.memset(ones, 1.0), [], [cB])
    S.op("dve", lambda e: e.tensor_copy(validb, valid), [cB], [cB])
    for n in g64:
        S.op("sp", lambda e, n=n: e.dma_start(out=g64[n], in_=W[n][0:1, :].partition_broadcast(128)), [], [gB], dma=True)
    for n in ("a_gq", "c_gq"):
        S.op("dve", lambda e, n=n: e.tensor_scalar(g64[n], g64[n], 0.125, None, ALU.mult), [gB], [gB])

    Q = [nc.alloc_psum_tensor(f"q{i}", [128, 1024], F32).ap() for i in range(4)]
    PS = [Q[i // 2][:, (i % 2) * 512:(i % 2 + 1) * 512] for i in range(6)]
    PSB = [Buf() for _ in range(6)]
    PT = [Q[3][:, h * 512:(h + 1) * 512].bitcast(BF16) for h in range(2)]
    PTB = [Buf() for _ in range(2)]
    P1 = Q[3][:, 0:512]
    P1B = PTB[0]
    rr = {"ps": 0, "pt": 0, "ptn": 2}

    def getps():
        i = rr["ps"]; rr["ps"] = (i + 1) % 6
        return PS[i], PSB[i]

    def getpt():
        if rr["ptn"] == 1:
            return PT[1], PTB[1]
        i = rr["pt"]; rr["pt"] = (i + 1) % 2
        return PT[i], PTB[i]

    GID = {"ffn1_g": 0, "ffn2_g": 2, "mix_g": 4, "mem_g_x": 6, "mem_g_m": 8}
    graw = sb("graw", [80, 128]); grawB = Buf()
    for n, gi in GID.items():
        S.op("sp", lambda e, n=n, gi=gi: e.dma_start(out=graw[gi * 8:(gi + 2) * 8, :],
                                                     in_=W[n].rearrange("l (k p) -> (l k) p", p=128)), [], [grawB], dma=True)
    identf = cst[:, 0:128]
    pg, pgB = getps()
    S.op("pe", lambda e: e.transpose(pg[:, 0:80], graw[0:80, :], identf[0:80, 0:80]), [grawB, cstB], [pgB])
    S.op("dve", lambda e: e.tensor_copy(gT, pg[:, 0:80]), [pgB], [gTB])

    ARENA = 15360
    arena = sb("arena", [128, ARENA])

    def av(off, n, dt=F32):
        v = arena[:, off:off + n]
        return v if dt == F32 else v.bitcast(dt)
    Wb = [av(3072 * i, 3072, BF16).rearrange("p (a f) -> p a f", a=3) for i in range(2)]
    WbB = [[Buf() for _ in range(3)] for _ in range(2)]
    stg = [av(6144 + 2048 * i, 2048) for i in range(3)]
    stgB = [Buf() for _ in range(3)]
    hid = [av(12288 + 512 * i, 512, BF16).rearrange("p (j t) -> p j t", j=2) for i in range(2)]
    hidB = [Buf() for _ in range(2)]
    sgt = [av(13312 + 512 * i, 512) for i in range(2)]
    sgB = [Buf() for _ in range(2)]
    xn = [av(14336 + 512 * i, 512, BF16) for i in range(2)]
    xnB = [Buf() for _ in range(2)]
    junk = sb("junk", [128, D], BF16); junkB = Buf()
    st = [sb(f"st{i}", [128, 8]) for i in range(4)]
    stB = [Buf() for _ in range(4)]
    cosT = sb("cosT", [128, NX, 8]); sinT = sb("sinT", [128, NX, 8]); csB = Buf()
    wk = [sb(f"wk{i}", [128, 256]) for i in range(4)]
    wkB = [Buf() for _ in range(4)]
    wkb = [sb(f"wkb{i}", [128, 264], BF16) for i in range(2)]
    wkbB = [Buf() for _ in range(2)]
    ktile = [sb(f"ktile{i}", [128, 2, 128], BF16) for i in range(2)]
    ktB = [Buf() for _ in range(2)]
    rp = [sb(f"rp{i}", [128, 4, 8]) for i in range(4)]
    rpB = [Buf() for _ in range(4)]
    cnt = {"stg": 0, "xn": 0, "st": 0, "wk": 0, "wkb": 0, "kt": 0, "hid": 0, "sg": 0}

    def nxt(k, n):
        i = cnt[k]; cnt[k] = (i + 1) % n
        return i

    rs = sb("rs", [128, NX]); rsB = Buf()

    def rstd_tiles(tiles):
        S.op("dve", lambda e: e.memset(rs, 0.0), [], [rsB])
        for (i, rows) in tiles:
            S.op("act", lambda e, i=i, rows=rows: e.activation(junk[:rows, :], X[:rows, i, :], AF.Square, accum_out=rs[:rows, i:i + 1]),
                 [Xb[i]], [junkB, rsB])
        S.op("dve", lambda e: e.tensor_scalar(rs, rs, 1.0 / D, EPS, ALU.mult, ALU.add), [rsB], [rsB])
        S.op("act", lambda e: e.activation(rs, rs, AF.Ln), [rsB], [rsB])
        S.op("act", lambda e: e.activation(rs, rs, AF.Exp, scale=-0.5), [rsB], [rsB])

    def norm_to_HT(i, rows):
        r, rB = rs[:, i:i + 1], rsB
        j = nxt("xn", 2)
        S.op("pool", lambda e: e.tensor_scalar(xn[j][:rows, :], X[:rows, i, :], r[:rows, 0:1], None, ALU.mult),
             [Xb[i], rB], [xnB[j]])
        pt, ptB = getpt()
        for k in range(8):
            S.op("pe", lambda e, k=k: e.transpose(pt[:, k * 128:k * 128 + rows], xn[j][:rows, k * 128:(k + 1) * 128],
                                                  ident[:rows, :rows]), [xnB[j], cB], [ptB])
        src = pt.rearrange("p (k t) -> p k t", k=8)[:, :, 0:rows]
        dst = HT[:, :, i * 128:i * 128 + rows]
        if i % 2 == 0:
            S.op("act", lambda e: e.copy(dst, src), [ptB], [HTb[i]])
        else:
            S.op("dve", lambda e: e.tensor_copy(dst, src), [ptB], [HTb[i]])

    def load_w(dst, dstB, src_ap, gcol, kparts, width):
        s = nxt("stg", lim["stg"])
        S.op("sp", lambda e: e.dma_start(out=stg[s][:, 0:kparts * width].rearrange("p (k f) -> p k f", k=kparts),
                                         in_=src_ap.rearrange("(k p) f -> p k f", p=128)), [], [stgB[s]], dma=True)
        d3 = dst.rearrange("p (k f) -> p k f", k=kparts)
        s3 = stg[s][:, 0:kparts * width].rearrange("p (k f) -> p k f", k=kparts)
        if gcol is None:
            S.op("pool", lambda e: e.tensor_copy(d3, s3), [stgB[s]], [dstB])
        else:
            S.op("pool", lambda e: e.tensor_tensor(d3, s3, gT[:, gcol:gcol + kparts].unsqueeze(2).to_broadcast([128, kparts, width]),
                                                   ALU.mult), [stgB[s], gTB], [dstB])

    def ffn(l, which, tiles):
        pre = f"ffn{which}_"
        gcol = (GID[pre + "g"] + l) * 8
        rstd_tiles(tiles)
        for (i, rows) in tiles:
            norm_to_HT(i, rows)
        blocks = [tiles[b:b + 4] for b in range(0, len(tiles), 4)]
        for fg in range(FF // 256):
            ws = fg % 2
            load_w(Wb[ws][:, 0, :], WbB[ws][0], W[pre + "wg"][l][:, fg * 256:(fg + 1) * 256], gcol, 8, 256)
            load_w(Wb[ws][:, 1, :], WbB[ws][1], W[pre + "wu"][l][:, fg * 256:(fg + 1) * 256], gcol, 8, 256)
            load_w(Wb[ws][:, 2, :], WbB[ws][2], W[pre + "wd"][l][fg * 256:(fg + 1) * 256, :], None, 2, 1024)
            wg3 = Wb[ws][:, 0, :].rearrange("p (k f) -> p k f", k=8)
            wu3 = Wb[ws][:, 1, :].rearrange("p (k f) -> p k f", k=8)
            wd3 = Wb[ws][:, 2, :].rearrange("p (k f) -> p k f", k=2)
            for blk in blocks:
                c0 = blk[0][0] * 128
                ncol = (blk[-1][0] - blk[0][0]) * 128 + blk[-1][1]
                hB = [HTb[i] for i, _ in blk]
                hi = nxt("hid", 2)
                for j in range(2):
                    pgt, pgtB = getps()
                    put, putB = getps()
                    for k in range(8):
                        S.op("pe", lambda e, k=k, j=j: e.matmul(pgt[:, 0:ncol], wg3[:, k, j * 128:(j + 1) * 128], HT[:, k, c0:c0 + ncol],
                                                                start=(k == 0), stop=(k == 7)), hB + [WbB[ws][0]], [pgtB])
                    for k in range(8):
                        S.op("pe", lambda e, k=k, j=j: e.matmul(put[:, 0:ncol], wu3[:, k, j * 128:(j + 1) * 128], HT[:, k, c0:c0 + ncol],
                                                                start=(k == 0), stop=(k == 7)), hB + [WbB[ws][1]], [putB])
                    si = nxt("sg", 2)
                    S.op("act", lambda e: e.activation(sgt[si][:, 0:ncol], pgt[:, 0:ncol], AF.Silu), [pgtB], [sgB[si]])
                    S.op("dve", lambda e, j=j: e.tensor_tensor(hid[hi][:, j, 0:ncol], sgt[si][:, 0:ncol], put[:, 0:ncol], ALU.mult),
                         [sgB[si], putB], [hidB[hi]])
                for (i, rows) in blk:
                    o = i * 128 - c0
                    for half in range(2):
                        pd, pdB = getps()
                        for j in range(2):
                            S.op("pe", lambda e, j=j: e.matmul(pd[:rows, :], hid[hi][:, j, o:o + rows], wd3[:, j, half * 512:(half + 1) * 512],
                                                               start=(j == 0), stop=(j == 1)), [hidB[hi], WbB[ws][2]], [pdB])
                        S.op("dve", lambda e: e.scalar_tensor_tensor(X[:rows, i, half * 512:(half + 1) * 512], pd[:rows, :], 0.5,
                                                                     X[:rows, i, half * 512:(half + 1) * 512], ALU.mult, ALU.add),
                             [pdB, Xb[i]], [Xb[i]])

    rt = sb("rt", [128, NX, 8]); rtf = sb("rtf", [128, NX, 8]); rti = sb("rti", [128, NX, 8], mybir.dt.int32); rtB = Buf()

    def rope_tables(i0, n, u0):
        sl = slice(i0, i0 + n)
        pb = pos[:, u0:u0 + n].unsqueeze(2).to_broadcast([128, n, 8])
        fb = cst[:, 512:520].unsqueeze(1).to_broadcast([128, n, 8])
        for tab, shift in ((sinT, 0.0), (cosT, 0.25)):
            S.op("dve", lambda e: e.tensor_tensor(rt[:, sl, :], pb, fb, ALU.mult), [cB, cstB], [rtB])
            if shift:
                S.op("dve", lambda e, shift=shift: e.tensor_scalar(rt[:, sl, :], rt[:, sl, :], shift, None, ALU.add), [rtB], [rtB])
            S.op("dve", lambda e: e.tensor_copy(rti[:, sl, :], rt[:, sl, :]), [rtB], [rtB])
            S.op("dve", lambda e: e.tensor_copy(rtf[:, sl, :], rti[:, sl, :]), [rtB], [rtB])
            S.op("dve", lambda e: e.tensor_tensor(rt[:, sl, :], rt[:, sl, :], rtf[:, sl, :], ALU.subtract), [rtB], [rtB])
            S.op("dve", lambda e: e.tensor_scalar(rtf[:, sl, :], rt[:, sl, :], 0.5, None, ALU.is_gt), [rtB], [rtB])
            S.op("dve", lambda e: e.tensor_tensor(rt[:, sl, :], rt[:, sl, :], rtf[:, sl, :], ALU.subtract), [rtB], [rtB])
            S.op("dve", lambda e: e.tensor_scalar(rtf[:, sl, :], rt[:, sl, :], -0.5, None, ALU.is_lt), [rtB], [rtB])
            S.op("dve", lambda e: e.tensor_tensor(rt[:, sl, :], rt[:, sl, :], rtf[:, sl, :], ALU.add), [rtB], [rtB])
            S.op("act", lambda e, tab=tab: e.activation(tab[:, sl, :], rt[:, sl, :], AF.Sin, scale=2 * math.pi), [rtB], [csB, rtB])

    def to_T_and_store(src_bf, srcB, rows, dst_ap, dstB):
        pt, ptB = getpt()
        for k in range(2):
            S.op("pe", lambda e, k=k: e.transpose(pt[:, k * 128:k * 128 + rows], src_bf[:rows, k * 128:(k + 1) * 128], ident[:rows, :rows]),
                 [srcB, cB], [ptB])
        t = nxt("kt", 2)
        S.op("act", lambda e: e.copy(ktile[t][:, :, 0:rows], pt[:, 0:256].rearrange("p (k t) -> p k t", k=2)[:, :, 0:rows]), [ptB], [ktB[t]])
        S.op("pool", lambda e: e.dma_start(out=dst_ap, in_=ktile[t][:, :, 0:rows]), [ktB[t]], [dstB], dma=True)

    def proj_ab(tiles, us):
        gcol = (GID["mix_g"] + 0) * 8
        for pc in range(int(os.environ.get("KPC", "12"))):
            ws = pc % 2
            kind, hp = ("aq", "ak", "av", "bq", "bk", "bv")[pc // 2], pc % 2
            load_w(Wb[ws][:, 0, :], WbB[ws][0], W["ab_w_in"][0][:, pc * 256:(pc + 1) * 256], gcol, 8, 256)
            w3 = Wb[ws][:, 0, :].rearrange("p (k f) -> p k f", k=8)
            for (i, rows), u in zip(tiles, us):
                own = (u < NOWN) or (u == SMP)
                isq = kind in ("aq", "bq")
                if isq and not (u < NQ or u == SMP):
                    continue
                qi = NQ if u == SMP else u
                pp, ppB = getps()
                for k in range(8):
                    S.op("pe", lambda e, k=k: e.matmul(pp[:rows, 0:256], HT[:, k, i * 128:i * 128 + rows], w3[:, k, :],
                                                       start=(k == 0), stop=(k == 7)), [HTb[i], WbB[ws][0]], [ppB])
                src = pp[:rows, 0:256]
                csl = slice(hp * 256, (hp + 1) * 256)
                if kind in ("aq", "ak"):
                    g = g64["a_gq" if kind == "aq" else "a_gk"]
                    a, b = nxt("wk", 4), nxt("wk", 4)
                    s_ = nxt("st", 4)
                    S.op("act", lambda e: e.copy(wk[b][:rows, :], src), [ppB], [wkB[b]])
                    S.op("dve", lambda e: e.tensor_tensor(wk[a][:rows, :], wk[b][:rows, :], wk[b][:rows, :], ALU.mult), [wkB[b]], [wkB[a]])
                    S.op("dve", lambda e: e.tensor_reduce(st[s_][:rows, 0:4], wk[a][:rows, :].rearrange("p (h d) -> p h d", h=4), AX.X, ALU.add),
                         [wkB[a]], [stB[s_]])
                    S.op("dve", lambda e: e.tensor_scalar(st[s_][:rows, 0:4], st[s_][:rows, 0:4], 1.0 / 64, EPS, ALU.mult, ALU.add), [stB[s_]], [stB[s_]])
                    S.op("act", lambda e: e.activation(st[s_][:rows, 0:4], st[s_][:rows, 0:4], AF.Ln), [stB[s_]], [stB[s_]])
                    S.op("act", lambda e: e.activation(st[s_][:rows, 0:4], st[s_][:rows, 0:4], AF.Exp, scale=-0.5), [stB[s_]], [stB[s_]])
                    w3a = wk[a][:rows, :].rearrange("p (h d) -> p h d", h=4)
                    w3b = wk[b][:rows, :].rearrange("p (h d) -> p h d", h=4)
                    S.op("dve", lambda e: e.tensor_tensor(w3a, src.rearrange("p (h d) -> p h d", h=4),
                                                          st[s_][:rows, 0:4].unsqueeze(2).to_broadcast([rows, 4, 64]), ALU.mult), [ppB, stB[s_]], [wkB[a]])
                    S.op("dve", lambda e: e.tensor_tensor(w3b, w3a, g[:rows, :].unsqueeze(1).to_broadcast([rows, 4, 64]), ALU.mult), [wkB[a], gB], [wkB[b]])
                    cs = cosT[:rows, i, :].unsqueeze(1).to_broadcast([rows, 4, 8])
                    sn = sinT[:rows, i, :].unsqueeze(1).to_broadcast([rows, 4, 8])
                    x1, x2 = w3b[:, :, 0:8], w3b[:, :, 8:16]
                    r = [rp[q][:rows] for q in range(4)]
                    S.op("dve", lambda e: e.tensor_tensor(r[0], x1, cs, ALU.mult), [wkB[b], csB], [rpB[0]])
                    S.op("dve", lambda e: e.tensor_tensor(r[1], x2, sn, ALU.mult), [wkB[b], csB], [rpB[1]])
                    S.op("dve", lambda e: e.tensor_tensor(r[2], x2, cs, ALU.mult), [wkB[b], csB], [rpB[2]])
                    S.op("dve", lambda e: e.tensor_tensor(r[3], x1, sn, ALU.mult), [wkB[b], csB], [rpB[3]])
                    S.op("dve", lambda e: e.tensor_tensor(x1, r[0], r[1], ALU.subtract), [rpB[0], rpB[1]], [wkB[b]])
                    S.op("dve", lambda e: e.tensor_tensor(x2, r[2], r[3], ALU.add), [rpB[2], rpB[3]], [wkB[b]])
                    fin, finB = wk[b], wkB[b]
                    if kind == "ak" and own:
                        dst = o_aks[:, csl] if u == SMP else o_ak[u][:, csl]
                        S.op("pool", lambda e, dst=dst: e.dma_start(out=dst[:rows], in_=fin[:rows, :]), [finB], [], dma=True)
                    c = nxt("wkb", 2)
                    S.op("pool", lambda e: e.tensor_copy(wkb[c][:rows, 0:256], fin[:rows, :]), [finB], [wkbB[c]])
                    if kind == "ak":
                        to_T_and_store(wkb[c], wkbB[c], rows, kta_s[u].rearrange("p (k t) -> p k t", k=4)[:, hp * 2:hp * 2 + 2, 0:rows], scrB["kta"][u])
                    else:
                        to_T_and_store(wkb[c], wkbB[c], rows, qta_s[qi].rearrange("p (k t) -> p k t", k=4)[:, hp * 2:hp * 2 + 2, 0:rows], scrB["qta"][qi])
                elif kind in ("bq", "bk"):
                    c = nxt("wkb", 2)
                    if kind == "bk":
                        a = nxt("wk", 4)
                        S.op("act", lambda e: e.copy(wk[a][:rows, :], src), [ppB], [wkB[a]])
                        if own:
                            dst = o_bks[:, csl] if u == SMP else o_bk[u][:, csl]
                            S.op("pool", lambda e, dst=dst: e.dma_start(out=dst[:rows], in_=wk[a][:rows, :]), [wkB[a]], [], dma=True)
                        S.op("dve", lambda e: e.tensor_copy(wkb[c][:rows, 0:256], wk[a][:rows, :]), [wkB[a]], [wkbB[c]])
                        to_T_and_store(wkb[c], wkbB[c], rows, ktb_s[u].rearrange("p (k t) -> p k t", k=4)[:, hp * 2:hp * 2 + 2, 0:rows], scrB["ktb"][u])
                    else:
                        S.op("dve", lambda e: e.tensor_scalar(wkb[c][:rows, 0:256], src, 0.125, None, ALU.mult), [ppB], [wkbB[c]])
                        to_T_and_store(wkb[c], wkbB[c], rows, qtb_s[qi].rearrange("p (k t) -> p k t", k=4)[:, hp * 2:hp * 2 + 2, 0:rows], scrB["qtb"][qi])
                else:
                    a = nxt("wk", 4)
                    S.op("act", lambda e: e.copy(wk[a][:rows, :], src), [ppB], [wkB[a]])
                    if own:
                        od = {"av": (o_avs, o_av), "bv": (o_bvs, o_bv)}[kind]
                        dst = od[0][:, csl] if u == SMP else od[1][u][:, csl]
                        S.op("pool", lambda e, dst=dst: e.dma_start(out=dst[:rows], in_=wk[a][:rows, :]), [wkB[a]], [], dma=True)
                    c = nxt("wkb", 2)
                    if kind == "av":
                        v3 = wkb[c][:rows, 0:260].rearrange("p (h d) -> p h d", h=2)
                        S.op("dve", lambda e: e.tensor_scalar(v3[:, :, 0:128], wk[a][:rows, :].rearrange("p (h d) -> p h d", h=2), valid[:rows, u:u + 1], None, ALU.mult),
                             [wkB[a], cB], [wkbB[c]])
                        S.op("dve", lambda e: e.tensor_copy(v3[:, :, 128:130], valid[:rows, u:u + 1].unsqueeze(1).to_broadcast([rows, 2, 2])), [cB], [wkbB[c]])
                        S.op("pool", lambda e: e.dma_start(out=va_s[u][:rows, hp * 260:(hp + 1) * 260], in_=wkb[c][:rows, 0:260]), [wkbB[c]], [scrB["va"][u]], dma=True)
                    else:
                        S.op("dve", lambda e: e.tensor_scalar(wkb[c][:rows, 0:256], wk[a][:rows, :], valid[:rows, u:u + 1], None, ALU.mult), [wkB[a], cB], [wkbB[c]])
                        S.op("pool", lambda e: e.dma_start(out=vb_s[u][:rows, csl], in_=wkb[c][:rows, 0:256]), [wkbB[c]], [scrB["vb"][u]], dma=True)

    def front0(us, with_sample):
        tiles = [(i, 128) for i in range(len(us))]
        uu = list(us)
        for i, u in enumerate(us):
            S.op("sp", lambda e, i=i, u=u: e.dma_start(out=X[:, i, :], in_=xp[u]), [], [Xb[i]], dma=True)
        if with_sample:
            i = len(us)
            S.op("sp", lambda e: e.dma_start(out=X[0:16, i, :], in_=xs), [], [Xb[i]], dma=True)
            tiles.append((i, 16)); uu.append(SMP)
        DBG = int(os.environ.get("KDBG", "9"))
        if DBG >= 1:
            rope_tables(0, len(us), us[0])
            if with_sample:
                rope_tables(len(us), 1, SMP)
        if DBG >= 3:
            ffn(0, 1, tiles)
        if DBG >= 2:
            rstd_tiles(tiles)
            for (i, rows) in tiles:
                norm_to_HT(i, rows)
        if DBG >= 4:
            proj_ab(tiles, uu)
        return tiles, uu

    SQ = NQ
    CIDX = NT + 1
    small = {n: sb("sm_" + n, shp) for n, shp in (("den", [128, 4, 8]), ("carry", [128, 4, 8]), ("ecarry", [128, 4, 8]),
                                                   ("rden", [128, 8]), ("s4", [128, 4]), ("lam", [128, 4]))}
    smB = {n: Buf() for n in small}
    lt = sb("lt", [128, 4, 64]); ltB = Buf()
    gsub = sb("gsub", [128, 128]); gsubB = Buf()

    def cache_prep():
        for t in range(8):
            rsl = slice(t * 128, (t + 1) * 128)
            for (src, dst, key) in ((ca_k, kta_s, "kta"), (cb_k, ktb_s, "ktb")):
                s_ = nxt("stg", 3)
                S.op("sp", lambda e, src=src: e.dma_start(out=stg[s_][:, 0:512], in_=src[rsl, :]), [], [stgB[s_]], dma=True)
                j = nxt("xn", 2)
                S.op("pool", lambda e: e.tensor_copy(xn[j][:, 0:512], stg[s_][:, 0:512]), [stgB[s_]], [xnB[j]])
                pt, ptB = getpt()
                for k in range(4):
                    S.op("pe", lambda e, k=k: e.transpose(pt[:, k * 128:(k + 1) * 128], xn[j][:, k * 128:(k + 1) * 128], ident), [xnB[j], cB], [ptB])
                S.op("act", lambda e: e.copy(xn[j][:, 512:1024], pt[:, 0:512]), [ptB], [xnB[j]])
                S.op("pool", lambda e, dst=dst: e.dma_start(out=dst[CIDX + t], in_=xn[j][:, 512:1024]), [xnB[j]], [scrB[key][CIDX + t]], dma=True)
            s_ = nxt("stg", 3)
            S.op("sp", lambda e: e.dma_start(out=stg[s_][:, 0:512], in_=ca_v[rsl, :]), [], [stgB[s_]], dma=True)
            j = nxt("xn", 2)
            v4 = xn[j][:, 0:520].rearrange("p (h d) -> p h d", h=4)
            S.op("pool", lambda e: e.tensor_copy(v4[:, :, 0:128], stg[s_][:, 0:512].rearrange("p (h d) -> p h d", h=4)), [stgB[s_]], [xnB[j]])
            S.op("pool", lambda e: e.memset(v4[:, :, 128:130], 1.0), [], [xnB[j]])
            S.op("pool", lambda e: e.dma_start(out=va_s[CIDX + t], in_=xn[j][:, 0:520]), [xnB[j]], [scrB["va"][CIDX + t]], dma=True)
            s_ = nxt("stg", 3)
            S.op("sp", lambda e: e.dma_start(out=stg[s_][:, 0:512], in_=cb_v[rsl, :]), [], [stgB[s_]], dma=True)
            j = nxt("xn", 2)
            S.op("pool", lambda e: e.tensor_copy(xn[j][:, 0:512], stg[s_][:, 0:512]), [stgB[s_]], [xnB[j]])
            S.op("pool", lambda e: e.dma_start(out=vb_s[CIDX + t], in_=xn[j][:, 0:512]), [xnB[j]], [scrB["vb"][CIDX + t]], dma=True)

    def lambda_prep():
        for q, n in enumerate(("a_lq1", "a_lk1", "a_lq2", "a_lk2")):
            S.op("sp", lambda e, q=q, n=n: e.dma_start(out=lt[:, q, :], in_=W[n][0:1, :].partition_broadcast(128)), [], [ltB], dma=True)
        S.op("sp", lambda e: e.dma_start(out=gsub, in_=W["a_subln_g"][0:1, :].partition_broadcast(128)), [], [gsubB], dma=True)
        S.op("dve", lambda e: e.tensor_scalar(gsub, gsub, 0.8, None, ALU.mult), [gsubB], [gsubB])
        lam = small["lam"]
        S.op("dve", lambda e: e.tensor_tensor(lt[:, 0, :], lt[:, 0, :], lt[:, 1, :], ALU.mult), [ltB], [ltB])
        S.op("dve", lambda e: e.tensor_tensor(lt[:, 2, :], lt[:, 2, :], lt[:, 3, :], ALU.mult), [ltB], [ltB])
        S.op("dve", lambda e: e.tensor_reduce(lam[:, 0:1], lt[:, 0, :], AX.X, ALU.add), [ltB], [smB["lam"]])
        S.op("dve", lambda e: e.tensor_reduce(lam[:, 1:2], lt[:, 2, :], AX.X, ALU.add), [ltB], [smB["lam"]])
        S.op("act", lambda e: e.activation(lam[:, 0:2], lam[:, 0:2], AF.Exp), [smB["lam"]], [smB["lam"]])
        S.op("dve", lambda e: e.tensor_tensor(lam[:, 2:3], lam[:, 1:2], lam[:, 0:1], ALU.subtract), [smB["lam"]], [smB["lam"]])
        S.op("dve", lambda e: e.tensor_scalar(lam[:, 2:3], lam[:, 2:3], -0.2, None, ALU.add), [smB["lam"]], [smB["lam"]])

    def attn_l0():
        accA = [av(1024 * j, 1024) for j in range(4)]; accAB = [Buf() for _ in range(4)]
        accB = [av(4096 + 512 * j, 512) for j in range(4)]; accBB = [Buf() for _ in range(4)]
        QTe = av(6144, 1024, BF16).rearrange("p (k t) -> p k t", k=4); QTB = Buf()
        QTo = av(13856, 1024, BF16).rearrange("p (k t) -> p k t", k=4)
        QTeo = (QTe, QTo)
        S.op("pool", lambda e: e.memset(QTe[64:128, :, :], 0.0), [], [QTB])
        S.op("pool", lambda e: e.memset(QTo[0:64, :, :], 0.0), [], [QTB])
        kvb = [av(7168 + 528 * i, 528, BF16) for i in range(2)]; kvB = [Buf() for _ in range(2)]
        Eb = [av(8224 + 512 * i, 512, BF16) for i in range(2)]; EbB = [[Buf(), Buf()] for _ in range(2)]
        ef = [av(9248 + 512 * i, 512) for i in range(2)]; efB = [Buf() for _ in range(2)]
        pvs = [av(10272 + 256 * i, 256) for i in range(2)]; pvsB = [Buf() for _ in range(2)]
        tmpB = [av(10784 + 256 * i, 256) for i in range(2)]; tmpBB = [Buf() for _ in range(2)]
        ofin = av(11296, 1024); ofinB = Buf()
        mg = av(12320, 512, BF16); mgB = Buf()
        sqf = av(12832, 512); sqfB = Buf()
        aof = av(13344, 512); aofB = Buf()
        QhB = [[Buf(), Buf()] for _ in range(3)]
        den, carry, ecarry = small["den"], small["carry"], small["ecarry"]
        zc = {"z": 0, "e": 0}

        SPB = [Eb[0], av(12832, 512, BF16)]; WBv = [Eb[1], av(13344, 512, BF16)]
        SPBB = [[Buf(), Buf()], [Buf(), Buf()]]; WBB = [[Buf(), Buf()], [Buf(), Buf()]]
        P1Bx = [P1B, Buf()]
        par_c = {"p": 0}

        def run_pipeline(gens):
            active = []
            for g in gens:
                active.insert(0, g)
                keep = []
                for gg in active:
                    try:
                        next(gg); keep.append(gg)
                    except StopIteration:
                        pass
                active = keep
            while active:
                keep = []
                for gg in active:
                    try:
                        next(gg); keep.append(gg)
                    except StopIteration:
                        pass
                active = keep

        def pairA(j, nq, nk, KT, V, kB, diag):
            zi = zc["z"]; zc["z"] = 1 - zi
            z = Q[zi]
            for m in range(8):
                S.op("pe", lambda e, m=m: e.matmul(z[:nk, m * nq:(m + 1) * nq], KT[:, m // 2, 0:nk],
                                                   QTeo[m % 2][:, m // 2, j * 128:j * 128 + nq], start=True, stop=True),
                     [kB, QTB], [QhB[zi][0], QhB[zi][1]])
            ei = zc["e"]; zc["e"] = 1 - ei
            E = Eb[ei]
            if 8 * nq > 512:
                for hh in range(2):
                    S.op("act", lambda e, hh=hh: e.activation(E[:nk, hh * 512:(hh + 1) * 512], z[:nk, hh * 512:(hh + 1) * 512], AF.Exp), [QhB[zi][hh]], [EbB[ei][hh]])
            else:
                S.op("act", lambda e: e.activation(E[:nk, 0:8 * nq], z[:nk, 0:8 * nq], AF.Exp), [QhB[zi][0], QhB[zi][1]], [EbB[ei][0], EbB[ei][1]])
            if diag:
                e3 = E[:nk, 0:8 * nq].rearrange("p (m q) -> p m q", m=8)
                S.op("pool", lambda e: e.tensor_tensor(e3, e3, maskA[:nk, :nq].unsqueeze(1).to_broadcast([nk, 8, nq]), ALU.mult),
                     [EbB[ei][0], EbB[ei][1], cB], [EbB[ei][0], EbB[ei][1]])
            yield
            pv = Q[2]
            for m in range(8):
                S.op("pe", lambda e, m=m: e.matmul(pv[:nq, m * 128:(m + 1) * 128], E[:nk, m * nq:(m + 1) * nq], V[:nk, m // 2, 0:128], start=True, stop=True),
                     [EbB[ei][0], EbB[ei][1], kB], [QhB[2][0], QhB[2][1]])
            for m in range(8):
                S.op("pe", lambda e, m=m: e.matmul(P1[:nq, m:m + 1], E[:nk, m * nq:(m + 1) * nq], V[:nk, m // 2, 128:129], start=True, stop=True),
                     [EbB[ei][0], EbB[ei][1], kB], [P1B])
            S.op("dve", lambda e: e.tensor_tensor(accA[j][:nq, :], accA[j][:nq, :], pv[:nq, :], ALU.add), [accAB[j], QhB[2][0], QhB[2][1]], [accAB[j]])
            S.op("dve", lambda e: e.tensor_tensor(den[:nq, j, :], den[:nq, j, :], P1[:nq, 0:8], ALU.add), [smB["den"], P1B], [smB["den"]])

        def pairB(j, nq, nk, KT, V, kB, diag):
            par = par_c["p"]; par_c["p"] = 1 - par
            vw = []
            for hh in range(2):
                vw.append(dict(zh=Q[0][:, hh * 512:hh * 512 + 4 * nq], ar=Q[1][:, hh * 512:hh * 512 + 4 * nq], pvB=Q[2][:, hh * 512:hh * 512 + 256],
                               spb=SPB[par][:, hh * 512:hh * 512 + 4 * nq], wb=WBv[par][:, hh * 512:hh * 512 + 4 * nq], hs=slice(4 * hh, 4 * hh + 4)))

            def qk(out, hh, h4, st_, sp_):
                h = 4 * hh + h4
                return lambda e: e.matmul(out[:nk, h4 * nq:(h4 + 1) * nq], KT[:, h // 2, 0:nk],
                                          QTeo[h % 2][:, h // 2, j * 128:j * 128 + nq], start=st_, stop=sp_)
            for hh in range(2):
                for h4 in range(4):
                    S.op("pe", qk(vw[hh]["zh"], hh, h4, True, True), [kB, QTB], [QhB[0][hh]])
            for hh in range(2):
                S.op("act", lambda e, hh=hh: e.activation(ef[hh][:nk, 0:4 * nq], vw[hh]["zh"][:nk, :], AF.Exp), [QhB[0][hh]], [efB[hh]])
            for hh in range(2):
                S.op("act", lambda e, hh=hh: e.activation(vw[hh]["spb"][:nk, :], ef[hh][:nk, 0:4 * nq], AF.Ln, bias=1.0), [efB[hh]], [SPBB[par][hh]])
                if diag:
                    s3 = vw[hh]["spb"][:nk, :].rearrange("p (m q) -> p m q", m=4)
                    S.op("pool", lambda e, s3=s3: e.tensor_tensor(s3, s3, maskB[:nk, :nq].unsqueeze(1).to_broadcast([nk, 4, nq]), ALU.mult),
                         [SPBB[par][hh], cB], [SPBB[par][hh]])
            yield
            for hh in range(2):
                for h4 in range(4):
                    S.op("pe", qk(vw[hh]["ar"], hh, h4, True, False), [kB, QTB], [QhB[1][hh]])
                    S.op("pe", lambda e, hh=hh, h4=h4: e.matmul(vw[hh]["ar"][:nk, h4 * nq:(h4 + 1) * nq], negU[:nk, :nk], vw[hh]["spb"][:nk, h4 * nq:(h4 + 1) * nq], start=False, stop=True),
                         [SPBB[par][hh], cB], [QhB[1][hh]])
            for hh in range(2):
                for h4 in range(4):
                    c_ = par * 8 + 4 * hh + h4
                    S.op("pe", lambda e, hh=hh, h4=h4, c_=c_: e.matmul(P1[:nq, c_:c_ + 1], vw[hh]["spb"][:nk, h4 * nq:(h4 + 1) * nq], ones[:nk, 0:1], start=True, stop=True),
                         [SPBB[par][hh], cB], [P1Bx[par]])
            for hh in range(2):
                S.op("act", lambda e, hh=hh: e.activation(vw[hh]["wb"][:nk, :], vw[hh]["ar"][:nk, :], AF.Exp), [QhB[1][hh]], [WBB[par][hh]])
                if diag:
                    w3 = vw[hh]["wb"][:nk, :].rearrange("p (m q) -> p m q", m=4)
                    S.op("pool", lambda e, w3=w3: e.tensor_tensor(w3, w3, maskB[:nk, :nq].unsqueeze(1).to_broadcast([nk, 4, nq]), ALU.mult),
                         [WBB[par][hh], cB], [WBB[par][hh]])
            yield
            for hh in range(2):
                for h4 in range(4):
                    S.op("pe", lambda e, hh=hh, h4=h4: e.matmul(vw[hh]["pvB"][:nq, h4 * 64:(h4 + 1) * 64], vw[hh]["wb"][:nk, h4 * nq:(h4 + 1) * nq], V[:nk, 4 * hh + h4, :], start=True, stop=True),
                         [WBB[par][hh], kB], [QhB[2][hh]])
            for hh in range(2):
                S.op("act", lambda e, hh=hh: e.copy(pvs[hh][:nq, :], vw[hh]["pvB"][:nq, :]), [QhB[2][hh]], [pvsB[hh]])
            for hh in range(2):
                hs = vw[hh]["hs"]
                ps_ = slice(par * 8 + 4 * hh, par * 8 + 4 * hh + 4)
                S.op("dve", lambda e, hh=hh, hs=hs: e.tensor_tensor(tmpB[hh][:nq, :].rearrange("p (h d) -> p h d", h=4), pvs[hh][:nq, :].rearrange("p (h d) -> p h d", h=4),
                                                                   ecarry[:nq, j, hs].unsqueeze(2).to_broadcast([nq, 4, 64]), ALU.mult), [pvsB[hh], smB["ecarry"]], [tmpBB[hh]])
                S.op("dve", lambda e, hh=hh: e.tensor_tensor(accB[j][:nq, hh * 256:(hh + 1) * 256], accB[j][:nq, hh * 256:(hh + 1) * 256], tmpB[hh][:nq, :], ALU.add),
                     [tmpBB[hh], accBB[j]], [accBB[j]])
                S.op("dve", lambda e, hs=hs, ps_=ps_: e.tensor_tensor(carry[:nq, j, hs], carry[:nq, j, hs], P1[:nq, ps_], ALU.subtract), [smB["carry"], P1Bx[par]], [smB["carry"]])
                S.op("act", lambda e, hs=hs: e.activation(ecarry[:nq, j, hs], carry[:nq, j, hs], AF.Exp), [smB["carry"]], [smB["ecarry"]])

        def finalize(j, nq, i):
            rden = small["rden"]
            S.op("dve", lambda e: e.tensor_scalar(rden[:nq, :], den[:nq, j, :], 1e-30, None, ALU.add), [smB["den"]], [smB["rden"]])
            S.op("dve", lambda e: e.reciprocal(rden[:nq, :], rden[:nq, :]), [smB["rden"]], [smB["rden"]])
            S.op("dve", lambda e: e.tensor_tensor(ofin[:nq, :].rearrange("p (m d) -> p m d", m=8), accA[j][:nq, :].rearrange("p (m d) -> p m d", m=8),
                                                  rden[:nq, :].unsqueeze(2).to_broadcast([nq, 8, 128]), ALU.mult), [accAB[j], smB["rden"]], [ofinB])
            o4 = ofin[:nq, :].rearrange("p (h t d) -> p h t d", h=4, t=2)
            ao3 = aof[:nq, :].rearrange("p (h d) -> p h d", h=4)
            S.op("dve", lambda e: e.scalar_tensor_tensor(ao3, o4[:, :, 1, :], small["lam"][:nq, 2:3], o4[:, :, 0, :], ALU.mult, ALU.add),
                 [ofinB, smB["lam"]], [aofB])
            s4 = small["s4"]
            S.op("dve", lambda e: e.tensor_tensor(sqf[:nq, :], aof[:nq, :], aof[:nq, :], ALU.mult), [aofB], [sqfB])
            S.op("dve", lambda e: e.tensor_reduce(s4[:nq, :], sqf[:nq, :].rearrange("p (h d) -> p h d", h=4), AX.X, ALU.add), [sqfB], [smB["s4"]])
            S.op("dve", lambda e: e.tensor_scalar(s4[:nq, :], s4[:nq, :], 1.0 / 128, EPS, ALU.mult, ALU.add), [smB["s4"]], [smB["s4"]])
            S.op("act", lambda e: e.activation(s4[:nq, :], s4[:nq, :], AF.Ln), [smB["s4"]], [smB["s4"]])
            S.op("act", lambda e: e.activation(s4[:nq, :], s4[:nq, :], AF.Exp, scale=-0.5), [smB["s4"]], [smB["s4"]])
            S.op("dve", lambda e: e.tensor_tensor(ao3, ao3, s4[:nq, :].unsqueeze(2).to_broadcast([nq, 4, 128]), ALU.mult), [aofB, smB["s4"]], [aofB])
            S.op("dve", lambda e: e.tensor_tensor(mg[:nq, 0:512].rearrange("p (h d) -> p h d", h=4), ao3, gsub[:nq, :].unsqueeze(1).to_broadcast([nq, 4, 128]), ALU.mult),
                 [aofB, gsubB], [mgB])
            S.op("pool", lambda e: e.tensor_copy(mg[:nq, 512:1024], accB[j][:nq, :]), [accBB[j]], [mgB])
            pt, ptB = getpt()
            for k in range(8):
                S.op("pe", lambda e, k=k: e.transpose(pt[:, k * 128:k * 128 + nq], mg[:nq, k * 128:(k + 1) * 128], ident[:nq, :nq]), [mgB, cB], [ptB])
            S.op("act", lambda e: e.copy(HT[:, :, i * 128:i * 128 + nq], pt.rearrange("p (k t) -> p k t", k=8)[:, :, 0:nq]), [ptB], [HTb[i]])

        groups = [list(range(g, g + 4)) for g in range(0, NQ, 4)] + [[SMP]]
        ngrp = int(os.environ.get("KNGRP", "99"))
        groups = groups[:ngrp] + ([groups[-1]] if ngrp < len(groups) else [])
        for grp in groups:
            sample = grp[0] == SMP
            nq = 16 if sample else 128
            keys = ([SMP] + [CIDX + t for t in range(7, -1, -1)]) if sample else list(range(grp[0], int(os.environ.get("KNKEY", str(NT)))))
            for pas in os.environ.get("KPASS", "AB"):
                for j, s in enumerate(grp):
                    qi = SQ if sample else s
                    src = (qta_s if pas == "A" else qtb_s)[qi].rearrange("p (k t) -> p k t", k=4)[:, :, 0:nq]
                    S.op("sp", lambda e, j=j, src=src: e.dma_start(out=QTe[0:64, :, j * 128:j * 128 + nq], in_=src[0:64]), [scrB["qta" if pas == "A" else "qtb"][qi]], [QTB], dma=True)
                    S.op("sp", lambda e, j=j, src=src: e.dma_start(out=QTo[64:128, :, j * 128:j * 128 + nq], in_=src[64:128]), [scrB["qta" if pas == "A" else "qtb"][qi]], [QTB], dma=True)
                    if pas == "A":
                        S.op("dve", lambda e, j=j: e.memset(accA[j], 0.0), [], [accAB[j]])
                    else:
                        S.op("dve", lambda e, j=j: e.memset(accB[j], 0.0), [], [accBB[j]])
                if pas == "A":
                    S.op("dve", lambda e: e.memset(den, 0.0), [], [smB["den"]])
                else:
                    S.op("dve", lambda e: e.memset(carry, 0.0), [], [smB["carry"]])
                    S.op("dve", lambda e: e.memset(ecarry, 1.0), [], [smB["ecarry"]])
                for idx, uk in enumerate(keys):
                    nk = 16 if uk == SMP else 128
                    b = idx % 2
                    KT = kvb[b][:, 0:512].rearrange("p (k t) -> p k t", k=4)
                    ksrc = (kta_s if pas == "A" else ktb_s)[uk].rearrange("p (k t) -> p k t", k=4)[:, :, 0:nk]
                    kkey, vkey = ("kta", "va") if pas == "A" else ("ktb", "vb")
                    S.op("sp", lambda e, KT=KT, ksrc=ksrc, nk=nk: e.dma_start(out=KT[:, :, 0:nk], in_=ksrc), [scrB[kkey][uk]], [kvB[b]], dma=True)
                    if pas == "A":
                        V = kvb[b][:, 512:1032].rearrange("p (h d) -> p h d", h=4)
                        S.op("sp", lambda e, b=b, uk=uk, nk=nk: e.dma_start(out=kvb[b][:nk, 512:1032], in_=va_s[uk][:nk, :]), [scrB[vkey][uk]], [kvB[b]], dma=True)
                    else:
                        V = kvb[b][:, 512:1024].rearrange("p (h d) -> p h d", h=8)
                        S.op("sp", lambda e, b=b, uk=uk, nk=nk: e.dma_start(out=kvb[b][:nk, 512:1024], in_=vb_s[uk][:nk, :]), [scrB[vkey][uk]], [kvB[b]], dma=True)
                    gens = []
                    for j, s in enumerate(grp):
                        if not sample and uk < s:
                            continue
                        diag = (uk == SMP) if sample else (uk == s)
                        gens.append((pairA if pas == "A" else pairB)(j, nq, nk, KT, V, kvB[b], diag))
                    run_pipeline(gens)
                S.barrier()
            if not os.environ.get("KNOFIN"):
                for j, s in enumerate(grp):
                    finalize(j, nq, NQ if sample else s)
            S.barrier()

    def linear_residual(wsrc, gcol, tiles, scale):
        for pc in range(4):
            ws = pc % 2
            load_w(Wb[ws][:, 0, :], WbB[ws][0], wsrc[:, pc * 256:(pc + 1) * 256], gcol, 8, 256)
            w3 = Wb[ws][:, 0, :].rearrange("p (k f) -> p k f", k=8)
            for (i, rows) in tiles:
                pp, ppB = getps()
                for k in range(8):
                    S.op("pe", lambda e, k=k: e.matmul(pp[:rows, 0:256], HT[:, k, i * 128:i * 128 + rows], w3[:, k, :], start=(k == 0), stop=(k == 7)),
                         [HTb[i], WbB[ws][0]], [ppB])
                S.op("dve", lambda e: e.scalar_tensor_tensor(X[:rows, i, pc * 256:(pc + 1) * 256], pp[:rows, 0:256], float(scale),
                                                             X[:rows, i, pc * 256:(pc + 1) * 256], ALU.mult, ALU.add), [ppB, Xb[i]], [Xb[i]])

    lim = {"stg": 3}
    gm = {"gq": lt.rearrange("p a b -> p (a b)"), "gk": sb("gm_gk", [128, 256])}; gmB = Buf()
    MKT = av(4096, 1024, BF16).rearrange("p (k t) -> p k t", k=8); MV = av(5120, 1024, BF16).rearrange("p (t f) -> p t f", t=2)
    MKTs = av(12288, 1024, BF16).rearrange("p (k t) -> p k t", k=8); MVs = av(13312, 1024, BF16).rearrange("p (t f) -> p t f", t=2)
    memB = {"k": Buf(), "v": Buf(), "ks": Buf(), "vs": Buf()}
    mw = [av(1024 + 1024 * i, 1024, BF16) for i in range(2)]; mwB = [Buf(), Buf()]
    MT = av(0, 1024, BF16).rearrange("p (k t) -> p k t", k=8); MTB = Buf()

    def head_norm(src_ps, srcB, rows, n, gtile, gtB, out_bf=None, out_bfB=None):
        a, b = nxt("wk", 4), nxt("wk", 4)
        s_ = nxt("st", 4)
        S.op("act", lambda e: e.copy(wk[a][:rows, :n], src_ps), [srcB], [wkB[a]])
        S.op("dve", lambda e: e.tensor_tensor(wk[b][:rows, :n], wk[a][:rows, :n], wk[a][:rows, :n], ALU.mult), [wkB[a]], [wkB[b]])
        S.op("dve", lambda e: e.tensor_reduce(st[s_][:rows, 0:1], wk[b][:rows, :n], AX.X, ALU.add), [wkB[b]], [stB[s_]])
        S.op("dve", lambda e: e.tensor_scalar(st[s_][:rows, 0:1], st[s_][:rows, 0:1], 1.0 / n, EPS, ALU.mult, ALU.add), [stB[s_]], [stB[s_]])
        S.op("act", lambda e: e.activation(st[s_][:rows, 0:1], st[s_][:rows, 0:1], AF.Ln), [stB[s_]], [stB[s_]])
        S.op("act", lambda e: e.activation(st[s_][:rows, 0:1], st[s_][:rows, 0:1], AF.Exp, scale=-0.5), [stB[s_]], [stB[s_]])
        S.op("dve", lambda e: e.tensor_scalar(wk[a][:rows, :n], wk[a][:rows, :n], st[s_][:rows, 0:1], None, ALU.mult), [wkB[a], stB[s_]], [wkB[a]])
        S.op("dve", lambda e: e.tensor_tensor(wk[b][:rows, :n], wk[a][:rows, :n], gtile[:rows, :n], ALU.mult), [wkB[a], gtB], [wkB[b]])
        return wk[b], wkB[b]

    def mem_prep(l):
        S.op("sp", lambda e: e.dma_start(out=gm["gq"], in_=W["mem_gq"][l:l + 1, :].partition_broadcast(128)), [], [gmB], dma=True)
        S.op("sp", lambda e: e.dma_start(out=gm["gk"], in_=W["mem_gk"][l:l + 1, :].partition_broadcast(128)), [], [gmB], dma=True)
        S.op("dve", lambda e: e.tensor_scalar(gm["gq"], gm["gq"], 1.0 / 16, None, ALU.mult), [gmB], [gmB])
        for t in range(2):
            s_ = nxt("stg", 3)
            S.op("sp", lambda e, t=t: e.dma_start(out=stg[s_][:, 0:1024], in_=memp[t * 128:(t + 1) * 128, :]), [], [stgB[s_]], dma=True)
            k_ = nxt("st", 4)
            S.op("dve", lambda e: e.memset(st[k_][:, 0:1], 0.0), [], [stB[k_]])
            S.op("act", lambda e: e.activation(junk, stg[s_][:, 0:1024], AF.Square, accum_out=st[k_][:, 0:1]), [stgB[s_]], [junkB, stB[k_]])
            S.op("dve", lambda e: e.tensor_scalar(st[k_][:, 0:1], st[k_][:, 0:1], 1.0 / D, EPS, ALU.mult, ALU.add), [stB[k_]], [stB[k_]])
            S.op("act", lambda e: e.activation(st[k_][:, 0:1], st[k_][:, 0:1], AF.Ln), [stB[k_]], [stB[k_]])
            S.op("act", lambda e: e.activation(st[k_][:, 0:1], st[k_][:, 0:1], AF.Exp, scale=-0.5), [stB[k_]], [stB[k_]])
            j = nxt("xn", 2)
            S.op("pool", lambda e: e.tensor_scalar(xn[j], stg[s_][:, 0:1024], st[k_][:, 0:1], None, ALU.mult), [stgB[s_], stB[k_]], [xnB[j]])
            pt, ptB = getpt()
            for k in range(8):
                S.op("pe", lambda e, k=k: e.transpose(pt[:, k * 128:(k + 1) * 128], xn[j][:, k * 128:(k + 1) * 128], ident), [xnB[j], cB], [ptB])
            S.op("act", lambda e, t=t: e.copy(MT[:, :, t * 128:(t + 1) * 128], pt.rearrange("p (k t) -> p k t", k=8)), [ptB], [MTB])
        gcol = (GID["mem_g_m"] + l) * 8
        for which in ("k", "v"):
            for h in range(4):
                ws = h % 2
                load_w(mw[ws], mwB[ws], W["mem_w" + which][l][:, h * 256:(h + 1) * 256], gcol, 8, 256)
                w3 = mw[ws].rearrange("p (k f) -> p k f", k=8)
                for t in range(2):
                    pp, ppB = getps()
                    for k in range(8):
                        S.op("pe", lambda e, k=k, t=t: e.matmul(pp[:, 0:256], MT[:, k, t * 128:(t + 1) * 128], w3[:, k, :], start=(k == 0), stop=(k == 7)),
                             [MTB, mwB[ws]], [ppB])
                    if which == "k":
                        fin, finB = head_norm(pp[:, 0:256], ppB, 128, 256, gm["gk"], gmB)
                        S.op("pool", lambda e, t=t, h=h: e.dma_start(out=o_mk[l][t * 128:(t + 1) * 128, h * 256:(h + 1) * 256], in_=fin[:, 0:256]), [finB], [], dma=True)
                        c = nxt("wkb", 2)
                        S.op("pool", lambda e: e.tensor_copy(wkb[c][:, 0:256], fin[:, 0:256]), [finB], [wkbB[c]])
                        pt, ptB = getpt()
                        for k in range(2):
                            S.op("pe", lambda e, k=k: e.transpose(pt[:, k * 128:(k + 1) * 128], wkb[c][:, k * 128:(k + 1) * 128], ident), [wkbB[c], cB], [ptB])
                        S.op("act", lambda e, t=t, h=h: e.copy(MKT[:, 2 * h:2 * h + 2, t * 128:(t + 1) * 128], pt[:, 0:256].rearrange("p (k t) -> p k t", k=2)), [ptB], [memB["k"]])
                    else:
                        a = nxt("wk", 4)
                        S.op("act", lambda e: e.copy(wk[a][:, :], pp[:, 0:256]), [ppB], [wkB[a]])
                        S.op("pool", lambda e, t=t, h=h: e.dma_start(out=o_mv[l][t * 128:(t + 1) * 128, h * 256:(h + 1) * 256], in_=wk[a][:, :]), [wkB[a]], [], dma=True)
                        S.op("pool", lambda e, t=t, h=h: e.tensor_copy(MV[:, t, h * 256:(h + 1) * 256], wk[a][:, :]), [wkB[a]], [memB["v"]])
        for t in range(2):
            s_ = nxt("stg", 3)
            S.op("sp", lambda e, t=t: e.dma_start(out=stg[s_][:, 0:1024], in_=cm_k[l][t * 128:(t + 1) * 128, :]), [], [stgB[s_]], dma=True)
            j = nxt("xn", 2)
            S.op("pool", lambda e: e.tensor_copy(xn[j], stg[s_][:, 0:1024]), [stgB[s_]], [xnB[j]])
            pt, ptB = getpt()
            for k in range(8):
                S.op("pe", lambda e, k=k: e.transpose(pt[:, k * 128:(k + 1) * 128], xn[j][:, k * 128:(k + 1) * 128], ident), [xnB[j], cB], [ptB])
            S.op("act", lambda e, t=t: e.copy(MKTs[:, :, t * 128:(t + 1) * 128], pt.rearrange("p (k t) -> p k t", k=8)), [ptB], [memB["ks"]])
            s_ = nxt("stg", 3)
            S.op("sp", lambda e, t=t: e.dma_start(out=stg[s_][:, 0:1024], in_=cm_v[l][t * 128:(t + 1) * 128, :]), [], [stgB[s_]], dma=True)
            S.op("pool", lambda e, t=t: e.tensor_copy(MVs[:, t, :], stg[s_][:, 0:1024]), [stgB[s_]], [memB["vs"]])

    def mem_attn(l, tiles):
        QM = av(10240, 512, BF16).rearrange("p (k t) -> p k t", k=8); QMB = Buf()
        Em = av(10752, 512, BF16); EmB = [Buf(), Buf()]
        og = av(11264, 1024); ogB = Buf()
        ogb = junk
        wq = [av(h * 1024, 1024, BF16) for h in range(4)]; wqB = [Buf() for _ in range(4)]
        rstd_tiles(tiles)
        for (i, rows) in tiles:
            norm_to_HT(i, rows)
        lim["stg"] = 2
        gcol = (GID["mem_g_x"] + l) * 8
        for h in range(4):
            load_w(wq[h], wqB[h], W["mem_wq"][l][:, h * 256:(h + 1) * 256], gcol, 8, 256)
        rr["ptn"] = 1
        zc = 0
        for (i, rows) in tiles:
            smp = (i == NQ)
            KTm, Vm, kB, vB = (MKTs, MVs, memB["ks"], memB["vs"]) if smp else (MKT, MV, memB["k"], memB["v"])
            for h in range(4):
                w3 = wq[h].rearrange("p (k f) -> p k f", k=8)
                pp, ppB = getps()
                for k in range(8):
                    S.op("pe", lambda e, k=k: e.matmul(pp[:rows, 0:256], HT[:, k, i * 128:i * 128 + rows], w3[:, k, :], start=(k == 0), stop=(k == 7)),
                         [HTb[i], wqB[h]], [ppB])
                fin, finB = head_norm(pp[:rows, 0:256], ppB, rows, 256, gm["gq"], gmB)
                c = nxt("wkb", 2)
                S.op("pool", lambda e: e.tensor_copy(wkb[c][:rows, 0:256], fin[:rows, 0:256]), [finB], [wkbB[c]])
                pt, ptB = getpt()
                for k in range(2):
                    S.op("pe", lambda e, k=k: e.transpose(pt[:, k * 128:k * 128 + rows], wkb[c][:rows, k * 128:(k + 1) * 128], ident[:rows, :rows]), [wkbB[c], cB], [ptB])
                S.op("act", lambda e, h=h: e.copy(QM[:, 2 * h:2 * h + 2, 0:rows], pt[:, 0:256].rearrange("p (k t) -> p k t", k=2)[:, :, 0:rows]), [ptB], [QMB])
            zi = zc; zc = 1 - zc
            z = Q[zi]; zB_ = [PSB[2 * zi], PSB[2 * zi + 1]]
            for h in range(4):
                for kt in range(2):
                    c0 = (h * 2 + kt) * rows
                    for cc in range(2):
                        S.op("pe", lambda e, h=h, kt=kt, cc=cc, c0=c0: e.matmul(z[:, c0:c0 + rows], KTm[:, 2 * h + cc, kt * 128:(kt + 1) * 128], QM[:, 2 * h + cc, 0:rows],
                                                                            start=(cc == 0), stop=(cc == 1)), [kB, QMB], zB_)
            if 8 * rows > 512:
                for hh in range(2):
                    S.op("act", lambda e, hh=hh: e.activation(Em[:, hh * 512:(hh + 1) * 512], z[:, hh * 512:(hh + 1) * 512], AF.Exp), [zB_[hh]], [EmB[hh]])
            else:
                S.op("act", lambda e: e.activation(Em[:, 0:8 * rows], z[:, 0:8 * rows], AF.Exp), zB_, EmB)
            pv = Q[2]; pvB_ = [PSB[4], PSB[5]]
            for h in range(4):
                for kt in range(2):
                    c0 = (h * 2 + kt) * rows
                    S.op("pe", lambda e, h=h, kt=kt, c0=c0: e.matmul(pv[:rows, h * 256:(h + 1) * 256], Em[:, c0:c0 + rows], Vm[:, kt, h * 256:(h + 1) * 256],
                                                                     start=(kt == 0), stop=(kt == 1)), EmB + [vB], [pvB_[h // 2]])
            for h in range(4):
                for kt in range(2):
                    c0 = (h * 2 + kt) * rows
                    S.op("pe", lambda e, h=h, kt=kt, c0=c0: e.matmul(P1[:rows, h:h + 1], Em[:, c0:c0 + rows], ones[:, 0:1], start=(kt == 0), stop=(kt == 1)), EmB + [cB], [P1B])
            for hh in range(2):
                S.op("act", lambda e, hh=hh: e.copy(og[:rows, hh * 512:(hh + 1) * 512], pv[:rows, hh * 512:(hh + 1) * 512]), [pvB_[hh]], [ogB])
            rden = small["rden"]
            S.op("dve", lambda e: e.reciprocal(rden[:rows, 0:4], P1[:rows, 0:4]), [P1B], [smB["rden"]])
            S.op("dve", lambda e: e.tensor_tensor(ogb[:rows, :].rearrange("p (h d) -> p h d", h=4), og[:rows, :].rearrange("p (h d) -> p h d", h=4),
                                                  rden[:rows, 0:4].unsqueeze(2).to_broadcast([rows, 4, 256]), ALU.mult), [ogB, smB["rden"]], [junkB])
            pt, ptB = getpt()
            for k in range(8):
                S.op("pe", lambda e, k=k: e.transpose(pt[:, k * 128:k * 128 + rows], ogb[:rows, k * 128:(k + 1) * 128], ident[:rows, :rows]), [junkB, cB], [ptB])
            S.op("act", lambda e: e.copy(HT[:, :, i * 128:i * 128 + rows], pt.rearrange("p (k t) -> p k t", k=8)[:, :, 0:rows]), [ptB], [HTb[i]])
        rr["ptn"] = 2
        S.barrier()
        linear_residual(W["mem_wo"][l], None, tiles, 1.0)
        lim["stg"] = 3

    NCS = NQ + 1 + 4
    kc_s = dscr("kc_s", [NCS, 128, 1024]); vc_s = dscr("vc_s", [NCS, 128, 1024]); qc_s = dscr("qc_s", [NOWN + 1, 128, 1024])
    kcB = [Buf() for _ in range(NCS)]; vcB = [Buf() for _ in range(NCS)]; qcB = [Buf() for _ in range(NOWN + 1)]
    tabx_t = nc.dram_tensor("tabx", [16, 513], F32)
    tabx = tabx_t.ap(); tabxB = Buf()
    negb = sb("negb", [128, 16]); negbB = Buf()
    mask4b = sb("mask4b", [128, 128], BF16)
    S.op("dve", lambda e: e.tensor_copy(mask4b, cst[:, 648:776]), [cstB], [cB])

    def norm4(src, ppB, rows, g):
        a, b = nxt("wk", 4), nxt("wk", 4)
        s_ = nxt("st", 4)
        S.op("act", lambda e: e.copy(wk[b][:rows, :], src), [ppB], [wkB[b]])
        S.op("dve", lambda e: e.tensor_tensor(wk[a][:rows, :], wk[b][:rows, :], wk[b][:rows, :], ALU.mult), [wkB[b]], [wkB[a]])
        S.op("dve", lambda e: e.tensor_reduce(st[s_][:rows, 0:4], wk[a][:rows, :].rearrange("p (h d) -> p h d", h=4), AX.X, ALU.add), [wkB[a]], [stB[s_]])
        S.op("dve", lambda e: e.tensor_scalar(st[s_][:rows, 0:4], st[s_][:rows, 0:4], 1.0 / 64, EPS, ALU.mult, ALU.add), [stB[s_]], [stB[s_]])
        S.op("act", lambda e: e.activation(st[s_][:rows, 0:4], st[s_][:rows, 0:4], AF.Ln), [stB[s_]], [stB[s_]])
        S.op("act", lambda e: e.activation(st[s_][:rows, 0:4], st[s_][:rows, 0:4], AF.Exp, scale=-0.5), [stB[s_]], [stB[s_]])
        w3a = wk[a][:rows, :].rearrange("p (h d) -> p h d", h=4)
        w3b = wk[b][:rows, :].rearrange("p (h d) -> p h d", h=4)
        S.op("dve", lambda e: e.tensor_tensor(w3a, w3b, st[s_][:rows, 0:4].unsqueeze(2).to_broadcast([rows, 4, 64]), ALU.mult), [wkB[b], stB[s_]], [wkB[a]])
        S.op("dve", lambda e: e.tensor_tensor(w3b, w3a, g[:rows, :].unsqueeze(1).to_broadcast([rows, 4, 64]), ALU.mult), [wkB[a], gB], [wkB[b]])
        return wk[b], wkB[b]

    def proj_c(tiles, us):
        gcol = (GID["mix_g"] + 1) * 8
        for pc in range(12):
            ws = pc % 2
            kind, hp = ("cq", "ck", "cv")[pc // 4], pc % 4
            load_w(Wb[ws][:, 0, :], WbB[ws][0], W["c_w_in"][0][:, pc * 256:(pc + 1) * 256], gcol, 8, 256)
            w3 = Wb[ws][:, 0, :].rearrange("p (k f) -> p k f", k=8)
            csl = slice(hp * 256, (hp + 1) * 256)
            for (i, rows), u in zip(tiles, us):
                smp = (u == SMP)
                if kind == "cq" and not (u < NOWN or smp):
                    continue
                ui = NQ if smp else u
                qi = NOWN if smp else u
                pp, ppB = getps()
                for k in range(8):
                    S.op("pe", lambda e, k=k: e.matmul(pp[:rows, 0:256], HT[:, k, i * 128:i * 128 + rows], w3[:, k, :], start=(k == 0), stop=(k == 7)),
                         [HTb[i], WbB[ws][0]], [ppB])
                src = pp[:rows, 0:256]
                if kind in ("cq", "ck"):
                    fin, finB = norm4(src, ppB, rows, g64["c_gq" if kind == "cq" else "c_gk"])
                    if kind == "ck" and (u < 4 or smp):
                        dst = o_cks[496:512, csl] if smp else o_ck[u][:, csl]
                        S.op("pool", lambda e, dst=dst: e.dma_start(out=dst[:rows], in_=fin[:rows, :]), [finB], [], dma=True)
                    c = nxt("wkb", 2)
                    S.op("pool", lambda e: e.tensor_copy(wkb[c][:rows, 0:256], fin[:rows, :]), [finB], [wkbB[c]])
                    if kind == "ck":
                        to_T_and_store(wkb[c], wkbB[c], rows, kc_s[ui].rearrange("p (k t) -> p k t", k=8)[:, hp * 2:hp * 2 + 2, 0:rows], kcB[ui])
                    else:
                        to_T_and_store(wkb[c], wkbB[c], rows, qc_s[qi].rearrange("p (k t) -> p k t", k=8)[:, hp * 2:hp * 2 + 2, 0:rows], qcB[qi])
                else:
                    a = nxt("wk", 4)
                    S.op("act", lambda e: e.copy(wk[a][:rows, :], src), [ppB], [wkB[a]])
                    if u < 4 or smp:
                        dst = o_cvs[496:512, csl] if smp else o_cv[u][:, csl]
                        S.op("pool", lambda e, dst=dst: e.dma_start(out=dst[:rows], in_=wk[a][:rows, :]), [wkB[a]], [], dma=True)
                    c = nxt("wkb", 2)
                    S.op("dve", lambda e: e.tensor_scalar(wkb[c][:rows, 0:256], wk[a][:rows, :], valid[:rows, u:u + 1], None, ALU.mult), [wkB[a], cB], [wkbB[c]])
                    S.op("pool", lambda e: e.dma_start(out=vc_s[ui][:rows, csl], in_=wkb[c][:rows, 0:256]), [wkbB[c]], [vcB[ui]], dma=True)

    def cache_prep_c():
        S.op("sp", lambda e: e.dma_start(out=o_cks[0:496, :], in_=cc_k[16:512, :]), [], [], dma=True)
        S.op("sp", lambda e: e.dma_start(out=o_cvs[0:496, :], in_=cc_v[16:512, :]), [], [], dma=True)
        for t in range(4):
            ci = NQ + 1 + t
            s_ = nxt("stg", 3)
            S.op("sp", lambda e, t=t: e.dma_start(out=stg[s_][:, 0:1024], in_=cc_k[t * 128:(t + 1) * 128, :]), [], [stgB[s_]], dma=True)
            S.op("pool", lambda e: e.tensor_copy(xn[0], stg[s_][:, 0:1024]), [stgB[s_]], [xnB[0]])
            pt, ptB = getpt()
            for k in range(8):
                S.op("pe", lambda e, k=k: e.transpose(pt[:, k * 128:(k + 1) * 128], xn[0][:, k * 128:(k + 1) * 128], ident), [xnB[0], cB], [ptB])
            S.op("act", lambda e: e.copy(xn[1], pt), [ptB], [xnB[1]])
            S.op("pool", lambda e, ci=ci: e.dma_start(out=kc_s[ci], in_=xn[1]), [xnB[1]], [kcB[ci]], dma=True)
            s_ = nxt("stg", 3)
            S.op("sp", lambda e, t=t: e.dma_start(out=stg[s_][:, 0:1024], in_=cc_v[t * 128:(t + 1) * 128, :]), [], [stgB[s_]], dma=True)
            S.op("pool", lambda e: e.tensor_copy(xn[0], stg[s_][:, 0:1024]), [stgB[s_]], [xnB[0]])
            S.op("pool", lambda e, ci=ci: e.dma_start(out=vc_s[ci], in_=xn[0]), [xnB[0]], [vcB[ci]], dma=True)

    def band_attn(tiles):
        EB = [av(1024 * d, 1024, BF16).rearrange("p (h q) -> p h q", h=16) for d in range(2)]; EBB = Buf()
        QTe = av(2048, 256, BF16).rearrange("p (k t) -> p k t", k=4); QTo = av(2304, 256, BF16).rearrange("p (k t) -> p k t", k=4); QTB = Buf()
        QTeo = (QTe, QTo)
        kvc = [av(2560 + 512 * i, 512, BF16) for i in range(5)]; kvcB = [Buf() for _ in range(5)]
        Ec = [av(5120 + 512 * i, 512, BF16) for i in range(5)]; EcB = [[Buf(), Buf()] for _ in range(5)]
        ogc = av(7680, 512); ogcB = Buf()
        mgc = av(8192, 512, BF16); mgcB = Buf()
        S.op("sp", lambda e: e.dma_start(out=tabx[:, 0:257], in_=W["c_bias"][0]), [], [tabxB], dma=True)
        S.op("sp", lambda e: e.dma_start(out=stg[2][0:16, 256:257], in_=W["c_bias"][0][:, 256:257], allow_slow_non_contiguous=True), [], [stgB[2]], dma=True)
        S.op("dve", lambda e: e.tensor_copy(stg[2][0:16, 0:256], stg[2][0:16, 256:257].to_broadcast([16, 256])), [stgB[2]], [stgB[2]])
        S.op("sp", lambda e: e.dma_start(out=tabx[:, 257:513], in_=stg[2][0:16, 0:256]), [stgB[2]], [tabxB], dma=True)
        for h in range(16):
            S.op("sp", lambda e, h=h: e.dma_start(out=negb[:, h:h + 1], in_=tabx[h:h + 1, 300:301].partition_broadcast(128)), [tabxB], [negbB], dma=True)
        S.op("dve", lambda e: e.tensor_scalar(negb, negb, -1.0, None, ALU.mult), [negbB], [negbB])
        for h in range(16):
            s_ = nxt("stg", 2)
            for d in range(2):
                S.op("sp", lambda e, h=h, d=d: e.dma_start(out=stg[s_][:, d * 128:(d + 1) * 128], in_=bass.AP(tabx_t, h * 513 + 1 + 128 * d, [[1, 128], [1, 128]])),
                     [tabxB], [stgB[s_]], dma=True)
            pp, ppB = getps()
            S.op("pe", lambda e: e.matmul(pp[:, 0:256], cst[:, 520:648], stg[s_][:, 0:256], start=True, stop=True), [stgB[s_], cstB], [ppB])
            for d in range(2):
                S.op("act", lambda e, h=h, d=d: e.activation(EB[d][:, h, :], pp[:, d * 128:(d + 1) * 128], AF.Exp, bias=negb[:, h:h + 1]), [ppB, negbB], [EBB])
        S.op("pool", lambda e: e.tensor_tensor(EB[0], EB[0], maskA.unsqueeze(1).to_broadcast([128, 16, 128]), ALU.mult), [EBB, cB], [EBB])
        S.barrier()
        S.op("pool", lambda e: e.memset(QTe[64:128, :, :], 0.0), [], [QTB])
        S.op("pool", lambda e: e.memset(QTo[0:64, :, :], 0.0), [], [QTB])
        zc = {"z": 0, "b": 0}
        for (i, nq) in tiles:
            smp = (i == NQ)
            qi = NOWN if smp else i
            if smp:
                keyspec = [(NQ, 0)] + [(NQ + 1 + t, 1 if t == 3 else 2) for t in (3, 2, 1, 0)]
            else:
                keyspec = [(i + d, typ) for d, typ in zip(range(5), (0, 1, 2, 2, 4))]
            for half in range(2):
                src = qc_s[qi].rearrange("p (k t) -> p k t", k=8)[:, 4 * half:4 * half + 4, 0:nq]
                S.op("sp", lambda e, src=src: e.dma_start(out=QTe[0:64, :, 0:nq], in_=src[0:64]), [qcB[qi]], [QTB], dma=True)
                S.op("sp", lambda e, src=src: e.dma_start(out=QTo[64:128, :, 0:nq], in_=src[64:128]), [qcB[qi]], [QTB], dma=True)
                nkeys = len(keyspec)
                for d, (ui, typ) in enumerate(keyspec):
                    nk = 16 if (smp and d == 0) else 128
                    b = d
                    Kh = kvc[b][:, 0:512].rearrange("p (k t) -> p k t", k=4)
                    Vh = kvc[b][:, 512:1024]
                    ks = kc_s[ui].rearrange("p (k t) -> p k t", k=8)[:, 4 * half:4 * half + 4, 0:nk]
                    S.op("sp", lambda e, Kh=Kh, ks=ks, nk=nk: e.dma_start(out=Kh[:, :, 0:nk], in_=ks), [kcB[ui]], [kvcB[b]], dma=True)
                    S.op("sp", lambda e, Vh=Vh, ui=ui, nk=nk: e.dma_start(out=Vh[:nk, :], in_=vc_s[ui][:nk, half * 512:(half + 1) * 512]), [vcB[ui]], [kvcB[b]], dma=True)
                    zi = zc["z"]; zc["z"] = 1 - zi
                    z = Q[zi]; zB_ = [PSB[2 * zi], PSB[2 * zi + 1]]
                    for m in range(8):
                        S.op("pe", lambda e, m=m: e.matmul(z[:nk, m * nq:(m + 1) * nq], Kh[:, m // 2, 0:nk], QTeo[m % 2][:, m // 2, 0:nq], start=True, stop=True),
                             [kvcB[b], QTB], zB_)
                    E = Ec[d]
                    if 8 * nq > 512:
                        for hh in range(2):
                            S.op("act", lambda e, hh=hh, E=E: e.activation(E[:nk, hh * 512:(hh + 1) * 512], z[:nk, hh * 512:(hh + 1) * 512], AF.Exp), [zB_[hh]], [EcB[d][hh]])
                    else:
                        S.op("act", lambda e, E=E: e.activation(E[:nk, 0:8 * nq], z[:nk, 0:8 * nq], AF.Exp), zB_, EcB[d])
                    e3 = E[:nk, 0:8 * nq].rearrange("p (m q) -> p m q", m=8)
                    if typ in (0, 1):
                        S.op("pool", lambda e, typ=typ, e3=e3, nk=nk: e.tensor_tensor(e3, e3, EB[typ][:nk, 8 * half:8 * half + 8, 0:nq], ALU.mult), EcB[d] + [EBB], EcB[d])
                    elif typ == 4:
                        S.op("pool", lambda e, e3=e3, nk=nk: e.tensor_tensor(e3, e3, mask4b[:nk, :nq].unsqueeze(1).to_broadcast([nk, 8, nq]), ALU.mult), EcB[d] + [cB], EcB[d])
                for m in range(8):
                    for d, (ui, typ) in enumerate(keyspec):
                        nk = 16 if (smp and d == 0) else 128
                        S.op("pe", lambda e, m=m, d=d, nk=nk: e.matmul(Q[2][:nq, m * 64:(m + 1) * 64], Ec[d][:nk, m * nq:(m + 1) * nq], kvc[d][:nk, 512 + m * 64:512 + (m + 1) * 64],
                                                                       start=(d == 0), stop=(d == nkeys - 1)), EcB[d] + [kvcB[d]], [PSB[4]])
                for m in range(8):
                    for d, (ui, typ) in enumerate(keyspec):
                        nk = 16 if (smp and d == 0) else 128
                        if smp:
                            vcol = validb[:nk, SMP:SMP + 1] if d == 0 else ones[:nk, 0:1]
                        else:
                            vcol = validb[:nk, ui:ui + 1]
                        S.op("pe", lambda e, m=m, d=d, nk=nk, vcol=vcol: e.matmul(P1[:nq, m:m + 1], Ec[d][:nk, m * nq:(m + 1) * nq], vcol, start=(d == 0), stop=(d == nkeys - 1)),
                             EcB[d] + [cB], [P1B])
                S.op("act", lambda e: e.copy(ogc[:nq, :], Q[2][:nq, 0:512]), [PSB[4]], [ogcB])
                rden = small["rden"]
                S.op("dve", lambda e: e.tensor_scalar(rden[:nq, :], P1[:nq, 0:8], 1e-30, None, ALU.add), [P1B], [smB["rden"]])
                S.op("dve", lambda e: e.reciprocal(rden[:nq, :], rden[:nq, :]), [smB["rden"]], [smB["rden"]])
                S.op("dve", lambda e: e.tensor_tensor(mgc[:nq, half * 512:(half + 1) * 512].rearrange("p (h d) -> p h d", h=8), ogc[:nq, :].rearrange("p (h d) -> p h d", h=8),
                                                      rden[:nq, :].unsqueeze(2).to_broadcast([nq, 8, 64]), ALU.mult), [ogcB, smB["rden"]], [mgcB])
            pt, ptB = getpt()
            for k in range(8):
                S.op("pe", lambda e, k=k: e.transpose(pt[:, k * 128:k * 128 + nq], mgc[:nq, k * 128:(k + 1) * 128], ident[:nq, :nq]), [mgcB, cB], [ptB])
            S.op("act", lambda e: e.copy(HT[:, :, i * 128:i * 128 + nq], pt.rearrange("p (k t) -> p k t", k=8)[:, :, 0:nq]), [ptB], [HTb[i]])

    nsb = int(os.environ.get("KNSB", "99"))
    older = [list(range(a, min(a + 16, NT))) for a in range(NQ, NT, 16)]
    if STAGE >= 2:
        cache_prep()
        lambda_prep()
    for sbk in older[:nsb]:
        front0(sbk, False)
    tiles, uu = front0(list(range(NQ)), True)

    if STAGE >= 2:
        S.barrier()
        rr["ptn"] = 1
        if not os.environ.get("KNOATT"):
            attn_l0()
        S.barrier()
        rr["ptn"] = 2
        linear_residual(W["ab_w_out"][0], None, tiles, 1.0)
    if STAGE >= 3:
        S.barrier()
        mem_prep(0)
        S.barrier()
        mem_attn(0, tiles)
    if STAGE >= 4:
        S.barrier()
        ffn(0, 2, tiles)

    tiles17 = [(i, 128) for i in range(NOWN)] + [(NQ, 16)]
    if STAGE >= 5:
        S.barrier()
        ffn(1, 1, tiles)
    if STAGE >= 6:
        rstd_tiles(tiles)
        for (i, rows) in tiles:
            norm_to_HT(i, rows)
        cache_prep_c()
        proj_c(tiles, uu)
        S.barrier()
        rr["ptn"] = 1
        band_attn(tiles17)
        rr["ptn"] = 2
        S.barrier()
        linear_residual(W["c_w_out"][0], None, tiles17, 1.0)
    if STAGE >= 7:
        S.barrier()
        mem_prep(1)
        S.barrier()
        mem_attn(1, tiles17)
    if STAGE >= 8:
        S.barrier()
        ffn(1, 2, tiles17)
    if True:
        for i in range(NOWN):
            S.op("pool", lambda e, i=i: e.dma_start(out=o_y[i], in_=X[:, i, :]), [Xb[i]], [], dma=True)
        S.op("pool", lambda e: e.dma_start(out=o_ys, in_=X[0:16, NQ, :]), [Xb[NQ]], [], dma=True)

    S.barrier()
    print("ops", S.nops, "sems", S.nsem, "sbuf_left", nc.sbuf_bytes_remaining)
    return nc


_NC = None


def _consts():
    c = np.zeros((128, 776), np.float32)
    c[:, 520:648] = np.eye(128)[::-1]
    c[:, 648:776] = ((np.arange(128)[None, :] // 64) <= (np.arange(128)[:, None] // 64)).astype(np.float32)
    c[:, 512:520] = ((500000.0 ** (-2.0 * np.arange(8) / 16.0)).astype(np.float32).astype(np.float64) / (2 * np.pi))[None]
    c[:, 0:128] = np.eye(128)
    j = np.arange(128)[:, None]; s = np.arange(128)[None, :]
    c[:, 128:256] = -(j >= s).astype(np.float32)
    c[:, 256:384] = ((j // 64) <= (s // 64)).astype(np.float32)
    c[:, 384:512] = (j < s).astype(np.float32)
    return c


def kernel(**inp):
    global _NC
    if _NC is None:
        _NC = build()
    nc = _NC
    f = lambda a: np.ascontiguousarray(np.asarray(a, dtype=np.float32))
    xprompt = f(inp["x_prompt"])[0].reshape(128, 128, D)
    in_maps = []
    wnames = ["ffn1_g", "ffn1_wg", "ffn1_wu", "ffn1_wd", "ffn2_g", "ffn2_wg", "ffn2_wu", "ffn2_wd", "mix_g", "ab_w_in", "ab_w_out",
              "a_gq", "a_gk", "a_lq1", "a_lk1", "a_lq2", "a_lk2", "a_subln_g", "c_w_in", "c_w_out", "c_gq", "c_gk", "c_bias",
              "mem_g_x", "mem_g_m", "mem_wq", "mem_wk", "mem_wv", "mem_wo", "mem_gq", "mem_gk"]
    wts = {n: f(inp[n]) for n in wnames}
    cst = _consts()
    for c in range(8):
        xpc = np.zeros((NT, 128, D), np.float32)
        pos = np.zeros((128, NT + 1), np.float32)
        val = np.zeros((128, NT + 1), np.float32)
        for u in range(NT):
            g = 16 * c + 15 - u
            if g >= 0:
                xpc[u] = xprompt[g]
                pos[:, u] = g * 128 + np.arange(128)
                val[:, u] = 1.0
        pos[:16, NT] = 1024 + np.arange(16)
        val[:16, NT] = 1.0
        m = {"xp": xpc, "xs": f(inp["x_sample"])[c], "pos": pos, "valid": val, "consts": cst,
             "ca_k": f(inp["cache_a_k"])[0, c].reshape(1024, 512), "ca_v": f(inp["cache_a_v"])[0, c].reshape(1024, 512),
             "cb_k": f(inp["cache_b_k"])[0, c].reshape(1024, 512), "cb_v": f(inp["cache_b_v"])[0, c].reshape(1024, 512),
             "cc_k": f(inp["cache_c_k"])[0, c].reshape(512, 1024), "cc_v": f(inp["cache_c_v"])[0, c].reshape(512, 1024),
             "cm_k": f(inp["cache_mem_k"])[:, c].reshape(2, 256, 1024), "cm_v": f(inp["cache_mem_v"])[:, c].reshape(2, 256, 1024),
             "memp": f(inp["mem_prompt"])[0]}
        m.update(wts)
        in_maps.append(m)
    res = run_bass_kernel_spmd(nc, in_maps, core_ids=list(range(8))).results

    def gat(name, width):
        out = np.zeros((128, 128, width), np.float32)
        for c in range(8):
            for u in range(NOWN):
                out[16 * c + 15 - u] = res[c][name][u]
        return out.reshape(16384, width)
    y_prompt = gat("o_y", D)[None]
    y_sample = np.stack([res[c]["o_ys"] for c in range(8)])
    a_k_p = gat("o_ak", 512).reshape(1, 1, 16384, 8, 64)
    a_v_p = gat("o_av", 512).reshape(1, 1, 16384, 4, 128)
    b_k_p = gat("o_bk", 512).reshape(1, 1, 16384, 8, 64)
    b_v_p = gat("o_bv", 512).reshape(1, 1, 16384, 8, 64)
    c_k_p = np.concatenate([res[7]["o_ck"][3 - t] for t in range(4)], 0).reshape(1, 1, 512, 16, 64)
    c_v_p = np.concatenate([res[7]["o_cv"][3 - t] for t in range(4)], 0).reshape(1, 1, 512, 16, 64)
    mem_k_p = res[0]["o_mk"].reshape(2, 1, 256, 4, 256)
    mem_v_p = res[0]["o_mv"].reshape(2, 1, 256, 4, 256)
    st = lambda n, shp: np.stack([res[c][n] for c in range(8)]).reshape(shp)
    a_k_s = st("o_aks", (1, 8, 16, 8, 64)); a_v_s = st("o_avs", (1, 8, 16, 4, 128))
    b_k_s = st("o_bks", (1, 8, 16, 8, 64)); b_v_s = st("o_bvs", (1, 8, 16, 8, 64))
    c_k_s = st("o_cks", (1, 8, 512, 16, 64)); c_v_s = st("o_cvs", (1, 8, 512, 16, 64))
    return (y_prompt, y_sample, a_k_p, a_v_p, b_k_p, b_v_p, c_k_p, c_v_p, mem_k_p, mem_v_p,
            a_k_s, a_v_s, b_k_s, b_v_s, c_k_s, c_v_s)
```

```python
import os
import math
import numpy as np
import concourse.bass as bass
import concourse.mybir as mybir
from concourse.bass_utils import run_bass_kernel_spmd

F32, BF16 = mybir.dt.float32, mybir.dt.bfloat16
ALU = mybir.AluOpType
AF = mybir.ActivationFunctionType
AX = mybir.AxisListType

D = 1024
FF = 2816
NT = 128
NQ = 20
NOWN = 16
SMP = 128
EPS = 1e-6
NDMA = 24
ROT = 30000
STAGE = int(os.environ.get("KSTAGE", "9"))


class Buf:
    __slots__ = ("w", "r")

    def __init__(self):
        self.w = None
        self.r = {}


class Sched:
    def __init__(self, nc):
        self.nc = nc
        self.E = {"pe": nc.tensor, "act": nc.scalar, "dve": nc.vector, "pool": nc.gpsimd, "sp": nc.sync}
        self.sem, self.cnt, self.nsem = {}, {}, 0
        self.seen = {e: {} for e in self.E}
        self.allsems = []
        for e in self.E:
            self._rot(e)
        self.dsem = [nc.alloc_semaphore(f"dq{i}") for i in range(NDMA)]
        self.dcnt = [0] * NDMA
        self.dnext = 0
        self.nops = 0

    def _rot(self, e):
        self.sem[e] = self.nc.alloc_semaphore(f"s{e}{self.nsem}")
        self.nsem += 1
        self.cnt[e] = 0
        self.allsems.append([self.sem[e], 0])

    def _wait(self, e, tok):
        sem, val = tok[1], tok[2]
        if self.seen[e].get(sem.num, 0) >= val:
            return
        self.E[e].wait_ge(sem, val)
        self.seen[e][sem.num] = val

    def op(self, e, fn, reads=(), writes=(), dma=False):
        for b in reads:
            if b.w is not None:
                t = b.w
                if not (t[0] == e and e == "pe" and not t[3]):
                    self._wait(e, t)
        for b in writes:
            for t in ([b.w] if b.w is not None else []) + list(b.r.values()):
                if t[0] == e and not t[3]:
                    continue
                self._wait(e, t)
        if dma:
            i = self.dnext
            self.dnext = (i + 1) % NDMA
            if self.dcnt[i] > 0:
                self._wait(e, (e, self.dsem[i], 16 * self.dcnt[i], True))
            ins = fn(self.E[e])
            self.dcnt[i] += 1
            ins.then_inc(self.dsem[i], 16)
            tok = (e, self.dsem[i], 16 * self.dcnt[i], True)
        else:
            if self.cnt[e] >= ROT:
                self._rot(e)
            ins = fn(self.E[e])
            self.cnt[e] += 1
            ins.then_inc(self.sem[e], 1)
            tok = (e, self.sem[e], self.cnt[e], False)
            for s in self.allsems:
                if s[0] is self.sem[e]:
                    s[1] = self.cnt[e]
        for b in reads:
            b.r[tok[1].num] = tok
        for b in writes:
            b.w = tok
            b.r = {}
        self.nops += 1
        return tok

    def barrier(self):
        for e in self.E:
            for s, v in self.allsems:
                if v > 0:
                    self._wait(e, (None, s, v, False))
            for i in range(NDMA):
                if self.dcnt[i] > 0:
                    self._wait(e, (None, self.dsem[i], 16 * self.dcnt[i], True))


def build():
    nc = bass.Bass("TRN2", target_bir_lowering=False)
    S = Sched(nc)

    def din(name, shape):
        return nc.dram_tensor(name, list(shape), F32, kind="ExternalInput").ap()

    def dout(name, shape):
        return nc.dram_tensor(name, list(shape), F32, kind="ExternalOutput").ap()

    def sb(name, shape, dt=F32):
        return nc.alloc_sbuf_tensor("sb_" + name, list(shape), dt).ap()

    xp = din("xp", [NT, 128, D])
    xs = din("xs", [16, D])
    posd = din("pos", [128, NT + 1])
    validd = din("valid", [128, NT + 1])
    constd = din("consts", [128, 776])
    ca_k = din("ca_k", [1024, 512]); ca_v = din("ca_v", [1024, 512])
    cb_k = din("cb_k", [1024, 512]); cb_v = din("cb_v", [1024, 512])
    cc_k = din("cc_k", [512, 1024]); cc_v = din("cc_v", [512, 1024])
    cm_k = din("cm_k", [2, 256, 1024]); cm_v = din("cm_v", [2, 256, 1024])
    memp = din("memp", [256, 1024])
    W = {}
    for nm, shp in [("ffn1_g", [2, D]), ("ffn1_wg", [2, D, FF]), ("ffn1_wu", [2, D, FF]), ("ffn1_wd", [2, FF, D]),
                    ("ffn2_g", [2, D]), ("ffn2_wg", [2, D, FF]), ("ffn2_wu", [2, D, FF]), ("ffn2_wd", [2, FF, D]),
                    ("mix_g", [2, D]), ("ab_w_in", [1, D, 3072]), ("ab_w_out", [1, D, D]),
                    ("a_gq", [1, 64]), ("a_gk", [1, 64]), ("a_lq1", [1, 64]), ("a_lk1", [1, 64]),
                    ("a_lq2", [1, 64]), ("a_lk2", [1, 64]), ("a_subln_g", [1, 128]),
                    ("c_w_in", [1, D, 3072]), ("c_w_out", [1, D, D]), ("c_gq", [1, 64]), ("c_gk", [1, 64]),
                    ("c_bias", [1, 16, 257]), ("mem_g_x", [2, D]), ("mem_g_m", [2, D]),
                    ("mem_wq", [2, D, D]), ("mem_wk", [2, D, D]), ("mem_wv", [2, D, D]), ("mem_wo", [2, D, D]),
                    ("mem_gq", [2, 256]), ("mem_gk", [2, 256])]:
        W[nm] = din(nm, shp)

    o_y = dout("o_y", [NOWN, 128, D]); o_ys = dout("o_ys", [16, D])
    o_ak = dout("o_ak", [NOWN, 128, 512]); o_av = dout("o_av", [NOWN, 128, 512])
    o_bk = dout("o_bk", [NOWN, 128, 512]); o_bv = dout("o_bv", [NOWN, 128, 512])
    o_ck = dout("o_ck", [4, 128, D]); o_cv = dout("o_cv", [4, 128, D])
    o_mk = dout("o_mk", [2, 256, D]); o_mv = dout("o_mv", [2, 256, D])
    o_aks = dout("o_aks", [16, 512]); o_avs = dout("o_avs", [16, 512])
    o_bks = dout("o_bks", [16, 512]); o_bvs = dout("o_bvs", [16, 512])
    o_cks = dout("o_cks", [512, D]); o_cvs = dout("o_cvs", [512, D])

    def dscr(name, shape):
        return nc.dram_tensor(name, list(shape), BF16).ap()
    kta_s = dscr("kta_s", [NT + 9, 128, 512]); ktb_s = dscr("ktb_s", [NT + 9, 128, 512])
    va_s = dscr("va_s", [NT + 9, 128, 520]); vb_s = dscr("vb_s", [NT + 9, 128, 512])
    qta_s = dscr("qta_s", [NQ + 1, 128, 512]); qtb_s = dscr("qtb_s", [NQ + 1, 128, 512])
    scrB = {"kta": [Buf() for _ in range(NT + 9)], "ktb": [Buf() for _ in range(NT + 9)],
            "va": [Buf() for _ in range(NT + 9)], "vb": [Buf() for _ in range(NT + 9)],
            "qta": [Buf() for _ in range(NQ + 1)], "qtb": [Buf() for _ in range(NQ + 1)]}
    outB = Buf()
    inB = Buf()

    NX = NQ + 1
    X = sb("X", [128, NX, D])
    Xb = [Buf() for _ in range(NX)]
    HT = sb("HT", [128, 8, NQ * 128 + 16], BF16)
    HTb = [Buf() for _ in range(NX)]
    cst = sb("cst", [128, 776]); cstB = Buf()
    ident = sb("ident", [128, 128], BF16); negU = sb("negU", [128, 128], BF16)
    maskA = sb("maskA", [128, 128], BF16); maskB = sb("maskB", [128, 128], BF16)
    ones = sb("ones", [128, 1], BF16)
    cB = Buf()
    pos = sb("pos", [128, NT + 1]); valid = sb("valid", [128, NT + 1]); validb = sb("validb", [128, NT + 1], BF16)
    gT = sb("gT", [128, 80]); gTB = Buf()
    g64 = {n: sb("g_" + n, [128, 64]) for n in ("a_gq", "a_gk", "c_gq", "c_gk")}
    gB = Buf()

    S.op("sp", lambda e: e.dma_start(out=cst, in_=constd), [], [cstB], dma=True)
    S.op("sp", lambda e: e.dma_start(out=pos, in_=posd), [], [cB], dma=True)
    S.op("sp", lambda e: e.dma_start(out=valid, in_=validd), [], [cB], dma=True)
    S.op("dve", lambda e: e.tensor_copy(ident, cst[:, 0:128]), [cstB], [cB])
    S.op("dve", lambda e: e.tensor_copy(negU, cst[:, 128:256]), [cstB], [cB])
    S.op("dve", lambda e: e.tensor_copy(maskA, cst[:, 256:384]), [cstB], [cB])
    S.op("dve", lambda e: e.tensor_copy(maskB, cst[:, 384:512]), [cstB], [cB])
    S.op("dve", lambda e: e.memset(ones, 1.0), [], [cB])
    S.op("dve", lambda e: e.tensor_copy(validb, valid), [cB], [cB])
    for n in g64:
        S.op("sp", lambda e, n=n: e.dma_start(out=g64[n], in_=W[n][0:1, :].partition_broadcast(128)), [], [gB], dma=True)
    for n in ("a_gq", "c_gq"):
        S.op("dve", lambda e, n=n: e.tensor_scalar(g64[n], g64[n], 0.125, None, ALU.mult), [gB], [gB])

    Q = [nc.alloc_psum_tensor(f"q{i}", [128, 1024], F32).ap() for i in range(4)]
    PS = [Q[i // 2][:, (i % 2) * 512:(i % 2 + 1) * 512] for i in range(6)]
    PSB = [Buf() for _ in range(6)]
    PT = [Q[3][:, h * 512:(h + 1) * 512].bitcast(BF16) for h in range(2)]
    PTB = [Buf() for _ in range(2)]
    P1 = Q[3][:, 0:512]
    P1B = PTB[0]
    rr = {"ps": 0, "pt": 0, "ptn": 2}

    def getps():
        i = rr["ps"]; rr["ps"] = (i + 1) % 6
        return PS[i], PSB[i]

    def getpt():
        if rr["ptn"] == 1:
            return PT[1], PTB[1]
        i = rr["pt"]; rr["pt"] = (i + 1) % 2
        return PT[i], PTB[i]

    GID = {"ffn1_g": 0, "ffn2_g": 2, "mix_g": 4, "mem_g_x": 6, "mem_g_m": 8}
    graw = sb("graw", [80, 128]); grawB = Buf()
    for n, gi in GID.items():
        S.op("sp", lambda e, n=n, gi=gi: e.dma_start(out=graw[gi * 8:(gi + 2) * 8, :],
                                                     in_=W[n].rearrange("l (k p) -> (l k) p", p=128)), [], [grawB], dma=True)
    identf = cst[:, 0:128]
    pg, pgB = getps()
    S.op("pe", lambda e: e.transpose(pg[:, 0:80], graw[0:80, :], identf[0:80, 0:80]), [grawB, cstB], [pgB])
    S.op("dve", lambda e: e.tensor_copy(gT, pg[:, 0:80]), [pgB], [gTB])

    ARENA = 15360
    arena = sb("arena", [128, ARENA])

    def av(off, n, dt=F32):
        v = arena[:, off:off + n]
        return v if dt == F32 else v.bitcast(dt)
    Wb = [av(3072 * i, 3072, BF16).rearrange("p (a f) -> p a f", a=3) for i in range(2)]
    WbB = [[Buf() for _ in range(3)] for _ in range(2)]
    stg = [av(6144 + 2048 * i, 2048) for i in range(3)]
    stgB = [Buf() for _ in range(3)]
    hid = [av(12288 + 512 * i, 512, BF16).rearrange("p (j t) -> p j t", j=2) for i in range(2)]
    hidB = [Buf() for _ in range(2)]
    sgt = [av(13312 + 512 * i, 512) for i in range(2)]
    sgB = [Buf() for _ in range(2)]
    xn = [av(14336 + 512 * i, 512, BF16) for i in range(2)]
    xnB = [Buf() for _ in range(2)]
    junk = sb("junk", [128, D], BF16); junkB = Buf()
    st = [sb(f"st{i}", [128, 8]) for i in range(4)]
    stB = [Buf() for _ in range(4)]
    cosT = sb("cosT", [128, NX, 8]); sinT = sb("sinT", [128, NX, 8]); csB = Buf()
    wk = [sb(f"wk{i}", [128, 256]) for i in range(4)]
    wkB = [Buf() for _ in range(4)]
    wkb = [sb(f"wkb{i}", [128, 264], BF16) for i in range(2)]
    wkbB = [Buf() for _ in range(2)]
    ktile = [sb(f"ktile{i}", [128, 2, 128], BF16) for i in range(2)]
    ktB = [Buf() for _ in range(2)]
    rp = [sb(f"rp{i}", [128, 4, 8]) for i in range(4)]
    rpB = [Buf() for _ in range(4)]
    cnt = {"stg": 0, "xn": 0, "st": 0, "wk": 0, "wkb": 0, "kt": 0, "hid": 0, "sg": 0}

    def nxt(k, n):
        i = cnt[k]; cnt[k] = (i + 1) % n
        return i

    rs = sb("rs", [128, NX]); rsB = Buf()

    def rstd_tiles(tiles):
        S.op("dve", lambda e: e.memset(rs, 0.0), [], [rsB])
        for (i, rows) in tiles:
            S.op("act", lambda e, i=i, rows=rows: e.activation(junk[:rows, :], X[:rows, i, :], AF.Square, accum_out=rs[:rows, i:i + 1]),
                 [Xb[i]], [junkB, rsB])
        S.op("dve", lambda e: e.tensor_scalar(rs, rs, 1.0 / D, EPS, ALU.mult, ALU.add), [rsB], [rsB])
        S.op("act", lambda e: e.activation(rs, rs, AF.Ln), [rsB], [rsB])
        S.op("act", lambda e: e.activation(rs, rs, AF.Exp, scale=-0.5), [rsB], [rsB])

    def norm_to_HT(i, rows):
        r, rB = rs[:, i:i + 1], rsB
        j = nxt("xn", 2)
        S.op("pool", lambda e: e.tensor_scalar(xn[j][:rows, :], X[:rows, i, :], r[:rows, 0:1], None, ALU.mult),
             [Xb[i], rB], [xnB[j]])
        pt, ptB = getpt()
        for k in range(8):
            S.op("pe", lambda e, k=k: e.transpose(pt[:, k * 128:k * 128 + rows], xn[j][:rows, k * 128:(k + 1) * 128],
                                                  ident[:rows, :rows]), [xnB[j], cB], [ptB])
        src = pt.rearrange("p (k t) -> p k t", k=8)[:, :, 0:rows]
        dst = HT[:, :, i * 128:i * 128 + rows]
        if i % 2 == 0:
            S.op("act", lambda e: e.copy(dst, src), [ptB], [HTb[i]])
        else:
            S.op("dve", lambda e: e.tensor_copy(dst, src), [ptB], [HTb[i]])

    def load_w(dst, dstB, src_ap, gcol, kparts, width):
        s = nxt("stg", lim["stg"])
        S.op("sp", lambda e: e.dma_start(out=stg[s][:, 0:kparts * width].rearrange("p (k f) -> p k f", k=kparts),
                                         in_=src_ap.rearrange("(k p) f -> p k f", p=128)), [], [stgB[s]], dma=True)
        d3 = dst.rearrange("p (k f) -> p k f", k=kparts)
        s3 = stg[s][:, 0:kparts * width].rearrange("p (k f) -> p k f", k=kparts)
        if gcol is None:
            S.op("pool", lambda e: e.tensor_copy(d3, s3), [stgB[s]], [dstB])
        else:
            S.op("pool", lambda e: e.tensor_tensor(d3, s3, gT[:, gcol:gcol + kparts].unsqueeze(2).to_broadcast([128, kparts, width]),
                                                   ALU.mult), [stgB[s], gTB], [dstB])

    def ffn(l, which, tiles):
        pre = f"ffn{which}_"
        gcol = (GID[pre + "g"] + l) * 8
        rstd_tiles(tiles)
        for (i, rows) in tiles:
            norm_to_HT(i, rows)
        blocks = [tiles[b:b + 4] for b in range(0, len(tiles), 4)]
        pending = []
        for fg in range(FF // 256):
            ws = fg % 2
            load_w(Wb[ws][:, 0, :], WbB[ws][0], W[pre + "wg"][l][:, fg * 256:(fg + 1) * 256], gcol, 8, 256)
            load_w(Wb[ws][:, 1, :], WbB[ws][1], W[pre + "wu"][l][:, fg * 256:(fg + 1) * 256], gcol, 8, 256)
            load_w(Wb[ws][:, 2, :], WbB[ws][2], W[pre + "wd"][l][fg * 256:(fg + 1) * 256, :], None, 2, 1024)
            wg3 = Wb[ws][:, 0, :].rearrange("p (k f) -> p k f", k=8)
            wu3 = Wb[ws][:, 1, :].rearrange("p (k f) -> p k f", k=8)
            wd3 = Wb[ws][:, 2, :].rearrange("p (k f) -> p k f", k=2)
            for blk in blocks:
                c0 = blk[0][0] * 128
                ncol = (blk[-1][0] - blk[0][0]) * 128 + blk[-1][1]
                hB = [HTb[i] for i, _ in blk]
                hi = nxt("hid", 2)
                for j in range(2):
                    pgt, pgtB = getps()
                    put, putB = getps()
                    for k in range(8):
                        S.op("pe", lambda e, k=k, j=j, pgt=pgt: e.matmul(pgt[:, 0:ncol], wg3[:, k, j * 128:(j + 1) * 128], HT[:, k, c0:c0 + ncol],
                                                                         start=(k == 0), stop=(k == 7)), hB + [WbB[ws][0]], [pgtB])
                    for k in range(8):
                        S.op("pe", lambda e, k=k, j=j, put=put: e.matmul(put[:, 0:ncol], wu3[:, k, j * 128:(j + 1) * 128], HT[:, k, c0:c0 + ncol],
                                                                         start=(k == 0), stop=(k == 7)), hB + [WbB[ws][1]], [putB])
                    si = nxt("sg", 2)
                    S.op("act", lambda e, si=si, pgt=pgt: e.activation(sgt[si][:, 0:ncol], pgt[:, 0:ncol], AF.Silu), [pgtB], [sgB[si]])
                    S.op("dve", lambda e, j=j, si=si, put=put: e.tensor_tensor(hid[hi][:, j, 0:ncol], sgt[si][:, 0:ncol], put[:, 0:ncol], ALU.mult),
                         [sgB[si], putB], [hidB[hi]])

                def down(blk=blk, c0=c0, hi=hi, ws=ws, wd3=wd3):
                    for (i, rows) in blk:
                        o = i * 128 - c0
                        for half in range(2):
                            pd, pdB = getps()
                            for j in range(2):
                                S.op("pe", lambda e, j=j, pd=pd: e.matmul(pd[:rows, :], hid[hi][:, j, o:o + rows], wd3[:, j, half * 512:(half + 1) * 512],
                                                                          start=(j == 0), stop=(j == 1)), [hidB[hi], WbB[ws][2]], [pdB])
                            S.op("dve", lambda e, pd=pd: e.scalar_tensor_tensor(X[:rows, i, half * 512:(half + 1) * 512], pd[:rows, :], 0.5,
                                                                                X[:rows, i, half * 512:(half + 1) * 512], ALU.mult, ALU.add),
                                 [pdB, Xb[i]], [Xb[i]])
                for p in pending:
                    p()
                pending = [down]
        for p in pending:
            p()

    rt = sb("rt", [128, NX, 8]); rtf = sb("rtf", [128, NX, 8]); rti = sb("rti", [128, NX, 8], mybir.dt.int32); rtB = Buf()

    def rope_tables(i0, n, u0):
        sl = slice(i0, i0 + n)
        pb = pos[:, u0:u0 + n].unsqueeze(2).to_broadcast([128, n, 8])
        fb = cst[:, 512:520].unsqueeze(1).to_broadcast([128, n, 8])
        for tab, shift in ((sinT, 0.0), (cosT, 0.25)):
            S.op("dve", lambda e: e.tensor_tensor(rt[:, sl, :], pb, fb, ALU.mult), [cB, cstB], [rtB])
            if shift:
                S.op("dve", lambda e, shift=shift: e.tensor_scalar(rt[:, sl, :], rt[:, sl, :], shift, None, ALU.add), [rtB], [rtB])
            S.op("dve", lambda e: e.tensor_copy(rti[:, sl, :], rt[:, sl, :]), [rtB], [rtB])
            S.op("dve", lambda e: e.tensor_copy(rtf[:, sl, :], rti[:, sl, :]), [rtB], [rtB])
            S.op("dve", lambda e: e.tensor_tensor(rt[:, sl, :], rt[:, sl, :], rtf[:, sl, :], ALU.subtract), [rtB], [rtB])
            S.op("dve", lambda e: e.tensor_scalar(rtf[:, sl, :], rt[:, sl, :], 0.5, None, ALU.is_gt), [rtB], [rtB])
            S.op("dve", lambda e: e.tensor_tensor(rt[:, sl, :], rt[:, sl, :], rtf[:, sl, :], ALU.subtract), [rtB], [rtB])
            S.op("dve", lambda e: e.tensor_scalar(rtf[:, sl, :], rt[:, sl, :], -0.5, None, ALU.is_lt), [rtB], [rtB])
            S.op("dve", lambda e: e.tensor_tensor(rt[:, sl, :], rt[:, sl, :], rtf[:, sl, :], ALU.add), [rtB], [rtB])
            S.op("act", lambda e, tab=tab: e.activation(tab[:, sl, :], rt[:, sl, :], AF.Sin, scale=2 * math.pi), [rtB], [csB, rtB])

    def to_T_and_store(src_bf, srcB, rows, dst_ap, dstB):
        pt, ptB = getpt()
        for k in range(2):
            S.op("pe", lambda e, k=k: e.transpose(pt[:, k * 128:k * 128 + rows], src_bf[:rows, k * 128:(k + 1) * 128], ident[:rows, :rows]),
                 [srcB, cB], [ptB])
        t = nxt("kt", 2)
        S.op("act", lambda e: e.copy(ktile[t][:, :, 0:rows], pt[:, 0:256].rearrange("p (k t) -> p k t", k=2)[:, :, 0:rows]), [ptB], [ktB[t]])
        S.op("pool", lambda e: e.dma_start(out=dst_ap, in_=ktile[t][:, :, 0:rows]), [ktB[t]], [dstB], dma=True)

    def proj_ab(tiles, us):
        gcol = (GID["mix_g"] + 0) * 8
        for pc in range(int(os.environ.get("KPC", "12"))):
            ws = pc % 2
            kind, hp = ("aq", "ak", "av", "bq", "bk", "bv")[pc // 2], pc % 2
            load_w(Wb[ws][:, 0, :], WbB[ws][0], W["ab_w_in"][0][:, pc * 256:(pc + 1) * 256], gcol, 8, 256)
            w3 = Wb[ws][:, 0, :].rearrange("p (k f) -> p k f", k=8)
            for (i, rows), u in zip(tiles, us):
                own = (u < NOWN) or (u == SMP)
                isq = kind in ("aq", "bq")
                if isq and not (u < NQ or u == SMP):
                    continue
                qi = NQ if u == SMP else u
                pp, ppB = getps()
                for k in range(8):
                    S.op("pe", lambda e, k=k: e.matmul(pp[:rows, 0:256], HT[:, k, i * 128:i * 128 + rows], w3[:, k, :],
                                                       start=(k == 0), stop=(k == 7)), [HTb[i], WbB[ws][0]], [ppB])
                src = pp[:rows, 0:256]
                csl = slice(hp * 256, (hp + 1) * 256)
                if kind in ("aq", "ak"):
                    g = g64["a_gq" if kind == "aq" else "a_gk"]
                    a, b = nxt("wk", 4), nxt("wk", 4)
                    s_ = nxt("st", 4)
                    S.op("act", lambda e: e.copy(wk[b][:rows, :], src), [ppB], [wkB[b]])
                    S.op("dve", lambda e: e.tensor_tensor(wk[a][:rows, :], wk[b][:rows, :], wk[b][:rows, :], ALU.mult), [wkB[b]], [wkB[a]])
                    S.op("dve", lambda e: e.tensor_reduce(st[s_][:rows, 0:4], wk[a][:rows, :].rearrange("p (h d) -> p h d", h=4), AX.X, ALU.add),
                         [wkB[a]], [stB[s_]])
                    S.op("dve", lambda e: e.tensor_scalar(st[s_][:rows, 0:4], st[s_][:rows, 0:4], 1.0 / 64, EPS, ALU.mult, ALU.add), [stB[s_]], [stB[s_]])
                    S.op("act", lambda e: e.activation(st[s_][:rows, 0:4], st[s_][:rows, 0:4], AF.Ln), [stB[s_]], [stB[s_]])
                    S.op("act", lambda e: e.activation(st[s_][:rows, 0:4], st[s_][:rows, 0:4], AF.Exp, scale=-0.5), [stB[s_]], [stB[s_]])
                    w3a = wk[a][:rows, :].rearrange("p (h d) -> p h d", h=4)
                    w3b = wk[b][:rows, :].rearrange("p (h d) -> p h d", h=4)
                    S.op("dve", lambda e: e.tensor_tensor(w3a, src.rearrange("p (h d) -> p h d", h=4),
                                                          st[s_][:rows, 0:4].unsqueeze(2).to_broadcast([rows, 4, 64]), ALU.mult), [ppB, stB[s_]], [wkB[a]])
                    S.op("dve", lambda e: e.tensor_tensor(w3b, w3a, g[:rows, :].unsqueeze(1).to_broadcast([rows, 4, 64]), ALU.mult), [wkB[a], gB], [wkB[b]])
                    cs = cosT[:rows, i, :].unsqueeze(1).to_broadcast([rows, 4, 8])
                    sn = sinT[:rows, i, :].unsqueeze(1).to_broadcast([rows, 4, 8])
                    x1, x2 = w3b[:, :, 0:8], w3b[:, :, 8:16]
                    r = [rp[q][:rows] for q in range(4)]
                    S.op("dve", lambda e: e.tensor_tensor(r[0], x1, cs, ALU.mult), [wkB[b], csB], [rpB[0]])
                    S.op("dve", lambda e: e.tensor_tensor(r[1], x2, sn, ALU.mult), [wkB[b], csB], [rpB[1]])
                    S.op("dve", lambda e: e.tensor_tensor(r[2], x2, cs, ALU.mult), [wkB[b], csB], [rpB[2]])
                    S.op("dve", lambda e: e.tensor_tensor(r[3], x1, sn, ALU.mult), [wkB[b], csB], [rpB[3]])
                    S.op("dve", lambda e: e.tensor_tensor(x1, r[0], r[1], ALU.subtract), [rpB[0], rpB[1]], [wkB[b]])
                    S.op("dve", lambda e: e.tensor_tensor(x2, r[2], r[3], ALU.add), [rpB[2], rpB[3]], [wkB[b]])
                    fin, finB = wk[b], wkB[b]
                    if kind == "ak" and own:
                        dst = o_aks[:, csl] if u == SMP else o_ak[u][:, csl]
                        S.op("pool", lambda e, dst=dst: e.dma_start(out=dst[:rows], in_=fin[:rows, :]), [finB], [], dma=True)
                    c = nxt("wkb", 2)
                    S.op("pool", lambda e: e.tensor_copy(wkb[c][:rows, 0:256], fin[:rows, :]), [finB], [wkbB[c]])
                    if kind == "ak":
                        to_T_and_store(wkb[c], wkbB[c], rows, kta_s[u].rearrange("p (k t) -> p k t", k=4)[:, hp * 2:hp * 2 + 2, 0:rows], scrB["kta"][u])
                    else:
                        to_T_and_store(wkb[c], wkbB[c], rows, qta_s[qi].rearrange("p (k t) -> p k t", k=4)[:, hp * 2:hp * 2 + 2, 0:rows], scrB["qta"][qi])
                elif kind in ("bq", "bk"):
                    c = nxt("wkb", 2)
                    if kind == "bk":
                        a = nxt("wk", 4)
                        S.op("act", lambda e: e.copy(wk[a][:rows, :], src), [ppB], [wkB[a]])
                        if own:
                            dst = o_bks[:, csl] if u == SMP else o_bk[u][:, csl]
                            S.op("pool", lambda e, dst=dst: e.dma_start(out=dst[:rows], in_=wk[a][:rows, :]), [wkB[a]], [], dma=True)
                        S.op("dve", lambda e: e.tensor_copy(wkb[c][:rows, 0:256], wk[a][:rows, :]), [wkB[a]], [wkbB[c]])
                        to_T_and_store(wkb[c], wkbB[c], rows, ktb_s[u].rearrange("p (k t) -> p k t", k=4)[:, hp * 2:hp * 2 + 2, 0:rows], scrB["ktb"][u])
                    else:
                        S.op("dve", lambda e: e.tensor_scalar(wkb[c][:rows, 0:256], src, 0.125, None, ALU.mult), [ppB], [wkbB[c]])
                        to_T_and_store(wkb[c], wkbB[c], rows, qtb_s[qi].rearrange("p (k t) -> p k t", k=4)[:, hp * 2:hp * 2 + 2, 0:rows], scrB["qtb"][qi])
                else:
                    a = nxt("wk", 4)
                    S.op("act", lambda e: e.copy(wk[a][:rows, :], src), [ppB], [wkB[a]])
                    if own:
                        od = {"av": (o_avs, o_av), "bv": (o_bvs, o_bv)}[kind]
                        dst = od[0][:, csl] if u == SMP else od[1][u][:, csl]
                        S.op("pool", lambda e, dst=dst: e.dma_start(out=dst[:rows], in_=wk[a][:rows, :]), [wkB[a]], [], dma=True)
                    c = nxt("wkb", 2)
                    if kind == "av":
                        v3 = wkb[c][:rows, 0:260].rearrange("p (h d) -> p h d", h=2)
                        S.op("dve", lambda e: e.tensor_scalar(v3[:, :, 0:128], wk[a][:rows, :].rearrange("p (h d) -> p h d", h=2), valid[:rows, u:u + 1], None, ALU.mult),
                             [wkB[a], cB], [wkbB[c]])
                        S.op("dve", lambda e: e.tensor_copy(v3[:, :, 128:130], valid[:rows, u:u + 1].unsqueeze(1).to_broadcast([rows, 2, 2])), [cB], [wkbB[c]])
                        S.op("pool", lambda e: e.dma_start(out=va_s[u][:rows, hp * 260:(hp + 1) * 260], in_=wkb[c][:rows, 0:260]), [wkbB[c]], [scrB["va"][u]], dma=True)
                    else:
                        S.op("dve", lambda e: e.tensor_scalar(wkb[c][:rows, 0:256], wk[a][:rows, :], valid[:rows, u:u + 1], None, ALU.mult), [wkB[a], cB], [wkbB[c]])
                        S.op("pool", lambda e: e.dma_start(out=vb_s[u][:rows, csl], in_=wkb[c][:rows, 0:256]), [wkbB[c]], [scrB["vb"][u]], dma=True)

    def front0(us, with_sample):
        tiles = [(i, 128) for i in range(len(us))]
        uu = list(us)
        for i, u in enumerate(us):
            S.op("sp", lambda e, i=i, u=u: e.dma_start(out=X[:, i, :], in_=xp[u]), [], [Xb[i]], dma=True)
        if with_sample:
            i = len(us)
            S.op("sp", lambda e: e.dma_start(out=X[0:16, i, :], in_=xs), [], [Xb[i]], dma=True)
            tiles.append((i, 16)); uu.append(SMP)
        DBG = int(os.environ.get("KDBG", "9"))
        if DBG >= 1:
            rope_tables(0, len(us), us[0])
            if with_sample:
                rope_tables(len(us), 1, SMP)
        if DBG >= 3:
            ffn(0, 1, tiles)
        if DBG >= 2:
            rstd_tiles(tiles)
            for (i, rows) in tiles:
                norm_to_HT(i, rows)
        if DBG >= 4:
            proj_ab(tiles, uu)
        return tiles, uu

    SQ = NQ
    CIDX = NT + 1
    small = {n: sb("sm_" + n, shp) for n, shp in (("den", [128, 4, 8]), ("carry", [128, 4, 8]), ("ecarry", [128, 4, 8]),
                                                   ("rden", [128, 8]), ("s4", [128, 4]), ("lam", [128, 4]))}
    smB = {n: Buf() for n in small}
    lt = sb("lt", [128, 4, 64]); ltB = Buf()
    gsub = sb("gsub", [128, 128]); gsubB = Buf()

    def cache_prep():
        for t in range(8):
            rsl = slice(t * 128, (t + 1) * 128)
            for (src, dst, key) in ((ca_k, kta_s, "kta"), (cb_k, ktb_s, "ktb")):
                s_ = nxt("stg", 3)
                S.op("sp", lambda e, src=src: e.dma_start(out=stg[s_][:, 0:512], in_=src[rsl, :]), [], [stgB[s_]], dma=True)
                j = nxt("xn", 2)
                S.op("pool", lambda e: e.tensor_copy(xn[j][:, 0:512], stg[s_][:, 0:512]), [stgB[s_]], [xnB[j]])
                pt, ptB = getpt()
                for k in range(4):
                    S.op("pe", lambda e, k=k: e.transpose(pt[:, k * 128:(k + 1) * 128], xn[j][:, k * 128:(k + 1) * 128], ident), [xnB[j], cB], [ptB])
                S.op("act", lambda e: e.copy(xn[j][:, 512:1024], pt[:, 0:512]), [ptB], [xnB[j]])
                S.op("pool", lambda e, dst=dst: e.dma_start(out=dst[CIDX + t], in_=xn[j][:, 512:1024]), [xnB[j]], [scrB[key][CIDX + t]], dma=True)
            s_ = nxt("stg", 3)
            S.op("sp", lambda e: e.dma_start(out=stg[s_][:, 0:512], in_=ca_v[rsl, :]), [], [stgB[s_]], dma=True)
            j = nxt("xn", 2)
            v4 = xn[j][:, 0:520].rearrange("p (h d) -> p h d", h=4)
            S.op("pool", lambda e: e.tensor_copy(v4[:, :, 0:128], stg[s_][:, 0:512].rearrange("p (h d) -> p h d", h=4)), [stgB[s_]], [xnB[j]])
            S.op("pool", lambda e: e.memset(v4[:, :, 128:130], 1.0), [], [xnB[j]])
            S.op("pool", lambda e: e.dma_start(out=va_s[CIDX + t], in_=xn[j][:, 0:520]), [xnB[j]], [scrB["va"][CIDX + t]], dma=True)
            s_ = nxt("stg", 3)
            S.op("sp", lambda e: e.dma_start(out=stg[s_][:, 0:512], in_=cb_v[rsl, :]), [], [stgB[s_]], dma=True)
            j = nxt("xn", 2)
            S.op("pool", lambda e: e.tensor_copy(xn[j][:, 0:512], stg[s_][:, 0:512]), [stgB[s_]], [xnB[j]])
            S.op("pool", lambda e: e.dma_start(out=vb_s[CIDX + t], in_=xn[j][:, 0:512]), [xnB[j]], [scrB["vb"][CIDX + t]], dma=True)

    def lambda_prep():
        for q, n in enumerate(("a_lq1", "a_lk1", "a_lq2", "a_lk2")):
            S.op("sp", lambda e, q=q, n=n: e.dma_start(out=lt[:, q, :], in_=W[n][0:1, :].partition_broadcast(128)), [], [ltB], dma=True)
        S.op("sp", lambda e: e.dma_start(out=gsub, in_=W["a_subln_g"][0:1, :].partition_broadcast(128)), [], [gsubB], dma=True)
        S.op("dve", lambda e: e.tensor_scalar(gsub, gsub, 0.8, None, ALU.mult), [gsubB], [gsubB])
        lam = small["lam"]
        S.op("dve", lambda e: e.tensor_tensor(lt[:, 0, :], lt[:, 0, :], lt[:, 1, :], ALU.mult), [ltB], [ltB])
        S.op("dve", lambda e: e.tensor_tensor(lt[:, 2, :], lt[:, 2, :], lt[:, 3, :], ALU.mult), [ltB], [ltB])
        S.op("dve", lambda e: e.tensor_reduce(lam[:, 0:1], lt[:, 0, :], AX.X, ALU.add), [ltB], [smB["lam"]])
        S.op("dve", lambda e: e.tensor_reduce(lam[:, 1:2], lt[:, 2, :], AX.X, ALU.add), [ltB], [smB["lam"]])
        S.op("act", lambda e: e.activation(lam[:, 0:2], lam[:, 0:2], AF.Exp), [smB["lam"]], [smB["lam"]])
        S.op("dve", lambda e: e.tensor_tensor(lam[:, 2:3], lam[:, 1:2], lam[:, 0:1], ALU.subtract), [smB["lam"]], [smB["lam"]])
        S.op("dve", lambda e: e.tensor_scalar(lam[:, 2:3], lam[:, 2:3], -0.2, None, ALU.add), [smB["lam"]], [smB["lam"]])

    def attn_l0():
        accA = [av(1024 * j, 1024) for j in range(4)]; accAB = [Buf() for _ in range(4)]
        accB = [av(4096 + 512 * j, 512) for j in range(4)]; accBB = [Buf() for _ in range(4)]
        QTe = av(6144, 1024, BF16).rearrange("p (k t) -> p k t", k=4); QTB = Buf()
        QTo = av(13856, 1024, BF16).rearrange("p (k t) -> p k t", k=4)
        QTeo = (QTe, QTo)
        S.op("pool", lambda e: e.memset(QTe[64:128, :, :], 0.0), [], [QTB])
        S.op("pool", lambda e: e.memset(QTo[0:64, :, :], 0.0), [], [QTB])
        kvb = [av(7168 + 528 * i, 528, BF16) for i in range(2)]; kvB = [Buf() for _ in range(2)]
        Eb = [av(8224 + 512 * i, 512, BF16) for i in range(2)]; EbB = [[Buf(), Buf()] for _ in range(2)]
        ef = [av(9248 + 512 * i, 512) for i in range(2)]; efB = [Buf() for _ in range(2)]
        pvs = [av(10272 + 256 * i, 256) for i in range(2)]; pvsB = [Buf() for _ in range(2)]
        tmpB = [av(10784 + 256 * i, 256) for i in range(2)]; tmpBB = [Buf() for _ in range(2)]
        ofin = av(11296, 1024); ofinB = Buf()
        mg = av(12320, 512, BF16); mgB = Buf()
        sqf = av(12832, 512); sqfB = Buf()
        aof = av(13344, 512); aofB = Buf()
        QhB = [[Buf(), Buf()] for _ in range(3)]
        den, carry, ecarry = small["den"], small["carry"], small["ecarry"]
        zc = {"z": 0, "e": 0}

        SPB = [Eb[0], av(12832, 512, BF16)]; WBv = [Eb[1], av(13344, 512, BF16)]
        SPBB = [[Buf(), Buf()], [Buf(), Buf()]]; WBB = [[Buf(), Buf()], [Buf(), Buf()]]
        P1Bx = [P1B, Buf()]
        par_c = {"p": 0}

        def run_pipeline(gens):
            active = []
            for g in gens:
                active.insert(0, g)
                keep = []
                for gg in active:
                    try:
                        next(gg); keep.append(gg)
                    except StopIteration:
                        pass
                active = keep
            while active:
                keep = []
                for gg in active:
                    try:
                        next(gg); keep.append(gg)
                    except StopIteration:
                        pass
                active = keep

        def pairA(j, nq, nk, KT, V, kB, diag):
            zi = zc["z"]; zc["z"] = 1 - zi
            z = Q[zi]
            for m in range(8):
                S.op("pe", lambda e, m=m: e.matmul(z[:nk, m * nq:(m + 1) * nq], KT[:, m // 2, 0:nk],
                                                   QTeo[m % 2][:, m // 2, j * 128:j * 128 + nq], start=True, stop=True),
                     [kB, QTB], [QhB[zi][0], QhB[zi][1]])
            ei = zc["e"]; zc["e"] = 1 - ei
            E = Eb[ei]
            if 8 * nq > 512:
                for hh in range(2):
                    S.op("act", lambda e, hh=hh: e.activation(E[:nk, hh * 512:(hh + 1) * 512], z[:nk, hh * 512:(hh + 1) * 512], AF.Exp), [QhB[zi][hh]], [EbB[ei][hh]])
            else:
                S.op("act", lambda e: e.activation(E[:nk, 0:8 * nq], z[:nk, 0:8 * nq], AF.Exp), [QhB[zi][0], QhB[zi][1]], [EbB[ei][0], EbB[ei][1]])
            if diag:
                e3 = E[:nk, 0:8 * nq].rearrange("p (m q) -> p m q", m=8)
                S.op("pool", lambda e: e.tensor_tensor(e3, e3, maskA[:nk, :nq].unsqueeze(1).to_broadcast([nk, 8, nq]), ALU.mult),
                     [EbB[ei][0], EbB[ei][1], cB], [EbB[ei][0], EbB[ei][1]])
            yield
            pv = Q[2]
            for m in range(8):
                S.op("pe", lambda e, m=m: e.matmul(pv[:nq, m * 128:(m + 1) * 128], E[:nk, m * nq:(m + 1) * nq], V[:nk, m // 2, 0:128], start=True, stop=True),
                     [EbB[ei][0], EbB[ei][1], kB], [QhB[2][0], QhB[2][1]])
            for m in range(8):
                S.op("pe", lambda e, m=m: e.matmul(P1[:nq, m:m + 1], E[:nk, m * nq:(m + 1) * nq], V[:nk, m // 2, 128:129], start=True, stop=True),
                     [EbB[ei][0], EbB[ei][1], kB], [P1B])
            S.op("dve", lambda e: e.tensor_tensor(accA[j][:nq, :], accA[j][:nq, :], pv[:nq, :], ALU.add), [accAB[j], QhB[2][0], QhB[2][1]], [accAB[j]])
            S.op("dve", lambda e: e.tensor_tensor(den[:nq, j, :], den[:nq, j, :], P1[:nq, 0:8], ALU.add), [smB["den"], P1B], [smB["den"]])

        def pairB(j, nq, nk, KT, V, kB, diag):
            par = par_c["p"]; par_c["p"] = 1 - par
            vw = []
            for hh in range(2):
                vw.append(dict(zh=Q[0][:, hh * 512:hh * 512 + 4 * nq], ar=Q[1][:, hh * 512:hh * 512 + 4 * nq], pvB=Q[2][:, hh * 512:hh * 512 + 256],
                               spb=SPB[par][:, hh * 512:hh * 512 + 4 * nq], wb=WBv[par][:, hh * 512:hh * 512 + 4 * nq], hs=slice(4 * hh, 4 * hh + 4)))

            def qk(out, hh, h4, st_, sp_):
                h = 4 * hh + h4
                return lambda e: e.matmul(out[:nk, h4 * nq:(h4 + 1) * nq], KT[:, h // 2, 0:nk],
                                          QTeo[h % 2][:, h // 2, j * 128:j * 128 + nq], start=st_, stop=sp_)
            for hh in range(2):
                for h4 in range(4):
                    S.op("pe", qk(vw[hh]["zh"], hh, h4, True, True), [kB, QTB], [QhB[0][hh]])
            for hh in range(2):
                S.op("act", lambda e, hh=hh: e.activation(ef[hh][:nk, 0:4 * nq], vw[hh]["zh"][:nk, :], AF.Exp), [QhB[0][hh]], [efB[hh]])
            for hh in range(2):
                S.op("act", lambda e, hh=hh: e.activation(vw[hh]["spb"][:nk, :], ef[hh][:nk, 0:4 * nq], AF.Ln, bias=1.0), [efB[hh]], [SPBB[par][hh]])
                if diag:
                    s3 = vw[hh]["spb"][:nk, :].rearrange("p (m q) -> p m q", m=4)
                    S.op("pool", lambda e, s3=s3: e.tensor_tensor(s3, s3, maskB[:nk, :nq].unsqueeze(1).to_broadcast([nk, 4, nq]), ALU.mult),
                         [SPBB[par][hh], cB], [SPBB[par][hh]])
            yield
            for hh in range(2):
                for h4 in range(4):
                    S.op("pe", qk(vw[hh]["ar"], hh, h4, True, False), [kB, QTB], [QhB[1][hh]])
                    S.op("pe", lambda e, hh=hh, h4=h4: e.matmul(vw[hh]["ar"][:nk, h4 * nq:(h4 + 1) * nq], negU[:nk, :nk], vw[hh]["spb"][:nk, h4 * nq:(h4 + 1) * nq], start=False, stop=True),
                         [SPBB[par][hh], cB], [QhB[1][hh]])
            for hh in range(2):
                for h4 in range(4):
                    c_ = par * 8 + 4 * hh + h4
                    S.op("pe", lambda e, hh=hh, h4=h4, c_=c_: e.matmul(P1[:nq, c_:c_ + 1], vw[hh]["spb"][:nk, h4 * nq:(h4 + 1) * nq], ones[:nk, 0:1], start=True, stop=True),
                         [SPBB[par][hh], cB], [P1Bx[par]])
            for hh in range(2):
                S.op("act", lambda e, hh=hh: e.activation(vw[hh]["wb"][:nk, :], vw[hh]["ar"][:nk, :], AF.Exp), [QhB[1][hh]], [WBB[par][hh]])
                if diag:
                    w3 = vw[hh]["wb"][:nk, :].rearrange("p (m q) -> p m q", m=4)
                    S.op("pool", lambda e, w3=w3: e.tensor_tensor(w3, w3, maskB[:nk, :nq].unsqueeze(1).to_broadcast([nk, 4, nq]), ALU.mult),
                         [WBB[par][hh], cB], [WBB[par][hh]])
            yield
            for hh in range(2):
                for h4 in range(4):
                    S.op("pe", lambda e, hh=hh, h4=h4: e.matmul(vw[hh]["pvB"][:nq, h4 * 64:(h4 + 1) * 64], vw[hh]["wb"][:nk, h4 * nq:(h4 + 1) * nq], V[:nk, 4 * hh + h4, :], start=True, stop=True),
                         [WBB[par][hh], kB], [QhB[2][hh]])
            for hh in range(2):
                S.op("act", lambda e, hh=hh: e.copy(pvs[hh][:nq, :], vw[hh]["pvB"][:nq, :]), [QhB[2][hh]], [pvsB[hh]])
            for hh in range(2):
                hs = vw[hh]["hs"]
                ps_ = slice(par * 8 + 4 * hh, par * 8 + 4 * hh + 4)
                S.op("dve", lambda e, hh=hh, hs=hs: e.tensor_tensor(tmpB[hh][:nq, :].rearrange("p (h d) -> p h d", h=4), pvs[hh][:nq, :].rearrange("p (h d) -> p h d", h=4),
                                                                   ecarry[:nq, j, hs].unsqueeze(2).to_broadcast([nq, 4, 64]), ALU.mult), [pvsB[hh], smB["ecarry"]], [tmpBB[hh]])
                S.op("dve", lambda e, hh=hh: e.tensor_tensor(accB[j][:nq, hh * 256:(hh + 1) * 256], accB[j][:nq, hh * 256:(hh + 1) * 256], tmpB[hh][:nq, :], ALU.add),
                     [tmpBB[hh], accBB[j]], [accBB[j]])
                S.op("dve", lambda e, hs=hs, ps_=ps_: e.tensor_tensor(carry[:nq, j, hs], carry[:nq, j, hs], P1[:nq, ps_], ALU.subtract), [smB["carry"], P1Bx[par]], [smB["carry"]])
                S.op("act", lambda e, hs=hs: e.activation(ecarry[:nq, j, hs], carry[:nq, j, hs], AF.Exp), [smB["carry"]], [smB["ecarry"]])

        def finalize(j, nq, i):
            rden = small["rden"]
            S.op("dve", lambda e: e.tensor_scalar(rden[:nq, :], den[:nq, j, :], 1e-30, None, ALU.add), [smB["den"]], [smB["rden"]])
            S.op("dve", lambda e: e.reciprocal(rden[:nq, :], rden[:nq, :]), [smB["rden"]], [smB["rden"]])
            S.op("dve", lambda e: e.tensor_tensor(ofin[:nq, :].rearrange("p (m d) -> p m d", m=8), accA[j][:nq, :].rearrange("p (m d) -> p m d", m=8),
                                                  rden[:nq, :].unsqueeze(2).to_broadcast([nq, 8, 128]), ALU.mult), [accAB[j], smB["rden"]], [ofinB])
            o4 = ofin[:nq, :].rearrange("p (h t d) -> p h t d", h=4, t=2)
            ao3 = aof[:nq, :].rearrange("p (h d) -> p h d", h=4)
            S.op("dve", lambda e: e.scalar_tensor_tensor(ao3, o4[:, :, 1, :], small["lam"][:nq, 2:3], o4[:, :, 0, :], ALU.mult, ALU.add),
                 [ofinB, smB["lam"]], [aofB])
            s4 = small["s4"]
            S.op("dve", lambda e: e.tensor_tensor(sqf[:nq, :], aof[:nq, :], aof[:nq, :], ALU.mult), [aofB], [sqfB])
            S.op("dve", lambda e: e.tensor_reduce(s4[:nq, :], sqf[:nq, :].rearrange("p (h d) -> p h d", h=4), AX.X, ALU.add), [sqfB], [smB["s4"]])
            S.op("dve", lambda e: e.tensor_scalar(s4[:nq, :], s4[:nq, :], 1.0 / 128, EPS, ALU.mult, ALU.add), [smB["s4"]], [smB["s4"]])
            S.op("act", lambda e: e.activation(s4[:nq, :], s4[:nq, :], AF.Ln), [smB["s4"]], [smB["s4"]])
            S.op("act", lambda e: e.activation(s4[:nq, :], s4[:nq, :], AF.Exp, scale=-0.5), [smB["s4"]], [smB["s4"]])
            S.op("dve", lambda e: e.tensor_tensor(ao3, ao3, s4[:nq, :].unsqueeze(2).to_broadcast([nq, 4, 128]), ALU.mult), [aofB, smB["s4"]], [aofB])
            S.op("dve", lambda e: e.tensor_tensor(mg[:nq, 0:512].rearrange("p (h d) -> p h d", h=4), ao3, gsub[:nq, :].unsqueeze(1).to_broadcast([nq, 4, 128]), ALU.mult),
                 [aofB, gsubB], [mgB])
            S.op("pool", lambda e: e.tensor_copy(mg[:nq, 512:1024], accB[j][:nq, :]), [accBB[j]], [mgB])
            pt, ptB = getpt()
            for k in range(8):
                S.op("pe", lambda e, k=k: e.transpose(pt[:, k * 128:k * 128 + nq], mg[:nq, k * 128:(k + 1) * 128], ident[:nq, :nq]), [mgB, cB], [ptB])
            S.op("act", lambda e: e.copy(HT[:, :, i * 128:i * 128 + nq], pt.rearrange("p (k t) -> p k t", k=8)[:, :, 0:nq]), [ptB], [HTb[i]])

        groups = [list(range(g, g + 4)) for g in range(0, NQ, 4)] + [[SMP]]
        ngrp = int(os.environ.get("KNGRP", "99"))
        groups = groups[:ngrp] + ([groups[-1]] if ngrp < len(groups) else [])
        for grp in groups:
            sample = grp[0] == SMP
            nq = 16 if sample else 128
            keys = ([SMP] + [CIDX + t for t in range(7, -1, -1)]) if sample else list(range(grp[0], int(os.environ.get("KNKEY", str(NT)))))
            for pas in os.environ.get("KPASS", "AB"):
                for j, s in enumerate(grp):
                    qi = SQ if sample else s
                    src = (qta_s if pas == "A" else qtb_s)[qi].rearrange("p (k t) -> p k t", k=4)[:, :, 0:nq]
                    S.op("sp", lambda e, j=j, src=src: e.dma_start(out=QTe[0:64, :, j * 128:j * 128 + nq], in_=src[0:64]), [scrB["qta" if pas == "A" else "qtb"][qi]], [QTB], dma=True)
                    S.op("sp", lambda e, j=j, src=src: e.dma_start(out=QTo[64:128, :, j * 128:j * 128 + nq], in_=src[64:128]), [scrB["qta" if pas == "A" else "qtb"][qi]], [QTB], dma=True)
                    if pas == "A":
                        S.op("dve", lambda e, j=j: e.memset(accA[j], 0.0), [], [accAB[j]])
                    else:
                        S.op("dve", lambda e, j=j: e.memset(accB[j], 0.0), [], [accBB[j]])
                if pas == "A":
                    S.op("dve", lambda e: e.memset(den, 0.0), [], [smB["den"]])
                else:
                    S.op("dve", lambda e: e.memset(carry, 0.0), [], [smB["carry"]])
                    S.op("dve", lambda e: e.memset(ecarry, 1.0), [], [smB["ecarry"]])
                for idx, uk in enumerate(keys):
                    nk = 16 if uk == SMP else 128
                    b = idx % 2
                    KT = kvb[b][:, 0:512].rearrange("p (k t) -> p k t", k=4)
                    ksrc = (kta_s if pas == "A" else ktb_s)[uk].rearrange("p (k t) -> p k t", k=4)[:, :, 0:nk]
                    kkey, vkey = ("kta", "va") if pas == "A" else ("ktb", "vb")
                    S.op("sp", lambda e, KT=KT, ksrc=ksrc, nk=nk: e.dma_start(out=KT[:, :, 0:nk], in_=ksrc), [scrB[kkey][uk]], [kvB[b]], dma=True)
                    if pas == "A":
                        V = kvb[b][:, 512:1032].rearrange("p (h d) -> p h d", h=4)
                        S.op("sp", lambda e, b=b, uk=uk, nk=nk: e.dma_start(out=kvb[b][:nk, 512:1032], in_=va_s[uk][:nk, :]), [scrB[vkey][uk]], [kvB[b]], dma=True)
                    else:
                        V = kvb[b][:, 512:1024].rearrange("p (h d) -> p h d", h=8)
                        S.op("sp", lambda e, b=b, uk=uk, nk=nk: e.dma_start(out=kvb[b][:nk, 512:1024], in_=vb_s[uk][:nk, :]), [scrB[vkey][uk]], [kvB[b]], dma=True)
                    gens = []
                    for j, s in enumerate(grp):
                        if not sample and uk < s:
                            continue
                        diag = (uk == SMP) if sample else (uk == s)
                        gens.append((pairA if pas == "A" else pairB)(j, nq, nk, KT, V, kvB[b], diag))
                    run_pipeline(gens)
                S.barrier()
            if not os.environ.get("KNOFIN"):
                for j, s in enumerate(grp):
                    finalize(j, nq, NQ if sample else s)
            S.barrier()

    def linear_residual(wsrc, gcol, tiles, scale):
        for pc in range(4):
            ws = pc % 2
            load_w(Wb[ws][:, 0, :], WbB[ws][0], wsrc[:, pc * 256:(pc + 1) * 256], gcol, 8, 256)
            w3 = Wb[ws][:, 0, :].rearrange("p (k f) -> p k f", k=8)
            for (i, rows) in tiles:
                pp, ppB = getps()
                for k in range(8):
                    S.op("pe", lambda e, k=k: e.matmul(pp[:rows, 0:256], HT[:, k, i * 128:i * 128 + rows], w3[:, k, :], start=(k == 0), stop=(k == 7)),
                         [HTb[i], WbB[ws][0]], [ppB])
                S.op("dve", lambda e: e.scalar_tensor_tensor(X[:rows, i, pc * 256:(pc + 1) * 256], pp[:rows, 0:256], float(scale),
                                                             X[:rows, i, pc * 256:(pc + 1) * 256], ALU.mult, ALU.add), [ppB, Xb[i]], [Xb[i]])

    lim = {"stg": 3}
    gm = {"gq": lt.rearrange("p a b -> p (a b)"), "gk": sb("gm_gk", [128, 256])}; gmB = Buf()
    MKT = av(4096, 1024, BF16).rearrange("p (k t) -> p k t", k=8); MV = av(5120, 1024, BF16).rearrange("p (t f) -> p t f", t=2)
    MKTs = av(12288, 1024, BF16).rearrange("p (k t) -> p k t", k=8); MVs = av(13312, 1024, BF16).rearrange("p (t f) -> p t f", t=2)
    memB = {"k": Buf(), "v": Buf(), "ks": Buf(), "vs": Buf()}
    mw = [av(1024 + 1024 * i, 1024, BF16) for i in range(2)]; mwB = [Buf(), Buf()]
    MT = av(0, 1024, BF16).rearrange("p (k t) -> p k t", k=8); MTB = Buf()

    def head_norm(src_ps, srcB, rows, n, gtile, gtB, out_bf=None, out_bfB=None):
        a, b = nxt("wk", 4), nxt("wk", 4)
        s_ = nxt("st", 4)
        S.op("act", lambda e: e.copy(wk[a][:rows, :n], src_ps), [srcB], [wkB[a]])
        S.op("dve", lambda e: e.tensor_tensor(wk[b][:rows, :n], wk[a][:rows, :n], wk[a][:rows, :n], ALU.mult), [wkB[a]], [wkB[b]])
        S.op("dve", lambda e: e.tensor_reduce(st[s_][:rows, 0:1], wk[b][:rows, :n], AX.X, ALU.add), [wkB[b]], [stB[s_]])
        S.op("dve", lambda e: e.tensor_scalar(st[s_][:rows, 0:1], st[s_][:rows, 0:1], 1.0 / n, EPS, ALU.mult, ALU.add), [stB[s_]], [stB[s_]])
        S.op("act", lambda e: e.activation(st[s_][:rows, 0:1], st[s_][:rows, 0:1], AF.Ln), [stB[s_]], [stB[s_]])
        S.op("act", lambda e: e.activation(st[s_][:rows, 0:1], st[s_][:rows, 0:1], AF.Exp, scale=-0.5), [stB[s_]], [stB[s_]])
        S.op("dve", lambda e: e.tensor_scalar(wk[a][:rows, :n], wk[a][:rows, :n], st[s_][:rows, 0:1], None, ALU.mult), [wkB[a], stB[s_]], [wkB[a]])
        S.op("dve", lambda e: e.tensor_tensor(wk[b][:rows, :n], wk[a][:rows, :n], gtile[:rows, :n], ALU.mult), [wkB[a], gtB], [wkB[b]])
        return wk[b], wkB[b]

    def mem_prep(l):
        S.op("sp", lambda e: e.dma_start(out=gm["gq"], in_=W["mem_gq"][l:l + 1, :].partition_broadcast(128)), [], [gmB], dma=True)
        S.op("sp", lambda e: e.dma_start(out=gm["gk"], in_=W["mem_gk"][l:l + 1, :].partition_broadcast(128)), [], [gmB], dma=True)
        S.op("dve", lambda e: e.tensor_scalar(gm["gq"], gm["gq"], 1.0 / 16, None, ALU.mult), [gmB], [gmB])
        for t in range(2):
            s_ = nxt("stg", 3)
            S.op("sp", lambda e, t=t: e.dma_start(out=stg[s_][:, 0:1024], in_=memp[t * 128:(t + 1) * 128, :]), [], [stgB[s_]], dma=True)
            k_ = nxt("st", 4)
            S.op("dve", lambda e: e.memset(st[k_][:, 0:1], 0.0), [], [stB[k_]])
            S.op("act", lambda e: e.activation(junk, stg[s_][:, 0:1024], AF.Square, accum_out=st[k_][:, 0:1]), [stgB[s_]], [junkB, stB[k_]])
            S.op("dve", lambda e: e.tensor_scalar(st[k_][:, 0:1], st[k_][:, 0:1], 1.0 / D, EPS, ALU.mult, ALU.add), [stB[k_]], [stB[k_]])
            S.op("act", lambda e: e.activation(st[k_][:, 0:1], st[k_][:, 0:1], AF.Ln), [stB[k_]], [stB[k_]])
            S.op("act", lambda e: e.activation(st[k_][:, 0:1], st[k_][:, 0:1], AF.Exp, scale=-0.5), [stB[k_]], [stB[k_]])
            j = nxt("xn", 2)
            S.op("pool", lambda e: e.tensor_scalar(xn[j], stg[s_][:, 0:1024], st[k_][:, 0:1], None, ALU.mult), [stgB[s_], stB[k_]], [xnB[j]])
            pt, ptB = getpt()
            for k in range(8):
                S.op("pe", lambda e, k=k: e.transpose(pt[:, k * 128:(k + 1) * 128], xn[j][:, k * 128:(k + 1) * 128], ident), [xnB[j], cB], [ptB])
            S.op("act", lambda e, t=t: e.copy(MT[:, :, t * 128:(t + 1) * 128], pt.rearrange("p (k t) -> p k t", k=8)), [ptB], [MTB])
        gcol = (GID["mem_g_m"] + l) * 8
        for which in ("k", "v"):
            for h in range(4):
                ws = h % 2
                load_w(mw[ws], mwB[ws], W["mem_w" + which][l][:, h * 256:(h + 1) * 256], gcol, 8, 256)
                w3 = mw[ws].rearrange("p (k f) -> p k f", k=8)
                for t in range(2):
                    pp, ppB = getps()
                    for k in range(8):
                        S.op("pe", lambda e, k=k, t=t: e.matmul(pp[:, 0:256], MT[:, k, t * 128:(t + 1) * 128], w3[:, k, :], start=(k == 0), stop=(k == 7)),
                             [MTB, mwB[ws]], [ppB])
                    if which == "k":
                        fin, finB = head_norm(pp[:, 0:256], ppB, 128, 256, gm["gk"], gmB)
                        S.op("pool", lambda e, t=t, h=h: e.dma_start(out=o_mk[l][t * 128:(t + 1) * 128, h * 256:(h + 1) * 256], in_=fin[:, 0:256]), [finB], [], dma=True)
                        c = nxt("wkb", 2)
                        S.op("pool", lambda e: e.tensor_copy(wkb[c][:, 0:256], fin[:, 0:256]), [finB], [wkbB[c]])
                        pt, ptB = getpt()
                        for k in range(2):
                            S.op("pe", lambda e, k=k: e.transpose(pt[:, k * 128:(k + 1) * 128], wkb[c][:, k * 128:(k + 1) * 128], ident), [wkbB[c], cB], [ptB])
                        S.op("act", lambda e, t=t, h=h: e.copy(MKT[:, 2 * h:2 * h + 2, t * 128:(t + 1) * 128], pt[:, 0:256].rearrange("p (k t) -> p k t", k=2)), [ptB], [memB["k"]])
                    else:
                        a = nxt("wk", 4)
                        S.op("act", lambda e: e.copy(wk[a][:, :], pp[:, 0:256]), [ppB], [wkB[a]])
                        S.op("pool", lambda e, t=t, h=h: e.dma_start(out=o_mv[l][t * 128:(t + 1) * 128, h * 256:(h + 1) * 256], in_=wk[a][:, :]), [wkB[a]], [], dma=True)
                        S.op("pool", lambda e, t=t, h=h: e.tensor_copy(MV[:, t, h * 256:(h + 1) * 256], wk[a][:, :]), [wkB[a]], [memB["v"]])
        for t in range(2):
            s_ = nxt("stg", 3)
            S.op("sp", lambda e, t=t: e.dma_start(out=stg[s_][:, 0:1024], in_=cm_k[l][t * 128:(t + 1) * 128, :]), [], [stgB[s_]], dma=True)
            j = nxt("xn", 2)
            S.op("pool", lambda e: e.tensor_copy(xn[j], stg[s_][:, 0:1024]), [stgB[s_]], [xnB[j]])
            pt, ptB = getpt()
            for k in range(8):
                S.op("pe", lambda e, k=k: e.transpose(pt[:, k * 128:(k + 1) * 128], xn[j][:, k * 128:(k + 1) * 128], ident), [xnB[j], cB], [ptB])
            S.op("act", lambda e, t=t: e.copy(MKTs[:, :, t * 128:(t + 1) * 128], pt.rearrange("p (k t) -> p k t", k=8)), [ptB], [memB["ks"]])
            s_ = nxt("stg", 3)
            S.op("sp", lambda e, t=t: e.dma_start(out=stg[s_][:, 0:1024], in_=cm_v[l][t * 128:(t + 1) * 128, :]), [], [stgB[s_]], dma=True)
            S.op("pool", lambda e, t=t: e.tensor_copy(MVs[:, t, :], stg[s_][:, 0:1024]), [stgB[s_]], [memB["vs"]])

    def mem_attn(l, tiles):
        QM = av(10240, 512, BF16).rearrange("p (k t) -> p k t", k=8); QMB = Buf()
        Em = av(10752, 512, BF16); EmB = [Buf(), Buf()]
        og = av(11264, 1024); ogB = Buf()
        ogb = junk
        wq = [av(h * 1024, 1024, BF16) for h in range(4)]; wqB = [Buf() for _ in range(4)]
        rstd_tiles(tiles)
        for (i, rows) in tiles:
            norm_to_HT(i, rows)
        lim["stg"] = 2
        gcol = (GID["mem_g_x"] + l) * 8
        for h in range(4):
            load_w(wq[h], wqB[h], W["mem_wq"][l][:, h * 256:(h + 1) * 256], gcol, 8, 256)
        rr["ptn"] = 1
        zc = 0
        for (i, rows) in tiles:
            smp = (i == NQ)
            KTm, Vm, kB, vB = (MKTs, MVs, memB["ks"], memB["vs"]) if smp else (MKT, MV, memB["k"], memB["v"])
            for h in range(4):
                w3 = wq[h].rearrange("p (k f) -> p k f", k=8)
                pp, ppB = getps()
                for k in range(8):
                    S.op("pe", lambda e, k=k: e.matmul(pp[:rows, 0:256], HT[:, k, i * 128:i * 128 + rows], w3[:, k, :], start=(k == 0), stop=(k == 7)),
                         [HTb[i], wqB[h]], [ppB])
                fin, finB = head_norm(pp[:rows, 0:256], ppB, rows, 256, gm["gq"], gmB)
                c = nxt("wkb", 2)
                S.op("pool", lambda e: e.tensor_copy(wkb[c][:rows, 0:256], fin[:rows, 0:256]), [finB], [wkbB[c]])
                pt, ptB = getpt()
                for k in range(2):
                    S.op("pe", lambda e, k=k: e.transpose(pt[:, k * 128:k * 128 + rows], wkb[c][:rows, k * 128:(k + 1) * 128], ident[:rows, :rows]), [wkbB[c], cB], [ptB])
                S.op("act", lambda e, h=h: e.copy(QM[:, 2 * h:2 * h + 2, 0:rows], pt[:, 0:256].rearrange("p (k t) -> p k t", k=2)[:, :, 0:rows]), [ptB], [QMB])
            zi = zc; zc = 1 - zc
            z = Q[zi]; zB_ = [PSB[2 * zi], PSB[2 * zi + 1]]
            for h in range(4):
                for kt in range(2):
                    c0 = (h * 2 + kt) * rows
                    for cc in range(2):
                        S.op("pe", lambda e, h=h, kt=kt, cc=cc, c0=c0: e.matmul(z[:, c0:c0 + rows], KTm[:, 2 * h + cc, kt * 128:(kt + 1) * 128], QM[:, 2 * h + cc, 0:rows],
                                                                            start=(cc == 0), stop=(cc == 1)), [kB, QMB], zB_)
            if 8 * rows > 512:
                for hh in range(2):
                    S.op("act", lambda e, hh=hh: e.activation(Em[:, hh * 512:(hh + 1) * 512], z[:, hh * 512:(hh + 1) * 512], AF.Exp), [zB_[hh]], [EmB[hh]])
            else:
                S.op("act", lambda e: e.activation(Em[:, 0:8 * rows], z[:, 0:8 * rows], AF.Exp), zB_, EmB)
            pv = Q[2]; pvB_ = [PSB[4], PSB[5]]
            for h in range(4):
                for kt in range(2):
                    c0 = (h * 2 + kt) * rows
                    S.op("pe", lambda e, h=h, kt=kt, c0=c0: e.matmul(pv[:rows, h * 256:(h + 1) * 256], Em[:, c0:c0 + rows], Vm[:, kt, h * 256:(h + 1) * 256],
                                                                     start=(kt == 0), stop=(kt == 1)), EmB + [vB], [pvB_[h // 2]])
            for h in range(4):
                for kt in range(2):
                    c0 = (h * 2 + kt) * rows
                    S.op("pe", lambda e, h=h, kt=kt, c0=c0: e.matmul(P1[:rows, h:h + 1], Em[:, c0:c0 + rows], ones[:, 0:1], start=(kt == 0), stop=(kt == 1)), EmB + [cB], [P1B])
            for hh in range(2):
                S.op("act", lambda e, hh=hh: e.copy(og[:rows, hh * 512:(hh + 1) * 512], pv[:rows, hh * 512:(hh + 1) * 512]), [pvB_[hh]], [ogB])
            rden = small["rden"]
            S.op("dve", lambda e: e.reciprocal(rden[:rows, 0:4], P1[:rows, 0:4]), [P1B], [smB["rden"]])
            S.op("dve", lambda e: e.tensor_tensor(ogb[:rows, :].rearrange("p (h d) -> p h d", h=4), og[:rows, :].rearrange("p (h d) -> p h d", h=4),
                                                  rden[:rows, 0:4].unsqueeze(2).to_broadcast([rows, 4, 256]), ALU.mult), [ogB, smB["rden"]], [junkB])
            pt, ptB = getpt()
            for k in range(8):
                S.op("pe", lambda e, k=k: e.transpose(pt[:, k * 128:k * 128 + rows], ogb[:rows, k * 128:(k + 1) * 128], ident[:rows, :rows]), [junkB, cB], [ptB])
            S.op("act", lambda e: e.copy(HT[:, :, i * 128:i * 128 + rows], pt.rearrange("p (k t) -> p k t", k=8)[:, :, 0:rows]), [ptB], [HTb[i]])
        rr["ptn"] = 2
        S.barrier()
        linear_residual(W["mem_wo"][l], None, tiles, 1.0)
        lim["stg"] = 3

    NCS = NQ + 1 + 4
    kc_s = dscr("kc_s", [NCS, 128, 1024]); vc_s = dscr("vc_s", [NCS, 128, 1024]); qc_s = dscr("qc_s", [NOWN + 1, 128, 1024])
    kcB = [Buf() for _ in range(NCS)]; vcB = [Buf() for _ in range(NCS)]; qcB = [Buf() for _ in range(NOWN + 1)]
    tabx_t = nc.dram_tensor("tabx", [16, 513], F32)
    tabx = tabx_t.ap(); tabxB = Buf()
    negb = sb("negb", [128, 16]); negbB = Buf()
    mask4b = sb("mask4b", [128, 128], BF16)
    S.op("dve", lambda e: e.tensor_copy(mask4b, cst[:, 648:776]), [cstB], [cB])

    def norm4(src, ppB, rows, g):
        a, b = nxt("wk", 4), nxt("wk", 4)
        s_ = nxt("st", 4)
        S.op("act", lambda e: e.copy(wk[b][:rows, :], src), [ppB], [wkB[b]])
        S.op("dve", lambda e: e.tensor_tensor(wk[a][:rows, :], wk[b][:rows, :], wk[b][:rows, :], ALU.mult), [wkB[b]], [wkB[a]])
        S.op("dve", lambda e: e.tensor_reduce(st[s_][:rows, 0:4], wk[a][:rows, :].rearrange("p (h d) -> p h d", h=4), AX.X, ALU.add), [wkB[a]], [stB[s_]])
        S.op("dve", lambda e: e.tensor_scalar(st[s_][:rows, 0:4], st[s_][:rows, 0:4], 1.0 / 64, EPS, ALU.mult, ALU.add), [stB[s_]], [stB[s_]])
        S.op("act", lambda e: e.activation(st[s_][:rows, 0:4], st[s_][:rows, 0:4], AF.Ln), [stB[s_]], [stB[s_]])
        S.op("act", lambda e: e.activation(st[s_][:rows, 0:4], st[s_][:rows, 0:4], AF.Exp, scale=-0.5), [stB[s_]], [stB[s_]])
        w3a = wk[a][:rows, :].rearrange("p (h d) -> p h d", h=4)
        w3b = wk[b][:rows, :].rearrange("p (h d) -> p h d", h=4)
        S.op("dve", lambda e: e.tensor_tensor(w3a, w3b, st[s_][:rows, 0:4].unsqueeze(2).to_broadcast([rows, 4, 64]), ALU.mult), [wkB[b], stB[s_]], [wkB[a]])
        S.op("dve", lambda e: e.tensor_tensor(w3b, w3a, g[:rows, :].unsqueeze(1).to_broadcast([rows, 4, 64]), ALU.mult), [wkB[a], gB], [wkB[b]])
        return wk[b], wkB[b]

    def proj_c(tiles, us):
        gcol = (GID["mix_g"] + 1) * 8
        for pc in range(12):
            ws = pc % 2
            kind, hp = ("cq", "ck", "cv")[pc // 4], pc % 4
            load_w(Wb[ws][:, 0, :], WbB[ws][0], W["c_w_in"][0][:, pc * 256:(pc + 1) * 256], gcol, 8, 256)
            w3 = Wb[ws][:, 0, :].rearrange("p (k f) -> p k f", k=8)
            csl = slice(hp * 256, (hp + 1) * 256)
            for (i, rows), u in zip(tiles, us):
                smp = (u == SMP)
                if kind == "cq" and not (u < NOWN or smp):
                    continue
                ui = NQ if smp else u
                qi = NOWN if smp else u
                pp, ppB = getps()
                for k in range(8):
                    S.op("pe", lambda e, k=k: e.matmul(pp[:rows, 0:256], HT[:, k, i * 128:i * 128 + rows], w3[:, k, :], start=(k == 0), stop=(k == 7)),
                         [HTb[i], WbB[ws][0]], [ppB])
                src = pp[:rows, 0:256]
                if kind in ("cq", "ck"):
                    fin, finB = norm4(src, ppB, rows, g64["c_gq" if kind == "cq" else "c_gk"])
                    if kind == "ck" and (u < 4 or smp):
                        dst = o_cks[496:512, csl] if smp else o_ck[u][:, csl]
                        S.op("pool", lambda e, dst=dst: e.dma_start(out=dst[:rows], in_=fin[:rows, :]), [finB], [], dma=True)
                    c = nxt("wkb", 2)
                    S.op("pool", lambda e: e.tensor_copy(wkb[c][:rows, 0:256], fin[:rows, :]), [finB], [wkbB[c]])
                    if kind == "ck":
                        to_T_and_store(wkb[c], wkbB[c], rows, kc_s[ui].rearrange("p (k t) -> p k t", k=8)[:, hp * 2:hp * 2 + 2, 0:rows], kcB[ui])
                    else:
                        to_T_and_store(wkb[c], wkbB[c], rows, qc_s[qi].rearrange("p (k t) -> p k t", k=8)[:, hp * 2:hp * 2 + 2, 0:rows], qcB[qi])
                else:
                    a = nxt("wk", 4)
                    S.op("act", lambda e: e.copy(wk[a][:rows, :], src), [ppB], [wkB[a]])
                    if u < 4 or smp:
                        dst = o_cvs[496:512, csl] if smp else o_cv[u][:, csl]
                        S.op("pool", lambda e, dst=dst: e.dma_start(out=dst[:rows], in_=wk[a][:rows, :]), [wkB[a]], [], dma=True)
                    c = nxt("wkb", 2)
                    S.op("dve", lambda e: e.tensor_scalar(wkb[c][:rows, 0:256], wk[a][:rows, :], valid[:rows, u:u + 1], None, ALU.mult), [wkB[a], cB], [wkbB[c]])
                    S.op("pool", lambda e: e.dma_start(out=vc_s[ui][:rows, csl], in_=wkb[c][:rows, 0:256]), [wkbB[c]], [vcB[ui]], dma=True)

    def cache_prep_c():
        S.op("sp", lambda e: e.dma_start(out=o_cks[0:496, :], in_=cc_k[16:512, :]), [], [], dma=True)
        S.op("sp", lambda e: e.dma_start(out=o_cvs[0:496, :], in_=cc_v[16:512, :]), [], [], dma=True)
        for t in range(4):
            ci = NQ + 1 + t
            s_ = nxt("stg", 3)
            S.op("sp", lambda e, t=t: e.dma_start(out=stg[s_][:, 0:1024], in_=cc_k[t * 128:(t + 1) * 128, :]), [], [stgB[s_]], dma=True)
            S.op("pool", lambda e: e.tensor_copy(xn[0], stg[s_][:, 0:1024]), [stgB[s_]], [xnB[0]])
            pt, ptB = getpt()
            for k in range(8):
                S.op("pe", lambda e, k=k: e.transpose(pt[:, k * 128:(k + 1) * 128], xn[0][:, k * 128:(k + 1) * 128], ident), [xnB[0], cB], [ptB])
            S.op("act", lambda e: e.copy(xn[1], pt), [ptB], [xnB[1]])
            S.op("pool", lambda e, ci=ci: e.dma_start(out=kc_s[ci], in_=xn[1]), [xnB[1]], [kcB[ci]], dma=True)
            s_ = nxt("stg", 3)
            S.op("sp", lambda e, t=t: e.dma_start(out=stg[s_][:, 0:1024], in_=cc_v[t * 128:(t + 1) * 128, :]), [], [stgB[s_]], dma=True)
            S.op("pool", lambda e: e.tensor_copy(xn[0], stg[s_][:, 0:1024]), [stgB[s_]], [xnB[0]])
            S.op("pool", lambda e, ci=ci: e.dma_start(out=vc_s[ci], in_=xn[0]), [xnB[0]], [vcB[ci]], dma=True)

    def band_attn(tiles):
        EB = [av(1024 * d, 1024, BF16).rearrange("p (h q) -> p h q", h=16) for d in range(2)]; EBB = Buf()
        QTe = av(2048, 256, BF16).rearrange("p (k t) -> p k t", k=4); QTo = av(2304, 256, BF16).rearrange("p (k t) -> p k t", k=4); QTB = Buf()
        QTeo = (QTe, QTo)
        kvc = [av(2560 + 512 * i, 512, BF16) for i in range(5)]; kvcB = [Buf() for _ in range(5)]
        Ec = [av(5120 + 512 * i, 512, BF16) for i in range(5)]; EcB = [[Buf(), Buf()] for _ in range(5)]
        ogc = av(7680, 512); ogcB = Buf()
        mgc = av(8192, 512, BF16); mgcB = Buf()
        S.op("sp", lambda e: e.dma_start(out=tabx[:, 0:257], in_=W["c_bias"][0]), [], [tabxB], dma=True)
        S.op("sp", lambda e: e.dma_start(out=stg[2][0:16, 256:257], in_=W["c_bias"][0][:, 256:257], allow_slow_non_contiguous=True), [], [stgB[2]], dma=True)
        S.op("dve", lambda e: e.tensor_copy(stg[2][0:16, 0:256], stg[2][0:16, 256:257].to_broadcast([16, 256])), [stgB[2]], [stgB[2]])
        S.op("sp", lambda e: e.dma_start(out=tabx[:, 257:513], in_=stg[2][0:16, 0:256]), [stgB[2]], [tabxB], dma=True)
        for h in range(16):
            S.op("sp", lambda e, h=h: e.dma_start(out=negb[:, h:h + 1], in_=tabx[h:h + 1, 300:301].partition_broadcast(128)), [tabxB], [negbB], dma=True)
        S.op("dve", lambda e: e.tensor_scalar(negb, negb, -1.0, None, ALU.mult), [negbB], [negbB])
        for h in range(16):
            s_ = nxt("stg", 2)
            for d in range(2):
                S.op("sp", lambda e, h=h, d=d: e.dma_start(out=stg[s_][:, d * 128:(d + 1) * 128], in_=bass.AP(tabx_t, h * 513 + 1 + 128 * d, [[1, 128], [1, 128]])),
                     [tabxB], [stgB[s_]], dma=True)
            pp, ppB = getps()
            S.op("pe", lambda e: e.matmul(pp[:, 0:256], cst[:, 520:648], stg[s_][:, 0:256], start=True, stop=True), [stgB[s_], cstB], [ppB])
            for d in range(2):
                S.op("act", lambda e, h=h, d=d: e.activation(EB[d][:, h, :], pp[:, d * 128:(d + 1) * 128], AF.Exp, bias=negb[:, h:h + 1]), [ppB, negbB], [EBB])
        S.op("pool", lambda e: e.tensor_tensor(EB[0], EB[0], maskA.unsqueeze(1).to_broadcast([128, 16, 128]), ALU.mult), [EBB, cB], [EBB])
        S.barrier()
        S.op("pool", lambda e: e.memset(QTe[64:128, :, :], 0.0), [], [QTB])
        S.op("pool", lambda e: e.memset(QTo[0:64, :, :], 0.0), [], [QTB])
        zc = {"z": 0, "b": 0}
        for (i, nq) in tiles:
            smp = (i == NQ)
            qi = NOWN if smp else i
            if smp:
                keyspec = [(NQ, 0)] + [(NQ + 1 + t, 1 if t == 3 else 2) for t in (3, 2, 1, 0)]
            else:
                keyspec = [(i + d, typ) for d, typ in zip(range(5), (0, 1, 2, 2, 4))]
            for half in range(2):
                src = qc_s[qi].rearrange("p (k t) -> p k t", k=8)[:, 4 * half:4 * half + 4, 0:nq]
                S.op("sp", lambda e, src=src: e.dma_start(out=QTe[0:64, :, 0:nq], in_=src[0:64]), [qcB[qi]], [QTB], dma=True)
                S.op("sp", lambda e, src=src: e.dma_start(out=QTo[64:128, :, 0:nq], in_=src[64:128]), [qcB[qi]], [QTB], dma=True)
                nkeys = len(keyspec)
                for d, (ui, typ) in enumerate(keyspec):
                    nk = 16 if (smp and d == 0) else 128
                    b = d
                    Kh = kvc[b][:, 0:512].rearrange("p (k t) -> p k t", k=4)
                    Vh = kvc[b][:, 512:1024]
                    ks = kc_s[ui].rearrange("p (k t) -> p k t", k=8)[:, 4 * half:4 * half + 4, 0:nk]
                    S.op("sp", lambda e, Kh=Kh, ks=ks, nk=nk: e.dma_start(out=Kh[:, :, 0:nk], in_=ks), [kcB[ui]], [kvcB[b]], dma=True)
                    S.op("sp", lambda e, Vh=Vh, ui=ui, nk=nk: e.dma_start(out=Vh[:nk, :], in_=vc_s[ui][:nk, half * 512:(half + 1) * 512]), [vcB[ui]], [kvcB[b]], dma=True)
                    zi = zc["z"]; zc["z"] = 1 - zi
                    z = Q[zi]; zB_ = [PSB[2 * zi], PSB[2 * zi + 1]]
                    for m in range(8):
                        S.op("pe", lambda e, m=m: e.matmul(z[:nk, m * nq:(m + 1) * nq], Kh[:, m // 2, 0:nk], QTeo[m % 2][:, m // 2, 0:nq], start=True, stop=True),
                             [kvcB[b], QTB], zB_)
                    E = Ec[d]
                    if 8 * nq > 512:
                        for hh in range(2):
                            S.op("act", lambda e, hh=hh, E=E: e.activation(E[:nk, hh * 512:(hh + 1) * 512], z[:nk, hh * 512:(hh + 1) * 512], AF.Exp), [zB_[hh]], [EcB[d][hh]])
                    else:
                        S.op("act", lambda e, E=E: e.activation(E[:nk, 0:8 * nq], z[:nk, 0:8 * nq], AF.Exp), zB_, EcB[d])
                    e3 = E[:nk, 0:8 * nq].rearrange("p (m q) -> p m q", m=8)
                    if typ in (0, 1):
                        S.op("pool", lambda e, typ=typ, e3=e3, nk=nk: e.tensor_tensor(e3, e3, EB[typ][:nk, 8 * half:8 * half + 8, 0:nq], ALU.mult), EcB[d] + [EBB], EcB[d])
                    elif typ == 4:
                        S.op("pool", lambda e, e3=e3, nk=nk: e.tensor_tensor(e3, e3, mask4b[:nk, :nq].unsqueeze(1).to_broadcast([nk, 8, nq]), ALU.mult), EcB[d] + [cB], EcB[d])
                for m in range(8):
                    for d, (ui, typ) in enumerate(keyspec):
                        nk = 16 if (smp and d == 0) else 128
                        S.op("pe", lambda e, m=m, d=d, nk=nk: e.matmul(Q[2][:nq, m * 64:(m + 1) * 64], Ec[d][:nk, m * nq:(m + 1) * nq], kvc[d][:nk, 512 + m * 64:512 + (m + 1) * 64],
                                                                       start=(d == 0), stop=(d == nkeys - 1)), EcB[d] + [kvcB[d]], [PSB[4]])
                for m in range(8):
                    for d, (ui, typ) in enumerate(keyspec):
                        nk = 16 if (smp and d == 0) else 128
                        if smp:
                            vcol = validb[:nk, SMP:SMP + 1] if d == 0 else ones[:nk, 0:1]
                        else:
                            vcol = validb[:nk, ui:ui + 1]
                        S.op("pe", lambda e, m=m, d=d, nk=nk, vcol=vcol: e.matmul(P1[:nq, m:m + 1], Ec[d][:nk, m * nq:(m + 1) * nq], vcol, start=(d == 0), stop=(d == nkeys - 1)),
                             EcB[d] + [cB], [P1B])
                S.op("act", lambda e: e.copy(ogc[:nq, :], Q[2][:nq, 0:512]), [PSB[4]], [ogcB])
                rden = small["rden"]
                S.op("dve", lambda e: e.tensor_scalar(rden[:nq, :], P1[:nq, 0:8], 1e-30, None, ALU.add), [P1B], [smB["rden"]])
                S.op("dve", lambda e: e.reciprocal(rden[:nq, :], rden[:nq, :]), [smB["rden"]], [smB["rden"]])
                S.op("dve", lambda e: e.tensor_tensor(mgc[:nq, half * 512:(half + 1) * 512].rearrange("p (h d) -> p h d", h=8), ogc[:nq, :].rearrange("p (h d) -> p h d", h=8),
                                                      rden[:nq, :].unsqueeze(2).to_broadcast([nq, 8, 64]), ALU.mult), [ogcB, smB["rden"]], [mgcB])
            pt, ptB = getpt()
            for k in range(8):
                S.op("pe", lambda e, k=k: e.transpose(pt[:, k * 128:k * 128 + nq], mgc[:nq, k * 128:(k + 1) * 128], ident[:nq, :nq]), [mgcB, cB], [ptB])
            S.op("act", lambda e: e.copy(HT[:, :, i * 128:i * 128 + nq], pt.rearrange("p (k t) -> p k t", k=8)[:, :, 0:nq]), [ptB], [HTb[i]])

    nsb = int(os.environ.get("KNSB", "99"))
    older = [list(range(a, min(a + 16, NT))) for a in range(NQ, NT, 16)]
    if STAGE >= 2:
        cache_prep()
        lambda_prep()
    for sbk in older[:nsb]:
        front0(sbk, False)
    tiles, uu = front0(list(range(NQ)), True)

    if STAGE >= 2:
        S.barrier()
        rr["ptn"] = 1
        if not os.environ.get("KNOATT"):
            attn_l0()
        S.barrier()
        rr["ptn"] = 2
        linear_residual(W["ab_w_out"][0], None, tiles, 1.0)
    if STAGE >= 3:
        S.barrier()
        mem_prep(0)
        S.barrier()
        mem_attn(0, tiles)
    if STAGE >= 4:
        S.barrier()
        ffn(0, 2, tiles)

    tiles17 = [(i, 128) for i in range(NOWN)] + [(NQ, 16)]
    if STAGE >= 5:
        S.barrier()
        ffn(1, 1, tiles)
    if STAGE >= 6:
        rstd_tiles(tiles)
        for (i, rows) in tiles:
            norm_to_HT(i, rows)
        cache_prep_c()
        proj_c(tiles, uu)
        S.barrier()
        rr["ptn"] = 1
        band_attn(tiles17)
        rr["ptn"] = 2
        S.barrier()
        linear_residual(W["c_w_out"][0], None, tiles17, 1.0)
    if STAGE >= 7:
        S.barrier()
        mem_prep(1)
        S.barrier()
        mem_attn(1, tiles17)
    if STAGE >= 8:
        S.barrier()
        ffn(1, 2, tiles17)
    if True:
        for i in range(NOWN):
            S.op("pool", lambda e, i=i: e.dma_start(out=o_y[i], in_=X[:, i, :]), [Xb[i]], [], dma=True)
        S.op("pool", lambda e: e.dma_start(out=o_ys, in_=X[0:16, NQ, :]), [Xb[NQ]], [], dma=True)

    S.barrier()
    print("ops", S.nops, "sems", S.nsem, "sbuf_left", nc.sbuf_bytes_remaining)
    return nc


_NC = None


def _consts():
    c = np.zeros((128, 776), np.float32)
    c[:, 520:648] = np.eye(128)[::-1]
    c[:, 648:776] = ((np.arange(128)[None, :] // 64) <= (np.arange(128)[:, None] // 64)).astype(np.float32)
    c[:, 512:520] = ((500000.0 ** (-2.0 * np.arange(8) / 16.0)).astype(np.float32).astype(np.float64) / (2 * np.pi))[None]
    c[:, 0:128] = np.eye(128)
    j = np.arange(128)[:, None]; s = np.arange(128)[None, :]
    c[:, 128:256] = -(j >= s).astype(np.float32)
    c[:, 256:384] = ((j // 64) <= (s // 64)).astype(np.float32)
    c[:, 384:512] = (j < s).astype(np.float32)
    return c


def kernel(**inp):
    global _NC
    if _NC is None:
        _NC = build()
    nc = _NC
    f = lambda a: np.ascontiguousarray(np.asarray(a, dtype=np.float32))
    xprompt = f(inp["x_prompt"])[0].reshape(128, 128, D)
    in_maps = []
    wnames = ["ffn1_g", "ffn1_wg", "ffn1_wu", "ffn1_wd", "ffn2_g", "ffn2_wg", "ffn2_wu", "ffn2_wd", "mix_g", "ab_w_in", "ab_w_out",
              "a_gq", "a_gk", "a_lq1", "a_lk1", "a_lq2", "a_lk2", "a_subln_g", "c_w_in", "c_w_out", "c_gq", "c_gk", "c_bias",
              "mem_g_x", "mem_g_m", "mem_wq", "mem_wk", "mem_wv", "mem_wo", "mem_gq", "mem_gk"]
    wts = {n: f(inp[n]) for n in wnames}
    cst = _consts()
    for c in range(8):
        xpc = np.zeros((NT, 128, D), np.float32)
        pos = np.zeros((128, NT + 1), np.float32)
        val = np.zeros((128, NT + 1), np.float32)
        for u in range(NT):
            g = 16 * c + 15 - u
            if g >= 0:
                xpc[u] = xprompt[g]
                pos[:, u] = g * 128 + np.arange(128)
                val[:, u] = 1.0
        pos[:16, NT] = 1024 + np.arange(16)
        val[:16, NT] = 1.0
        m = {"xp": xpc, "xs": f(inp["x_sample"])[c], "pos": pos, "valid": val, "consts": cst,
             "ca_k": f(inp["cache_a_k"])[0, c].reshape(1024, 512), "ca_v": f(inp["cache_a_v"])[0, c].reshape(1024, 512),
             "cb_k": f(inp["cache_b_k"])[0, c].reshape(1024, 512), "cb_v": f(inp["cache_b_v"])[0, c].reshape(1024, 512),
             "cc_k": f(inp["cache_c_k"])[0, c].reshape(512, 1024), "cc_v": f(inp["cache_c_v"])[0, c].reshape(512, 1024),
             "cm_k": f(inp["cache_mem_k"])[:, c].reshape(2, 256, 1024), "cm_v": f(inp["cache_mem_v"])[:, c].reshape(2, 256, 1024),
             "memp": f(inp["mem_prompt"])[0]}
        m.update(wts)
        in_maps.append(m)
    res = run_bass_kernel_spmd(nc, in_maps, core_ids=list(range(8))).results

    def gat(name, width):
        out = np.zeros((128, 128, width), np.float32)
        for c in range(8):
            for u in range(NOWN):
                out[16 * c + 15 - u] = res[c][name][u]
        return out.reshape(16384, width)
    y_prompt = gat("o_y", D)[None]
    y_sample = np.stack([res[c]["o_ys"] for c in range(8)])
    a_k_p = gat("o_ak", 512).reshape(1, 1, 16384, 8, 64)
    a_v_p = gat("o_av", 512).reshape(1, 1, 16384, 4, 128)
    b_k_p = gat("o_bk", 512).reshape(1, 1, 16384, 8, 64)
    b_v_p = gat("o_bv", 512).reshape(1, 1, 16384, 8, 64)
    c_k_p = np.concatenate([res[7]["o_ck"][3 - t] for t in range(4)], 0).reshape(1, 1, 512, 16, 64)
    c_v_p = np.concatenate([res[7]["o_cv"][3 - t] for t in range(4)], 0).reshape(1, 1, 512, 16, 64)
    mem_k_p = res[0]["o_mk"].reshape(2, 1, 256, 4, 256)
    mem_v_p = res[0]["o_mv"].reshape(2, 1, 256, 4, 256)
    st = lambda n, shp: np.stack([res[c][n] for c in range(8)]).reshape(shp)
    a_k_s = st("o_aks", (1, 8, 16, 8, 64)); a_v_s = st("o_avs", (1, 8, 16, 4, 128))
    b_k_s = st("o_bks", (1, 8, 16, 8, 64)); b_v_s = st("o_bvs", (1, 8, 16, 8, 64))
    c_k_s = st("o_cks", (1, 8, 512, 16, 64)); c_v_s = st("o_cvs", (1, 8, 512, 16, 64))
    return (y_prompt, y_sample, a_k_p, a_v_p, b_k_p, b_v_p, c_k_p, c_v_p, mem_k_p, mem_v_p,
            a_k_s, a_v_s, b_k_s, b_v_s, c_k_s, c_v_s)
```

```python
import os
import math
import numpy as np
import concourse.bass as bass
import concourse.mybir as mybir
from concourse.bass_utils import run_bass_kernel_spmd

F32, BF16 = mybir.dt.float32, mybir.dt.bfloat16
ALU = mybir.AluOpType
AF = mybir.ActivationFunctionType
AX = mybir.AxisListType

D = 1024
FF = 2816
NT = 128
NQ = 20
NOWN = 16
SMP = 128
EPS = 1e-6
NDMA = 24
ROT = 30000
STAGE = int(os.environ.get("KSTAGE", "9"))


class Buf:
    __slots__ = ("w", "r")

    def __init__(self):
        self.w = None
        self.r = {}


class Sched:
    def __init__(self, nc):
        self.nc = nc
        self.E = {"pe": nc.tensor, "act": nc.scalar, "dve": nc.vector, "pool": nc.gpsimd, "sp": nc.sync}
        self.sem, self.cnt, self.nsem = {}, {}, 0
        self.seen = {e: {} for e in self.E}
        self.allsems = []
        for e in self.E:
            self._rot(e)
        self.dsem = [nc.alloc_semaphore(f"dq{i}") for i in range(NDMA)]
        self.dcnt = [0] * NDMA
        self.dnext = 0
        self.nops = 0

    def _rot(self, e):
        self.sem[e] = self.nc.alloc_semaphore(f"s{e}{self.nsem}")
        self.nsem += 1
        self.cnt[e] = 0
        self.allsems.append([self.sem[e], 0])

    def _wait(self, e, tok):
        sem, val = tok[1], tok[2]
        if self.seen[e].get(sem.num, 0) >= val:
            return
        self.E[e].wait_ge(sem, val)
        self.seen[e][sem.num] = val

    def op(self, e, fn, reads=(), writes=(), dma=False):
        for b in reads:
            if b.w is not None:
                t = b.w
                if not (t[0] == e and e == "pe" and not t[3]):
                    self._wait(e, t)
        for b in writes:
            for t in ([b.w] if b.w is not None else []) + list(b.r.values()):
                if t[0] == e and not t[3]:
                    continue
                self._wait(e, t)
        if dma:
            i = self.dnext
            self.dnext = (i + 1) % NDMA
            if self.dcnt[i] > 0:
                self._wait(e, (e, self.dsem[i], 16 * self.dcnt[i], True))
            ins = fn(self.E[e])
            self.dcnt[i] += 1
            ins.then_inc(self.dsem[i], 16)
            tok = (e, self.dsem[i], 16 * self.dcnt[i], True)
        else:
            if self.cnt[e] >= ROT:
                self._rot(e)
            ins = fn(self.E[e])
            self.cnt[e] += 1
            ins.then_inc(self.sem[e], 1)
            tok = (e, self.sem[e], self.cnt[e], False)
            for s in self.allsems:
                if s[0] is self.sem[e]:
                    s[1] = self.cnt[e]
        for b in reads:
            b.r[tok[1].num] = tok
        for b in writes:
            b.w = tok
            b.r = {}
        self.nops += 1
        return tok

    def barrier(self):
        for e in self.E:
            for s, v in self.allsems:
                if v > 0:
                    self._wait(e, (None, s, v, False))
            for i in range(NDMA):
                if self.dcnt[i] > 0:
                    self._wait(e, (None, self.dsem[i], 16 * self.dcnt[i], True))


def build():
    nc = bass.Bass("TRN2", target_bir_lowering=False)
    S = Sched(nc)

    def din(name, shape):
        return nc.dram_tensor(name, list(shape), F32, kind="ExternalInput").ap()

    def dout(name, shape):
        return nc.dram_tensor(name, list(shape), F32, kind="ExternalOutput").ap()

    def sb(name, shape, dt=F32):
        return nc.alloc_sbuf_tensor("sb_" + name, list(shape), dt).ap()

    xp = din("xp", [NT, 128, D])
    xs = din("xs", [16, D])
    posd = din("pos", [128, NT + 1])
    validd = din("valid", [128, NT + 1])
    constd = din("consts", [128, 776])
    ca_k = din("ca_k", [1024, 512]); ca_v = din("ca_v", [1024, 512])
    cb_k = din("cb_k", [1024, 512]); cb_v = din("cb_v", [1024, 512])
    cc_k = din("cc_k", [512, 1024]); cc_v = din("cc_v", [512, 1024])
    cm_k = din("cm_k", [2, 256, 1024]); cm_v = din("cm_v", [2, 256, 1024])
    memp = din("memp", [256, 1024])
    W = {}
    for nm, shp in [("ffn1_g", [2, D]), ("ffn1_wg", [2, D, FF]), ("ffn1_wu", [2, D, FF]), ("ffn1_wd", [2, FF, D]),
                    ("ffn2_g", [2, D]), ("ffn2_wg", [2, D, FF]), ("ffn2_wu", [2, D, FF]), ("ffn2_wd", [2, FF, D]),
                    ("mix_g", [2, D]), ("ab_w_in", [1, D, 3072]), ("ab_w_out", [1, D, D]),
                    ("a_gq", [1, 64]), ("a_gk", [1, 64]), ("a_lq1", [1, 64]), ("a_lk1", [1, 64]),
                    ("a_lq2", [1, 64]), ("a_lk2", [1, 64]), ("a_subln_g", [1, 128]),
                    ("c_w_in", [1, D, 3072]), ("c_w_out", [1, D, D]), ("c_gq", [1, 64]), ("c_gk", [1, 64]),
                    ("c_bias", [1, 16, 257]), ("mem_g_x", [2, D]), ("mem_g_m", [2, D]),
                    ("mem_wq", [2, D, D]), ("mem_wk", [2, D, D]), ("mem_wv", [2, D, D]), ("mem_wo", [2, D, D]),
                    ("mem_gq", [2, 256]), ("mem_gk", [2, 256])]:
        W[nm] = din(nm, shp)

    o_y = dout("o_y", [NOWN, 128, D]); o_ys = dout("o_ys", [16, D])
    o_ak = dout("o_ak", [NOWN, 128, 512]); o_av = dout("o_av", [NOWN, 128, 512])
    o_bk = dout("o_bk", [NOWN, 128, 512]); o_bv = dout("o_bv", [NOWN, 128, 512])
    o_ck = dout("o_ck", [4, 128, D]); o_cv = dout("o_cv", [4, 128, D])
    o_mk = dout("o_mk", [2, 256, D]); o_mv = dout("o_mv", [2, 256, D])
    o_aks = dout("o_aks", [16, 512]); o_avs = dout("o_avs", [16, 512])
    o_bks = dout("o_bks", [16, 512]); o_bvs = dout("o_bvs", [16, 512])
    o_cks = dout("o_cks", [512, D]); o_cvs = dout("o_cvs", [512, D])

    def dscr(name, shape):
        return nc.dram_tensor(name, list(shape), BF16).ap()
    kta_s = dscr("kta_s", [NT + 9, 128, 512]); ktb_s = dscr("ktb_s", [NT + 9, 128, 512])
    va_s = dscr("va_s", [NT + 9, 128, 520]); vb_s = dscr("vb_s", [NT + 9, 128, 512])
    qta_s = dscr("qta_s", [NQ + 1, 128, 512]); qtb_s = dscr("qtb_s", [NQ + 1, 128, 512])
    scrB = {"kta": [Buf() for _ in range(NT + 9)], "ktb": [Buf() for _ in range(NT + 9)],
            "va": [Buf() for _ in range(NT + 9)], "vb": [Buf() for _ in range(NT + 9)],
            "qta": [Buf() for _ in range(NQ + 1)], "qtb": [Buf() for _ in range(NQ + 1)]}
    outB = Buf()
    inB = Buf()

    NX = NQ + 1
    X = sb("X", [128, NX, D])
    Xb = [Buf() for _ in range(NX)]
    HT = sb("HT", [128, 8, NQ * 128 + 16], BF16)
    HTb = [Buf() for _ in range(NX)]
    cst = sb("cst", [128, 776]); cstB = Buf()
    ident = sb("ident", [128, 128], BF16); negU = sb("negU", [128, 128], BF16)
    maskA = sb("maskA", [128, 128], BF16); maskB = sb("maskB", [128, 128], BF16)
    ones = sb("ones", [128, 1], BF16)
    cB = Buf()
    pos = sb("pos", [128, NT + 1]); valid = sb("valid", [128, NT + 1]); validb = sb("validb", [128, NT + 1], BF16)
    gT = sb("gT", [128, 80]); gTB = Buf()
    g64 = {n: sb("g_" + n, [128, 64]) for n in ("a_gq", "a_gk", "c_gq", "c_gk")}
    gB = Buf()

    S.op("sp", lambda e: e.dma_start(out=cst, in_=constd), [], [cstB], dma=True)
    S.op("sp", lambda e: e.dma_start(out=pos, in_=posd), [], [cB], dma=True)
    S.op("sp", lambda e: e.dma_start(out=valid, in_=validd), [], [cB], dma=True)
    S.op("dve", lambda e: e.tensor_copy(ident, cst[:, 0:128]), [cstB], [cB])
    S.op("dve", lambda e: e.tensor_copy(negU, cst[:, 128:256]), [cstB], [cB])
    S.op("dve", lambda e: e.tensor_copy(maskA, cst[:, 256:384]), [cstB], [cB])
    S.op("dve", lambda e: e.tensor_copy(maskB, cst[:, 384:512]), [cstB], [cB])
    S.op("dve", lambda e: e.memset(ones, 1.0), [], [cB])
    S.op("dve", lambda e: e.tensor_copy(validb, valid), [cB], [cB])
    for n in g64:
        S.op("sp", lambda e, n=n: e.dma_start(out=g64[n], in_=W[n][0:1, :].partition_broadcast(128)), [], [gB], dma=True)
    for n in ("a_gq", "c_gq"):
        S.op("dve", lambda e, n=n: e.tensor_scalar(g64[n], g64[n], 0.125, None, ALU.mult), [gB], [gB])

    Q = [nc.alloc_psum_tensor(f"q{i}", [128, 1024], F32).ap() for i in range(4)]
    PS = [Q[i // 2][:, (i % 2) * 512:(i % 2 + 1) * 512] for i in range(6)]
    PSB = [Buf() for _ in range(6)]
    PT = [Q[3][:, h * 512:(h + 1) * 512].bitcast(BF16) for h in range(2)]
    PTB = [Buf() for _ in range(2)]
    P1 = Q[3][:, 0:512]
    P1B = PTB[0]
    rr = {"ps": 0, "pt": 0, "ptn": 2}

    def getps():
        i = rr["ps"]; rr["ps"] = (i + 1) % 6
        return PS[i], PSB[i]

    def getpt():
        if rr["ptn"] == 1:
            return PT[1], PTB[1]
        i = rr["pt"]; rr["pt"] = (i + 1) % 2
        return PT[i], PTB[i]

    GID = {"ffn1_g": 0, "ffn2_g": 2, "mix_g": 4, "mem_g_x": 6, "mem_g_m": 8}
    graw = sb("graw", [80, 128]); grawB = Buf()
    for n, gi in GID.items():
        S.op("sp", lambda e, n=n, gi=gi: e.dma_start(out=graw[gi * 8:(gi + 2) * 8, :],
                                                     in_=W[n].rearrange("l (k p) -> (l k) p", p=128)), [], [grawB], dma=True)
    identf = cst[:, 0:128]
    pg, pgB = getps()
    S.op("pe", lambda e: e.transpose(pg[:, 0:80], graw[0:80, :], identf[0:80, 0:80]), [grawB, cstB], [pgB])
    S.op("dve", lambda e: e.tensor_copy(gT, pg[:, 0:80]), [pgB], [gTB])

    ARENA = 15360
    arena = sb("arena", [128, ARENA])

    def av(off, n, dt=F32):
        v = arena[:, off:off + n]
        return v if dt == F32 else v.bitcast(dt)
    Wb = [av(3072 * i, 3072, BF16).rearrange("p (a f) -> p a f", a=3) for i in range(2)]
    WbB = [[Buf() for _ in range(3)] for _ in range(2)]
    stg = [av(6144 + 2048 * i, 2048) for i in range(3)]
    stgB = [Buf() for _ in range(3)]
    hid = [av(12288 + 512 * i, 512, BF16).rearrange("p (j t) -> p j t", j=2) for i in range(2)]
    hidB = [Buf() for _ in range(2)]
    sgt = [av(13312 + 512 * i, 512) for i in range(2)]
    sgB = [Buf() for _ in range(2)]
    xn = [av(14336 + 512 * i, 512, BF16) for i in range(2)]
    xnB = [Buf() for _ in range(2)]
    junk = sb("junk", [128, D], BF16); junkB = Buf()
    st = [sb(f"st{i}", [128, 8]) for i in range(4)]
    stB = [Buf() for _ in range(4)]
    cosT = sb("cosT", [128, NX, 8]); sinT = sb("sinT", [128, NX, 8]); csB = Buf()
    wk = [sb(f"wk{i}", [128, 256]) for i in range(4)]
    wkB = [Buf() for _ in range(4)]
    wkb = [sb(f"wkb{i}", [128, 264], BF16) for i in range(2)]
    wkbB = [Buf() for _ in range(2)]
    ktile = [sb(f"ktile{i}", [128, 2, 128], BF16) for i in range(2)]
    ktB = [Buf() for _ in range(2)]
    rp = [sb(f"rp{i}", [128, 4, 8]) for i in range(4)]
    rpB = [Buf() for _ in range(4)]
    cnt = {"stg": 0, "xn": 0, "st": 0, "wk": 0, "wkb": 0, "kt": 0, "hid": 0, "sg": 0}

    def nxt(k, n):
        i = cnt[k]; cnt[k] = (i + 1) % n
        return i

    rs = sb("rs", [128, NX]); rsB = Buf()

    def rstd_tiles(tiles):
        S.op("dve", lambda e: e.memset(rs, 0.0), [], [rsB])
        for (i, rows) in tiles:
            S.op("act", lambda e, i=i, rows=rows: e.activation(junk[:rows, :], X[:rows, i, :], AF.Square, accum_out=rs[:rows, i:i + 1]),
                 [Xb[i]], [junkB, rsB])
        S.op("dve", lambda e: e.tensor_scalar(rs, rs, 1.0 / D, EPS, ALU.mult, ALU.add), [rsB], [rsB])
        S.op("act", lambda e: e.activation(rs, rs, AF.Ln), [rsB], [rsB])
        S.op("act", lambda e: e.activation(rs, rs, AF.Exp, scale=-0.5), [rsB], [rsB])

    def norm_to_HT(i, rows):
        r, rB = rs[:, i:i + 1], rsB
        j = nxt("xn", 2)
        S.op("pool", lambda e: e.tensor_scalar(xn[j][:rows, :], X[:rows, i, :], r[:rows, 0:1], None, ALU.mult),
             [Xb[i], rB], [xnB[j]])
        pt, ptB = getpt()
        for k in range(8):
            S.op("pe", lambda e, k=k: e.transpose(pt[:, k * 128:k * 128 + rows], xn[j][:rows, k * 128:(k + 1) * 128],
                                                  ident[:rows, :rows]), [xnB[j], cB], [ptB])
        src = pt.rearrange("p (k t) -> p k t", k=8)[:, :, 0:rows]
        dst = HT[:, :, i * 128:i * 128 + rows]
        if i % 2 == 0:
            S.op("act", lambda e: e.copy(dst, src), [ptB], [HTb[i]])
        else:
            S.op("dve", lambda e: e.tensor_copy(dst, src), [ptB], [HTb[i]])

    def load_w(dst, dstB, src_ap, gcol, kparts, width):
        s = nxt("stg", lim["stg"])
        S.op("sp", lambda e: e.dma_start(out=stg[s][:, 0:kparts * width].rearrange("p (k f) -> p k f", k=kparts),
                                         in_=src_ap.rearrange("(k p) f -> p k f", p=128)), [], [stgB[s]], dma=True)
        d3 = dst.rearrange("p (k f) -> p k f", k=kparts)
        s3 = stg[s][:, 0:kparts * width].rearrange("p (k f) -> p k f", k=kparts)
        if gcol is None:
            S.op("pool", lambda e: e.tensor_copy(d3, s3), [stgB[s]], [dstB])
        else:
            S.op("pool", lambda e: e.tensor_tensor(d3, s3, gT[:, gcol:gcol + kparts].unsqueeze(2).to_broadcast([128, kparts, width]),
                                                   ALU.mult), [stgB[s], gTB], [dstB])

    def ffn(l, which, tiles):
        pre = f"ffn{which}_"
        gcol = (GID[pre + "g"] + l) * 8
        rstd_tiles(tiles)
        for (i, rows) in tiles:
            norm_to_HT(i, rows)
        blocks = [tiles[b:b + 4] for b in range(0, len(tiles), 4)]
        pending = []
        for fg in range(FF // 256):
            ws = fg % 2
            load_w(Wb[ws][:, 0, :], WbB[ws][0], W[pre + "wg"][l][:, fg * 256:(fg + 1) * 256], gcol, 8, 256)
            load_w(Wb[ws][:, 1, :], WbB[ws][1], W[pre + "wu"][l][:, fg * 256:(fg + 1) * 256], gcol, 8, 256)
            load_w(Wb[ws][:, 2, :], WbB[ws][2], W[pre + "wd"][l][fg * 256:(fg + 1) * 256, :], None, 2, 1024)
            wg3 = Wb[ws][:, 0, :].rearrange("p (k f) -> p k f", k=8)
            wu3 = Wb[ws][:, 1, :].rearrange("p (k f) -> p k f", k=8)
            wd3 = Wb[ws][:, 2, :].rearrange("p (k f) -> p k f", k=2)
            for blk in blocks:
                c0 = blk[0][0] * 128
                ncol = (blk[-1][0] - blk[0][0]) * 128 + blk[-1][1]
                hB = [HTb[i] for i, _ in blk]
                hi = nxt("hid", 2)
                for j in range(2):
                    pgt, pgtB = getps()
                    put, putB = getps()
                    for k in range(8):
                        S.op("pe", lambda e, k=k, j=j, pgt=pgt: e.matmul(pgt[:, 0:ncol], wg3[:, k, j * 128:(j + 1) * 128], HT[:, k, c0:c0 + ncol],
                                                                         start=(k == 0), stop=(k == 7)), hB + [WbB[ws][0]], [pgtB])
                    for k in range(8):
                        S.op("pe", lambda e, k=k, j=j, put=put: e.matmul(put[:, 0:ncol], wu3[:, k, j * 128:(j + 1) * 128], HT[:, k, c0:c0 + ncol],
                                                                         start=(k == 0), stop=(k == 7)), hB + [WbB[ws][1]], [putB])
                    si = nxt("sg", 2)
                    S.op("act", lambda e, si=si, pgt=pgt: e.activation(sgt[si][:, 0:ncol], pgt[:, 0:ncol], AF.Silu), [pgtB], [sgB[si]])
                    S.op("dve", lambda e, j=j, si=si, put=put: e.tensor_tensor(hid[hi][:, j, 0:ncol], sgt[si][:, 0:ncol], put[:, 0:ncol], ALU.mult),
                         [sgB[si], putB], [hidB[hi]])

                def down(blk=blk, c0=c0, hi=hi, ws=ws, wd3=wd3):
                    for (i, rows) in blk:
                        o = i * 128 - c0
                        for half in range(2):
                            pd, pdB = getps()
                            for j in range(2):
                                S.op("pe", lambda e, j=j, pd=pd: e.matmul(pd[:rows, :], hid[hi][:, j, o:o + rows], wd3[:, j, half * 512:(half + 1) * 512],
                                                                          start=(j == 0), stop=(j == 1)), [hidB[hi], WbB[ws][2]], [pdB])
                            S.op("dve", lambda e, pd=pd: e.scalar_tensor_tensor(X[:rows, i, half * 512:(half + 1) * 512], pd[:rows, :], 0.5,
                                                                                X[:rows, i, half * 512:(half + 1) * 512], ALU.mult, ALU.add),
                                 [pdB, Xb[i]], [Xb[i]])
                for p in pending:
                    p()
                pending = [down]
        for p in pending:
            p()

    rt = sb("rt", [128, NX, 8]); rtf = sb("rtf", [128, NX, 8]); rti = sb("rti", [128, NX, 8], mybir.dt.int32); rtB = Buf()

    def rope_tables(i0, n, u0):
        sl = slice(i0, i0 + n)
        pb = pos[:, u0:u0 + n].unsqueeze(2).to_broadcast([128, n, 8])
        fb = cst[:, 512:520].unsqueeze(1).to_broadcast([128, n, 8])
        for tab, shift in ((sinT, 0.0), (cosT, 0.25)):
            S.op("dve", lambda e: e.tensor_tensor(rt[:, sl, :], pb, fb, ALU.mult), [cB, cstB], [rtB])
            if shift:
                S.op("dve", lambda e, shift=shift: e.tensor_scalar(rt[:, sl, :], rt[:, sl, :], shift, None, ALU.add), [rtB], [rtB])
            S.op("dve", lambda e: e.tensor_copy(rti[:, sl, :], rt[:, sl, :]), [rtB], [rtB])
            S.op("dve", lambda e: e.tensor_copy(rtf[:, sl, :], rti[:, sl, :]), [rtB], [rtB])
            S.op("dve", lambda e: e.tensor_tensor(rt[:, sl, :], rt[:, sl, :], rtf[:, sl, :], ALU.subtract), [rtB], [rtB])
            S.op("dve", lambda e: e.tensor_scalar(rtf[:, sl, :], rt[:, sl, :], 0.5, None, ALU.is_gt), [rtB], [rtB])
            S.op("dve", lambda e: e.tensor_tensor(rt[:, sl, :], rt[:, sl, :], rtf[:, sl, :], ALU.subtract), [rtB], [rtB])
            S.op("dve", lambda e: e.tensor_scalar(rtf[:, sl, :], rt[:, sl, :], -0.5, None, ALU.is_lt), [rtB], [rtB])
            S.op("dve", lambda e: e.tensor_tensor(rt[:, sl, :], rt[:, sl, :], rtf[:, sl, :], ALU.add), [rtB], [rtB])
            S.op("act", lambda e, tab=tab: e.activation(tab[:, sl, :], rt[:, sl, :], AF.Sin, scale=2 * math.pi), [rtB], [csB, rtB])

    def to_T_and_store(src_bf, srcB, rows, dst_ap, dstB):
        pt, ptB = getpt()
        for k in range(2):
            S.op("pe", lambda e, k=k: e.transpose(pt[:, k * 128:k * 128 + rows], src_bf[:rows, k * 128:(k + 1) * 128], ident[:rows, :rows]),
                 [srcB, cB], [ptB])
        t = nxt("kt", 2)
        S.op("act", lambda e: e.copy(ktile[t][:, :, 0:rows], pt[:, 0:256].rearrange("p (k t) -> p k t", k=2)[:, :, 0:rows]), [ptB], [ktB[t]])
        S.op("pool", lambda e: e.dma_start(out=dst_ap, in_=ktile[t][:, :, 0:rows]), [ktB[t]], [dstB], dma=True)

    def proj_ab(tiles, us):
        gcol = (GID["mix_g"] + 0) * 8
        for pc in range(int(os.environ.get("KPC", "12"))):
            ws = pc % 2
            kind, hp = ("aq", "ak", "av", "bq", "bk", "bv")[pc // 2], pc % 2
            load_w(Wb[ws][:, 0, :], WbB[ws][0], W["ab_w_in"][0][:, pc * 256:(pc + 1) * 256], gcol, 8, 256)
            w3 = Wb[ws][:, 0, :].rearrange("p (k f) -> p k f", k=8)
            for (i, rows), u in zip(tiles, us):
                own = (u < NOWN) or (u == SMP)
                isq = kind in ("aq", "bq")
                if isq and not (u < NQ or u == SMP):
                    continue
                qi = NQ if u == SMP else u
                pp, ppB = getps()
                for k in range(8):
                    S.op("pe", lambda e, k=k: e.matmul(pp[:rows, 0:256], HT[:, k, i * 128:i * 128 + rows], w3[:, k, :],
                                                       start=(k == 0), stop=(k == 7)), [HTb[i], WbB[ws][0]], [ppB])
                src = pp[:rows, 0:256]
                csl = slice(hp * 256, (hp + 1) * 256)
                if kind in ("aq", "ak"):
                    g = g64["a_gq" if kind == "aq" else "a_gk"]
                    a, b = nxt("wk", 4), nxt("wk", 4)
                    s_ = nxt("st", 4)
                    S.op("act", lambda e: e.copy(wk[b][:rows, :], src), [ppB], [wkB[b]])
                    S.op("dve", lambda e: e.tensor_tensor(wk[a][:rows, :], wk[b][:rows, :], wk[b][:rows, :], ALU.mult), [wkB[b]], [wkB[a]])
                    S.op("dve", lambda e: e.tensor_reduce(st[s_][:rows, 0:4], wk[a][:rows, :].rearrange("p (h d) -> p h d", h=4), AX.X, ALU.add),
                         [wkB[a]], [stB[s_]])
                    S.op("dve", lambda e: e.tensor_scalar(st[s_][:rows, 0:4], st[s_][:rows, 0:4], 1.0 / 64, EPS, ALU.mult, ALU.add), [stB[s_]], [stB[s_]])
                    S.op("act", lambda e: e.activation(st[s_][:rows, 0:4], st[s_][:rows, 0:4], AF.Ln), [stB[s_]], [stB[s_]])
                    S.op("act", lambda e: e.activation(st[s_][:rows, 0:4], st[s_][:rows, 0:4], AF.Exp, scale=-0.5), [stB[s_]], [stB[s_]])
                    w3a = wk[a][:rows, :].rearrange("p (h d) -> p h d", h=4)
                    w3b = wk[b][:rows, :].rearrange("p (h d) -> p h d", h=4)
                    S.op("dve", lambda e: e.tensor_tensor(w3a, src.rearrange("p (h d) -> p h d", h=4),
                                                          st[s_][:rows, 0:4].unsqueeze(2).to_broadcast([rows, 4, 64]), ALU.mult), [ppB, stB[s_]], [wkB[a]])
                    S.op("dve", lambda e: e.tensor_tensor(w3b, w3a, g[:rows, :].unsqueeze(1).to_broadcast([rows, 4, 64]), ALU.mult), [wkB[a], gB], [wkB[b]])
                    cs = cosT[:rows, i, :].unsqueeze(1).to_broadcast([rows, 4, 8])
                    sn = sinT[:rows, i, :].unsqueeze(1).to_broadcast([rows, 4, 8])
                    x1, x2 = w3b[:, :, 0:8], w3b[:, :, 8:16]
                    r = [rp[q][:rows] for q in range(4)]
                    S.op("dve", lambda e: e.tensor_tensor(r[0], x1, cs, ALU.mult), [wkB[b], csB], [rpB[0]])
                    S.op("dve", lambda e: e.tensor_tensor(r[1], x2, sn, ALU.mult), [wkB[b], csB], [rpB[1]])
                    S.op("dve", lambda e: e.tensor_tensor(r[2], x2, cs, ALU.mult), [wkB[b], csB], [rpB[2]])
                    S.op("dve", lambda e: e.tensor_tensor(r[3], x1, sn, ALU.mult), [wkB[b], csB], [rpB[3]])
                    S.op("dve", lambda e: e.tensor_tensor(x1, r[0], r[1], ALU.subtract), [rpB[0], rpB[1]], [wkB[b]])
                    S.op("dve", lambda e: e.tensor_tensor(x2, r[2], r[3], ALU.add), [rpB[2], rpB[3]], [wkB[b]])
                    fin, finB = wk[b], wkB[b]
                    if kind == "ak" and own:
                        dst = o_aks[:, csl] if u == SMP else o_ak[u][:, csl]
                        S.op("pool", lambda e, dst=dst: e.dma_start(out=dst[:rows], in_=fin[:rows, :]), [finB], [], dma=True)
                    c = nxt("wkb", 2)
                    S.op("pool", lambda e: e.tensor_copy(wkb[c][:rows, 0:256], fin[:rows, :]), [finB], [wkbB[c]])
                    if kind == "ak":
                        to_T_and_store(wkb[c], wkbB[c], rows, kta_s[u].rearrange("p (k t) -> p k t", k=4)[:, hp * 2:hp * 2 + 2, 0:rows], scrB["kta"][u])
                    else:
                        to_T_and_store(wkb[c], wkbB[c], rows, qta_s[qi].rearrange("p (k t) -> p k t", k=4)[:, hp * 2:hp * 2 + 2, 0:rows], scrB["qta"][qi])
                elif kind in ("bq", "bk"):
                    c = nxt("wkb", 2)
                    if kind == "bk":
                        a = nxt("wk", 4)
                        S.op("act", lambda e: e.copy(wk[a][:rows, :], src), [ppB], [wkB[a]])
                        if own:
                            dst = o_bks[:, csl] if u == SMP else o_bk[u][:, csl]
                            S.op("pool", lambda e, dst=dst: e.dma_start(out=dst[:rows], in_=wk[a][:rows, :]), [wkB[a]], [], dma=True)
                        S.op("dve", lambda e: e.tensor_copy(wkb[c][:rows, 0:256], wk[a][:rows, :]), [wkB[a]], [wkbB[c]])
                        to_T_and_store(wkb[c], wkbB[c], rows, ktb_s[u].rearrange("p (k t) -> p k t", k=4)[:, hp * 2:hp * 2 + 2, 0:rows], scrB["ktb"][u])
                    else:
                        S.op("dve", lambda e: e.tensor_scalar(wkb[c][:rows, 0:256], src, 0.125, None, ALU.mult), [ppB], [wkbB[c]])
                        to_T_and_store(wkb[c], wkbB[c], rows, qtb_s[qi].rearrange("p (k t) -> p k t", k=4)[:, hp * 2:hp * 2 + 2, 0:rows], scrB["qtb"][qi])
                else:
                    a = nxt("wk", 4)
                    S.op("act", lambda e: e.copy(wk[a][:rows, :], src), [ppB], [wkB[a]])
                    if own:
                        od = {"av": (o_avs, o_av), "bv": (o_bvs, o_bv)}[kind]
                        dst = od[0][:, csl] if u == SMP else od[1][u][:, csl]
                        S.op("pool", lambda e, dst=dst: e.dma_start(out=dst[:rows], in_=wk[a][:rows, :]), [wkB[a]], [], dma=True)
                    c = nxt("wkb", 2)
                    if kind == "av":
                        v3 = wkb[c][:rows, 0:260].rearrange("p (h d) -> p h d", h=2)
                        S.op("dve", lambda e: e.tensor_scalar(v3[:, :, 0:128], wk[a][:rows, :].rearrange("p (h d) -> p h d", h=2), valid[:rows, u:u + 1], None, ALU.mult),
                             [wkB[a], cB], [wkbB[c]])
                        S.op("dve", lambda e: e.tensor_copy(v3[:, :, 128:130], valid[:rows, u:u + 1].unsqueeze(1).to_broadcast([rows, 2, 2])), [cB], [wkbB[c]])
                        S.op("pool", lambda e: e.dma_start(out=va_s[u][:rows, hp * 260:(hp + 1) * 260], in_=wkb[c][:rows, 0:260]), [wkbB[c]], [scrB["va"][u]], dma=True)
                    else:
                        S.op("dve", lambda e: e.tensor_scalar(wkb[c][:rows, 0:256], wk[a][:rows, :], valid[:rows, u:u + 1], None, ALU.mult), [wkB[a], cB], [wkbB[c]])
                        S.op("pool", lambda e: e.dma_start(out=vb_s[u][:rows, csl], in_=wkb[c][:rows, 0:256]), [wkbB[c]], [scrB["vb"][u]], dma=True)

    def front0(us, with_sample):
        tiles = [(i, 128) for i in range(len(us))]
        uu = list(us)
        for i, u in enumerate(us):
            S.op("sp", lambda e, i=i, u=u: e.dma_start(out=X[:, i, :], in_=xp[u]), [], [Xb[i]], dma=True)
        if with_sample:
            i = len(us)
            S.op("sp", lambda e: e.dma_start(out=X[0:16, i, :], in_=xs), [], [Xb[i]], dma=True)
            tiles.append((i, 16)); uu.append(SMP)
        DBG = int(os.environ.get("KDBG", "9"))
        if DBG >= 1:
            rope_tables(0, len(us), us[0])
            if with_sample:
                rope_tables(len(us), 1, SMP)
        if DBG >= 3:
            ffn(0, 1, tiles)
        if DBG >= 2:
            rstd_tiles(tiles)
            for (i, rows) in tiles:
                norm_to_HT(i, rows)
        if DBG >= 4:
            proj_ab(tiles, uu)
        return tiles, uu

    SQ = NQ
    CIDX = NT + 1
    small = {n: sb("sm_" + n, shp) for n, shp in (("den", [128, 4, 8]), ("carry", [128, 4, 8]), ("ecarry", [128, 4, 8]),
                                                   ("rden", [128, 8]), ("s4", [128, 4]), ("lam", [128, 4]))}
    smB = {n: Buf() for n in small}
    lt = sb("lt", [128, 4, 64]); ltB = Buf()
    gsub = sb("gsub", [128, 128]); gsubB = Buf()

    def cache_prep():
        for t in range(8):
            rsl = slice(t * 128, (t + 1) * 128)
            for (src, dst, key) in ((ca_k, kta_s, "kta"), (cb_k, ktb_s, "ktb")):
                s_ = nxt("stg", 3)
                S.op("sp", lambda e, src=src: e.dma_start(out=stg[s_][:, 0:512], in_=src[rsl, :]), [], [stgB[s_]], dma=True)
                j = nxt("xn", 2)
                S.op("pool", lambda e: e.tensor_copy(xn[j][:, 0:512], stg[s_][:, 0:512]), [stgB[s_]], [xnB[j]])
                pt, ptB = getpt()
                for k in range(4):
                    S.op("pe", lambda e, k=k: e.transpose(pt[:, k * 128:(k + 1) * 128], xn[j][:, k * 128:(k + 1) * 128], ident), [xnB[j], cB], [ptB])
                S.op("act", lambda e: e.copy(xn[j][:, 512:1024], pt[:, 0:512]), [ptB], [xnB[j]])
                S.op("pool", lambda e, dst=dst: e.dma_start(out=dst[CIDX + t], in_=xn[j][:, 512:1024]), [xnB[j]], [scrB[key][CIDX + t]], dma=True)
            s_ = nxt("stg", 3)
            S.op("sp", lambda e: e.dma_start(out=stg[s_][:, 0:512], in_=ca_v[rsl, :]), [], [stgB[s_]], dma=True)
            j = nxt("xn", 2)
            v4 = xn[j][:, 0:520].rearrange("p (h d) -> p h d", h=4)
            S.op("pool", lambda e: e.tensor_copy(v4[:, :, 0:128], stg[s_][:, 0:512].rearrange("p (h d) -> p h d", h=4)), [stgB[s_]], [xnB[j]])
            S.op("pool", lambda e: e.memset(v4[:, :, 128:130], 1.0), [], [xnB[j]])
            S.op("pool", lambda e: e.dma_start(out=va_s[CIDX + t], in_=xn[j][:, 0:520]), [xnB[j]], [scrB["va"][CIDX + t]], dma=True)
            s_ = nxt("stg", 3)
            S.op("sp", lambda e: e.dma_start(out=stg[s_][:, 0:512], in_=cb_v[rsl, :]), [], [stgB[s_]], dma=True)
            j = nxt("xn", 2)
            S.op("pool", lambda e: e.tensor_copy(xn[j][:, 0:512], stg[s_][:, 0:512]), [stgB[s_]], [xnB[j]])
            S.op("pool", lambda e: e.dma_start(out=vb_s[CIDX + t], in_=xn[j][:, 0:512]), [xnB[j]], [scrB["vb"][CIDX + t]], dma=True)

    def lambda_prep():
        for q, n in enumerate(("a_lq1", "a_lk1", "a_lq2", "a_lk2")):
            S.op("sp", lambda e, q=q, n=n: e.dma_start(out=lt[:, q, :], in_=W[n][0:1, :].partition_broadcast(128)), [], [ltB], dma=True)
        S.op("sp", lambda e: e.dma_start(out=gsub, in_=W["a_subln_g"][0:1, :].partition_broadcast(128)), [], [gsubB], dma=True)
        S.op("dve", lambda e: e.tensor_scalar(gsub, gsub, 0.8, None, ALU.mult), [gsubB], [gsubB])
        lam = small["lam"]
        S.op("dve", lambda e: e.tensor_tensor(lt[:, 0, :], lt[:, 0, :], lt[:, 1, :], ALU.mult), [ltB], [ltB])
        S.op("dve", lambda e: e.tensor_tensor(lt[:, 2, :], lt[:, 2, :], lt[:, 3, :], ALU.mult), [ltB], [ltB])
        S.op("dve", lambda e: e.tensor_reduce(lam[:, 0:1], lt[:, 0, :], AX.X, ALU.add), [ltB], [smB["lam"]])
        S.op("dve", lambda e: e.tensor_reduce(lam[:, 1:2], lt[:, 2, :], AX.X, ALU.add), [ltB], [smB["lam"]])
        S.op("act", lambda e: e.activation(lam[:, 0:2], lam[:, 0:2], AF.Exp), [smB["lam"]], [smB["lam"]])
        S.op("dve", lambda e: e.tensor_tensor(lam[:, 2:3], lam[:, 1:2], lam[:, 0:1], ALU.subtract), [smB["lam"]], [smB["lam"]])
        S.op("dve", lambda e: e.tensor_scalar(lam[:, 2:3], lam[:, 2:3], -0.2, None, ALU.add), [smB["lam"]], [smB["lam"]])

    def attn_l0():
        accA = [av(1024 * j, 1024) for j in range(4)]; accAB = [Buf() for _ in range(4)]
        accB = [av(4096 + 512 * j, 512) for j in range(4)]; accBB = [Buf() for _ in range(4)]
        QTe = av(6144, 1024, BF16).rearrange("p (k t) -> p k t", k=4); QTB = Buf()
        QTo = av(13856, 1024, BF16).rearrange("p (k t) -> p k t", k=4)
        QTeo = (QTe, QTo)
        S.op("pool", lambda e: e.memset(QTe[64:128, :, :], 0.0), [], [QTB])
        S.op("pool", lambda e: e.memset(QTo[0:64, :, :], 0.0), [], [QTB])
        kvb = [av(7168 + 528 * i, 528, BF16) for i in range(2)]; kvB = [Buf() for _ in range(2)]
        Eb = [av(8224 + 512 * i, 512, BF16) for i in range(2)]; EbB = [[Buf(), Buf()] for _ in range(2)]
        ef = [av(9248 + 512 * i, 512) for i in range(2)]; efB = [Buf() for _ in range(2)]
        pvs = [av(10272 + 256 * i, 256) for i in range(2)]; pvsB = [Buf() for _ in range(2)]
        tmpB = [av(10784 + 256 * i, 256) for i in range(2)]; tmpBB = [Buf() for _ in range(2)]
        ofin = av(11296, 1024); ofinB = Buf()
        mg = av(12320, 512, BF16); mgB = Buf()
        sqf = av(12832, 512); sqfB = Buf()
        aof = av(13344, 512); aofB = Buf()
        QhB = [[Buf(), Buf()] for _ in range(3)]
        den, carry, ecarry = small["den"], small["carry"], small["ecarry"]
        zc = {"z": 0, "e": 0}

        SPB = [Eb[0], av(12832, 512, BF16)]; WBv = [Eb[1], av(13344, 512, BF16)]
        SPBB = [[Buf(), Buf()], [Buf(), Buf()]]; WBB = [[Buf(), Buf()], [Buf(), Buf()]]
        P1Bx = [P1B, Buf()]
        par_c = {"p": 0}

        pipe = {"active": []}

        def pipe_round(g=None):
            act_ = pipe["active"]
            if g is not None:
                act_.insert(0, g)
            keep = []
            for gg in act_:
                try:
                    next(gg); keep.append(gg)
                except StopIteration:
                    pass
            pipe["active"] = keep

        def pipe_drain():
            while pipe["active"]:
                pipe_round()

        def pairA(j, nq, nk, KT, V, kB, diag):
            zi = zc["z"]; zc["z"] = 1 - zi
            z = Q[zi]
            for m in range(8):
                S.op("pe", lambda e, m=m: e.matmul(z[:nk, m * nq:(m + 1) * nq], KT[:, m // 2, 0:nk],
                                                   QTeo[m % 2][:, m // 2, j * 128:j * 128 + nq], start=True, stop=True),
                     [kB, QTB], [QhB[zi][0], QhB[zi][1]])
            ei = zc["e"]; zc["e"] = 1 - ei
            E = Eb[ei]
            if 8 * nq > 512:
                for hh in range(2):
                    S.op("act", lambda e, hh=hh: e.activation(E[:nk, hh * 512:(hh + 1) * 512], z[:nk, hh * 512:(hh + 1) * 512], AF.Exp), [QhB[zi][hh]], [EbB[ei][hh]])
            else:
                S.op("act", lambda e: e.activation(E[:nk, 0:8 * nq], z[:nk, 0:8 * nq], AF.Exp), [QhB[zi][0], QhB[zi][1]], [EbB[ei][0], EbB[ei][1]])
            if diag:
                e3 = E[:nk, 0:8 * nq].rearrange("p (m q) -> p m q", m=8)
                S.op("pool", lambda e: e.tensor_tensor(e3, e3, maskA[:nk, :nq].unsqueeze(1).to_broadcast([nk, 8, nq]), ALU.mult),
                     [EbB[ei][0], EbB[ei][1], cB], [EbB[ei][0], EbB[ei][1]])
            yield
            pv = Q[2]
            for m in range(8):
                S.op("pe", lambda e, m=m: e.matmul(pv[:nq, m * 128:(m + 1) * 128], E[:nk, m * nq:(m + 1) * nq], V[:nk, m // 2, 0:128], start=True, stop=True),
                     [EbB[ei][0], EbB[ei][1], kB], [QhB[2][0], QhB[2][1]])
            for m in range(8):
                S.op("pe", lambda e, m=m: e.matmul(P1[:nq, m:m + 1], E[:nk, m * nq:(m + 1) * nq], V[:nk, m // 2, 128:129], start=True, stop=True),
                     [EbB[ei][0], EbB[ei][1], kB], [P1B])
            S.op("dve", lambda e: e.tensor_tensor(accA[j][:nq, :], accA[j][:nq, :], pv[:nq, :], ALU.add), [accAB[j], QhB[2][0], QhB[2][1]], [accAB[j]])
            S.op("dve", lambda e: e.tensor_tensor(den[:nq, j, :], den[:nq, j, :], P1[:nq, 0:8], ALU.add), [smB["den"], P1B], [smB["den"]])

        def pairB(j, nq, nk, KT, V, kB, diag):
            par = par_c["p"]; par_c["p"] = 1 - par
            vw = []
            for hh in range(2):
                vw.append(dict(zh=Q[0][:, hh * 512:hh * 512 + 4 * nq], ar=Q[1][:, hh * 512:hh * 512 + 4 * nq], pvB=Q[2][:, hh * 512:hh * 512 + 256],
                               spb=SPB[par][:, hh * 512:hh * 512 + 4 * nq], wb=WBv[par][:, hh * 512:hh * 512 + 4 * nq], hs=slice(4 * hh, 4 * hh + 4)))

            def qk(out, hh, h4, st_, sp_):
                h = 4 * hh + h4
                return lambda e: e.matmul(out[:nk, h4 * nq:(h4 + 1) * nq], KT[:, h // 2, 0:nk],
                                          QTeo[h % 2][:, h // 2, j * 128:j * 128 + nq], start=st_, stop=sp_)
            for hh in range(2):
                for h4 in range(4):
                    S.op("pe", qk(vw[hh]["zh"], hh, h4, True, True), [kB, QTB], [QhB[0][hh]])
            for hh in range(2):
                S.op("act", lambda e, hh=hh: e.activation(ef[hh][:nk, 0:4 * nq], vw[hh]["zh"][:nk, :], AF.Exp), [QhB[0][hh]], [efB[hh]])
            for hh in range(2):
                S.op("act", lambda e, hh=hh: e.activation(vw[hh]["spb"][:nk, :], ef[hh][:nk, 0:4 * nq], AF.Ln, bias=1.0), [efB[hh]], [SPBB[par][hh]])
                if diag:
                    s3 = vw[hh]["spb"][:nk, :].rearrange("p (m q) -> p m q", m=4)
                    S.op("pool", lambda e, s3=s3: e.tensor_tensor(s3, s3, maskB[:nk, :nq].unsqueeze(1).to_broadcast([nk, 4, nq]), ALU.mult),
                         [SPBB[par][hh], cB], [SPBB[par][hh]])
            yield
            for hh in range(2):
                for h4 in range(4):
                    S.op("pe", qk(vw[hh]["ar"], hh, h4, True, False), [kB, QTB], [QhB[1][hh]])
                    S.op("pe", lambda e, hh=hh, h4=h4: e.matmul(vw[hh]["ar"][:nk, h4 * nq:(h4 + 1) * nq], negU[:nk, :nk], vw[hh]["spb"][:nk, h4 * nq:(h4 + 1) * nq], start=False, stop=True),
                         [SPBB[par][hh], cB], [QhB[1][hh]])
            for hh in range(2):
                for h4 in range(4):
                    c_ = par * 8 + 4 * hh + h4
                    S.op("pe", lambda e, hh=hh, h4=h4, c_=c_: e.matmul(P1[:nq, c_:c_ + 1], vw[hh]["spb"][:nk, h4 * nq:(h4 + 1) * nq], ones[:nk, 0:1], start=True, stop=True),
                         [SPBB[par][hh], cB], [P1Bx[par]])
            for hh in range(2):
                S.op("act", lambda e, hh=hh: e.activation(vw[hh]["wb"][:nk, :], vw[hh]["ar"][:nk, :], AF.Exp), [QhB[1][hh]], [WBB[par][hh]])
                if diag:
                    w3 = vw[hh]["wb"][:nk, :].rearrange("p (m q) -> p m q", m=4)
                    S.op("pool", lambda e, w3=w3: e.tensor_tensor(w3, w3, maskB[:nk, :nq].unsqueeze(1).to_broadcast([nk, 4, nq]), ALU.mult),
                         [WBB[par][hh], cB], [WBB[par][hh]])
            yield
            for hh in range(2):
                for h4 in range(4):
                    S.op("pe", lambda e, hh=hh, h4=h4: e.matmul(vw[hh]["pvB"][:nq, h4 * 64:(h4 + 1) * 64], vw[hh]["wb"][:nk, h4 * nq:(h4 + 1) * nq], V[:nk, 4 * hh + h4, :], start=True, stop=True),
                         [WBB[par][hh], kB], [QhB[2][hh]])
            for hh in range(2):
                S.op("act", lambda e, hh=hh: e.copy(pvs[hh][:nq, :], vw[hh]["pvB"][:nq, :]), [QhB[2][hh]], [pvsB[hh]])
            for hh in range(2):
                hs = vw[hh]["hs"]
                ps_ = slice(par * 8 + 4 * hh, par * 8 + 4 * hh + 4)
                S.op("dve", lambda e, hh=hh, hs=hs: e.tensor_tensor(tmpB[hh][:nq, :].rearrange("p (h d) -> p h d", h=4), pvs[hh][:nq, :].rearrange("p (h d) -> p h d", h=4),
                                                                   ecarry[:nq, j, hs].unsqueeze(2).to_broadcast([nq, 4, 64]), ALU.mult), [pvsB[hh], smB["ecarry"]], [tmpBB[hh]])
                S.op("dve", lambda e, hh=hh: e.tensor_tensor(accB[j][:nq, hh * 256:(hh + 1) * 256], accB[j][:nq, hh * 256:(hh + 1) * 256], tmpB[hh][:nq, :], ALU.add),
                     [tmpBB[hh], accBB[j]], [accBB[j]])
                S.op("dve", lambda e, hs=hs, ps_=ps_: e.tensor_tensor(carry[:nq, j, hs], carry[:nq, j, hs], P1[:nq, ps_], ALU.subtract), [smB["carry"], P1Bx[par]], [smB["carry"]])
                S.op("act", lambda e, hs=hs: e.activation(ecarry[:nq, j, hs], carry[:nq, j, hs], AF.Exp), [smB["carry"]], [smB["ecarry"]])

        def finalize(j, nq, i):
            rden = small["rden"]
            S.op("dve", lambda e: e.tensor_scalar(rden[:nq, :], den[:nq, j, :], 1e-30, None, ALU.add), [smB["den"]], [smB["rden"]])
            S.op("dve", lambda e: e.reciprocal(rden[:nq, :], rden[:nq, :]), [smB["rden"]], [smB["rden"]])
            S.op("dve", lambda e: e.tensor_tensor(ofin[:nq, :].rearrange("p (m d) -> p m d", m=8), accA[j][:nq, :].rearrange("p (m d) -> p m d", m=8),
                                                  rden[:nq, :].unsqueeze(2).to_broadcast([nq, 8, 128]), ALU.mult), [accAB[j], smB["rden"]], [ofinB])
            o4 = ofin[:nq, :].rearrange("p (h t d) -> p h t d", h=4, t=2)
            ao3 = aof[:nq, :].rearrange("p (h d) -> p h d", h=4)
            S.op("dve", lambda e: e.scalar_tensor_tensor(ao3, o4[:, :, 1, :], small["lam"][:nq, 2:3], o4[:, :, 0, :], ALU.mult, ALU.add),
                 [ofinB, smB["lam"]], [aofB])
            s4 = small["s4"]
            S.op("dve", lambda e: e.tensor_tensor(sqf[:nq, :], aof[:nq, :], aof[:nq, :], ALU.mult), [aofB], [sqfB])
            S.op("dve", lambda e: e.tensor_reduce(s4[:nq, :], sqf[:nq, :].rearrange("p (h d) -> p h d", h=4), AX.X, ALU.add), [sqfB], [smB["s4"]])
            S.op("dve", lambda e: e.tensor_scalar(s4[:nq, :], s4[:nq, :], 1.0 / 128, EPS, ALU.mult, ALU.add), [smB["s4"]], [smB["s4"]])
            S.op("act", lambda e: e.activation(s4[:nq, :], s4[:nq, :], AF.Ln), [smB["s4"]], [smB["s4"]])
            S.op("act", lambda e: e.activation(s4[:nq, :], s4[:nq, :], AF.Exp, scale=-0.5), [smB["s4"]], [smB["s4"]])
            S.op("dve", lambda e: e.tensor_tensor(ao3, ao3, s4[:nq, :].unsqueeze(2).to_broadcast([nq, 4, 128]), ALU.mult), [aofB, smB["s4"]], [aofB])
            S.op("dve", lambda e: e.tensor_tensor(mg[:nq, 0:512].rearrange("p (h d) -> p h d", h=4), ao3, gsub[:nq, :].unsqueeze(1).to_broadcast([nq, 4, 128]), ALU.mult),
                 [aofB, gsubB], [mgB])
            S.op("pool", lambda e: e.tensor_copy(mg[:nq, 512:1024], accB[j][:nq, :]), [accBB[j]], [mgB])
            pt, ptB = getpt()
            for k in range(8):
                S.op("pe", lambda e, k=k: e.transpose(pt[:, k * 128:k * 128 + nq], mg[:nq, k * 128:(k + 1) * 128], ident[:nq, :nq]), [mgB, cB], [ptB])
            S.op("act", lambda e: e.copy(HT[:, :, i * 128:i * 128 + nq], pt.rearrange("p (k t) -> p k t", k=8)[:, :, 0:nq]), [ptB], [HTb[i]])

        groups = [list(range(g, g + 4)) for g in range(0, NQ, 4)] + [[SMP]]
        ngrp = int(os.environ.get("KNGRP", "99"))
        groups = groups[:ngrp] + ([groups[-1]] if ngrp < len(groups) else [])
        for grp in groups:
            sample = grp[0] == SMP
            nq = 16 if sample else 128
            keys = ([SMP] + [CIDX + t for t in range(7, -1, -1)]) if sample else list(range(grp[0], int(os.environ.get("KNKEY", str(NT)))))
            for pas in os.environ.get("KPASS", "AB"):
                for j, s in enumerate(grp):
                    qi = SQ if sample else s
                    src = (qta_s if pas == "A" else qtb_s)[qi].rearrange("p (k t) -> p k t", k=4)[:, :, 0:nq]
                    S.op("sp", lambda e, j=j, src=src: e.dma_start(out=QTe[0:64, :, j * 128:j * 128 + nq], in_=src[0:64]), [scrB["qta" if pas == "A" else "qtb"][qi]], [QTB], dma=True)
                    S.op("sp", lambda e, j=j, src=src: e.dma_start(out=QTo[64:128, :, j * 128:j * 128 + nq], in_=src[64:128]), [scrB["qta" if pas == "A" else "qtb"][qi]], [QTB], dma=True)
                    if pas == "A":
                        S.op("dve", lambda e, j=j: e.memset(accA[j], 0.0), [], [accAB[j]])
                    else:
                        S.op("dve", lambda e, j=j: e.memset(accB[j], 0.0), [], [accBB[j]])
                if pas == "A":
                    S.op("dve", lambda e: e.memset(den, 0.0), [], [smB["den"]])
                else:
                    S.op("dve", lambda e: e.memset(carry, 0.0), [], [smB["carry"]])
                    S.op("dve", lambda e: e.memset(ecarry, 1.0), [], [smB["ecarry"]])
                prev_pairs = 0
                for idx, uk in enumerate(keys):
                    nk = 16 if uk == SMP else 128
                    b = idx % 2
                    npairs = sum(1 for s in grp if sample or uk >= s)
                    if prev_pairs < 4 or npairs < 4:
                        pipe_drain()
                    prev_pairs = npairs
                    KT = kvb[b][:, 0:512].rearrange("p (k t) -> p k t", k=4)
                    ksrc = (kta_s if pas == "A" else ktb_s)[uk].rearrange("p (k t) -> p k t", k=4)[:, :, 0:nk]
                    kkey, vkey = ("kta", "va") if pas == "A" else ("ktb", "vb")
                    S.op("sp", lambda e, KT=KT, ksrc=ksrc, nk=nk: e.dma_start(out=KT[:, :, 0:nk], in_=ksrc), [scrB[kkey][uk]], [kvB[b]], dma=True)
                    if pas == "A":
                        V = kvb[b][:, 512:1032].rearrange("p (h d) -> p h d", h=4)
                        S.op("sp", lambda e, b=b, uk=uk, nk=nk: e.dma_start(out=kvb[b][:nk, 512:1032], in_=va_s[uk][:nk, :]), [scrB[vkey][uk]], [kvB[b]], dma=True)
                    else:
                        V = kvb[b][:, 512:1024].rearrange("p (h d) -> p h d", h=8)
                        S.op("sp", lambda e, b=b, uk=uk, nk=nk: e.dma_start(out=kvb[b][:nk, 512:1024], in_=vb_s[uk][:nk, :]), [scrB[vkey][uk]], [kvB[b]], dma=True)
                    gens = []
                    for j, s in enumerate(grp):
                        if not sample and uk < s:
                            continue
                        diag = (uk == SMP) if sample else (uk == s)
                        gens.append((pairA if pas == "A" else pairB)(j, nq, nk, KT, V, kvB[b], diag))
                    for g_ in gens:
                        pipe_round(g_)
                pipe_drain()
                S.barrier()
            if not os.environ.get("KNOFIN"):
                for j, s in enumerate(grp):
                    finalize(j, nq, NQ if sample else s)
            S.barrier()

    def linear_residual(wsrc, gcol, tiles, scale):
        for pc in range(4):
            ws = pc % 2
            load_w(Wb[ws][:, 0, :], WbB[ws][0], wsrc[:, pc * 256:(pc + 1) * 256], gcol, 8, 256)
            w3 = Wb[ws][:, 0, :].rearrange("p (k f) -> p k f", k=8)
            for (i, rows) in tiles:
                pp, ppB = getps()
                for k in range(8):
                    S.op("pe", lambda e, k=k: e.matmul(pp[:rows, 0:256], HT[:, k, i * 128:i * 128 + rows], w3[:, k, :], start=(k == 0), stop=(k == 7)),
                         [HTb[i], WbB[ws][0]], [ppB])
                S.op("dve", lambda e: e.scalar_tensor_tensor(X[:rows, i, pc * 256:(pc + 1) * 256], pp[:rows, 0:256], float(scale),
                                                             X[:rows, i, pc * 256:(pc + 1) * 256], ALU.mult, ALU.add), [ppB, Xb[i]], [Xb[i]])

    lim = {"stg": 3}
    gm = {"gq": lt.rearrange("p a b -> p (a b)"), "gk": sb("gm_gk", [128, 256])}; gmB = Buf()
    MKT = av(4096, 1024, BF16).rearrange("p (k t) -> p k t", k=8); MV = av(5120, 1024, BF16).rearrange("p (t f) -> p t f", t=2)
    MKTs = av(12288, 1024, BF16).rearrange("p (k t) -> p k t", k=8); MVs = av(13312, 1024, BF16).rearrange("p (t f) -> p t f", t=2)
    memB = {"k": Buf(), "v": Buf(), "ks": Buf(), "vs": Buf()}
    mw = [av(1024 + 1024 * i, 1024, BF16) for i in range(2)]; mwB = [Buf(), Buf()]
    MT = av(0, 1024, BF16).rearrange("p (k t) -> p k t", k=8); MTB = Buf()

    def head_norm(src_ps, srcB, rows, n, gtile, gtB, out_bf=None, out_bfB=None):
        a, b = nxt("wk", 4), nxt("wk", 4)
        s_ = nxt("st", 4)
        S.op("act", lambda e: e.copy(wk[a][:rows, :n], src_ps), [srcB], [wkB[a]])
        S.op("dve", lambda e: e.tensor_tensor(wk[b][:rows, :n], wk[a][:rows, :n], wk[a][:rows, :n], ALU.mult), [wkB[a]], [wkB[b]])
        S.op("dve", lambda e: e.tensor_reduce(st[s_][:rows, 0:1], wk[b][:rows, :n], AX.X, ALU.add), [wkB[b]], [stB[s_]])
        S.op("dve", lambda e: e.tensor_scalar(st[s_][:rows, 0:1], st[s_][:rows, 0:1], 1.0 / n, EPS, ALU.mult, ALU.add), [stB[s_]], [stB[s_]])
        S.op("act", lambda e: e.activation(st[s_][:rows, 0:1], st[s_][:rows, 0:1], AF.Ln), [stB[s_]], [stB[s_]])
        S.op("act", lambda e: e.activation(st[s_][:rows, 0:1], st[s_][:rows, 0:1], AF.Exp, scale=-0.5), [stB[s_]], [stB[s_]])
        S.op("dve", lambda e: e.tensor_scalar(wk[a][:rows, :n], wk[a][:rows, :n], st[s_][:rows, 0:1], None, ALU.mult), [wkB[a], stB[s_]], [wkB[a]])
        S.op("dve", lambda e: e.tensor_tensor(wk[b][:rows, :n], wk[a][:rows, :n], gtile[:rows, :n], ALU.mult), [wkB[a], gtB], [wkB[b]])
        return wk[b], wkB[b]

    def mem_prep(l):
        S.op("sp", lambda e: e.dma_start(out=gm["gq"], in_=W["mem_gq"][l:l + 1, :].partition_broadcast(128)), [], [gmB], dma=True)
        S.op("sp", lambda e: e.dma_start(out=gm["gk"], in_=W["mem_gk"][l:l + 1, :].partition_broadcast(128)), [], [gmB], dma=True)
        S.op("dve", lambda e: e.tensor_scalar(gm["gq"], gm["gq"], 1.0 / 16, None, ALU.mult), [gmB], [gmB])
        for t in range(2):
            s_ = nxt("stg", 3)
            S.op("sp", lambda e, t=t: e.dma_start(out=stg[s_][:, 0:1024], in_=memp[t * 128:(t + 1) * 128, :]), [], [stgB[s_]], dma=True)
            k_ = nxt("st", 4)
            S.op("dve", lambda e: e.memset(st[k_][:, 0:1], 0.0), [], [stB[k_]])
            S.op("act", lambda e: e.activation(junk, stg[s_][:, 0:1024], AF.Square, accum_out=st[k_][:, 0:1]), [stgB[s_]], [junkB, stB[k_]])
            S.op("dve", lambda e: e.tensor_scalar(st[k_][:, 0:1], st[k_][:, 0:1], 1.0 / D, EPS, ALU.mult, ALU.add), [stB[k_]], [stB[k_]])
            S.op("act", lambda e: e.activation(st[k_][:, 0:1], st[k_][:, 0:1], AF.Ln), [stB[k_]], [stB[k_]])
            S.op("act", lambda e: e.activation(st[k_][:, 0:1], st[k_][:, 0:1], AF.Exp, scale=-0.5), [stB[k_]], [stB[k_]])
            j = nxt("xn", 2)
            S.op("pool", lambda e: e.tensor_scalar(xn[j], stg[s_][:, 0:1024], st[k_][:, 0:1], None, ALU.mult), [stgB[s_], stB[k_]], [xnB[j]])
            pt, ptB = getpt()
            for k in range(8):
                S.op("pe", lambda e, k=k: e.transpose(pt[:, k * 128:(k + 1) * 128], xn[j][:, k * 128:(k + 1) * 128], ident), [xnB[j], cB], [ptB])
            S.op("act", lambda e, t=t: e.copy(MT[:, :, t * 128:(t + 1) * 128], pt.rearrange("p (k t) -> p k t", k=8)), [ptB], [MTB])
        gcol = (GID["mem_g_m"] + l) * 8
        for which in ("k", "v"):
            for h in range(4):
                ws = h % 2
                load_w(mw[ws], mwB[ws], W["mem_w" + which][l][:, h * 256:(h + 1) * 256], gcol, 8, 256)
                w3 = mw[ws].rearrange("p (k f) -> p k f", k=8)
                for t in range(2):
                    pp, ppB = getps()
                    for k in range(8):
                        S.op("pe", lambda e, k=k, t=t: e.matmul(pp[:, 0:256], MT[:, k, t * 128:(t + 1) * 128], w3[:, k, :], start=(k == 0), stop=(k == 7)),
                             [MTB, mwB[ws]], [ppB])
                    if which == "k":
                        fin, finB = head_norm(pp[:, 0:256], ppB, 128, 256, gm["gk"], gmB)
                        S.op("pool", lambda e, t=t, h=h: e.dma_start(out=o_mk[l][t * 128:(t + 1) * 128, h * 256:(h + 1) * 256], in_=fin[:, 0:256]), [finB], [], dma=True)
                        c = nxt("wkb", 2)
                        S.op("pool", lambda e: e.tensor_copy(wkb[c][:, 0:256], fin[:, 0:256]), [finB], [wkbB[c]])
                        pt, ptB = getpt()
                        for k in range(2):
                            S.op("pe", lambda e, k=k: e.transpose(pt[:, k * 128:(k + 1) * 128], wkb[c][:, k * 128:(k + 1) * 128], ident), [wkbB[c], cB], [ptB])
                        S.op("act", lambda e, t=t, h=h: e.copy(MKT[:, 2 * h:2 * h + 2, t * 128:(t + 1) * 128], pt[:, 0:256].rearrange("p (k t) -> p k t", k=2)), [ptB], [memB["k"]])
                    else:
                        a = nxt("wk", 4)
                        S.op("act", lambda e: e.copy(wk[a][:, :], pp[:, 0:256]), [ppB], [wkB[a]])
                        S.op("pool", lambda e, t=t, h=h: e.dma_start(out=o_mv[l][t * 128:(t + 1) * 128, h * 256:(h + 1) * 256], in_=wk[a][:, :]), [wkB[a]], [], dma=True)
                        S.op("pool", lambda e, t=t, h=h: e.tensor_copy(MV[:, t, h * 256:(h + 1) * 256], wk[a][:, :]), [wkB[a]], [memB["v"]])
        for t in range(2):
            s_ = nxt("stg", 3)
            S.op("sp", lambda e, t=t: e.dma_start(out=stg[s_][:, 0:1024], in_=cm_k[l][t * 128:(t + 1) * 128, :]), [], [stgB[s_]], dma=True)
            j = nxt("xn", 2)
            S.op("pool", lambda e: e.tensor_copy(xn[j], stg[s_][:, 0:1024]), [stgB[s_]], [xnB[j]])
            pt, ptB = getpt()
            for k in range(8):
                S.op("pe", lambda e, k=k: e.transpose(pt[:, k * 128:(k + 1) * 128], xn[j][:, k * 128:(k + 1) * 128], ident), [xnB[j], cB], [ptB])
            S.op("act", lambda e, t=t: e.copy(MKTs[:, :, t * 128:(t + 1) * 128], pt.rearrange("p (k t) -> p k t", k=8)), [ptB], [memB["ks"]])
            s_ = nxt("stg", 3)
            S.op("sp", lambda e, t=t: e.dma_start(out=stg[s_][:, 0:1024], in_=cm_v[l][t * 128:(t + 1) * 128, :]), [], [stgB[s_]], dma=True)
            S.op("pool", lambda e, t=t: e.tensor_copy(MVs[:, t, :], stg[s_][:, 0:1024]), [stgB[s_]], [memB["vs"]])

    def mem_attn(l, tiles):
        QM = av(10240, 512, BF16).rearrange("p (k t) -> p k t", k=8); QMB = Buf()
        Em = av(10752, 512, BF16); EmB = [Buf(), Buf()]
        og = av(11264, 1024); ogB = Buf()
        ogb = junk
        wq = [av(h * 1024, 1024, BF16) for h in range(4)]; wqB = [Buf() for _ in range(4)]
        rstd_tiles(tiles)
        for (i, rows) in tiles:
            norm_to_HT(i, rows)
        lim["stg"] = 2
        gcol = (GID["mem_g_x"] + l) * 8
        for h in range(4):
            load_w(wq[h], wqB[h], W["mem_wq"][l][:, h * 256:(h + 1) * 256], gcol, 8, 256)
        rr["ptn"] = 1
        zc = 0
        for (i, rows) in tiles:
            smp = (i == NQ)
            KTm, Vm, kB, vB = (MKTs, MVs, memB["ks"], memB["vs"]) if smp else (MKT, MV, memB["k"], memB["v"])
            for h in range(4):
                w3 = wq[h].rearrange("p (k f) -> p k f", k=8)
                pp, ppB = getps()
                for k in range(8):
                    S.op("pe", lambda e, k=k: e.matmul(pp[:rows, 0:256], HT[:, k, i * 128:i * 128 + rows], w3[:, k, :], start=(k == 0), stop=(k == 7)),
                         [HTb[i], wqB[h]], [ppB])
                fin, finB = head_norm(pp[:rows, 0:256], ppB, rows, 256, gm["gq"], gmB)
                c = nxt("wkb", 2)
                S.op("pool", lambda e: e.tensor_copy(wkb[c][:rows, 0:256], fin[:rows, 0:256]), [finB], [wkbB[c]])
                pt, ptB = getpt()
                for k in range(2):
                    S.op("pe", lambda e, k=k: e.transpose(pt[:, k * 128:k * 128 + rows], wkb[c][:rows, k * 128:(k + 1) * 128], ident[:rows, :rows]), [wkbB[c], cB], [ptB])
                S.op("act", lambda e, h=h: e.copy(QM[:, 2 * h:2 * h + 2, 0:rows], pt[:, 0:256].rearrange("p (k t) -> p k t", k=2)[:, :, 0:rows]), [ptB], [QMB])
            zi = zc; zc = 1 - zc
            z = Q[zi]; zB_ = [PSB[2 * zi], PSB[2 * zi + 1]]
            for h in range(4):
                for kt in range(2):
                    c0 = (h * 2 + kt) * rows
                    for cc in range(2):
                        S.op("pe", lambda e, h=h, kt=kt, cc=cc, c0=c0: e.matmul(z[:, c0:c0 + rows], KTm[:, 2 * h + cc, kt * 128:(kt + 1) * 128], QM[:, 2 * h + cc, 0:rows],
                                                                            start=(cc == 0), stop=(cc == 1)), [kB, QMB], zB_)
            if 8 * rows > 512:
                for hh in range(2):
                    S.op("act", lambda e, hh=hh: e.activation(Em[:, hh * 512:(hh + 1) * 512], z[:, hh * 512:(hh + 1) * 512], AF.Exp), [zB_[hh]], [EmB[hh]])
            else:
                S.op("act", lambda e: e.activation(Em[:, 0:8 * rows], z[:, 0:8 * rows], AF.Exp), zB_, EmB)
            pv = Q[2]; pvB_ = [PSB[4], PSB[5]]
            for h in range(4):
                for kt in range(2):
                    c0 = (h * 2 + kt) * rows
                    S.op("pe", lambda e, h=h, kt=kt, c0=c0: e.matmul(pv[:rows, h * 256:(h + 1) * 256], Em[:, c0:c0 + rows], Vm[:, kt, h * 256:(h + 1) * 256],
                                                                     start=(kt == 0), stop=(kt == 1)), EmB + [vB], [pvB_[h // 2]])
            for h in range(4):
                for kt in range(2):
                    c0 = (h * 2 + kt) * rows
                    S.op("pe", lambda e, h=h, kt=kt, c0=c0: e.matmul(P1[:rows, h:h + 1], Em[:, c0:c0 + rows], ones[:, 0:1], start=(kt == 0), stop=(kt == 1)), EmB + [cB], [P1B])
            for hh in range(2):
                S.op("act", lambda e, hh=hh: e.copy(og[:rows, hh * 512:(hh + 1) * 512], pv[:rows, hh * 512:(hh + 1) * 512]), [pvB_[hh]], [ogB])
            rden = small["rden"]
            S.op("dve", lambda e: e.reciprocal(rden[:rows, 0:4], P1[:rows, 0:4]), [P1B], [smB["rden"]])
            S.op("dve", lambda e: e.tensor_tensor(ogb[:rows, :].rearrange("p (h d) -> p h d", h=4), og[:rows, :].rearrange("p (h d) -> p h d", h=4),
                                                  rden[:rows, 0:4].unsqueeze(2).to_broadcast([rows, 4, 256]), ALU.mult), [ogB, smB["rden"]], [junkB])
            pt, ptB = getpt()
            for k in range(8):
                S.op("pe", lambda e, k=k: e.transpose(pt[:, k * 128:k * 128 + rows], ogb[:rows, k * 128:(k + 1) * 128], ident[:rows, :rows]), [junkB, cB], [ptB])
            S.op("act", lambda e: e.copy(HT[:, :, i * 128:i * 128 + rows], pt.rearrange("p (k t) -> p k t", k=8)[:, :, 0:rows]), [ptB], [HTb[i]])
        rr["ptn"] = 2
        S.barrier()
        linear_residual(W["mem_wo"][l], None, tiles, 1.0)
        lim["stg"] = 3

    NCS = NQ + 1 + 4
    kc_s = dscr("kc_s", [NCS, 128, 1024]); vc_s = dscr("vc_s", [NCS, 128, 1024]); qc_s = dscr("qc_s", [NOWN + 1, 128, 1024])
    kcB = [Buf() for _ in range(NCS)]; vcB = [Buf() for _ in range(NCS)]; qcB = [Buf() for _ in range(NOWN + 1)]
    tabx_t = nc.dram_tensor("tabx", [16, 513], F32)
    tabx = tabx_t.ap(); tabxB = Buf()
    negb = sb("negb", [128, 16]); negbB = Buf()
    mask4b = sb("mask4b", [128, 128], BF16)
    S.op("dve", lambda e: e.tensor_copy(mask4b, cst[:, 648:776]), [cstB], [cB])

    def norm4(src, ppB, rows, g):
        a, b = nxt("wk", 4), nxt("wk", 4)
        s_ = nxt("st", 4)
        S.op("act", lambda e: e.copy(wk[b][:rows, :], src), [ppB], [wkB[b]])
        S.op("dve", lambda e: e.tensor_tensor(wk[a][:rows, :], wk[b][:rows, :], wk[b][:rows, :], ALU.mult), [wkB[b]], [wkB[a]])
        S.op("dve", lambda e: e.tensor_reduce(st[s_][:rows, 0:4], wk[a][:rows, :].rearrange("p (h d) -> p h d", h=4), AX.X, ALU.add), [wkB[a]], [stB[s_]])
        S.op("dve", lambda e: e.tensor_scalar(st[s_][:rows, 0:4], st[s_][:rows, 0:4], 1.0 / 64, EPS, ALU.mult, ALU.add), [stB[s_]], [stB[s_]])
        S.op("act", lambda e: e.activation(st[s_][:rows, 0:4], st[s_][:rows, 0:4], AF.Ln), [stB[s_]], [stB[s_]])
        S.op("act", lambda e: e.activation(st[s_][:rows, 0:4], st[s_][:rows, 0:4], AF.Exp, scale=-0.5), [stB[s_]], [stB[s_]])
        w3a = wk[a][:rows, :].rearrange("p (h d) -> p h d", h=4)
        w3b = wk[b][:rows, :].rearrange("p (h d) -> p h d", h=4)
        S.op("dve", lambda e: e.tensor_tensor(w3a, w3b, st[s_][:rows, 0:4].unsqueeze(2).to_broadcast([rows, 4, 64]), ALU.mult), [wkB[b], stB[s_]], [wkB[a]])
        S.op("dve", lambda e: e.tensor_tensor(w3b, w3a, g[:rows, :].unsqueeze(1).to_broadcast([rows, 4, 64]), ALU.mult), [wkB[a], gB], [wkB[b]])
        return wk[b], wkB[b]

    def proj_c(tiles, us):
        gcol = (GID["mix_g"] + 1) * 8
        for pc in range(12):
            ws = pc % 2
            kind, hp = ("cq", "ck", "cv")[pc // 4], pc % 4
            load_w(Wb[ws][:, 0, :], WbB[ws][0], W["c_w_in"][0][:, pc * 256:(pc + 1) * 256], gcol, 8, 256)
            w3 = Wb[ws][:, 0, :].rearrange("p (k f) -> p k f", k=8)
            csl = slice(hp * 256, (hp + 1) * 256)
            for (i, rows), u in zip(tiles, us):
                smp = (u == SMP)
                if kind == "cq" and not (u < NOWN or smp):
                    continue
                ui = NQ if smp else u
                qi = NOWN if smp else u
                pp, ppB = getps()
                for k in range(8):
                    S.op("pe", lambda e, k=k: e.matmul(pp[:rows, 0:256], HT[:, k, i * 128:i * 128 + rows], w3[:, k, :], start=(k == 0), stop=(k == 7)),
                         [HTb[i], WbB[ws][0]], [ppB])
                src = pp[:rows, 0:256]
                if kind in ("cq", "ck"):
                    fin, finB = norm4(src, ppB, rows, g64["c_gq" if kind == "cq" else "c_gk"])
                    if kind == "ck" and (u < 4 or smp):
                        dst = o_cks[496:512, csl] if smp else o_ck[u][:, csl]
                        S.op("pool", lambda e, dst=dst: e.dma_start(out=dst[:rows], in_=fin[:rows, :]), [finB], [], dma=True)
                    c = nxt("wkb", 2)
                    S.op("pool", lambda e: e.tensor_copy(wkb[c][:rows, 0:256], fin[:rows, :]), [finB], [wkbB[c]])
                    if kind == "ck":
                        to_T_and_store(wkb[c], wkbB[c], rows, kc_s[ui].rearrange("p (k t) -> p k t", k=8)[:, hp * 2:hp * 2 + 2, 0:rows], kcB[ui])
                    else:
                        to_T_and_store(wkb[c], wkbB[c], rows, qc_s[qi].rearrange("p (k t) -> p k t", k=8)[:, hp * 2:hp * 2 + 2, 0:rows], qcB[qi])
                else:
                    a = nxt("wk", 4)
                    S.op("act", lambda e: e.copy(wk[a][:rows, :], src), [ppB], [wkB[a]])
                    if u < 4 or smp:
                        dst = o_cvs[496:512, csl] if smp else o_cv[u][:, csl]
                        S.op("pool", lambda e, dst=dst: e.dma_start(out=dst[:rows], in_=wk[a][:rows, :]), [wkB[a]], [], dma=True)
                    c = nxt("wkb", 2)
                    S.op("dve", lambda e: e.tensor_scalar(wkb[c][:rows, 0:256], wk[a][:rows, :], valid[:rows, u:u + 1], None, ALU.mult), [wkB[a], cB], [wkbB[c]])
                    S.op("pool", lambda e: e.dma_start(out=vc_s[ui][:rows, csl], in_=wkb[c][:rows, 0:256]), [wkbB[c]], [vcB[ui]], dma=True)

    def cache_prep_c():
        S.op("sp", lambda e: e.dma_start(out=o_cks[0:496, :], in_=cc_k[16:512, :]), [], [], dma=True)
        S.op("sp", lambda e: e.dma_start(out=o_cvs[0:496, :], in_=cc_v[16:512, :]), [], [], dma=True)
        for t in range(4):
            ci = NQ + 1 + t
            s_ = nxt("stg", 3)
            S.op("sp", lambda e, t=t: e.dma_start(out=stg[s_][:, 0:1024], in_=cc_k[t * 128:(t + 1) * 128, :]), [], [stgB[s_]], dma=True)
            S.op("pool", lambda e: e.tensor_copy(xn[0], stg[s_][:, 0:1024]), [stgB[s_]], [xnB[0]])
            pt, ptB = getpt()
            for k in range(8):
                S.op("pe", lambda e, k=k: e.transpose(pt[:, k * 128:(k + 1) * 128], xn[0][:, k * 128:(k + 1) * 128], ident), [xnB[0], cB], [ptB])
            S.op("act", lambda e: e.copy(xn[1], pt), [ptB], [xnB[1]])
            S.op("pool", lambda e, ci=ci: e.dma_start(out=kc_s[ci], in_=xn[1]), [xnB[1]], [kcB[ci]], dma=True)
            s_ = nxt("stg", 3)
            S.op("sp", lambda e, t=t: e.dma_start(out=stg[s_][:, 0:1024], in_=cc_v[t * 128:(t + 1) * 128, :]), [], [stgB[s_]], dma=True)
            S.op("pool", lambda e: e.tensor_copy(xn[0], stg[s_][:, 0:1024]), [stgB[s_]], [xnB[0]])
            S.op("pool", lambda e, ci=ci: e.dma_start(out=vc_s[ci], in_=xn[0]), [xnB[0]], [vcB[ci]], dma=True)

    def band_attn(tiles):
        EB = [av(1024 * d, 1024, BF16).rearrange("p (h q) -> p h q", h=16) for d in range(2)]; EBB = Buf()
        QTe = av(2048, 256, BF16).rearrange("p (k t) -> p k t", k=4); QTo = av(2304, 256, BF16).rearrange("p (k t) -> p k t", k=4); QTB = Buf()
        QTeo = (QTe, QTo)
        kvc = [av(2560 + 512 * i, 512, BF16) for i in range(5)]; kvcB = [Buf() for _ in range(5)]
        Ec = [av(5120 + 512 * i, 512, BF16) for i in range(5)]; EcB = [[Buf(), Buf()] for _ in range(5)]
        ogc = av(7680, 512); ogcB = Buf()
        mgc = av(8192, 512, BF16); mgcB = Buf()
        S.op("sp", lambda e: e.dma_start(out=tabx[:, 0:257], in_=W["c_bias"][0]), [], [tabxB], dma=True)
        S.op("sp", lambda e: e.dma_start(out=stg[2][0:16, 256:257], in_=W["c_bias"][0][:, 256:257], allow_slow_non_contiguous=True), [], [stgB[2]], dma=True)
        S.op("dve", lambda e: e.tensor_copy(stg[2][0:16, 0:256], stg[2][0:16, 256:257].to_broadcast([16, 256])), [stgB[2]], [stgB[2]])
        S.op("sp", lambda e: e.dma_start(out=tabx[:, 257:513], in_=stg[2][0:16, 0:256]), [stgB[2]], [tabxB], dma=True)
        for h in range(16):
            S.op("sp", lambda e, h=h: e.dma_start(out=negb[:, h:h + 1], in_=tabx[h:h + 1, 300:301].partition_broadcast(128)), [tabxB], [negbB], dma=True)
        S.op("dve", lambda e: e.tensor_scalar(negb, negb, -1.0, None, ALU.mult), [negbB], [negbB])
        for h in range(16):
            s_ = nxt("stg", 2)
            for d in range(2):
                S.op("sp", lambda e, h=h, d=d: e.dma_start(out=stg[s_][:, d * 128:(d + 1) * 128], in_=bass.AP(tabx_t, h * 513 + 1 + 128 * d, [[1, 128], [1, 128]])),
                     [tabxB], [stgB[s_]], dma=True)
            pp, ppB = getps()
            S.op("pe", lambda e: e.matmul(pp[:, 0:256], cst[:, 520:648], stg[s_][:, 0:256], start=True, stop=True), [stgB[s_], cstB], [ppB])
            for d in range(2):
                S.op("act", lambda e, h=h, d=d: e.activation(EB[d][:, h, :], pp[:, d * 128:(d + 1) * 128], AF.Exp, bias=negb[:, h:h + 1]), [ppB, negbB], [EBB])
        S.op("pool", lambda e: e.tensor_tensor(EB[0], EB[0], maskA.unsqueeze(1).to_broadcast([128, 16, 128]), ALU.mult), [EBB, cB], [EBB])
        S.barrier()
        S.op("pool", lambda e: e.memset(QTe[64:128, :, :], 0.0), [], [QTB])
        S.op("pool", lambda e: e.memset(QTo[0:64, :, :], 0.0), [], [QTB])
        zc = {"z": 0, "b": 0}
        for (i, nq) in tiles:
            smp = (i == NQ)
            qi = NOWN if smp else i
            if smp:
                keyspec = [(NQ, 0)] + [(NQ + 1 + t, 1 if t == 3 else 2) for t in (3, 2, 1, 0)]
            else:
                keyspec = [(i + d, typ) for d, typ in zip(range(5), (0, 1, 2, 2, 4))]
            for half in range(2):
                src = qc_s[qi].rearrange("p (k t) -> p k t", k=8)[:, 4 * half:4 * half + 4, 0:nq]
                S.op("sp", lambda e, src=src: e.dma_start(out=QTe[0:64, :, 0:nq], in_=src[0:64]), [qcB[qi]], [QTB], dma=True)
                S.op("sp", lambda e, src=src: e.dma_start(out=QTo[64:128, :, 0:nq], in_=src[64:128]), [qcB[qi]], [QTB], dma=True)
                nkeys = len(keyspec)
                for d, (ui, typ) in enumerate(keyspec):
                    nk = 16 if (smp and d == 0) else 128
                    b = d
                    Kh = kvc[b][:, 0:512].rearrange("p (k t) -> p k t", k=4)
                    Vh = kvc[b][:, 512:1024]
                    ks = kc_s[ui].rearrange("p (k t) -> p k t", k=8)[:, 4 * half:4 * half + 4, 0:nk]
                    S.op("sp", lambda e, Kh=Kh, ks=ks, nk=nk: e.dma_start(out=Kh[:, :, 0:nk], in_=ks), [kcB[ui]], [kvcB[b]], dma=True)
                    S.op("sp", lambda e, Vh=Vh, ui=ui, nk=nk: e.dma_start(out=Vh[:nk, :], in_=vc_s[ui][:nk, half * 512:(half + 1) * 512]), [vcB[ui]], [kvcB[b]], dma=True)
                    zi = zc["z"]; zc["z"] = 1 - zi
                    z = Q[zi]; zB_ = [PSB[2 * zi], PSB[2 * zi + 1]]
                    for m in range(8):
                        S.op("pe", lambda e, m=m: e.matmul(z[:nk, m * nq:(m + 1) * nq], Kh[:, m // 2, 0:nk], QTeo[m % 2][:, m // 2, 0:nq], start=True, stop=True),
                             [kvcB[b], QTB], zB_)
                    E = Ec[d]
                    if 8 * nq > 512:
                        for hh in range(2):
                            S.op("act", lambda e, hh=hh, E=E: e.activation(E[:nk, hh * 512:(hh + 1) * 512], z[:nk, hh * 512:(hh + 1) * 512], AF.Exp), [zB_[hh]], [EcB[d][hh]])
                    else:
                        S.op("act", lambda e, E=E: e.activation(E[:nk, 0:8 * nq], z[:nk, 0:8 * nq], AF.Exp), zB_, EcB[d])
                    e3 = E[:nk, 0:8 * nq].rearrange("p (m q) -> p m q", m=8)
                    if typ in (0, 1):
                        S.op("pool", lambda e, typ=typ, e3=e3, nk=nk: e.tensor_tensor(e3, e3, EB[typ][:nk, 8 * half:8 * half + 8, 0:nq], ALU.mult), EcB[d] + [EBB], EcB[d])
                    elif typ == 4:
                        S.op("pool", lambda e, e3=e3, nk=nk: e.tensor_tensor(e3, e3, mask4b[:nk, :nq].unsqueeze(1).to_broadcast([nk, 8, nq]), ALU.mult), EcB[d] + [cB], EcB[d])
                for m in range(8):
                    for d, (ui, typ) in enumerate(keyspec):
                        nk = 16 if (smp and d == 0) else 128
                        S.op("pe", lambda e, m=m, d=d, nk=nk: e.matmul(Q[2][:nq, m * 64:(m + 1) * 64], Ec[d][:nk, m * nq:(m + 1) * nq], kvc[d][:nk, 512 + m * 64:512 + (m + 1) * 64],
                                                                       start=(d == 0), stop=(d == nkeys - 1)), EcB[d] + [kvcB[d]], [PSB[4]])
                for m in range(8):
                    for d, (ui, typ) in enumerate(keyspec):
                        nk = 16 if (smp and d == 0) else 128
                        if smp:
                            vcol = validb[:nk, SMP:SMP + 1] if d == 0 else ones[:nk, 0:1]
                        else:
                            vcol = validb[:nk, ui:ui + 1]
                        S.op("pe", lambda e, m=m, d=d, nk=nk, vcol=vcol: e.matmul(P1[:nq, m:m + 1], Ec[d][:nk, m * nq:(m + 1) * nq], vcol, start=(d == 0), stop=(d == nkeys - 1)),
                             EcB[d] + [cB], [P1B])
                S.op("act", lambda e: e.copy(ogc[:nq, :], Q[2][:nq, 0:512]), [PSB[4]], [ogcB])
                rden = small["rden"]
                S.op("dve", lambda e: e.tensor_scalar(rden[:nq, :], P1[:nq, 0:8], 1e-30, None, ALU.add), [P1B], [smB["rden"]])
                S.op("dve", lambda e: e.reciprocal(rden[:nq, :], rden[:nq, :]), [smB["rden"]], [smB["rden"]])
                S.op("dve", lambda e: e.tensor_tensor(mgc[:nq, half * 512:(half + 1) * 512].rearrange("p (h d) -> p h d", h=8), ogc[:nq, :].rearrange("p (h d) -> p h d", h=8),
                                                      rden[:nq, :].unsqueeze(2).to_broadcast([nq, 8, 64]), ALU.mult), [ogcB, smB["rden"]], [mgcB])
            pt, ptB = getpt()
            for k in range(8):
                S.op("pe", lambda e, k=k: e.transpose(pt[:, k * 128:k * 128 + nq], mgc[:nq, k * 128:(k + 1) * 128], ident[:nq, :nq]), [mgcB, cB], [ptB])
            S.op("act", lambda e: e.copy(HT[:, :, i * 128:i * 128 + nq], pt.rearrange("p (k t) -> p k t", k=8)[:, :, 0:nq]), [ptB], [HTb[i]])

    nsb = int(os.environ.get("KNSB", "99"))
    older = [list(range(a, min(a + 16, NT))) for a in range(NQ, NT, 16)]
    if STAGE >= 2:
        cache_prep()
        lambda_prep()
    for sbk in older[:nsb]:
        front0(sbk, False)
    tiles, uu = front0(list(range(NQ)), True)

    if STAGE >= 2:
        S.barrier()
        rr["ptn"] = 1
        if not os.environ.get("KNOATT"):
            attn_l0()
        S.barrier()
        rr["ptn"] = 2
        linear_residual(W["ab_w_out"][0], None, tiles, 1.0)
    if STAGE >= 3:
        S.barrier()
        mem_prep(0)
        S.barrier()
        mem_attn(0, tiles)
    if STAGE >= 4:
        S.barrier()
        ffn(0, 2, tiles)

    tiles17 = [(i, 128) for i in range(NOWN)] + [(NQ, 16)]
    if STAGE >= 5:
        S.barrier()
        ffn(1, 1, tiles)
    if STAGE >= 6:
        rstd_tiles(tiles)
        for (i, rows) in tiles:
            norm_to_HT(i, rows)
        cache_prep_c()
        proj_c(tiles, uu)
        S.barrier()
        rr["ptn"] = 1
        band_attn(tiles17)
        rr["ptn"] = 2
        S.barrier()
        linear_residual(W["c_w_out"][0], None, tiles17, 1.0)
    if STAGE >= 7:
        S.barrier()
        mem_prep(1)
        S.barrier()
        mem_attn(1, tiles17)
    if STAGE >= 8:
        S.barrier()
        ffn(1, 2, tiles17)
    if True:
        for i in range(NOWN):
            S.op("pool", lambda e, i=i: e.dma_start(out=o_y[i], in_=X[:, i, :]), [Xb[i]], [], dma=True)
        S.op("pool", lambda e: e.dma_start(out=o_ys, in_=X[0:16, NQ, :]), [Xb[NQ]], [], dma=True)

    S.barrier()
    print("ops", S.nops, "sems", S.nsem, "sbuf_left", nc.sbuf_bytes_remaining)
    return nc


_NC = None


def _consts():
    c = np.zeros((128, 776), np.float32)
    c[:, 520:648] = np.eye(128)[::-1]
    c[:, 648:776] = ((np.arange(128)[None, :] // 64) <= (np.arange(128)[:, None] // 64)).astype(np.float32)
    c[:, 512:520] = ((500000.0 ** (-2.0 * np.arange(8) / 16.0)).astype(np.float32).astype(np.float64) / (2 * np.pi))[None]
    c[:, 0:128] = np.eye(128)
    j = np.arange(128)[:, None]; s = np.arange(128)[None, :]
    c[:, 128:256] = -(j >= s).astype(np.float32)
    c[:, 256:384] = ((j // 64) <= (s // 64)).astype(np.float32)
    c[:, 384:512] = (j < s).astype(np.float32)
    return c


def kernel(**inp):
    global _NC
    if _NC is None:
        _NC = build()
    nc = _NC
    f = lambda a: np.ascontiguousarray(np.asarray(a, dtype=np.float32))
    xprompt = f(inp["x_prompt"])[0].reshape(128, 128, D)
    in_maps = []
    wnames = ["ffn1_g", "ffn1_wg", "ffn1_wu", "ffn1_wd", "ffn2_g", "ffn2_wg", "ffn2_wu", "ffn2_wd", "mix_g", "ab_w_in", "ab_w_out",
              "a_gq", "a_gk", "a_lq1", "a_lk1", "a_lq2", "a_lk2", "a_subln_g", "c_w_in", "c_w_out", "c_gq", "c_gk", "c_bias",
              "mem_g_x", "mem_g_m", "mem_wq", "mem_wk", "mem_wv", "mem_wo", "mem_gq", "mem_gk"]
    wts = {n: f(inp[n]) for n in wnames}
    cst = _consts()
    for c in range(8):
        xpc = np.zeros((NT, 128, D), np.float32)
        pos = np.zeros((128, NT + 1), np.float32)
        val = np.zeros((128, NT + 1), np.float32)
        for u in range(NT):
            g = 16 * c + 15 - u
            if g >= 0:
                xpc[u] = xprompt[g]
                pos[:, u] = g * 128 + np.arange(128)
                val[:, u] = 1.0
        pos[:16, NT] = 1024 + np.arange(16)
        val[:16, NT] = 1.0
        m = {"xp": xpc, "xs": f(inp["x_sample"])[c], "pos": pos, "valid": val, "consts": cst,
             "ca_k": f(inp["cache_a_k"])[0, c].reshape(1024, 512), "ca_v": f(inp["cache_a_v"])[0, c].reshape(1024, 512),
             "cb_k": f(inp["cache_b_k"])[0, c].reshape(1024, 512), "cb_v": f(inp["cache_b_v"])[0, c].reshape(1024, 512),
             "cc_k": f(inp["cache_c_k"])[0, c].reshape(512, 1024), "cc_v": f(inp["cache_c_v"])[0, c].reshape(512, 1024),
             "cm_k": f(inp["cache_mem_k"])[:, c].reshape(2, 256, 1024), "cm_v": f(inp["cache_mem_v"])[:, c].reshape(2, 256, 1024),
             "memp": f(inp["mem_prompt"])[0]}
        m.update(wts)
        in_maps.append(m)
    res = run_bass_kernel_spmd(nc, in_maps, core_ids=list(range(8))).results

    def gat(name, width):
        out = np.zeros((128, 128, width), np.float32)
        for c in range(8):
            for u in range(NOWN):
                out[16 * c + 15 - u] = res[c][name][u]
        return out.reshape(16384, width)
    y_prompt = gat("o_y", D)[None]
    y_sample = np.stack([res[c]["o_ys"] for c in range(8)])
    a_k_p = gat("o_ak", 512).reshape(1, 1, 16384, 8, 64)
    a_v_p = gat("o_av", 512).reshape(1, 1, 16384, 4, 128)
    b_k_p = gat("o_bk", 512).reshape(1, 1, 16384, 8, 64)
    b_v_p = gat("o_bv", 512).reshape(1, 1, 16384, 8, 64)
    c_k_p = np.concatenate([res[7]["o_ck"][3 - t] for t in range(4)], 0).reshape(1, 1, 512, 16, 64)
    c_v_p = np.concatenate([res[7]["o_cv"][3 - t] for t in range(4)], 0).reshape(1, 1, 512, 16, 64)
    mem_k_p = res[0]["o_mk"].reshape(2, 1, 256, 4, 256)
    mem_v_p = res[0]["o_mv"].reshape(2, 1, 256, 4, 256)
    st = lambda n, shp: np.stack([res[c][n] for c in range(8)]).reshape(shp)
    a_k_s = st("o_aks", (1, 8, 16, 8, 64)); a_v_s = st("o_avs", (1, 8, 16, 4, 128))
    b_k_s = st("o_bks", (1, 8, 16, 8, 64)); b_v_s = st("o_bvs", (1, 8, 16, 8, 64))
    c_k_s = st("o_cks", (1, 8, 512, 16, 64)); c_v_s = st("o_cvs", (1, 8, 512, 16, 64))
    return (y_prompt, y_sample, a_k_p, a_v_p, b_k_p, b_v_p, c_k_p, c_v_p, mem_k_p, mem_v_p,
            a_k_s, a_v_s, b_k_s, b_v_s, c_k_s, c_v_s)
```

```python
import os
import math
import numpy as np
import concourse.bass as bass
import concourse.mybir as mybir
from concourse.bass_utils import run_bass_kernel_spmd

F32, BF16 = mybir.dt.float32, mybir.dt.bfloat16
ALU = mybir.AluOpType
AF = mybir.ActivationFunctionType
AX = mybir.AxisListType

D = 1024
FF = 2816
NT = 128
NQ = 20
NOWN = 16
SMP = 128
EPS = 1e-6
NDMA = 24
ROT = 30000
STAGE = int(os.environ.get("KSTAGE", "9"))


class Buf:
    __slots__ = ("w", "r")

    def __init__(self):
        self.w = None
        self.r = {}


class Sched:
    def __init__(self, nc):
        self.nc = nc
        self.E = {"pe": nc.tensor, "act": nc.scalar, "dve": nc.vector, "pool": nc.gpsimd, "sp": nc.sync}
        self.sem, self.cnt, self.nsem = {}, {}, 0
        self.seen = {e: {} for e in self.E}
        self.allsems = []
        for e in self.E:
            self._rot(e)
        self.dsem = [nc.alloc_semaphore(f"dq{i}") for i in range(NDMA)]
        self.dcnt = [0] * NDMA
        self.dnext = 0
        self.nops = 0

    def _rot(self, e):
        self.sem[e] = self.nc.alloc_semaphore(f"s{e}{self.nsem}")
        self.nsem += 1
        self.cnt[e] = 0
        self.allsems.append([self.sem[e], 0])

    def _wait(self, e, tok):
        sem, val = tok[1], tok[2]
        if self.seen[e].get(sem.num, 0) >= val:
            return
        self.E[e].wait_ge(sem, val)
        self.seen[e][sem.num] = val

    def op(self, e, fn, reads=(), writes=(), dma=False):
        for b in reads:
            if b.w is not None:
                t = b.w
                if not (t[0] == e and e == "pe" and not t[3]):
                    self._wait(e, t)
        for b in writes:
            for t in ([b.w] if b.w is not None else []) + list(b.r.values()):
                if t[0] == e and not t[3]:
                    continue
                self._wait(e, t)
        if dma:
            i = self.dnext
            self.dnext = (i + 1) % NDMA
            if self.dcnt[i] > 0:
                self._wait(e, (e, self.dsem[i], 16 * self.dcnt[i], True))
            ins = fn(self.E[e])
            self.dcnt[i] += 1
            ins.then_inc(self.dsem[i], 16)
            tok = (e, self.dsem[i], 16 * self.dcnt[i], True)
        else:
            if self.cnt[e] >= ROT:
                self._rot(e)
            ins = fn(self.E[e])
            self.cnt[e] += 1
            ins.then_inc(self.sem[e], 1)
            tok = (e, self.sem[e], self.cnt[e], False)
            for s in self.allsems:
                if s[0] is self.sem[e]:
                    s[1] = self.cnt[e]
        for b in reads:
            b.r[tok[1].num] = tok
        for b in writes:
            b.w = tok
            b.r = {}
        self.nops += 1
        return tok

    def barrier(self):
        for e in self.E:
            for s, v in self.allsems:
                if v > 0:
                    self._wait(e, (None, s, v, False))
            for i in range(NDMA):
                if self.dcnt[i] > 0:
                    self._wait(e, (None, self.dsem[i], 16 * self.dcnt[i], True))


def build():
    nc = bass.Bass("TRN2", target_bir_lowering=False)
    S = Sched(nc)

    def din(name, shape):
        return nc.dram_tensor(name, list(shape), F32, kind="ExternalInput").ap()

    def dout(name, shape):
        return nc.dram_tensor(name, list(shape), F32, kind="ExternalOutput").ap()

    def sb(name, shape, dt=F32):
        return nc.alloc_sbuf_tensor("sb_" + name, list(shape), dt).ap()

    xp = din("xp", [NT, 128, D])
    xs = din("xs", [16, D])
    posd = din("pos", [128, NT + 1])
    validd = din("valid", [128, NT + 1])
    constd = din("consts", [128, 776])
    ca_k = din("ca_k", [1024, 512]); ca_v = din("ca_v", [1024, 512])
    cb_k = din("cb_k", [1024, 512]); cb_v = din("cb_v", [1024, 512])
    cc_k = din("cc_k", [512, 1024]); cc_v = din("cc_v", [512, 1024])
    cm_k = din("cm_k", [2, 256, 1024]); cm_v = din("cm_v", [2, 256, 1024])
    memp = din("memp", [256, 1024])
    W = {}
    for nm, shp in [("ffn1_g", [2, D]), ("ffn1_wg", [2, D, FF]), ("ffn1_wu", [2, D, FF]), ("ffn1_wd", [2, FF, D]),
                    ("ffn2_g", [2, D]), ("ffn2_wg", [2, D, FF]), ("ffn2_wu", [2, D, FF]), ("ffn2_wd", [2, FF, D]),
                    ("mix_g", [2, D]), ("ab_w_in", [1, D, 3072]), ("ab_w_out", [1, D, D]),
                    ("a_gq", [1, 64]), ("a_gk", [1, 64]), ("a_lq1", [1, 64]), ("a_lk1", [1, 64]),
                    ("a_lq2", [1, 64]), ("a_lk2", [1, 64]), ("a_subln_g", [1, 128]),
                    ("c_w_in", [1, D, 3072]), ("c_w_out", [1, D, D]), ("c_gq", [1, 64]), ("c_gk", [1, 64]),
                    ("c_bias", [1, 16, 257]), ("mem_g_x", [2, D]), ("mem_g_m", [2, D]),
                    ("mem_wq", [2, D, D]), ("mem_wk", [2, D, D]), ("mem_wv", [2, D, D]), ("mem_wo", [2, D, D]),
                    ("mem_gq", [2, 256]), ("mem_gk", [2, 256])]:
        W[nm] = din(nm, shp)

    o_y = dout("o_y", [NOWN, 128, D]); o_ys = dout("o_ys", [16, D])
    o_ak = dout("o_ak", [NOWN, 128, 512]); o_av = dout("o_av", [NOWN, 128, 512])
    o_bk = dout("o_bk", [NOWN, 128, 512]); o_bv = dout("o_bv", [NOWN, 128, 512])
    o_ck = dout("o_ck", [4, 128, D]); o_cv = dout("o_cv", [4, 128, D])
    o_mk = dout("o_mk", [2, 256, D]); o_mv = dout("o_mv", [2, 256, D])
    o_aks = dout("o_aks", [16, 512]); o_avs = dout("o_avs", [16, 512])
    o_bks = dout("o_bks", [16, 512]); o_bvs = dout("o_bvs", [16, 512])
    o_cks = dout("o_cks", [512, D]); o_cvs = dout("o_cvs", [512, D])

    def dscr(name, shape):
        return nc.dram_tensor(name, list(shape), BF16).ap()
    kta_s = dscr("kta_s", [NT + 9, 128, 512]); ktb_s = dscr("ktb_s", [NT + 9, 128, 512])
    va_s = dscr("va_s", [NT + 9, 128, 520]); vb_s = dscr("vb_s", [NT + 9, 128, 512])
    qta_s = dscr("qta_s", [NQ + 1, 128, 512]); qtb_s = dscr("qtb_s", [NQ + 1, 128, 512])
    scrB = {"kta": [Buf() for _ in range(NT + 9)], "ktb": [Buf() for _ in range(NT + 9)],
            "va": [Buf() for _ in range(NT + 9)], "vb": [Buf() for _ in range(NT + 9)],
            "qta": [Buf() for _ in range(NQ + 1)], "qtb": [Buf() for _ in range(NQ + 1)]}
    outB = Buf()
    inB = Buf()

    NX = NQ + 1
    X = sb("X", [128, NX, D])
    Xb = [Buf() for _ in range(NX)]
    HT = sb("HT", [128, 8, NQ * 128 + 16], BF16)
    HTb = [Buf() for _ in range(NX)]
    cst = sb("cst", [128, 776]); cstB = Buf()
    ident = sb("ident", [128, 128], BF16); negU = sb("negU", [128, 128], BF16)
    maskA = sb("maskA", [128, 128], BF16); maskB = sb("maskB", [128, 128], BF16)
    ones = sb("ones", [128, 1], BF16)
    cB = Buf()
    pos = sb("pos", [128, NT + 1]); valid = sb("valid", [128, NT + 1]); validb = sb("validb", [128, NT + 1], BF16)
    gT = sb("gT", [128, 80]); gTB = Buf()
    g64 = {n: sb("g_" + n, [128, 64]) for n in ("a_gq", "a_gk", "c_gq", "c_gk")}
    gB = Buf()

    S.op("sp", lambda e: e.dma_start(out=cst, in_=constd), [], [cstB], dma=True)
    S.op("sp", lambda e: e.dma_start(out=pos, in_=posd), [], [cB], dma=True)
    S.op("sp", lambda e: e.dma_start(out=valid, in_=validd), [], [cB], dma=True)
    S.op("dve", lambda e: e.tensor_copy(ident, cst[:, 0:128]), [cstB], [cB])
    S.op("dve", lambda e: e.tensor_copy(negU, cst[:, 128:256]), [cstB], [cB])
    S.op("dve", lambda e: e.tensor_copy(maskA, cst[:, 256:384]), [cstB], [cB])
    S.op("dve", lambda e: e.tensor_copy(maskB, cst[:, 384:512]), [cstB], [cB])
    S.op("dve", lambda e: e.memset(ones, 1.0), [], [cB])
    S.op("dve", lambda e: e.tensor_copy(validb, valid), [cB], [cB])
    for n in g64:
        S.op("sp", lambda e, n=n: e.dma_start(out=g64[n], in_=W[n][0:1, :].partition_broadcast(128)), [], [gB], dma=True)
    for n in ("a_gq", "c_gq"):
        S.op("dve", lambda e, n=n: e.tensor_scalar(g64[n], g64[n], 0.125, None, ALU.mult), [gB], [gB])

    Q = [nc.alloc_psum_tensor(f"q{i}", [128, 1024], F32).ap() for i in range(4)]
    PS = [Q[i // 2][:, (i % 2) * 512:(i % 2 + 1) * 512] for i in range(6)]
    PSB = [Buf() for _ in range(6)]
    PT = [Q[3][:, h * 512:(h + 1) * 512].bitcast(BF16) for h in range(2)]
    PTB = [Buf() for _ in range(2)]
    P1 = Q[3][:, 0:512]
    P1B = PTB[0]
    rr = {"ps": 0, "pt": 0, "ptn": 2}

    def getps():
        i = rr["ps"]; rr["ps"] = (i + 1) % 6
        return PS[i], PSB[i]

    def getpt():
        if rr["ptn"] == 1:
            return PT[1], PTB[1]
        i = rr["pt"]; rr["pt"] = (i + 1) % 2
        return PT[i], PTB[i]

    GID = {"ffn1_g": 0, "ffn2_g": 2, "mix_g": 4, "mem_g_x": 6, "mem_g_m": 8}
    graw = sb("graw", [80, 128]); grawB = Buf()
    for n, gi in GID.items():
        S.op("sp", lambda e, n=n, gi=gi: e.dma_start(out=graw[gi * 8:(gi + 2) * 8, :],
                                                     in_=W[n].rearrange("l (k p) -> (l k) p", p=128)), [], [grawB], dma=True)
    identf = cst[:, 0:128]
    pg, pgB = getps()
    S.op("pe", lambda e: e.transpose(pg[:, 0:80], graw[0:80, :], identf[0:80, 0:80]), [grawB, cstB], [pgB])
    S.op("dve", lambda e: e.tensor_copy(gT, pg[:, 0:80]), [pgB], [gTB])

    ARENA = 15360
    arena = sb("arena", [128, ARENA])

    def av(off, n, dt=F32):
        v = arena[:, off:off + n]
        return v if dt == F32 else v.bitcast(dt)
    Wb = [av(3072 * i, 3072, BF16).rearrange("p (a f) -> p a f", a=3) for i in range(2)]
    WbB = [[Buf() for _ in range(3)] for _ in range(2)]
    stg = [av(6144 + 2048 * i, 2048) for i in range(3)]
    stgB = [Buf() for _ in range(3)]
    hid = [av(12288 + 512 * i, 512, BF16).rearrange("p (j t) -> p j t", j=2) for i in range(2)]
    hidB = [Buf() for _ in range(2)]
    sgt = [av(13312 + 512 * i, 512) for i in range(2)]
    sgB = [Buf() for _ in range(2)]
    xn = [av(14336 + 512 * i, 512, BF16) for i in range(2)]
    xnB = [Buf() for _ in range(2)]
    junk = sb("junk", [128, D], BF16); junkB = Buf()
    st = [sb(f"st{i}", [128, 8]) for i in range(4)]
    stB = [Buf() for _ in range(4)]
    cosT = sb("cosT", [128, NX, 8]); sinT = sb("sinT", [128, NX, 8]); csB = Buf()
    wk = [sb(f"wk{i}", [128, 256]) for i in range(4)]
    wkB = [Buf() for _ in range(4)]
    wkb = [sb(f"wkb{i}", [128, 264], BF16) for i in range(2)]
    wkbB = [Buf() for _ in range(2)]
    ktile = [sb(f"ktile{i}", [128, 2, 128], BF16) for i in range(2)]
    ktB = [Buf() for _ in range(2)]
    rp = [sb(f"rp{i}", [128, 4, 8]) for i in range(4)]
    rpB = [Buf() for _ in range(4)]
    cnt = {"stg": 0, "xn": 0, "st": 0, "wk": 0, "wkb": 0, "kt": 0, "hid": 0, "sg": 0}

    def nxt(k, n):
        i = cnt[k]; cnt[k] = (i + 1) % n
        return i

    rs = sb("rs", [128, NX]); rsB = Buf()

    def rstd_tiles(tiles):
        S.op("dve", lambda e: e.memset(rs, 0.0), [], [rsB])
        for (i, rows) in tiles:
            S.op("act", lambda e, i=i, rows=rows: e.activation(junk[:rows, :], X[:rows, i, :], AF.Square, accum_out=rs[:rows, i:i + 1]),
                 [Xb[i]], [junkB, rsB])
        S.op("dve", lambda e: e.tensor_scalar(rs, rs, 1.0 / D, EPS, ALU.mult, ALU.add), [rsB], [rsB])
        S.op("act", lambda e: e.activation(rs, rs, AF.Ln), [rsB], [rsB])
        S.op("act", lambda e: e.activation(rs, rs, AF.Exp, scale=-0.5), [rsB], [rsB])

    def norm_to_HT(i, rows):
        r, rB = rs[:, i:i + 1], rsB
        j = nxt("xn", 2)
        S.op("pool", lambda e: e.tensor_scalar(xn[j][:rows, :], X[:rows, i, :], r[:rows, 0:1], None, ALU.mult),
             [Xb[i], rB], [xnB[j]])
        pt, ptB = getpt()
        for k in range(8):
            S.op("pe", lambda e, k=k: e.transpose(pt[:, k * 128:k * 128 + rows], xn[j][:rows, k * 128:(k + 1) * 128],
                                                  ident[:rows, :rows]), [xnB[j], cB], [ptB])
        src = pt.rearrange("p (k t) -> p k t", k=8)[:, :, 0:rows]
        dst = HT[:, :, i * 128:i * 128 + rows]
        if i % 2 == 0:
            S.op("act", lambda e: e.copy(dst, src), [ptB], [HTb[i]])
        else:
            S.op("dve", lambda e: e.tensor_copy(dst, src), [ptB], [HTb[i]])

    def load_w(dst, dstB, src_ap, gcol, kparts, width):
        s = nxt("stg", lim["stg"])
        S.op("sp", lambda e: e.dma_start(out=stg[s][:, 0:kparts * width].rearrange("p (k f) -> p k f", k=kparts),
                                         in_=src_ap.rearrange("(k p) f -> p k f", p=128)), [], [stgB[s]], dma=True)
        d3 = dst.rearrange("p (k f) -> p k f", k=kparts)
        s3 = stg[s][:, 0:kparts * width].rearrange("p (k f) -> p k f", k=kparts)
        if gcol is None:
            S.op("pool", lambda e: e.tensor_copy(d3, s3), [stgB[s]], [dstB])
        else:
            S.op("pool", lambda e: e.tensor_tensor(d3, s3, gT[:, gcol:gcol + kparts].unsqueeze(2).to_broadcast([128, kparts, width]),
                                                   ALU.mult), [stgB[s], gTB], [dstB])

    def ffn(l, which, tiles):
        pre = f"ffn{which}_"
        gcol = (GID[pre + "g"] + l) * 8
        rstd_tiles(tiles)
        for (i, rows) in tiles:
            norm_to_HT(i, rows)
        blocks = [tiles[b:b + 4] for b in range(0, len(tiles), 4)]
        pending = []
        for fg in range(FF // 256):
            ws = fg % 2
            load_w(Wb[ws][:, 0, :], WbB[ws][0], W[pre + "wg"][l][:, fg * 256:(fg + 1) * 256], gcol, 8, 256)
            load_w(Wb[ws][:, 1, :], WbB[ws][1], W[pre + "wu"][l][:, fg * 256:(fg + 1) * 256], gcol, 8, 256)
            load_w(Wb[ws][:, 2, :], WbB[ws][2], W[pre + "wd"][l][fg * 256:(fg + 1) * 256, :], None, 2, 1024)
            wg3 = Wb[ws][:, 0, :].rearrange("p (k f) -> p k f", k=8)
            wu3 = Wb[ws][:, 1, :].rearrange("p (k f) -> p k f", k=8)
            wd3 = Wb[ws][:, 2, :].rearrange("p (k f) -> p k f", k=2)
            for blk in blocks:
                c0 = blk[0][0] * 128
                ncol = (blk[-1][0] - blk[0][0]) * 128 + blk[-1][1]
                hB = [HTb[i] for i, _ in blk]
                hi = nxt("hid", 2)
                for j in range(2):
                    pgt, pgtB = getps()
                    put, putB = getps()
                    for k in range(8):
                        S.op("pe", lambda e, k=k, j=j, pgt=pgt: e.matmul(pgt[:, 0:ncol], wg3[:, k, j * 128:(j + 1) * 128], HT[:, k, c0:c0 + ncol],
                                                                         start=(k == 0), stop=(k == 7)), hB + [WbB[ws][0]], [pgtB])
                    for k in range(8):
                        S.op("pe", lambda e, k=k, j=j, put=put: e.matmul(put[:, 0:ncol], wu3[:, k, j * 128:(j + 1) * 128], HT[:, k, c0:c0 + ncol],
                                                                         start=(k == 0), stop=(k == 7)), hB + [WbB[ws][1]], [putB])
                    si = nxt("sg", 2)
                    S.op("act", lambda e, si=si, pgt=pgt: e.activation(sgt[si][:, 0:ncol], pgt[:, 0:ncol], AF.Silu), [pgtB], [sgB[si]])
                    S.op("dve", lambda e, j=j, si=si, put=put: e.tensor_tensor(hid[hi][:, j, 0:ncol], sgt[si][:, 0:ncol], put[:, 0:ncol], ALU.mult),
                         [sgB[si], putB], [hidB[hi]])

                def down(blk=blk, c0=c0, hi=hi, ws=ws, wd3=wd3):
                    for (i, rows) in blk:
                        o = i * 128 - c0
                        for half in range(2):
                            pd, pdB = getps()
                            for j in range(2):
                                S.op("pe", lambda e, j=j, pd=pd: e.matmul(pd[:rows, :], hid[hi][:, j, o:o + rows], wd3[:, j, half * 512:(half + 1) * 512],
                                                                          start=(j == 0), stop=(j == 1)), [hidB[hi], WbB[ws][2]], [pdB])
                            S.op("dve", lambda e, pd=pd: e.scalar_tensor_tensor(X[:rows, i, half * 512:(half + 1) * 512], pd[:rows, :], 0.5,
                                                                                X[:rows, i, half * 512:(half + 1) * 512], ALU.mult, ALU.add),
                                 [pdB, Xb[i]], [Xb[i]])
                for p in pending:
                    p()
                pending = [down]
        for p in pending:
            p()

    rt = sb("rt", [128, NX, 8]); rtf = sb("rtf", [128, NX, 8]); rti = sb("rti", [128, NX, 8], mybir.dt.int32); rtB = Buf()

    def rope_tables(i0, n, u0):
        sl = slice(i0, i0 + n)
        pb = pos[:, u0:u0 + n].unsqueeze(2).to_broadcast([128, n, 8])
        fb = cst[:, 512:520].unsqueeze(1).to_broadcast([128, n, 8])
        for tab, shift in ((sinT, 0.0), (cosT, 0.25)):
            S.op("dve", lambda e: e.tensor_tensor(rt[:, sl, :], pb, fb, ALU.mult), [cB, cstB], [rtB])
            if shift:
                S.op("dve", lambda e, shift=shift: e.tensor_scalar(rt[:, sl, :], rt[:, sl, :], shift, None, ALU.add), [rtB], [rtB])
            S.op("dve", lambda e: e.tensor_copy(rti[:, sl, :], rt[:, sl, :]), [rtB], [rtB])
            S.op("dve", lambda e: e.tensor_copy(rtf[:, sl, :], rti[:, sl, :]), [rtB], [rtB])
            S.op("dve", lambda e: e.tensor_tensor(rt[:, sl, :], rt[:, sl, :], rtf[:, sl, :], ALU.subtract), [rtB], [rtB])
            S.op("dve", lambda e: e.tensor_scalar(rtf[:, sl, :], rt[:, sl, :], 0.5, None, ALU.is_gt), [rtB], [rtB])
            S.op("dve", lambda e: e.tensor_tensor(rt[:, sl, :], rt[:, sl, :], rtf[:, sl, :], ALU.subtract), [rtB], [rtB])
            S.op("dve", lambda e: e.tensor_scalar(rtf[:, sl, :], rt[:, sl, :], -0.5, None, ALU.is_lt), [rtB], [rtB])
            S.op("dve", lambda e: e.tensor_tensor(rt[:, sl, :], rt[:, sl, :], rtf[:, sl, :], ALU.add), [rtB], [rtB])
            S.op("act", lambda e, tab=tab: e.activation(tab[:, sl, :], rt[:, sl, :], AF.Sin, scale=2 * math.pi), [rtB], [csB, rtB])

    def to_T_and_store(src_bf, srcB, rows, dst_ap, dstB):
        pt, ptB = getpt()
        for k in range(2):
            S.op("pe", lambda e, k=k: e.transpose(pt[:, k * 128:k * 128 + rows], src_bf[:rows, k * 128:(k + 1) * 128], ident[:rows, :rows]),
                 [srcB, cB], [ptB])
        t = nxt("kt", 2)
        S.op("act", lambda e: e.copy(ktile[t][:, :, 0:rows], pt[:, 0:256].rearrange("p (k t) -> p k t", k=2)[:, :, 0:rows]), [ptB], [ktB[t]])
        S.op("pool", lambda e: e.dma_start(out=dst_ap, in_=ktile[t][:, :, 0:rows]), [ktB[t]], [dstB], dma=True)

    def proj_ab(tiles, us):
        gcol = (GID["mix_g"] + 0) * 8
        for pc in range(int(os.environ.get("KPC", "12"))):
            ws = pc % 2
            kind, hp = ("aq", "ak", "av", "bq", "bk", "bv")[pc // 2], pc % 2
            load_w(Wb[ws][:, 0, :], WbB[ws][0], W["ab_w_in"][0][:, pc * 256:(pc + 1) * 256], gcol, 8, 256)
            w3 = Wb[ws][:, 0, :].rearrange("p (k f) -> p k f", k=8)
            for (i, rows), u in zip(tiles, us):
                own = (u < NOWN) or (u == SMP)
                isq = kind in ("aq", "bq")
                if isq and not (u < NQ or u == SMP):
                    continue
                qi = NQ if u == SMP else u
                pp, ppB = getps()
                for k in range(8):
                    S.op("pe", lambda e, k=k: e.matmul(pp[:rows, 0:256], HT[:, k, i * 128:i * 128 + rows], w3[:, k, :],
                                                       start=(k == 0), stop=(k == 7)), [HTb[i], WbB[ws][0]], [ppB])
                src = pp[:rows, 0:256]
                csl = slice(hp * 256, (hp + 1) * 256)
                if kind in ("aq", "ak"):
                    g = g64["a_gq" if kind == "aq" else "a_gk"]
                    a, b = nxt("wk", 4), nxt("wk", 4)
                    s_ = nxt("st", 4)
                    S.op("act", lambda e: e.copy(wk[b][:rows, :], src), [ppB], [wkB[b]])
                    S.op("dve", lambda e: e.tensor_tensor(wk[a][:rows, :], wk[b][:rows, :], wk[b][:rows, :], ALU.mult), [wkB[b]], [wkB[a]])
                    S.op("dve", lambda e: e.tensor_reduce(st[s_][:rows, 0:4], wk[a][:rows, :].rearrange("p (h d) -> p h d", h=4), AX.X, ALU.add),
                         [wkB[a]], [stB[s_]])
                    S.op("dve", lambda e: e.tensor_scalar(st[s_][:rows, 0:4], st[s_][:rows, 0:4], 1.0 / 64, EPS, ALU.mult, ALU.add), [stB[s_]], [stB[s_]])
                    S.op("act", lambda e: e.activation(st[s_][:rows, 0:4], st[s_][:rows, 0:4], AF.Ln), [stB[s_]], [stB[s_]])
                    S.op("act", lambda e: e.activation(st[s_][:rows, 0:4], st[s_][:rows, 0:4], AF.Exp, scale=-0.5), [stB[s_]], [stB[s_]])
                    w3a = wk[a][:rows, :].rearrange("p (h d) -> p h d", h=4)
                    w3b = wk[b][:rows, :].rearrange("p (h d) -> p h d", h=4)
                    S.op("dve", lambda e: e.tensor_tensor(w3a, src.rearrange("p (h d) -> p h d", h=4),
                                                          st[s_][:rows, 0:4].unsqueeze(2).to_broadcast([rows, 4, 64]), ALU.mult), [ppB, stB[s_]], [wkB[a]])
                    S.op("dve", lambda e: e.tensor_tensor(w3b, w3a, g[:rows, :].unsqueeze(1).to_broadcast([rows, 4, 64]), ALU.mult), [wkB[a], gB], [wkB[b]])
                    cs = cosT[:rows, i, :].unsqueeze(1).to_broadcast([rows, 4, 8])
                    sn = sinT[:rows, i, :].unsqueeze(1).to_broadcast([rows, 4, 8])
                    x1, x2 = w3b[:, :, 0:8], w3b[:, :, 8:16]
                    r = [rp[q][:rows] for q in range(4)]
                    S.op("dve", lambda e: e.tensor_tensor(r[0], x1, cs, ALU.mult), [wkB[b], csB], [rpB[0]])
                    S.op("dve", lambda e: e.tensor_tensor(r[1], x2, sn, ALU.mult), [wkB[b], csB], [rpB[1]])
                    S.op("dve", lambda e: e.tensor_tensor(r[2], x2, cs, ALU.mult), [wkB[b], csB], [rpB[2]])
                    S.op("dve", lambda e: e.tensor_tensor(r[3], x1, sn, ALU.mult), [wkB[b], csB], [rpB[3]])
                    S.op("dve", lambda e: e.tensor_tensor(x1, r[0], r[1], ALU.subtract), [rpB[0], rpB[1]], [wkB[b]])
                    S.op("dve", lambda e: e.tensor_tensor(x2, r[2], r[3], ALU.add), [rpB[2], rpB[3]], [wkB[b]])
                    fin, finB = wk[b], wkB[b]
                    if kind == "ak" and own:
                        dst = o_aks[:, csl] if u == SMP else o_ak[u][:, csl]
                        S.op("pool", lambda e, dst=dst: e.dma_start(out=dst[:rows], in_=fin[:rows, :]), [finB], [], dma=True)
                    c = nxt("wkb", 2)
                    S.op("pool", lambda e: e.tensor_copy(wkb[c][:rows, 0:256], fin[:rows, :]), [finB], [wkbB[c]])
                    if kind == "ak":
                        to_T_and_store(wkb[c], wkbB[c], rows, kta_s[u].rearrange("p (k t) -> p k t", k=4)[:, hp * 2:hp * 2 + 2, 0:rows], scrB["kta"][u])
                    else:
                        to_T_and_store(wkb[c], wkbB[c], rows, qta_s[qi].rearrange("p (k t) -> p k t", k=4)[:, hp * 2:hp * 2 + 2, 0:rows], scrB["qta"][qi])
                elif kind in ("bq", "bk"):
                    c = nxt("wkb", 2)
                    if kind == "bk":
                        a = nxt("wk", 4)
                        S.op("act", lambda e: e.copy(wk[a][:rows, :], src), [ppB], [wkB[a]])
                        if own:
                            dst = o_bks[:, csl] if u == SMP else o_bk[u][:, csl]
                            S.op("pool", lambda e, dst=dst: e.dma_start(out=dst[:rows], in_=wk[a][:rows, :]), [wkB[a]], [], dma=True)
                        S.op("dve", lambda e: e.tensor_copy(wkb[c][:rows, 0:256], wk[a][:rows, :]), [wkB[a]], [wkbB[c]])
                        to_T_and_store(wkb[c], wkbB[c], rows, ktb_s[u].rearrange("p (k t) -> p k t", k=4)[:, hp * 2:hp * 2 + 2, 0:rows], scrB["ktb"][u])
                    else:
                        S.op("dve", lambda e: e.tensor_scalar(wkb[c][:rows, 0:256], src, 0.125, None, ALU.mult), [ppB], [wkbB[c]])
                        to_T_and_store(wkb[c], wkbB[c], rows, qtb_s[qi].rearrange("p (k t) -> p k t", k=4)[:, hp * 2:hp * 2 + 2, 0:rows], scrB["qtb"][qi])
                else:
                    a = nxt("wk", 4)
                    S.op("act", lambda e: e.copy(wk[a][:rows, :], src), [ppB], [wkB[a]])
                    if own:
                        od = {"av": (o_avs, o_av), "bv": (o_bvs, o_bv)}[kind]
                        dst = od[0][:, csl] if u == SMP else od[1][u][:, csl]
                        S.op("pool", lambda e, dst=dst: e.dma_start(out=dst[:rows], in_=wk[a][:rows, :]), [wkB[a]], [], dma=True)
                    c = nxt("wkb", 2)
                    if kind == "av":
                        v3 = wkb[c][:rows, 0:260].rearrange("p (h d) -> p h d", h=2)
                        S.op("dve", lambda e: e.tensor_scalar(v3[:, :, 0:128], wk[a][:rows, :].rearrange("p (h d) -> p h d", h=2), valid[:rows, u:u + 1], None, ALU.mult),
                             [wkB[a], cB], [wkbB[c]])
                        S.op("dve", lambda e: e.tensor_copy(v3[:, :, 128:130], valid[:rows, u:u + 1].unsqueeze(1).to_broadcast([rows, 2, 2])), [cB], [wkbB[c]])
                        S.op("pool", lambda e: e.dma_start(out=va_s[u][:rows, hp * 260:(hp + 1) * 260], in_=wkb[c][:rows, 0:260]), [wkbB[c]], [scrB["va"][u]], dma=True)
                    else:
                        S.op("dve", lambda e: e.tensor_scalar(wkb[c][:rows, 0:256], wk[a][:rows, :], valid[:rows, u:u + 1], None, ALU.mult), [wkB[a], cB], [wkbB[c]])
                        S.op("pool", lambda e: e.dma_start(out=vb_s[u][:rows, csl], in_=wkb[c][:rows, 0:256]), [wkbB[c]], [scrB["vb"][u]], dma=True)

    def front0(us, with_sample):
        tiles = [(i, 128) for i in range(len(us))]
        uu = list(us)
        for i, u in enumerate(us):
            S.op("sp", lambda e, i=i, u=u: e.dma_start(out=X[:, i, :], in_=xp[u]), [], [Xb[i]], dma=True)
        if with_sample:
            i = len(us)
            S.op("sp", lambda e: e.dma_start(out=X[0:16, i, :], in_=xs), [], [Xb[i]], dma=True)
            tiles.append((i, 16)); uu.append(SMP)
        DBG = int(os.environ.get("KDBG", "9"))
        if DBG >= 1:
            rope_tables(0, len(us), us[0])
            if with_sample:
                rope_tables(len(us), 1, SMP)
        if DBG >= 3:
            ffn(0, 1, tiles)
        if DBG >= 2:
            rstd_tiles(tiles)
            for (i, rows) in tiles:
                norm_to_HT(i, rows)
        if DBG >= 4:
            proj_ab(tiles, uu)
        return tiles, uu

    SQ = NQ
    CIDX = NT + 1
    small = {n: sb("sm_" + n, shp) for n, shp in (("den", [128, 4, 8]), ("carry", [128, 4, 8]), ("ecarry", [128, 4, 8]),
                                                   ("rden", [128, 8]), ("s4", [128, 4]), ("lam", [128, 4]))}
    smB = {n: Buf() for n in small}
    lt = sb("lt", [128, 4, 64]); ltB = Buf()
    gsub = sb("gsub", [128, 128]); gsubB = Buf()

    def cache_prep():
        for t in range(8):
            rsl = slice(t * 128, (t + 1) * 128)
            for (src, dst, key) in ((ca_k, kta_s, "kta"), (cb_k, ktb_s, "ktb")):
                s_ = nxt("stg", 3)
                S.op("sp", lambda e, src=src: e.dma_start(out=stg[s_][:, 0:512], in_=src[rsl, :]), [], [stgB[s_]], dma=True)
                j = nxt("xn", 2)
                S.op("pool", lambda e: e.tensor_copy(xn[j][:, 0:512], stg[s_][:, 0:512]), [stgB[s_]], [xnB[j]])
                pt, ptB = getpt()
                for k in range(4):
                    S.op("pe", lambda e, k=k: e.transpose(pt[:, k * 128:(k + 1) * 128], xn[j][:, k * 128:(k + 1) * 128], ident), [xnB[j], cB], [ptB])
                S.op("act", lambda e: e.copy(xn[j][:, 512:1024], pt[:, 0:512]), [ptB], [xnB[j]])
                S.op("pool", lambda e, dst=dst: e.dma_start(out=dst[CIDX + t], in_=xn[j][:, 512:1024]), [xnB[j]], [scrB[key][CIDX + t]], dma=True)
            s_ = nxt("stg", 3)
            S.op("sp", lambda e: e.dma_start(out=stg[s_][:, 0:512], in_=ca_v[rsl, :]), [], [stgB[s_]], dma=True)
            j = nxt("xn", 2)
            v4 = xn[j][:, 0:520].rearrange("p (h d) -> p h d", h=4)
            S.op("pool", lambda e: e.tensor_copy(v4[:, :, 0:128], stg[s_][:, 0:512].rearrange("p (h d) -> p h d", h=4)), [stgB[s_]], [xnB[j]])
            S.op("pool", lambda e: e.memset(v4[:, :, 128:130], 1.0), [], [xnB[j]])
            S.op("pool", lambda e: e.dma_start(out=va_s[CIDX + t], in_=xn[j][:, 0:520]), [xnB[j]], [scrB["va"][CIDX + t]], dma=True)
            s_ = nxt("stg", 3)
            S.op("sp", lambda e: e.dma_start(out=stg[s_][:, 0:512], in_=cb_v[rsl, :]), [], [stgB[s_]], dma=True)
            j = nxt("xn", 2)
            S.op("pool", lambda e: e.tensor_copy(xn[j][:, 0:512], stg[s_][:, 0:512]), [stgB[s_]], [xnB[j]])
            S.op("pool", lambda e: e.dma_start(out=vb_s[CIDX + t], in_=xn[j][:, 0:512]), [xnB[j]], [scrB["vb"][CIDX + t]], dma=True)

    def lambda_prep():
        for q, n in enumerate(("a_lq1", "a_lk1", "a_lq2", "a_lk2")):
            S.op("sp", lambda e, q=q, n=n: e.dma_start(out=lt[:, q, :], in_=W[n][0:1, :].partition_broadcast(128)), [], [ltB], dma=True)
        S.op("sp", lambda e: e.dma_start(out=gsub, in_=W["a_subln_g"][0:1, :].partition_broadcast(128)), [], [gsubB], dma=True)
        S.op("dve", lambda e: e.tensor_scalar(gsub, gsub, 0.8, None, ALU.mult), [gsubB], [gsubB])
        lam = small["lam"]
        S.op("dve", lambda e: e.tensor_tensor(lt[:, 0, :], lt[:, 0, :], lt[:, 1, :], ALU.mult), [ltB], [ltB])
        S.op("dve", lambda e: e.tensor_tensor(lt[:, 2, :], lt[:, 2, :], lt[:, 3, :], ALU.mult), [ltB], [ltB])
        S.op("dve", lambda e: e.tensor_reduce(lam[:, 0:1], lt[:, 0, :], AX.X, ALU.add), [ltB], [smB["lam"]])
        S.op("dve", lambda e: e.tensor_reduce(lam[:, 1:2], lt[:, 2, :], AX.X, ALU.add), [ltB], [smB["lam"]])
        S.op("act", lambda e: e.activation(lam[:, 0:2], lam[:, 0:2], AF.Exp), [smB["lam"]], [smB["lam"]])
        S.op("dve", lambda e: e.tensor_tensor(lam[:, 2:3], lam[:, 1:2], lam[:, 0:1], ALU.subtract), [smB["lam"]], [smB["lam"]])
        S.op("dve", lambda e: e.tensor_scalar(lam[:, 2:3], lam[:, 2:3], -0.2, None, ALU.add), [smB["lam"]], [smB["lam"]])

    def attn_l0():
        accA = [av(1024 * j, 1024) for j in range(4)]; accAB = [Buf() for _ in range(4)]
        accB = [av(4096 + 512 * j, 512) for j in range(4)]; accBB = [Buf() for _ in range(4)]
        QTe = av(6144, 1024, BF16).rearrange("p (k t) -> p k t", k=4); QTB = Buf()
        QTo = av(13856, 1024, BF16).rearrange("p (k t) -> p k t", k=4)
        QTeo = (QTe, QTo)
        S.op("pool", lambda e: e.memset(QTe[64:128, :, :], 0.0), [], [QTB])
        S.op("pool", lambda e: e.memset(QTo[0:64, :, :], 0.0), [], [QTB])
        kvb = [av(7168 + 528 * i, 528, BF16) for i in range(2)]; kvB = [Buf() for _ in range(2)]
        Eb = [av(8224 + 512 * i, 512, BF16) for i in range(2)]; EbB = [[Buf(), Buf()] for _ in range(2)]
        ef = [av(9248 + 512 * i, 512) for i in range(2)]; efB = [Buf() for _ in range(2)]
        pvs = [av(10272 + 256 * i, 256) for i in range(2)]; pvsB = [Buf() for _ in range(2)]
        tmpB = [av(10784 + 256 * i, 256) for i in range(2)]; tmpBB = [Buf() for _ in range(2)]
        ofin = av(11296, 1024); ofinB = Buf()
        mg = av(12320, 512, BF16); mgB = Buf()
        sqf = av(12832, 512); sqfB = Buf()
        aof = av(13344, 512); aofB = Buf()
        QhB = [[Buf(), Buf()] for _ in range(3)]
        den, carry, ecarry = small["den"], small["carry"], small["ecarry"]
        zc = {"z": 0, "e": 0}

        SPB = [Eb[0], av(12832, 512, BF16)]; WBv = [Eb[1], av(13344, 512, BF16)]
        SPBB = [[Buf(), Buf()], [Buf(), Buf()]]; WBB = [[Buf(), Buf()], [Buf(), Buf()]]
        P1Bx = [P1B, Buf()]
        par_c = {"p": 0}

        pipe = {"active": []}

        def pipe_round(g=None):
            act_ = pipe["active"]
            if g is not None:
                act_.insert(0, g)
            keep = []
            for gg in act_:
                try:
                    next(gg); keep.append(gg)
                except StopIteration:
                    pass
            pipe["active"] = keep

        def pipe_drain():
            while pipe["active"]:
                pipe_round()

        def pairA(j, nq, nk, KT, V, kB, diag):
            zi = zc["z"]; zc["z"] = 1 - zi
            z = Q[zi]
            for m in range(8):
                S.op("pe", lambda e, m=m: e.matmul(z[:nk, m * nq:(m + 1) * nq], KT[:, m // 2, 0:nk],
                                                   QTeo[m % 2][:, m // 2, j * 128:j * 128 + nq], start=True, stop=True),
                     [kB, QTB], [QhB[zi][0], QhB[zi][1]])
            ei = zc["e"]; zc["e"] = 1 - ei
            E = Eb[ei]
            if 8 * nq > 512:
                for hh in range(2):
                    S.op("act", lambda e, hh=hh: e.activation(E[:nk, hh * 512:(hh + 1) * 512], z[:nk, hh * 512:(hh + 1) * 512], AF.Exp), [QhB[zi][hh]], [EbB[ei][hh]])
            else:
                S.op("act", lambda e: e.activation(E[:nk, 0:8 * nq], z[:nk, 0:8 * nq], AF.Exp), [QhB[zi][0], QhB[zi][1]], [EbB[ei][0], EbB[ei][1]])
            if diag:
                e3 = E[:nk, 0:8 * nq].rearrange("p (m q) -> p m q", m=8)
                S.op("pool", lambda e: e.tensor_tensor(e3, e3, maskA[:nk, :nq].unsqueeze(1).to_broadcast([nk, 8, nq]), ALU.mult),
                     [EbB[ei][0], EbB[ei][1], cB], [EbB[ei][0], EbB[ei][1]])
            yield
            pv = Q[2]
            for m in range(8):
                S.op("pe", lambda e, m=m: e.matmul(pv[:nq, m * 128:(m + 1) * 128], E[:nk, m * nq:(m + 1) * nq], V[:nk, m // 2, 0:128], start=True, stop=True),
                     [EbB[ei][0], EbB[ei][1], kB], [QhB[2][0], QhB[2][1]])
            for m in range(8):
                S.op("pe", lambda e, m=m: e.matmul(P1[:nq, m:m + 1], E[:nk, m * nq:(m + 1) * nq], V[:nk, m // 2, 128:129], start=True, stop=True),
                     [EbB[ei][0], EbB[ei][1], kB], [P1B])
            S.op("dve", lambda e: e.tensor_tensor(accA[j][:nq, :], accA[j][:nq, :], pv[:nq, :], ALU.add), [accAB[j], QhB[2][0], QhB[2][1]], [accAB[j]])
            S.op("dve", lambda e: e.tensor_tensor(den[:nq, j, :], den[:nq, j, :], P1[:nq, 0:8], ALU.add), [smB["den"], P1B], [smB["den"]])

        def pairB(j, nq, nk, KT, V, kB, diag):
            par = par_c["p"]; par_c["p"] = 1 - par
            vw = []
            for hh in range(2):
                vw.append(dict(zh=Q[0][:, hh * 512:hh * 512 + 4 * nq], ar=Q[1][:, hh * 512:hh * 512 + 4 * nq], pvB=Q[2][:, hh * 512:hh * 512 + 256],
                               spb=SPB[par][:, hh * 512:hh * 512 + 4 * nq], wb=WBv[par][:, hh * 512:hh * 512 + 4 * nq], hs=slice(4 * hh, 4 * hh + 4)))

            def qk(out, hh, h4, st_, sp_):
                h = 4 * hh + h4
                return lambda e: e.matmul(out[:nk, h4 * nq:(h4 + 1) * nq], KT[:, h // 2, 0:nk],
                                          QTeo[h % 2][:, h // 2, j * 128:j * 128 + nq], start=st_, stop=sp_)
            for hh in range(2):
                for h4 in range(4):
                    S.op("pe", qk(vw[hh]["zh"], hh, h4, True, True), [kB, QTB], [QhB[0][hh]])
            for hh in range(2):
                S.op("act", lambda e, hh=hh: e.activation(ef[hh][:nk, 0:4 * nq], vw[hh]["zh"][:nk, :], AF.Exp), [QhB[0][hh]], [efB[hh]])
            for hh in range(2):
                S.op("act", lambda e, hh=hh: e.activation(vw[hh]["spb"][:nk, :], ef[hh][:nk, 0:4 * nq], AF.Ln, bias=1.0), [efB[hh]], [SPBB[par][hh]])
                if diag:
                    s3 = vw[hh]["spb"][:nk, :].rearrange("p (m q) -> p m q", m=4)
                    S.op("pool", lambda e, s3=s3: e.tensor_tensor(s3, s3, maskB[:nk, :nq].unsqueeze(1).to_broadcast([nk, 4, nq]), ALU.mult),
                         [SPBB[par][hh], cB], [SPBB[par][hh]])
            yield
            for hh in range(2):
                for h4 in range(4):
                    S.op("pe", qk(vw[hh]["ar"], hh, h4, True, False), [kB, QTB], [QhB[1][hh]])
                    S.op("pe", lambda e, hh=hh, h4=h4: e.matmul(vw[hh]["ar"][:nk, h4 * nq:(h4 + 1) * nq], negU[:nk, :nk], vw[hh]["spb"][:nk, h4 * nq:(h4 + 1) * nq], start=False, stop=True),
                         [SPBB[par][hh], cB], [QhB[1][hh]])
            for hh in range(2):
                for h4 in range(4):
                    c_ = par * 8 + 4 * hh + h4
                    S.op("pe", lambda e, hh=hh, h4=h4, c_=c_: e.matmul(P1[:nq, c_:c_ + 1], vw[hh]["spb"][:nk, h4 * nq:(h4 + 1) * nq], ones[:nk, 0:1], start=True, stop=True),
                         [SPBB[par][hh], cB], [P1Bx[par]])
            for hh in range(2):
                S.op("act", lambda e, hh=hh: e.activation(vw[hh]["wb"][:nk, :], vw[hh]["ar"][:nk, :], AF.Exp), [QhB[1][hh]], [WBB[par][hh]])
                if diag:
                    w3 = vw[hh]["wb"][:nk, :].rearrange("p (m q) -> p m q", m=4)
                    S.op("pool", lambda e, w3=w3: e.tensor_tensor(w3, w3, maskB[:nk, :nq].unsqueeze(1).to_broadcast([nk, 4, nq]), ALU.mult),
                         [WBB[par][hh], cB], [WBB[par][hh]])
            yield
            for hh in range(2):
                for h4 in range(4):
                    S.op("pe", lambda e, hh=hh, h4=h4: e.matmul(vw[hh]["pvB"][:nq, h4 * 64:(h4 + 1) * 64], vw[hh]["wb"][:nk, h4 * nq:(h4 + 1) * nq], V[:nk, 4 * hh + h4, :], start=True, stop=True),
                         [WBB[par][hh], kB], [QhB[2][hh]])
            for hh in range(2):
                S.op("dve", lambda e, hh=hh: e.tensor_copy(pvs[hh][:nq, :], vw[hh]["pvB"][:nq, :]), [QhB[2][hh]], [pvsB[hh]])
            for hh in range(2):
                hs = vw[hh]["hs"]
                ps_ = slice(par * 8 + 4 * hh, par * 8 + 4 * hh + 4)
                S.op("dve", lambda e, hh=hh, hs=hs: e.tensor_tensor(tmpB[hh][:nq, :].rearrange("p (h d) -> p h d", h=4), pvs[hh][:nq, :].rearrange("p (h d) -> p h d", h=4),
                                                                   ecarry[:nq, j, hs].unsqueeze(2).to_broadcast([nq, 4, 64]), ALU.mult), [pvsB[hh], smB["ecarry"]], [tmpBB[hh]])
                S.op("dve", lambda e, hh=hh: e.tensor_tensor(accB[j][:nq, hh * 256:(hh + 1) * 256], accB[j][:nq, hh * 256:(hh + 1) * 256], tmpB[hh][:nq, :], ALU.add),
                     [tmpBB[hh], accBB[j]], [accBB[j]])
                S.op("dve", lambda e, hs=hs, ps_=ps_: e.tensor_tensor(carry[:nq, j, hs], carry[:nq, j, hs], P1[:nq, ps_], ALU.subtract), [smB["carry"], P1Bx[par]], [smB["carry"]])
                S.op("act", lambda e, hs=hs: e.activation(ecarry[:nq, j, hs], carry[:nq, j, hs], AF.Exp), [smB["carry"]], [smB["ecarry"]])

        def finalize(j, nq, i):
            rden = small["rden"]
            S.op("dve", lambda e: e.tensor_scalar(rden[:nq, :], den[:nq, j, :], 1e-30, None, ALU.add), [smB["den"]], [smB["rden"]])
            S.op("dve", lambda e: e.reciprocal(rden[:nq, :], rden[:nq, :]), [smB["rden"]], [smB["rden"]])
            S.op("dve", lambda e: e.tensor_tensor(ofin[:nq, :].rearrange("p (m d) -> p m d", m=8), accA[j][:nq, :].rearrange("p (m d) -> p m d", m=8),
                                                  rden[:nq, :].unsqueeze(2).to_broadcast([nq, 8, 128]), ALU.mult), [accAB[j], smB["rden"]], [ofinB])
            o4 = ofin[:nq, :].rearrange("p (h t d) -> p h t d", h=4, t=2)
            ao3 = aof[:nq, :].rearrange("p (h d) -> p h d", h=4)
            S.op("dve", lambda e: e.scalar_tensor_tensor(ao3, o4[:, :, 1, :], small["lam"][:nq, 2:3], o4[:, :, 0, :], ALU.mult, ALU.add),
                 [ofinB, smB["lam"]], [aofB])
            s4 = small["s4"]
            S.op("dve", lambda e: e.tensor_tensor(sqf[:nq, :], aof[:nq, :], aof[:nq, :], ALU.mult), [aofB], [sqfB])
            S.op("dve", lambda e: e.tensor_reduce(s4[:nq, :], sqf[:nq, :].rearrange("p (h d) -> p h d", h=4), AX.X, ALU.add), [sqfB], [smB["s4"]])
            S.op("dve", lambda e: e.tensor_scalar(s4[:nq, :], s4[:nq, :], 1.0 / 128, EPS, ALU.mult, ALU.add), [smB["s4"]], [smB["s4"]])
            S.op("act", lambda e: e.activation(s4[:nq, :], s4[:nq, :], AF.Ln), [smB["s4"]], [smB["s4"]])
            S.op("act", lambda e: e.activation(s4[:nq, :], s4[:nq, :], AF.Exp, scale=-0.5), [smB["s4"]], [smB["s4"]])
            S.op("dve", lambda e: e.tensor_tensor(ao3, ao3, s4[:nq, :].unsqueeze(2).to_broadcast([nq, 4, 128]), ALU.mult), [aofB, smB["s4"]], [aofB])
            S.op("dve", lambda e: e.tensor_tensor(mg[:nq, 0:512].rearrange("p (h d) -> p h d", h=4), ao3, gsub[:nq, :].unsqueeze(1).to_broadcast([nq, 4, 128]), ALU.mult),
                 [aofB, gsubB], [mgB])
            S.op("pool", lambda e: e.tensor_copy(mg[:nq, 512:1024], accB[j][:nq, :]), [accBB[j]], [mgB])
            pt, ptB = getpt()
            for k in range(8):
                S.op("pe", lambda e, k=k: e.transpose(pt[:, k * 128:k * 128 + nq], mg[:nq, k * 128:(k + 1) * 128], ident[:nq, :nq]), [mgB, cB], [ptB])
            S.op("act", lambda e: e.copy(HT[:, :, i * 128:i * 128 + nq], pt.rearrange("p (k t) -> p k t", k=8)[:, :, 0:nq]), [ptB], [HTb[i]])

        groups = [list(range(g, g + 4)) for g in range(0, NQ, 4)] + [[SMP]]
        ngrp = int(os.environ.get("KNGRP", "99"))
        groups = groups[:ngrp] + ([groups[-1]] if ngrp < len(groups) else [])
        for grp in groups:
            sample = grp[0] == SMP
            nq = 16 if sample else 128
            keys = ([SMP] + [CIDX + t for t in range(7, -1, -1)]) if sample else list(range(grp[0], int(os.environ.get("KNKEY", str(NT)))))
            for pas in os.environ.get("KPASS", "AB"):
                for j, s in enumerate(grp):
                    qi = SQ if sample else s
                    src = (qta_s if pas == "A" else qtb_s)[qi].rearrange("p (k t) -> p k t", k=4)[:, :, 0:nq]
                    S.op("sp", lambda e, j=j, src=src: e.dma_start(out=QTe[0:64, :, j * 128:j * 128 + nq], in_=src[0:64]), [scrB["qta" if pas == "A" else "qtb"][qi]], [QTB], dma=True)
                    S.op("sp", lambda e, j=j, src=src: e.dma_start(out=QTo[64:128, :, j * 128:j * 128 + nq], in_=src[64:128]), [scrB["qta" if pas == "A" else "qtb"][qi]], [QTB], dma=True)
                    if pas == "A":
                        S.op("dve", lambda e, j=j: e.memset(accA[j], 0.0), [], [accAB[j]])
                    else:
                        S.op("dve", lambda e, j=j: e.memset(accB[j], 0.0), [], [accBB[j]])
                if pas == "A":
                    S.op("dve", lambda e: e.memset(den, 0.0), [], [smB["den"]])
                else:
                    S.op("dve", lambda e: e.memset(carry, 0.0), [], [smB["carry"]])
                    S.op("dve", lambda e: e.memset(ecarry, 1.0), [], [smB["ecarry"]])
                prev_pairs = 0
                for idx, uk in enumerate(keys):
                    nk = 16 if uk == SMP else 128
                    b = idx % 2
                    npairs = sum(1 for s in grp if sample or uk >= s)
                    if prev_pairs < 4 or npairs < 4:
                        pipe_drain()
                    prev_pairs = npairs
                    KT = kvb[b][:, 0:512].rearrange("p (k t) -> p k t", k=4)
                    ksrc = (kta_s if pas == "A" else ktb_s)[uk].rearrange("p (k t) -> p k t", k=4)[:, :, 0:nk]
                    kkey, vkey = ("kta", "va") if pas == "A" else ("ktb", "vb")
                    S.op("sp", lambda e, KT=KT, ksrc=ksrc, nk=nk: e.dma_start(out=KT[:, :, 0:nk], in_=ksrc), [scrB[kkey][uk]], [kvB[b]], dma=True)
                    if pas == "A":
                        V = kvb[b][:, 512:1032].rearrange("p (h d) -> p h d", h=4)
                        S.op("sp", lambda e, b=b, uk=uk, nk=nk: e.dma_start(out=kvb[b][:nk, 512:1032], in_=va_s[uk][:nk, :]), [scrB[vkey][uk]], [kvB[b]], dma=True)
                    else:
                        V = kvb[b][:, 512:1024].rearrange("p (h d) -> p h d", h=8)
                        S.op("sp", lambda e, b=b, uk=uk, nk=nk: e.dma_start(out=kvb[b][:nk, 512:1024], in_=vb_s[uk][:nk, :]), [scrB[vkey][uk]], [kvB[b]], dma=True)
                    gens = []
                    for j, s in enumerate(grp):
                        if not sample and uk < s:
                            continue
                        diag = (uk == SMP) if sample else (uk == s)
                        gens.append((pairA if pas == "A" else pairB)(j, nq, nk, KT, V, kvB[b], diag))
                    for g_ in gens:
                        pipe_round(g_)
                pipe_drain()
                S.barrier()
            if not os.environ.get("KNOFIN"):
                for j, s in enumerate(grp):
                    finalize(j, nq, NQ if sample else s)
            S.barrier()

    def linear_residual(wsrc, gcol, tiles, scale):
        for pc in range(4):
            ws = pc % 2
            load_w(Wb[ws][:, 0, :], WbB[ws][0], wsrc[:, pc * 256:(pc + 1) * 256], gcol, 8, 256)
            w3 = Wb[ws][:, 0, :].rearrange("p (k f) -> p k f", k=8)
            for (i, rows) in tiles:
                pp, ppB = getps()
                for k in range(8):
                    S.op("pe", lambda e, k=k: e.matmul(pp[:rows, 0:256], HT[:, k, i * 128:i * 128 + rows], w3[:, k, :], start=(k == 0), stop=(k == 7)),
                         [HTb[i], WbB[ws][0]], [ppB])
                S.op("dve", lambda e: e.scalar_tensor_tensor(X[:rows, i, pc * 256:(pc + 1) * 256], pp[:rows, 0:256], float(scale),
                                                             X[:rows, i, pc * 256:(pc + 1) * 256], ALU.mult, ALU.add), [ppB, Xb[i]], [Xb[i]])

    lim = {"stg": 3}
    gm = {"gq": lt.rearrange("p a b -> p (a b)"), "gk": sb("gm_gk", [128, 256])}; gmB = Buf()
    MKT = av(4096, 1024, BF16).rearrange("p (k t) -> p k t", k=8); MV = av(5120, 1024, BF16).rearrange("p (t f) -> p t f", t=2)
    MKTs = av(12288, 1024, BF16).rearrange("p (k t) -> p k t", k=8); MVs = av(13312, 1024, BF16).rearrange("p (t f) -> p t f", t=2)
    memB = {"k": Buf(), "v": Buf(), "ks": Buf(), "vs": Buf()}
    mw = [av(1024 + 1024 * i, 1024, BF16) for i in range(2)]; mwB = [Buf(), Buf()]
    MT = av(0, 1024, BF16).rearrange("p (k t) -> p k t", k=8); MTB = Buf()

    def head_norm(src_ps, srcB, rows, n, gtile, gtB, out_bf=None, out_bfB=None):
        a, b = nxt("wk", 4), nxt("wk", 4)
        s_ = nxt("st", 4)
        S.op("act", lambda e: e.copy(wk[a][:rows, :n], src_ps), [srcB], [wkB[a]])
        S.op("dve", lambda e: e.tensor_tensor(wk[b][:rows, :n], wk[a][:rows, :n], wk[a][:rows, :n], ALU.mult), [wkB[a]], [wkB[b]])
        S.op("dve", lambda e: e.tensor_reduce(st[s_][:rows, 0:1], wk[b][:rows, :n], AX.X, ALU.add), [wkB[b]], [stB[s_]])
        S.op("dve", lambda e: e.tensor_scalar(st[s_][:rows, 0:1], st[s_][:rows, 0:1], 1.0 / n, EPS, ALU.mult, ALU.add), [stB[s_]], [stB[s_]])
        S.op("act", lambda e: e.activation(st[s_][:rows, 0:1], st[s_][:rows, 0:1], AF.Ln), [stB[s_]], [stB[s_]])
        S.op("act", lambda e: e.activation(st[s_][:rows, 0:1], st[s_][:rows, 0:1], AF.Exp, scale=-0.5), [stB[s_]], [stB[s_]])
        S.op("dve", lambda e: e.tensor_scalar(wk[a][:rows, :n], wk[a][:rows, :n], st[s_][:rows, 0:1], None, ALU.mult), [wkB[a], stB[s_]], [wkB[a]])
        S.op("dve", lambda e: e.tensor_tensor(wk[b][:rows, :n], wk[a][:rows, :n], gtile[:rows, :n], ALU.mult), [wkB[a], gtB], [wkB[b]])
        return wk[b], wkB[b]

    def mem_prep(l):
        S.op("sp", lambda e: e.dma_start(out=gm["gq"], in_=W["mem_gq"][l:l + 1, :].partition_broadcast(128)), [], [gmB], dma=True)
        S.op("sp", lambda e: e.dma_start(out=gm["gk"], in_=W["mem_gk"][l:l + 1, :].partition_broadcast(128)), [], [gmB], dma=True)
        S.op("dve", lambda e: e.tensor_scalar(gm["gq"], gm["gq"], 1.0 / 16, None, ALU.mult), [gmB], [gmB])
        for t in range(2):
            s_ = nxt("stg", 3)
            S.op("sp", lambda e, t=t: e.dma_start(out=stg[s_][:, 0:1024], in_=memp[t * 128:(t + 1) * 128, :]), [], [stgB[s_]], dma=True)
            k_ = nxt("st", 4)
            S.op("dve", lambda e: e.memset(st[k_][:, 0:1], 0.0), [], [stB[k_]])
            S.op("act", lambda e: e.activation(junk, stg[s_][:, 0:1024], AF.Square, accum_out=st[k_][:, 0:1]), [stgB[s_]], [junkB, stB[k_]])
            S.op("dve", lambda e: e.tensor_scalar(st[k_][:, 0:1], st[k_][:, 0:1], 1.0 / D, EPS, ALU.mult, ALU.add), [stB[k_]], [stB[k_]])
            S.op("act", lambda e: e.activation(st[k_][:, 0:1], st[k_][:, 0:1], AF.Ln), [stB[k_]], [stB[k_]])
            S.op("act", lambda e: e.activation(st[k_][:, 0:1], st[k_][:, 0:1], AF.Exp, scale=-0.5), [stB[k_]], [stB[k_]])
            j = nxt("xn", 2)
            S.op("pool", lambda e: e.tensor_scalar(xn[j], stg[s_][:, 0:1024], st[k_][:, 0:1], None, ALU.mult), [stgB[s_], stB[k_]], [xnB[j]])
            pt, ptB = getpt()
            for k in range(8):
                S.op("pe", lambda e, k=k: e.transpose(pt[:, k * 128:(k + 1) * 128], xn[j][:, k * 128:(k + 1) * 128], ident), [xnB[j], cB], [ptB])
            S.op("act", lambda e, t=t: e.copy(MT[:, :, t * 128:(t + 1) * 128], pt.rearrange("p (k t) -> p k t", k=8)), [ptB], [MTB])
        gcol = (GID["mem_g_m"] + l) * 8
        for which in ("k", "v"):
            for h in range(4):
                ws = h % 2
                load_w(mw[ws], mwB[ws], W["mem_w" + which][l][:, h * 256:(h + 1) * 256], gcol, 8, 256)
                w3 = mw[ws].rearrange("p (k f) -> p k f", k=8)
                for t in range(2):
                    pp, ppB = getps()
                    for k in range(8):
                        S.op("pe", lambda e, k=k, t=t: e.matmul(pp[:, 0:256], MT[:, k, t * 128:(t + 1) * 128], w3[:, k, :], start=(k == 0), stop=(k == 7)),
                             [MTB, mwB[ws]], [ppB])
                    if which == "k":
                        fin, finB = head_norm(pp[:, 0:256], ppB, 128, 256, gm["gk"], gmB)
                        S.op("pool", lambda e, t=t, h=h: e.dma_start(out=o_mk[l][t * 128:(t + 1) * 128, h * 256:(h + 1) * 256], in_=fin[:, 0:256]), [finB], [], dma=True)
                        c = nxt("wkb", 2)
                        S.op("pool", lambda e: e.tensor_copy(wkb[c][:, 0:256], fin[:, 0:256]), [finB], [wkbB[c]])
                        pt, ptB = getpt()
                        for k in range(2):
                            S.op("pe", lambda e, k=k: e.transpose(pt[:, k * 128:(k + 1) * 128], wkb[c][:, k * 128:(k + 1) * 128], ident), [wkbB[c], cB], [ptB])
                        S.op("act", lambda e, t=t, h=h: e.copy(MKT[:, 2 * h:2 * h + 2, t * 128:(t + 1) * 128], pt[:, 0:256].rearrange("p (k t) -> p k t", k=2)), [ptB], [memB["k"]])
                    else:
                        a = nxt("wk", 4)
                        S.op("act", lambda e: e.copy(wk[a][:, :], pp[:, 0:256]), [ppB], [wkB[a]])
                        S.op("pool", lambda e, t=t, h=h: e.dma_start(out=o_mv[l][t * 128:(t + 1) * 128, h * 256:(h + 1) * 256], in_=wk[a][:, :]), [wkB[a]], [], dma=True)
                        S.op("pool", lambda e, t=t, h=h: e.tensor_copy(MV[:, t, h * 256:(h + 1) * 256], wk[a][:, :]), [wkB[a]], [memB["v"]])
        for t in range(2):
            s_ = nxt("stg", 3)
            S.op("sp", lambda e, t=t: e.dma_start(out=stg[s_][:, 0:1024], in_=cm_k[l][t * 128:(t + 1) * 128, :]), [], [stgB[s_]], dma=True)
            j = nxt("xn", 2)
            S.op("pool", lambda e: e.tensor_copy(xn[j], stg[s_][:, 0:1024]), [stgB[s_]], [xnB[j]])
            pt, ptB = getpt()
            for k in range(8):
                S.op("pe", lambda e, k=k: e.transpose(pt[:, k * 128:(k + 1) * 128], xn[j][:, k * 128:(k + 1) * 128], ident), [xnB[j], cB], [ptB])
            S.op("act", lambda e, t=t: e.copy(MKTs[:, :, t * 128:(t + 1) * 128], pt.rearrange("p (k t) -> p k t", k=8)), [ptB], [memB["ks"]])
            s_ = nxt("stg", 3)
            S.op("sp", lambda e, t=t: e.dma_start(out=stg[s_][:, 0:1024], in_=cm_v[l][t * 128:(t + 1) * 128, :]), [], [stgB[s_]], dma=True)
            S.op("pool", lambda e, t=t: e.tensor_copy(MVs[:, t, :], stg[s_][:, 0:1024]), [stgB[s_]], [memB["vs"]])

    def mem_attn(l, tiles):
        QM = av(10240, 512, BF16).rearrange("p (k t) -> p k t", k=8); QMB = Buf()
        Em = av(10752, 512, BF16); EmB = [Buf(), Buf()]
        og = av(11264, 1024); ogB = Buf()
        ogb = junk
        wq = [av(h * 1024, 1024, BF16) for h in range(4)]; wqB = [Buf() for _ in range(4)]
        rstd_tiles(tiles)
        for (i, rows) in tiles:
            norm_to_HT(i, rows)
        lim["stg"] = 2
        gcol = (GID["mem_g_x"] + l) * 8
        for h in range(4):
            load_w(wq[h], wqB[h], W["mem_wq"][l][:, h * 256:(h + 1) * 256], gcol, 8, 256)
        rr["ptn"] = 1
        zc = 0
        for (i, rows) in tiles:
            smp = (i == NQ)
            KTm, Vm, kB, vB = (MKTs, MVs, memB["ks"], memB["vs"]) if smp else (MKT, MV, memB["k"], memB["v"])
            for h in range(4):
                w3 = wq[h].rearrange("p (k f) -> p k f", k=8)
                pp, ppB = getps()
                for k in range(8):
                    S.op("pe", lambda e, k=k: e.matmul(pp[:rows, 0:256], HT[:, k, i * 128:i * 128 + rows], w3[:, k, :], start=(k == 0), stop=(k == 7)),
                         [HTb[i], wqB[h]], [ppB])
                fin, finB = head_norm(pp[:rows, 0:256], ppB, rows, 256, gm["gq"], gmB)
                c = nxt("wkb", 2)
                S.op("pool", lambda e: e.tensor_copy(wkb[c][:rows, 0:256], fin[:rows, 0:256]), [finB], [wkbB[c]])
                pt, ptB = getpt()
                for k in range(2):
                    S.op("pe", lambda e, k=k: e.transpose(pt[:, k * 128:k * 128 + rows], wkb[c][:rows, k * 128:(k + 1) * 128], ident[:rows, :rows]), [wkbB[c], cB], [ptB])
                S.op("act", lambda e, h=h: e.copy(QM[:, 2 * h:2 * h + 2, 0:rows], pt[:, 0:256].rearrange("p (k t) -> p k t", k=2)[:, :, 0:rows]), [ptB], [QMB])
            zi = zc; zc = 1 - zc
            z = Q[zi]; zB_ = [PSB[2 * zi], PSB[2 * zi + 1]]
            for h in range(4):
                for kt in range(2):
                    c0 = (h * 2 + kt) * rows
                    for cc in range(2):
                        S.op("pe", lambda e, h=h, kt=kt, cc=cc, c0=c0: e.matmul(z[:, c0:c0 + rows], KTm[:, 2 * h + cc, kt * 128:(kt + 1) * 128], QM[:, 2 * h + cc, 0:rows],
                                                                            start=(cc == 0), stop=(cc == 1)), [kB, QMB], zB_)
            if 8 * rows > 512:
                for hh in range(2):
                    S.op("act", lambda e, hh=hh: e.activation(Em[:, hh * 512:(hh + 1) * 512], z[:, hh * 512:(hh + 1) * 512], AF.Exp), [zB_[hh]], [EmB[hh]])
            else:
                S.op("act", lambda e: e.activation(Em[:, 0:8 * rows], z[:, 0:8 * rows], AF.Exp), zB_, EmB)
            pv = Q[2]; pvB_ = [PSB[4], PSB[5]]
            for h in range(4):
                for kt in range(2):
                    c0 = (h * 2 + kt) * rows
                    S.op("pe", lambda e, h=h, kt=kt, c0=c0: e.matmul(pv[:rows, h * 256:(h + 1) * 256], Em[:, c0:c0 + rows], Vm[:, kt, h * 256:(h + 1) * 256],
                                                                     start=(kt == 0), stop=(kt == 1)), EmB + [vB], [pvB_[h // 2]])
            for h in range(4):
                for kt in range(2):
                    c0 = (h * 2 + kt) * rows
                    S.op("pe", lambda e, h=h, kt=kt, c0=c0: e.matmul(P1[:rows, h:h + 1], Em[:, c0:c0 + rows], ones[:, 0:1], start=(kt == 0), stop=(kt == 1)), EmB + [cB], [P1B])
            for hh in range(2):
                S.op("act", lambda e, hh=hh: e.copy(og[:rows, hh * 512:(hh + 1) * 512], pv[:rows, hh * 512:(hh + 1) * 512]), [pvB_[hh]], [ogB])
            rden = small["rden"]
            S.op("dve", lambda e: e.reciprocal(rden[:rows, 0:4], P1[:rows, 0:4]), [P1B], [smB["rden"]])
            S.op("dve", lambda e: e.tensor_tensor(ogb[:rows, :].rearrange("p (h d) -> p h d", h=4), og[:rows, :].rearrange("p (h d) -> p h d", h=4),
                                                  rden[:rows, 0:4].unsqueeze(2).to_broadcast([rows, 4, 256]), ALU.mult), [ogB, smB["rden"]], [junkB])
            pt, ptB = getpt()
            for k in range(8):
                S.op("pe", lambda e, k=k: e.transpose(pt[:, k * 128:k * 128 + rows], ogb[:rows, k * 128:(k + 1) * 128], ident[:rows, :rows]), [junkB, cB], [ptB])
            S.op("act", lambda e: e.copy(HT[:, :, i * 128:i * 128 + rows], pt.rearrange("p (k t) -> p k t", k=8)[:, :, 0:rows]), [ptB], [HTb[i]])
        rr["ptn"] = 2
        S.barrier()
        linear_residual(W["mem_wo"][l], None, tiles, 1.0)
        lim["stg"] = 3

    NCS = NQ + 1 + 4
    kc_s = dscr("kc_s", [NCS, 128, 1024]); vc_s = dscr("vc_s", [NCS, 128, 1024]); qc_s = dscr("qc_s", [NOWN + 1, 128, 1024])
    kcB = [Buf() for _ in range(NCS)]; vcB = [Buf() for _ in range(NCS)]; qcB = [Buf() for _ in range(NOWN + 1)]
    tabx_t = nc.dram_tensor("tabx", [16, 513], F32)
    tabx = tabx_t.ap(); tabxB = Buf()
    negb = sb("negb", [128, 16]); negbB = Buf()
    mask4b = sb("mask4b", [128, 128], BF16)
    S.op("dve", lambda e: e.tensor_copy(mask4b, cst[:, 648:776]), [cstB], [cB])

    def norm4(src, ppB, rows, g):
        a, b = nxt("wk", 4), nxt("wk", 4)
        s_ = nxt("st", 4)
        S.op("act", lambda e: e.copy(wk[b][:rows, :], src), [ppB], [wkB[b]])
        S.op("dve", lambda e: e.tensor_tensor(wk[a][:rows, :], wk[b][:rows, :], wk[b][:rows, :], ALU.mult), [wkB[b]], [wkB[a]])
        S.op("dve", lambda e: e.tensor_reduce(st[s_][:rows, 0:4], wk[a][:rows, :].rearrange("p (h d) -> p h d", h=4), AX.X, ALU.add), [wkB[a]], [stB[s_]])
        S.op("dve", lambda e: e.tensor_scalar(st[s_][:rows, 0:4], st[s_][:rows, 0:4], 1.0 / 64, EPS, ALU.mult, ALU.add), [stB[s_]], [stB[s_]])
        S.op("act", lambda e: e.activation(st[s_][:rows, 0:4], st[s_][:rows, 0:4], AF.Ln), [stB[s_]], [stB[s_]])
        S.op("act", lambda e: e.activation(st[s_][:rows, 0:4], st[s_][:rows, 0:4], AF.Exp, scale=-0.5), [stB[s_]], [stB[s_]])
        w3a = wk[a][:rows, :].rearrange("p (h d) -> p h d", h=4)
        w3b = wk[b][:rows, :].rearrange("p (h d) -> p h d", h=4)
        S.op("dve", lambda e: e.tensor_tensor(w3a, w3b, st[s_][:rows, 0:4].unsqueeze(2).to_broadcast([rows, 4, 64]), ALU.mult), [wkB[b], stB[s_]], [wkB[a]])
        S.op("dve", lambda e: e.tensor_tensor(w3b, w3a, g[:rows, :].unsqueeze(1).to_broadcast([rows, 4, 64]), ALU.mult), [wkB[a], gB], [wkB[b]])
        return wk[b], wkB[b]

    def proj_c(tiles, us):
        gcol = (GID["mix_g"] + 1) * 8
        for pc in range(12):
            ws = pc % 2
            kind, hp = ("cq", "ck", "cv")[pc // 4], pc % 4
            load_w(Wb[ws][:, 0, :], WbB[ws][0], W["c_w_in"][0][:, pc * 256:(pc + 1) * 256], gcol, 8, 256)
            w3 = Wb[ws][:, 0, :].rearrange("p (k f) -> p k f", k=8)
            csl = slice(hp * 256, (hp + 1) * 256)
            for (i, rows), u in zip(tiles, us):
                smp = (u == SMP)
                if kind == "cq" and not (u < NOWN or smp):
                    continue
                ui = NQ if smp else u
                qi = NOWN if smp else u
                pp, ppB = getps()
                for k in range(8):
                    S.op("pe", lambda e, k=k: e.matmul(pp[:rows, 0:256], HT[:, k, i * 128:i * 128 + rows], w3[:, k, :], start=(k == 0), stop=(k == 7)),
                         [HTb[i], WbB[ws][0]], [ppB])
                src = pp[:rows, 0:256]
                if kind in ("cq", "ck"):
                    fin, finB = norm4(src, ppB, rows, g64["c_gq" if kind == "cq" else "c_gk"])
                    if kind == "ck" and (u < 4 or smp):
                        dst = o_cks[496:512, csl] if smp else o_ck[u][:, csl]
                        S.op("pool", lambda e, dst=dst: e.dma_start(out=dst[:rows], in_=fin[:rows, :]), [finB], [], dma=True)
                    c = nxt("wkb", 2)
                    S.op("pool", lambda e: e.tensor_copy(wkb[c][:rows, 0:256], fin[:rows, :]), [finB], [wkbB[c]])
                    if kind == "ck":
                        to_T_and_store(wkb[c], wkbB[c], rows, kc_s[ui].rearrange("p (k t) -> p k t", k=8)[:, hp * 2:hp * 2 + 2, 0:rows], kcB[ui])
                    else:
                        to_T_and_store(wkb[c], wkbB[c], rows, qc_s[qi].rearrange("p (k t) -> p k t", k=8)[:, hp * 2:hp * 2 + 2, 0:rows], qcB[qi])
                else:
                    a = nxt("wk", 4)
                    S.op("act", lambda e: e.copy(wk[a][:rows, :], src), [ppB], [wkB[a]])
                    if u < 4 or smp:
                        dst = o_cvs[496:512, csl] if smp else o_cv[u][:, csl]
                        S.op("pool", lambda e, dst=dst: e.dma_start(out=dst[:rows], in_=wk[a][:rows, :]), [wkB[a]], [], dma=True)
                    c = nxt("wkb", 2)
                    S.op("dve", lambda e: e.tensor_scalar(wkb[c][:rows, 0:256], wk[a][:rows, :], valid[:rows, u:u + 1], None, ALU.mult), [wkB[a], cB], [wkbB[c]])
                    S.op("pool", lambda e: e.dma_start(out=vc_s[ui][:rows, csl], in_=wkb[c][:rows, 0:256]), [wkbB[c]], [vcB[ui]], dma=True)

    def cache_prep_c():
        S.op("sp", lambda e: e.dma_start(out=o_cks[0:496, :], in_=cc_k[16:512, :]), [], [], dma=True)
        S.op("sp", lambda e: e.dma_start(out=o_cvs[0:496, :], in_=cc_v[16:512, :]), [], [], dma=True)
        for t in range(4):
            ci = NQ + 1 + t
            s_ = nxt("stg", 3)
            S.op("sp", lambda e, t=t: e.dma_start(out=stg[s_][:, 0:1024], in_=cc_k[t * 128:(t + 1) * 128, :]), [], [stgB[s_]], dma=True)
            S.op("pool", lambda e: e.tensor_copy(xn[0], stg[s_][:, 0:1024]), [stgB[s_]], [xnB[0]])
            pt, ptB = getpt()
            for k in range(8):
                S.op("pe", lambda e, k=k: e.transpose(pt[:, k * 128:(k + 1) * 128], xn[0][:, k * 128:(k + 1) * 128], ident), [xnB[0], cB], [ptB])
            S.op("act", lambda e: e.copy(xn[1], pt), [ptB], [xnB[1]])
            S.op("pool", lambda e, ci=ci: e.dma_start(out=kc_s[ci], in_=xn[1]), [xnB[1]], [kcB[ci]], dma=True)
            s_ = nxt("stg", 3)
            S.op("sp", lambda e, t=t: e.dma_start(out=stg[s_][:, 0:1024], in_=cc_v[t * 128:(t + 1) * 128, :]), [], [stgB[s_]], dma=True)
            S.op("pool", lambda e: e.tensor_copy(xn[0], stg[s_][:, 0:1024]), [stgB[s_]], [xnB[0]])
            S.op("pool", lambda e, ci=ci: e.dma_start(out=vc_s[ci], in_=xn[0]), [xnB[0]], [vcB[ci]], dma=True)

    def band_attn(tiles):
        EB = [av(1024 * d, 1024, BF16).rearrange("p (h q) -> p h q", h=16) for d in range(2)]; EBB = Buf()
        QTe = av(2048, 256, BF16).rearrange("p (k t) -> p k t", k=4); QTo = av(2304, 256, BF16).rearrange("p (k t) -> p k t", k=4); QTB = Buf()
        QTeo = (QTe, QTo)
        kvc = [av(2560 + 512 * i, 512, BF16) for i in range(5)]; kvcB = [Buf() for _ in range(5)]
        Ec = [av(5120 + 512 * i, 512, BF16) for i in range(5)]; EcB = [[Buf(), Buf()] for _ in range(5)]
        ogc = av(7680, 512); ogcB = Buf()
        mgc = av(8192, 512, BF16); mgcB = Buf()
        S.op("sp", lambda e: e.dma_start(out=tabx[:, 0:257], in_=W["c_bias"][0]), [], [tabxB], dma=True)
        S.op("sp", lambda e: e.dma_start(out=stg[2][0:16, 256:257], in_=W["c_bias"][0][:, 256:257], allow_slow_non_contiguous=True), [], [stgB[2]], dma=True)
        S.op("dve", lambda e: e.tensor_copy(stg[2][0:16, 0:256], stg[2][0:16, 256:257].to_broadcast([16, 256])), [stgB[2]], [stgB[2]])
        S.op("sp", lambda e: e.dma_start(out=tabx[:, 257:513], in_=stg[2][0:16, 0:256]), [stgB[2]], [tabxB], dma=True)
        for h in range(16):
            S.op("sp", lambda e, h=h: e.dma_start(out=negb[:, h:h + 1], in_=tabx[h:h + 1, 300:301].partition_broadcast(128)), [tabxB], [negbB], dma=True)
        S.op("dve", lambda e: e.tensor_scalar(negb, negb, -1.0, None, ALU.mult), [negbB], [negbB])
        for h in range(16):
            s_ = nxt("stg", 2)
            for d in range(2):
                S.op("sp", lambda e, h=h, d=d: e.dma_start(out=stg[s_][:, d * 128:(d + 1) * 128], in_=bass.AP(tabx_t, h * 513 + 1 + 128 * d, [[1, 128], [1, 128]])),
                     [tabxB], [stgB[s_]], dma=True)
            pp, ppB = getps()
            S.op("pe", lambda e: e.matmul(pp[:, 0:256], cst[:, 520:648], stg[s_][:, 0:256], start=True, stop=True), [stgB[s_], cstB], [ppB])
            for d in range(2):
                S.op("act", lambda e, h=h, d=d: e.activation(EB[d][:, h, :], pp[:, d * 128:(d + 1) * 128], AF.Exp, bias=negb[:, h:h + 1]), [ppB, negbB], [EBB])
        S.op("pool", lambda e: e.tensor_tensor(EB[0], EB[0], maskA.unsqueeze(1).to_broadcast([128, 16, 128]), ALU.mult), [EBB, cB], [EBB])
        S.barrier()
        S.op("pool", lambda e: e.memset(QTe[64:128, :, :], 0.0), [], [QTB])
        S.op("pool", lambda e: e.memset(QTo[0:64, :, :], 0.0), [], [QTB])
        zc = {"z": 0, "b": 0}
        for (i, nq) in tiles:
            smp = (i == NQ)
            qi = NOWN if smp else i
            if smp:
                keyspec = [(NQ, 0)] + [(NQ + 1 + t, 1 if t == 3 else 2) for t in (3, 2, 1, 0)]
            else:
                keyspec = [(i + d, typ) for d, typ in zip(range(5), (0, 1, 2, 2, 4))]
            for half in range(2):
                src = qc_s[qi].rearrange("p (k t) -> p k t", k=8)[:, 4 * half:4 * half + 4, 0:nq]
                S.op("sp", lambda e, src=src: e.dma_start(out=QTe[0:64, :, 0:nq], in_=src[0:64]), [qcB[qi]], [QTB], dma=True)
                S.op("sp", lambda e, src=src: e.dma_start(out=QTo[64:128, :, 0:nq], in_=src[64:128]), [qcB[qi]], [QTB], dma=True)
                nkeys = len(keyspec)
                for d, (ui, typ) in enumerate(keyspec):
                    nk = 16 if (smp and d == 0) else 128
                    b = d
                    Kh = kvc[b][:, 0:512].rearrange("p (k t) -> p k t", k=4)
                    Vh = kvc[b][:, 512:1024]
                    ks = kc_s[ui].rearrange("p (k t) -> p k t", k=8)[:, 4 * half:4 * half + 4, 0:nk]
                    S.op("sp", lambda e, Kh=Kh, ks=ks, nk=nk: e.dma_start(out=Kh[:, :, 0:nk], in_=ks), [kcB[ui]], [kvcB[b]], dma=True)
                    S.op("sp", lambda e, Vh=Vh, ui=ui, nk=nk: e.dma_start(out=Vh[:nk, :], in_=vc_s[ui][:nk, half * 512:(half + 1) * 512]), [vcB[ui]], [kvcB[b]], dma=True)
                    zi = zc["z"]; zc["z"] = 1 - zi
                    z = Q[zi]; zB_ = [PSB[2 * zi], PSB[2 * zi + 1]]
                    for m in range(8):
                        S.op("pe", lambda e, m=m: e.matmul(z[:nk, m * nq:(m + 1) * nq], Kh[:, m // 2, 0:nk], QTeo[m % 2][:, m // 2, 0:nq], start=True, stop=True),
                             [kvcB[b], QTB], zB_)
                    E = Ec[d]
                    if 8 * nq > 512:
                        for hh in range(2):
                            S.op("act", lambda e, hh=hh, E=E: e.activation(E[:nk, hh * 512:(hh + 1) * 512], z[:nk, hh * 512:(hh + 1) * 512], AF.Exp), [zB_[hh]], [EcB[d][hh]])
                    else:
                        S.op("act", lambda e, E=E: e.activation(E[:nk, 0:8 * nq], z[:nk, 0:8 * nq], AF.Exp), zB_, EcB[d])
                    e3 = E[:nk, 0:8 * nq].rearrange("p (m q) -> p m q", m=8)
                    if typ in (0, 1):
                        S.op("pool", lambda e, typ=typ, e3=e3, nk=nk: e.tensor_tensor(e3, e3, EB[typ][:nk, 8 * half:8 * half + 8, 0:nq], ALU.mult), EcB[d] + [EBB], EcB[d])
                    elif typ == 4:
                        S.op("pool", lambda e, e3=e3, nk=nk: e.tensor_tensor(e3, e3, mask4b[:nk, :nq].unsqueeze(1).to_broadcast([nk, 8, nq]), ALU.mult), EcB[d] + [cB], EcB[d])
                for m in range(8):
                    for d, (ui, typ) in enumerate(keyspec):
                        nk = 16 if (smp and d == 0) else 128
                        S.op("pe", lambda e, m=m, d=d, nk=nk: e.matmul(Q[2][:nq, m * 64:(m + 1) * 64], Ec[d][:nk, m * nq:(m + 1) * nq], kvc[d][:nk, 512 + m * 64:512 + (m + 1) * 64],
                                                                       start=(d == 0), stop=(d == nkeys - 1)), EcB[d] + [kvcB[d]], [PSB[4]])
                for m in range(8):
                    for d, (ui, typ) in enumerate(keyspec):
                        nk = 16 if (smp and d == 0) else 128
                        if smp:
                            vcol = validb[:nk, SMP:SMP + 1] if d == 0 else ones[:nk, 0:1]
                        else:
                            vcol = validb[:nk, ui:ui + 1]
                        S.op("pe", lambda e, m=m, d=d, nk=nk, vcol=vcol: e.matmul(P1[:nq, m:m + 1], Ec[d][:nk, m * nq:(m + 1) * nq], vcol, start=(d == 0), stop=(d == nkeys - 1)),
                             EcB[d] + [cB], [P1B])
                S.op("act", lambda e: e.copy(ogc[:nq, :], Q[2][:nq, 0:512]), [PSB[4]], [ogcB])
                rden = small["rden"]
                S.op("dve", lambda e: e.tensor_scalar(rden[:nq, :], P1[:nq, 0:8], 1e-30, None, ALU.add), [P1B], [smB["rden"]])
                S.op("dve", lambda e: e.reciprocal(rden[:nq, :], rden[:nq, :]), [smB["rden"]], [smB["rden"]])
                S.op("dve", lambda e: e.tensor_tensor(mgc[:nq, half * 512:(half + 1) * 512].rearrange("p (h d) -> p h d", h=8), ogc[:nq, :].rearrange("p (h d) -> p h d", h=8),
                                                      rden[:nq, :].unsqueeze(2).to_broadcast([nq, 8, 64]), ALU.mult), [ogcB, smB["rden"]], [mgcB])
            pt, ptB = getpt()
            for k in range(8):
                S.op("pe", lambda e, k=k: e.transpose(pt[:, k * 128:k * 128 + nq], mgc[:nq, k * 128:(k + 1) * 128], ident[:nq, :nq]), [mgcB, cB], [ptB])
            S.op("act", lambda e: e.copy(HT[:, :, i * 128:i * 128 + nq], pt.rearrange("p (k t) -> p k t", k=8)[:, :, 0:nq]), [ptB], [HTb[i]])

    nsb = int(os.environ.get("KNSB", "99"))
    older = [list(range(a, min(a + 16, NT))) for a in range(NQ, NT, 16)]
    if STAGE >= 2:
        cache_prep()
        lambda_prep()
    for sbk in older[:nsb]:
        front0(sbk, False)
    tiles, uu = front0(list(range(NQ)), True)

    if STAGE >= 2:
        S.barrier()
        rr["ptn"] = 1
        if not os.environ.get("KNOATT"):
            attn_l0()
        S.barrier()
        rr["ptn"] = 2
        linear_residual(W["ab_w_out"][0], None, tiles, 1.0)
    if STAGE >= 3:
        S.barrier()
        mem_prep(0)
        S.barrier()
        mem_attn(0, tiles)
    if STAGE >= 4:
        S.barrier()
        ffn(0, 2, tiles)

    tiles17 = [(i, 128) for i in range(NOWN)] + [(NQ, 16)]
    if STAGE >= 5:
        S.barrier()
        ffn(1, 1, tiles)
    if STAGE >= 6:
        rstd_tiles(tiles)
        for (i, rows) in tiles:
            norm_to_HT(i, rows)
        cache_prep_c()
        proj_c(tiles, uu)
        S.barrier()
        rr["ptn"] = 1
        band_attn(tiles17)
        rr["ptn"] = 2
        S.barrier()
        linear_residual(W["c_w_out"][0], None, tiles17, 1.0)
    if STAGE >= 7:
        S.barrier()
        mem_prep(1)
        S.barrier()
        mem_attn(1, tiles17)
    if STAGE >= 8:
        S.barrier()
        ffn(1, 2, tiles17)
    if True:
        for i in range(NOWN):
            S.op("pool", lambda e, i=i: e.dma_start(out=o_y[i], in_=X[:, i, :]), [Xb[i]], [], dma=True)
        S.op("pool", lambda e: e.dma_start(out=o_ys, in_=X[0:16, NQ, :]), [Xb[NQ]], [], dma=True)

    S.barrier()
    print("ops", S.nops, "sems", S.nsem, "sbuf_left", nc.sbuf_bytes_remaining)
    return nc


_NC = None


def _consts():
    c = np.zeros((128, 776), np.float32)
    c[:, 520:648] = np.eye(128)[::-1]
    c[:, 648:776] = ((np.arange(128)[None, :] // 64) <= (np.arange(128)[:, None] // 64)).astype(np.float32)
    c[:, 512:520] = ((500000.0 ** (-2.0 * np.arange(8) / 16.0)).astype(np.float32).astype(np.float64) / (2 * np.pi))[None]
    c[:, 0:128] = np.eye(128)
    j = np.arange(128)[:, None]; s = np.arange(128)[None, :]
    c[:, 128:256] = -(j >= s).astype(np.float32)
    c[:, 256:384] = ((j // 64) <= (s // 64)).astype(np.float32)
    c[:, 384:512] = (j < s).astype(np.float32)
    return c


def kernel(**inp):
    global _NC
    if _NC is None:
        _NC = build()
    nc = _NC
    f = lambda a: np.ascontiguousarray(np.asarray(a, dtype=np.float32))
    xprompt = f(inp["x_prompt"])[0].reshape(128, 128, D)
    in_maps = []
    wnames = ["ffn1_g", "ffn1_wg", "ffn1_wu", "ffn1_wd", "ffn2_g", "ffn2_wg", "ffn2_wu", "ffn2_wd", "mix_g", "ab_w_in", "ab_w_out",
              "a_gq", "a_gk", "a_lq1", "a_lk1", "a_lq2", "a_lk2", "a_subln_g", "c_w_in", "c_w_out", "c_gq", "c_gk", "c_bias",
              "mem_g_x", "mem_g_m", "mem_wq", "mem_wk", "mem_wv", "mem_wo", "mem_gq", "mem_gk"]
    wts = {n: f(inp[n]) for n in wnames}
    cst = _consts()
    for c in range(8):
        xpc = np.zeros((NT, 128, D), np.float32)
        pos = np.zeros((128, NT + 1), np.float32)
        val = np.zeros((128, NT + 1), np.float32)
        for u in range(NT):
            g = 16 * c + 15 - u
            if g >= 0:
                xpc[u] = xprompt[g]
                pos[:, u] = g * 128 + np.arange(128)
                val[:, u] = 1.0
        pos[:16, NT] = 1024 + np.arange(16)
        val[:16, NT] = 1.0
        m = {"xp": xpc, "xs": f(inp["x_sample"])[c], "pos": pos, "valid": val, "consts": cst,
             "ca_k": f(inp["cache_a_k"])[0, c].reshape(1024, 512), "ca_v": f(inp["cache_a_v"])[0, c].reshape(1024, 512),
             "cb_k": f(inp["cache_b_k"])[0, c].reshape(1024, 512), "cb_v": f(inp["cache_b_v"])[0, c].reshape(1024, 512),
             "cc_k": f(inp["cache_c_k"])[0, c].reshape(512, 1024), "cc_v": f(inp["cache_c_v"])[0, c].reshape(512, 1024),
             "cm_k": f(inp["cache_mem_k"])[:, c].reshape(2, 256, 1024), "cm_v": f(inp["cache_mem_v"])[:, c].reshape(2, 256, 1024),
             "memp": f(inp["mem_prompt"])[0]}
        m.update(wts)
        in_maps.append(m)
    res = run_bass_kernel_spmd(nc, in_maps, core_ids=list(range(8))).results

    def gat(name, width):
        out = np.zeros((128, 128, width), np.float32)
        for c in range(8):
            for u in range(NOWN):
                out[16 * c + 15 - u] = res[c][name][u]
        return out.reshape(16384, width)
    y_prompt = gat("o_y", D)[None]
    y_sample = np.stack([res[c]["o_ys"] for c in range(8)])
    a_k_p = gat("o_ak", 512).reshape(1, 1, 16384, 8, 64)
    a_v_p = gat("o_av", 512).reshape(1, 1, 16384, 4, 128)
    b_k_p = gat("o_bk", 512).reshape(1, 1, 16384, 8, 64)
    b_v_p = gat("o_bv", 512).reshape(1, 1, 16384, 8, 64)
    c_k_p = np.concatenate([res[7]["o_ck"][3 - t] for t in range(4)], 0).reshape(1, 1, 512, 16, 64)
    c_v_p = np.concatenate([res[7]["o_cv"][3 - t] for t in range(4)], 0).reshape(1, 1, 512, 16, 64)
    mem_k_p = res[0]["o_mk"].reshape(2, 1, 256, 4, 256)
    mem_v_p = res[0]["o_mv"].reshape(2, 1, 256, 4, 256)
    st = lambda n, shp: np.stack([res[c][n] for c in range(8)]).reshape(shp)
    a_k_s = st("o_aks", (1, 8, 16, 8, 64)); a_v_s = st("o_avs", (1, 8, 16, 4, 128))
    b_k_s = st("o_bks", (1, 8, 16, 8, 64)); b_v_s = st("o_bvs", (1, 8, 16, 8, 64))
    c_k_s = st("o_cks", (1, 8, 512, 16, 64)); c_v_s = st("o_cvs", (1, 8, 512, 16, 64))
    return (y_prompt, y_sample, a_k_p, a_v_p, b_k_p, b_v_p, c_k_p, c_v_p, mem_k_p, mem_v_p,
            a_k_s, a_v_s, b_k_s, b_v_s, c_k_s, c_v_s)
```
